# Optimizing a Trainium2 kernel written in Bass

```python
import math
import jax, jax.numpy as jnp
from jax import lax
import numpy as np

D_MODEL = 1024
BATCH = 4
SEQ = 8192
DEPTH = 4

GRID_W = 64
CTX_LEN = 256
ATT_HEADS = 4
ATT_KV_HEADS = 2
ATT_GROUP = ATT_HEADS // ATT_KV_HEADS
ATT_HEAD_DIM = 64
ATT_WIDTH = ATT_HEADS * ATT_HEAD_DIM
ATT_KV_WIDTH = ATT_KV_HEADS * ATT_HEAD_DIM
Q_BLOCK = 128
ROPE_THETA = 10000.0
GLA_HEADS = 4
GLA_DK = 64
GLA_DV = 128
GLA_K_WIDTH = GLA_HEADS * GLA_DK
GLA_V_WIDTH = GLA_HEADS * GLA_DV
GLA_GATE_RANK = 16
GLA_GATE_NORM = 16.0
GLA_CHUNK = 64
S5_GROUPS = 16
S5_GROUP_CH = 16
S5_WIDTH = S5_GROUPS * S5_GROUP_CH
S5_STATE = 64
S5_DT_MIN = 0.001
S5_DT_MAX = 0.1
D_FF = 2816
N_BRANCH = 3
EPS = 1e-6

IN_SPLITS = (ATT_WIDTH, ATT_KV_WIDTH, ATT_KV_WIDTH,
             GLA_K_WIDTH, GLA_K_WIDTH, GLA_V_WIDTH, GLA_V_WIDTH,
             GLA_GATE_RANK, GLA_GATE_RANK,
             S5_WIDTH,
             N_BRANCH * D_MODEL)
D_IN = sum(IN_SPLITS)
IN_SPLIT_POINTS = tuple(int(v) for v in np.cumsum(IN_SPLITS)[:-1])

kernel_name = "hybrid_gla_s5_gqa_prefix_dit"


def rms_norm(x, gain):
    xf = x.astype(jnp.float32)
    xf = xf * lax.rsqrt(jnp.mean(xf * xf, axis=-1, keepdims=True) + EPS)
    return (xf * gain.astype(jnp.float32)).astype(x.dtype)


def modulate(h, shift, scale):
    return h * (1.0 + scale) + shift


def axial_rope_tables(n_tokens):
    rows = n_tokens // GRID_W
    row = jnp.repeat(jnp.arange(rows, dtype=jnp.float32), GRID_W)
    col = jnp.tile(jnp.arange(GRID_W, dtype=jnp.float32), rows)
    n_freq = ATT_HEAD_DIM // 4
    inv_freq = ROPE_THETA ** (-jnp.arange(n_freq, dtype=jnp.float32) / n_freq)
    ang = jnp.stack([row[:, None] * inv_freq, col[:, None] * inv_freq], axis=1)
    return jnp.cos(ang), jnp.sin(ang)


def apply_axial_rope(x, cos, sin):
    bsz, n_t, nh, hd = x.shape
    xr = x.astype(jnp.float32).reshape(bsz, n_t, nh, 2, 2, hd // 4)
    x1, x2 = xr[..., 0, :], xr[..., 1, :]
    cs, sn = cos[None, :, None], sin[None, :, None]
    out = jnp.stack([x1 * cs - x2 * sn, x2 * cs + x1 * sn], axis=-2)
    return out.reshape(bsz, n_t, nh, hd).astype(x.dtype)


def blocked_gqa(q, k, v):
    bsz, n_q = q.shape[0], q.shape[1]
    n_blk = n_q // Q_BLOCK
    qb = q.reshape(bsz, n_blk, Q_BLOCK, ATT_KV_HEADS, ATT_GROUP, ATT_HEAD_DIM)
    qb = jnp.moveaxis(qb, 1, 0)
    scale = ATT_HEAD_DIM ** -0.5

    def attend(q_blk):
        s = jnp.einsum("bqkgd,btkd->bkgqt", q_blk, k, preferred_element_type=jnp.float32) * scale
        p = jax.nn.softmax(s, axis=-1).astype(v.dtype)
        return jnp.einsum("bkgqt,btkd->bqkgd", p, v)

    o = lax.map(attend, qb)
    return jnp.moveaxis(o, 0, 1).reshape(bsz, n_q, ATT_WIDTH)


def gla_chunk_scan(q, k, v, g, s0):
    bsz, nh, n_t, _ = q.shape
    n_c = n_t // GLA_CHUNK

    def chunks(a):
        return a.astype(jnp.float32).reshape(bsz, nh, n_c, GLA_CHUNK, a.shape[-1])

    qc, kc, vc, gc = chunks(q), chunks(k), chunks(v), chunks(g)
    b = jnp.cumsum(gc, axis=3)
    b_end = b[:, :, :, -1:, :]
    q_in = qc * jnp.exp(b)
    k_in = kc * jnp.exp(-b)
    k_end = kc * jnp.exp(b_end - b)
    causal = jnp.tril(jnp.ones((GLA_CHUNK, GLA_CHUNK), dtype=bool))
    att = jnp.where(causal, jnp.einsum("bhnid,bhnjd->bhnij", q_in, k_in), 0.0)
    o_intra = jnp.einsum("bhnij,bhnje->bhnie", att, vc)
    kv_chunk = jnp.einsum("bhnjd,bhnje->nbhde", k_end, vc)
    decay = jnp.moveaxis(jnp.exp(b_end[:, :, :, 0, :]), 2, 0)

    def step(state, inp):
        dec, kv = inp
        return dec[..., None] * state + kv, state

    s_final, s_before = lax.scan(step, s0, (decay, kv_chunk))
    o_inter = jnp.einsum("bhnid,nbhde->bhnie", q_in, s_before)
    o = (o_intra + o_inter).reshape(bsz, nh, n_t, GLA_DV)
    return o.astype(q.dtype), s_final


def gla_bidir(q, k, v, g_f, g_b, s0_f, s0_b):
    o_f, s_f = gla_chunk_scan(q, k, v, g_f, s0_f)
    flip = lambda a: jnp.flip(a, axis=2)
    o_b, s_b = gla_chunk_scan(flip(q), flip(k), flip(v), flip(g_b), s0_b)
    return o_f + flip(o_b), s_f, s_b


def s5_discretize(a_re, a_im, log_dt, b_re, b_im):
    f32 = jnp.float32
    a_re, a_im, b_re, b_im = a_re.astype(f32), a_im.astype(f32), b_re.astype(f32), b_im.astype(f32)
    dt = jnp.exp(log_dt.astype(f32))[:, None]
    mag = jnp.exp(a_re * dt)
    ab_re, ab_im = mag * jnp.cos(a_im * dt), mag * jnp.sin(a_im * dt)
    den = a_re * a_re + a_im * a_im
    f_re = ((ab_re - 1.0) * a_re + ab_im * a_im) / den
    f_im = (ab_im * a_re - (ab_re - 1.0) * a_im) / den
    bb_re = f_re[..., None] * b_re - f_im[..., None] * b_im
    bb_im = f_re[..., None] * b_im + f_im[..., None] * b_re
    return ab_re, ab_im, bb_re, bb_im


def complex_affine_combine(e1, e2):
    a1r, a1i, b1r, b1i = e1
    a2r, a2i, b2r, b2i = e2
    return (a2r * a1r - a2i * a1i, a2r * a1i + a2i * a1r,
            a2r * b1r - a2i * b1i + b2r, a2r * b1i + a2i * b1r + b2i)


def s5_direction(u, lp, d, s0):
    ab_re, ab_im, bb_re, bb_im = s5_discretize(lp["s5_a_re"][d], lp["s5_a_im"][d], lp["s5_log_dt"][d],
                                               lp["s5_b_re"][d], lp["s5_b_im"][d])
    bu_re = jnp.einsum("btgh,gph->btgp", u, bb_re)
    bu_im = jnp.einsum("btgh,gph->btgp", u, bb_im)
    s0_re, s0_im = s0
    bu_re = bu_re.at[:, 0].add(ab_re * s0_re - ab_im * s0_im)
    bu_im = bu_im.at[:, 0].add(ab_re * s0_im + ab_im * s0_re)
    a_re = jnp.broadcast_to(ab_re, bu_re.shape)
    a_im = jnp.broadcast_to(ab_im, bu_im.shape)
    _, _, s_re, s_im = lax.associative_scan(complex_affine_combine, (a_re, a_im, bu_re, bu_im), axis=1)
    c_re = lp["s5_c_re"][d].astype(jnp.float32)
    c_im = lp["s5_c_im"][d].astype(jnp.float32)
    y = jnp.einsum("btgp,ghp->btgh", s_re, c_re) - jnp.einsum("btgp,ghp->btgh", s_im, c_im)
    return y, (s_re[:, -1], s_im[:, -1])


def s5_bidir(u, lp, init_f, init_b):
    f32 = jnp.float32
    bsz, n_t, _ = u.shape
    uf = u.astype(f32)
    ug = uf.reshape(bsz, n_t, S5_GROUPS, S5_GROUP_CH)
    y_f, st_f = s5_direction(ug, lp, 0, init_f)
    y_b, st_b = s5_direction(ug[:, ::-1], lp, 1, init_b)
    y = (y_f + y_b[:, ::-1]).reshape(bsz, n_t, S5_WIDTH) + lp["s5_d"].astype(f32) * uf
    y = jax.nn.gelu(y)
    y = y * jax.nn.sigmoid(y @ lp["s5_glu_w"].astype(f32) + lp["s5_glu_b"].astype(f32))
    return y.astype(u.dtype), st_f, st_b


def mixer_inputs(h, lp, rope):
    bsz, n_t, _ = h.shape
    (aq, ak, av, gq, gk, gv, gr, glf, glb, su, bg) = jnp.split(h @ lp["w_in"], IN_SPLIT_POINTS, axis=-1)
    aq = rms_norm(aq.reshape(bsz, n_t, ATT_HEADS, ATT_HEAD_DIM), lp["q_norm"])
    ak = rms_norm(ak.reshape(bsz, n_t, ATT_KV_HEADS, ATT_HEAD_DIM), lp["k_norm"])
    av = av.reshape(bsz, n_t, ATT_KV_HEADS, ATT_HEAD_DIM)
    if rope is not None:
        aq = apply_axial_rope(aq, rope[0], rope[1])
        ak = apply_axial_rope(ak, rope[0], rope[1])

    def heads(a, dim):
        return a.reshape(bsz, n_t, GLA_HEADS, dim).transpose(0, 2, 1, 3)

    def log_decay(low, d):
        z = (low @ lp["gla_gate_w"][d] + lp["gla_gate_b"][d]).astype(jnp.float32)
        return heads(jax.nn.log_sigmoid(z) / GLA_GATE_NORM, GLA_DK)

    return dict(att_q=aq, att_k=ak, att_v=av,
                gla_q=heads(gq, GLA_DK) * (GLA_DK ** -0.5), gla_k=heads(gk, GLA_DK), gla_v=heads(gv, GLA_DV),
                gla_r=gr, gla_gf=log_decay(glf, 0), gla_gb=log_decay(glb, 1),
                s5_u=su, gates=bg)


def merge_branches(o_att, o_gla, gla_r, o_s5, gates, lp):
    bsz, n_t, _ = o_att.shape
    o_gla = rms_norm(o_gla.transpose(0, 2, 1, 3), lp["gla_out_norm"]).reshape(bsz, n_t, GLA_V_WIDTH)
    o_gla = o_gla * jax.nn.silu(gla_r)
    g_att, g_gla, g_s5 = jnp.split(jax.nn.sigmoid(gates), N_BRANCH, axis=-1)
    merged = (g_att * (o_att @ lp["w_br_att"]) + g_gla * (o_gla @ lp["w_br_gla"])
              + g_s5 * (o_s5 @ lp["w_br_s5"]))
    return merged @ lp["w_out"]


def token_mixer(h_ctx, h_lat, lp, rope, with_ctx_out):
    ic = mixer_inputs(h_ctx, lp, None)
    il = mixer_inputs(h_lat, lp, rope)
    bsz = h_lat.shape[0]
    zero_gla = jnp.zeros((bsz, GLA_HEADS, GLA_DK, GLA_DV), jnp.float32)
    zero_s5 = (jnp.zeros((bsz, S5_GROUPS, S5_STATE), jnp.float32),
               jnp.zeros((bsz, S5_GROUPS, S5_STATE), jnp.float32))
    o_gla_c, gla_st_f, gla_st_b = gla_bidir(ic["gla_q"], ic["gla_k"], ic["gla_v"], ic["gla_gf"], ic["gla_gb"],
                                            zero_gla, zero_gla)
    o_s5_c, s5_st_f, s5_st_b = s5_bidir(ic["s5_u"], lp, zero_s5, zero_s5)
    k_all = jnp.concatenate([ic["att_k"], il["att_k"]], axis=1)
    v_all = jnp.concatenate([ic["att_v"], il["att_v"]], axis=1)
    o_att_l = blocked_gqa(il["att_q"], k_all, v_all)
    o_gla_l, _, _ = gla_bidir(il["gla_q"], il["gla_k"], il["gla_v"], il["gla_gf"], il["gla_gb"],
                              gla_st_f, gla_st_b)
    o_s5_l, _, _ = s5_bidir(il["s5_u"], lp, s5_st_f, s5_st_b)
    y_lat = merge_branches(o_att_l, o_gla_l, il["gla_r"], o_s5_l, il["gates"], lp)
    if not with_ctx_out:
        return None, y_lat
    o_att_c = blocked_gqa(ic["att_q"], ic["att_k"], ic["att_v"])
    y_ctx = merge_branches(o_att_c, o_gla_c, ic["gla_r"], o_s5_c, ic["gates"], lp)
    return y_ctx, y_lat


def dwconv3(x, w, b):
    xp = jnp.pad(x, ((0, 0), (1, 1), (0, 0)))
    return xp[:, :-2] * w[0] + xp[:, 1:-1] * w[1] + xp[:, 2:] * w[2] + b


def conv_ffn(h, lp):
    u = dwconv3(h @ lp["ffn_up"], lp["ffn_conv_w"], lp["ffn_conv_b"])
    a, v = jnp.split(u, 2, axis=-1)
    return (jax.nn.silu(a) * v) @ lp["ffn_down"]


def setup_inputs(seed: int = 0) -> dict:
    key = jax.random.key(seed)
    ks = iter(jax.random.split(key, 48))
    f32 = jnp.float32
    L = DEPTH

    def nrm(shape, scale):
        return jax.random.normal(next(ks), shape, f32) * scale

    def gain(shape):
        return 1.0 + nrm(shape, 0.05)

    n_idx = jnp.arange(S5_STATE, dtype=f32)
    s5_shape = (L, 2, S5_GROUPS, S5_STATE)
    return {
        "x": nrm((BATCH, SEQ, D_MODEL), 1.0),
        "c": nrm((BATCH, D_MODEL), 1.0),
        "ctx": nrm((BATCH, CTX_LEN, D_MODEL), 1.0),
        "c_ctx": nrm((D_MODEL,), 1.0),
        "ada_w": nrm((L, D_MODEL, 6 * D_MODEL), D_MODEL ** -0.5),
        "ada_b": nrm((L, 6 * D_MODEL), 0.02),
        "norm_mix_pre": gain((L, D_MODEL)),
        "norm_mix_post": gain((L, D_MODEL)),
        "norm_ffn_pre": gain((L, D_MODEL)),
        "norm_ffn_post": gain((L, D_MODEL)),
        "w_in": nrm((L, D_MODEL, D_IN), D_MODEL ** -0.5),
        "q_norm": gain((L, ATT_HEAD_DIM)),
        "k_norm": gain((L, ATT_HEAD_DIM)),
        "gla_gate_w": nrm((L, 2, GLA_GATE_RANK, GLA_K_WIDTH), GLA_GATE_RANK ** -0.5),
        "gla_gate_b": nrm((L, 2, GLA_K_WIDTH), 0.1),
        "gla_out_norm": gain((L, GLA_DV)),
        "s5_a_re": -0.5 * jnp.exp(nrm(s5_shape, 0.01)),
        "s5_a_im": math.pi * n_idx + nrm(s5_shape, 0.01),
        "s5_log_dt": jax.random.uniform(next(ks), (L, 2, S5_GROUPS), f32,
                                        math.log(S5_DT_MIN), math.log(S5_DT_MAX)),
        "s5_b_re": nrm((L, 2, S5_GROUPS, S5_STATE, S5_GROUP_CH), (2.0 * S5_GROUP_CH) ** -0.5),
        "s5_b_im": nrm((L, 2, S5_GROUPS, S5_STATE, S5_GROUP_CH), (2.0 * S5_GROUP_CH) ** -0.5),
        "s5_c_re": nrm((L, 2, S5_GROUPS, S5_GROUP_CH, S5_STATE), S5_STATE ** -0.5),
        "s5_c_im": nrm((L, 2, S5_GROUPS, S5_GROUP_CH, S5_STATE), S5_STATE ** -0.5),
        "s5_d": nrm((L, S5_WIDTH), 1.0),
        "s5_glu_w": nrm((L, S5_WIDTH, S5_WIDTH), S5_WIDTH ** -0.5),
        "s5_glu_b": nrm((L, S5_WIDTH), 0.02),
        "w_br_att": nrm((L, ATT_WIDTH, D_MODEL), ATT_WIDTH ** -0.5),
        "w_br_gla": nrm((L, GLA_V_WIDTH, D_MODEL), GLA_V_WIDTH ** -0.5),
        "w_br_s5": nrm((L, S5_WIDTH, D_MODEL), S5_WIDTH ** -0.5),
        "w_out": nrm((L, D_MODEL, D_MODEL), D_MODEL ** -0.5),
        "ffn_up": nrm((L, D_MODEL, 2 * D_FF), D_MODEL ** -0.5),
        "ffn_conv_w": nrm((L, 3, 2 * D_FF), 3.0 ** -0.5),
        "ffn_conv_b": nrm((L, 2 * D_FF), 0.02),
        "ffn_down": nrm((L, D_FF, D_MODEL), D_FF ** -0.5),
    }


def reference(x, c, ctx, c_ctx, ada_w, ada_b, norm_mix_pre, norm_mix_post, norm_ffn_pre, norm_ffn_post,
              w_in, q_norm, k_norm, gla_gate_w, gla_gate_b, gla_out_norm,
              s5_a_re, s5_a_im, s5_log_dt, s5_b_re, s5_b_im, s5_c_re, s5_c_im, s5_d, s5_glu_w, s5_glu_b,
              w_br_att, w_br_gla, w_br_s5, w_out, ffn_up, ffn_conv_w, ffn_conv_b, ffn_down):
    rope = axial_rope_tables(x.shape[1])
    for i in range(DEPTH):
        last = i == DEPTH - 1
        lp = {
            "w_in": w_in[i], "q_norm": q_norm[i], "k_norm": k_norm[i],
            "gla_gate_w": gla_gate_w[i], "gla_gate_b": gla_gate_b[i], "gla_out_norm": gla_out_norm[i],
            "s5_a_re": s5_a_re[i], "s5_a_im": s5_a_im[i], "s5_log_dt": s5_log_dt[i],
            "s5_b_re": s5_b_re[i], "s5_b_im": s5_b_im[i], "s5_c_re": s5_c_re[i], "s5_c_im": s5_c_im[i],
            "s5_d": s5_d[i], "s5_glu_w": s5_glu_w[i], "s5_glu_b": s5_glu_b[i],
            "w_br_att": w_br_att[i], "w_br_gla": w_br_gla[i], "w_br_s5": w_br_s5[i], "w_out": w_out[i],
            "ffn_up": ffn_up[i], "ffn_conv_w": ffn_conv_w[i], "ffn_conv_b": ffn_conv_b[i],
            "ffn_down": ffn_down[i],
        }
        m_lat = jnp.split((jax.nn.silu(c) @ ada_w[i] + ada_b[i])[:, None, :], 6, axis=-1)
        m_ctx = jnp.split((jax.nn.silu(c_ctx) @ ada_w[i] + ada_b[i])[None, None, :], 6, axis=-1)

        h_lat = modulate(rms_norm(x, norm_mix_pre[i]), m_lat[0], m_lat[1])
        h_ctx = modulate(rms_norm(ctx, norm_mix_pre[i]), m_ctx[0], m_ctx[1])
        y_ctx, y_lat = token_mixer(h_ctx, h_lat, lp, rope, not last)
        x = x + m_lat[2] * rms_norm(y_lat, norm_mix_post[i])
        h_lat = modulate(rms_norm(x, norm_ffn_pre[i]), m_lat[3], m_lat[4])
        x = x + m_lat[5] * rms_norm(conv_ffn(h_lat, lp), norm_ffn_post[i])

        if not last:
            ctx = ctx + m_ctx[2] * rms_norm(y_ctx, norm_mix_post[i])
            h_ctx = modulate(rms_norm(ctx, norm_ffn_pre[i]), m_ctx[3], m_ctx[4])
            ctx = ctx + m_ctx[5] * rms_norm(conv_ffn(h_ctx, lp), norm_ffn_post[i])
    return x
```

```python
import math
from contextlib import ExitStack
import numpy as np
import ml_dtypes
import concourse.bass as bass
import concourse.mybir as mybir
from concourse.ap import AP
from concourse.bass_utils import run_bass_kernel_spmd

F32 = mybir.dt.float32
BF16 = mybir.dt.bfloat16
I32 = mybir.dt.int32
AF = mybir.ActivationFunctionType
ALU = mybir.AluOpType
AX = mybir.AxisListType

D = 1024
KC = 8
DIN = 5408
DFF = 2816
NA = 2336
EPS = 1e-6
PI = math.pi
STOP_AFTER = ''


class Buf:
    __slots__ = ("name", "writers", "pws", "readers", "dsem", "dcount", "key")

    def __init__(self, name):
        self.name = name
        self.writers = {}
        self.pws = {}
        self.readers = {}
        self.dsem = None
        self.dcount = 0
        self.key = None


class DSem:
    __slots__ = ("sem", "key", "count", "sw")

    def __init__(self, sem, key, sw):
        self.sem, self.key, self.count, self.sw = sem, key, 0, sw


class Eng:
    def __init__(self, name, eng, sem, is_pe=False):
        self.name, self.eng, self.sem, self.count, self.seen, self.is_pe = name, eng, sem, 0, {}, is_pe


class Sched:
    def __init__(self, nc):
        self.nc = nc
        self.sems = {}
        self.pe = Eng("pe", nc.tensor, nc.alloc_semaphore("s_pe"), True)
        self.dve = Eng("dve", nc.vector, nc.alloc_semaphore("s_dve"))
        self.act = Eng("act", nc.scalar, nc.alloc_semaphore("s_act"))
        self.pool = Eng("pool", nc.gpsimd, nc.alloc_semaphore("s_pool"))
        self.sp = Eng("sp", nc.sync, nc.alloc_semaphore("s_sp"))
        self.engs = [self.pe, self.dve, self.act, self.pool, self.sp]
        for e in self.engs:
            self.sems[e.name] = e.sem
        self.bufs = {}
        self.dkeys = {}
        self.free_dsems = {True: [], False: []}
        self.ccount = {}
        self.ninst = 0
        self.nwait = 0

    def buf(self, name):
        b = self.bufs.get(name)
        if b is None:
            b = Buf(name)
            self.bufs[name] = b
        return b

    def _deps(self, reads, writes, pwrites):
        deps = {}

        def add(dd):
            for k, v in dd.items():
                if deps.get(k, 0) < v:
                    deps[k] = v
        for b in reads:
            add(b.writers)
            add(b.pws)
        for b in writes:
            add(b.writers)
            add(b.pws)
            add(b.readers)
        for b in pwrites:
            add(b.writers)
            add(b.readers)
        return deps

    def _wait(self, e, deps):
        for k, v in deps.items():
            if k == e.name and e.is_pe:
                continue
            if e.seen.get(k, 0) < v and k in self.dkeys:
                v = self.dkeys[k].count
            if e.seen.get(k, 0) < v:
                e.eng.wait_ge(self.sems[k], v)
                e.seen[k] = v
                self.nwait += 1

    def _post(self, ev, reads, writes, pwrites):
        for b in reads:
            if b.readers.get(ev[0], 0) < ev[1]:
                b.readers[ev[0]] = ev[1]
        for b in writes:
            b.writers = {ev[0]: ev[1]}
            b.pws = {}
            b.readers = {}
        for b in pwrites:
            b.pws[ev[0]] = ev[1]

    def op(self, e, fn, reads=(), writes=(), pwrites=()):
        self._wait(e, self._deps(reads, writes, pwrites))
        inst = fn()
        e.count += 1
        inst.then_inc(e.sem, 1)
        self.ninst += 1
        self._post((e.name, e.count), reads, writes, pwrites)
        return inst

    def dma(self, e, out, in_, reads=(), writes=(), pwrites=(), sbuf=None, **kw):
        if sbuf is None:
            sbuf = (list(writes) + list(pwrites) + list(reads))[0]
        sw = e is self.pool
        if sbuf.dsem is None:
            if self.free_dsems[sw]:
                sbuf.dsem = self.free_dsems[sw].pop()
            else:
                key = "D%d" % len(self.sems)
                sbuf.dsem = DSem(self.nc.alloc_semaphore("d%d" % len(self.sems)), key, sw)
                self.sems[key] = sbuf.dsem.sem
                self.dkeys[key] = sbuf.dsem
        ds = sbuf.dsem
        assert ds.sw == sw, ("mixed SW/HW DGE on one buffer semaphore", sbuf.name)
        self._wait(e, self._deps(reads, writes, pwrites))
        inst = e.eng.dma_start(out=out, in_=in_, **kw)
        ds.count += 16
        inst.then_inc(ds.sem, 16)
        self.ninst += 1
        self._post((ds.key, ds.count), reads, writes, pwrites)
        return inst

    def recycle(self, bufs):
        for b in bufs:
            if b.dsem is not None:
                self.free_dsems[b.dsem.sw].append(b.dsem)
                b.dsem = None

    def collective(self, kind, op, groups, src, dst, reads, writes, site):
        key = "C_" + site
        if key not in self.sems:
            self.sems[key] = self.nc.alloc_semaphore("c_" + site)
            self.ccount[key] = 0
        e = self.pool
        self._wait(e, self._deps(reads, writes, ()))
        inst = self.nc.gpsimd.collective_compute(kind, op, replica_groups=groups, ins=[src], outs=[dst])
        self.ccount[key] += 1
        inst.then_inc(self.sems[key], 1)
        self.ninst += 1
        self._post((key, self.ccount[key]), reads, writes, ())

    def all_events(self):
        deps = {e.name: e.count for e in self.engs if e.count > 0}
        for k, ds in self.dkeys.items():
            if ds.count > 0:
                deps[k] = ds.count
        for k, v in self.ccount.items():
            if v > 0:
                deps[k] = v
        return deps

    def barrier(self):
        deps = self.all_events()
        for e in self.engs:
            d = dict(deps)
            d.pop(e.name, None)
            self._wait(e, d)

    def finish(self):
        self._wait(self.sp, self.all_events())


def rev(ap2d):
    a = ap2d.ap
    assert len(a) == 2, a
    n = a[1][1]
    st = a[1][0]
    return AP(ap2d.tensor, ap2d.offset + (n - 1) * st, [list(a[0]), [-st, n]])


class Builder:
    def __init__(self, n_lat, depth, dbg=()):
        assert n_lat % 512 == 0
        self.n_lat, self.depth, self.dbgnames = n_lat, depth, tuple(dbg)
        self.T = 256 + n_lat
        self.NT = self.T // 128
        self.NL = n_lat
        self.NTK = (256 + 2 * n_lat) // 128
        self.groups = [[0, 1], [2, 3], [4, 5], [6, 7]]
        self.NCH = self.T // 64
        self.nc = bass.Bass("TRN2", target_bir_lowering=False)
        self.S = Sched(self.nc)
        self.blocks = [(0, 256, True)] + [(256 + 512 * j, 512, False) for j in range(n_lat // 512)]
        self.uid = 0
        self.phase_bufs = []

    def din(self, name, shape, dt=F32):
        return self.nc.dram_tensor(name, list(shape), dt, kind="ExternalInput").ap()

    def dscr(self, name, shape, dt):
        return self.nc.dram_tensor(name, list(shape), dt, kind="Internal").ap()

    def sb(self, es, name, shape, dt=F32):
        self.uid += 1
        t = es.enter_context(self.nc.sbuf_tensor("%s_%d" % (name, self.uid), list(shape), dt))
        b = self.S.buf(name)
        if es is not getattr(self, "ges", None):
            self.phase_bufs.append(b)
        return t, b

    def V(self, fn, r=(), w=(), pw=()):
        return self.S.op(self.S.dve, fn, r, w, pw)

    def A(self, fn, r=(), w=(), pw=()):
        return self.S.op(self.S.act, fn, r, w, pw)

    def P(self, fn, r=(), w=(), pw=()):
        return self.S.op(self.S.pool, fn, r, w, pw)

    def M(self, fn, r=(), w=(), pw=()):
        return self.S.op(self.S.pe, fn, r, w, pw)

    def ld(self, out, in_, r=(), w=(), pw=(), sbuf=None, **kw):
        return self.S.dma(self.S.sp, out, in_, r, w, pw, sbuf, **kw)

    def ldc(self, out, in_, r=(), w=(), pw=(), sbuf=None):
        return self.S.dma(self.S.pool, out, in_, r, w, pw, sbuf, max_dma_last_dim=4096)

    def st(self, out, in_, r=(), w=(), pw=(), sbuf=None, **kw):
        return self.S.dma(self.S.sp, out, in_, r, w, pw, sbuf, **kw)

    def build(self):
        nc, S = self.nc, self.S
        T, L, n_lat = self.T, self.depth, self.n_lat
        i = self.inp = {}
        i["xcat"] = self.din("xcat", [T, D])
        i["cc"] = self.din("cc", [128, 16])
        i["ada_w"] = self.din("ada_w", [L, D, 6 * D])
        i["ada_b"] = self.din("ada_b", [L, 6 * D])
        i["npre"] = self.din("npre", [L, 128, 2, 8])
        i["npost"] = self.din("npost", [L, 2, D])
        i["w_in"] = self.din("w_in", [L, D, DIN])
        i["qkg"] = self.din("qkg", [L, 384])
        i["gatew"] = self.din("gatew", [L, 32, 2, 256])
        i["gateb"] = self.din("gateb", [L, 128, 4])
        i["glanorm"] = self.din("glanorm", [L, 512])
        i["s5a"] = self.din("s5a", [L, 2, 128, 3, 8])
        i["s5b"] = self.din("s5b", [L, 2, 32, 2, 8, 128])
        i["s5c"] = self.din("s5c", [L, 2, 128, 2, 8, 64])
        i["s5d"] = self.din("s5d", [L, 128, 2])
        i["s5glub"] = self.din("s5glub", [L, 128, 2])
        i["s5gluw"] = self.din("s5gluw", [L, 256, 256])
        i["w_br_att"] = self.din("w_br_att", [L, 256, D])
        i["w_br_gla"] = self.din("w_br_gla", [L, 512, D])
        i["w_br_s5"] = self.din("w_br_s5", [L, 256, D])
        i["w_out"] = self.din("w_out", [L, D, D])
        i["ffn_up"] = self.din("ffn_up", [L, D, 2 * DFF])
        i["ffn_down"] = self.din("ffn_down", [L, DFF, D])
        i["convp"] = self.din("convp", [L, 128, 4, 44])
        i["ident"] = self.din("ident", [128, 128])
        i["rope"] = self.din("rope", [n_lat, 64])
        i["gmask"] = self.din("gmask", [128, 2, 64])
        i["scanmask"] = self.din("scanmask", [128, 2, 512])
        i["selw"] = self.din("selw", [128, 2])
        self.out = nc.dram_tensor("out", [n_lat, D], F32, kind="ExternalOutput").ap()
        self.dbg = {}

        d = self.scr = {}
        d["X1"] = self.dscr("X1", [T, D], F32)
        d["X2"] = self.dscr("X2", [T, D], F32)
        d["HT"] = self.dscr("HT", [D, T], BF16)
        d["QT"] = self.dscr("QT", [256, T], BF16)
        d["KT"] = self.dscr("KT", [128, 256], BF16)
        d["VV"] = self.dscr("VV", [256, 128], BF16)
        d["KVS"] = self.dscr("KVS", [256, n_lat], BF16)
        d["KVG"] = self.dscr("KVG", [512, n_lat], BF16)
        d["SFS"] = self.dscr("SFS", [256, 128], F32)
        d["SFG"] = self.dscr("SFG", [512, 128], F32)
        d["CFS"] = self.dscr("CFS", [128, 16], F32)
        d["CFG"] = self.dscr("CFG", [256, 16], F32)
        d["XHS"] = self.dscr("XHS", [1, D], F32)
        d["XHG"] = self.dscr("XHG", [2, D], F32)
        d["XHX"] = self.dscr("XHX", [1, D], F32)
        d["GT"] = self.dscr("GT", [4, 256, T], BF16)
        d["KE"] = self.dscr("KE", [2, T, 256], BF16)
        d["GV"] = self.dscr("GV", [T, 512], BF16)
        d["GR"] = self.dscr("GR", [T, 512], BF16)
        d["SU"] = self.dscr("SU", [256, T], BF16)
        d["OATT"] = self.dscr("OATT", [256, T], BF16)
        d["OGLA"] = self.dscr("OGLA", [512, T], BF16)
        d["OS5"] = self.dscr("OS5", [256, T], BF16)
        d["SST"] = self.dscr("SST", [2, self.NCH, 2, 128, 128], BF16)
        d["YF"] = self.dscr("YF", [256, T], F32)
        self.dbufs = {k: S.buf("dram_" + k) for k in d}
        self.xin_buf = S.buf("dram_xcat")
        self.out_buf = S.buf("dram_out")

        with ExitStack() as ges:
            self.ges = ges
            self.ps = []
            for k in range(8):
                t = ges.enter_context(nc.psum_tensor("ps%d" % k, [128, 512], F32))
                self.ps.append((t, S.buf("ps%d" % k)))
            self.ident, self.b_ident = self.sb(ges, "ident", [128, 128])
            self.identb, self.b_identb = self.sb(ges, "identb", [128, 128], BF16)
            self.ones, self.b_ones = self.sb(ges, "ones", [128, 128])
            self.gmask, self.b_gmask = self.sb(ges, "gmask", [128, 2, 64])
            self.scanmask, self.b_scanmask = self.sb(ges, "scanmask", [128, 2, 512])
            self.GG, self.b_GG = self.sb(ges, "GG", [128, 2, 2, D])
            self.modc, self.b_modc = self.sb(ges, "modc", [128, 2, 6, 8])
            self.prec, self.b_prec = self.sb(ges, "prec", [128, 2, 2, 2, 8])
            self.dec, self.b_dec = self.sb(ges, "dec", [128, 2, 2, self.NCH])
            self.epsc, self.b_epsc = self.sb(ges, "epsc", [128, 1])

            self.ld(self.ident[:], i["ident"], w=[self.b_ident])
            self.V(lambda: nc.vector.tensor_copy(out=self.identb[:], in_=self.ident[:]), [self.b_ident], [self.b_identb])
            self.V(lambda: nc.vector.memset(self.ones[:], 1.0), w=[self.b_ones])
            self.V(lambda: nc.vector.memset(self.epsc[:], EPS), w=[self.b_epsc])
            self.ld(self.gmask[:], i["gmask"], w=[self.b_gmask])
            self.ld(self.scanmask[:], i["scanmask"], w=[self.b_scanmask])
            self.selw, self.b_selw = self.sb(ges, "selw", [128, 2])
            self.ld(self.selw[:], i["selw"], w=[self.b_selw])
            S.barrier()
            S.recycle(self.phase_bufs)
            self.phase_bufs = []

            for l in range(L):
                last = (l == L - 1)
                xin = i["xcat"] if l == 0 else d["X1"]
                xin_b = self.xin_buf if l == 0 else self.dbufs["X1"]
                phases = [("mod", lambda: self.phase_mod(l)), ("a", lambda: self.phase_a(l, xin, xin_b)), ("bc3", lambda: self.phase_bc3(l)),
                          ("c1", lambda: self.phase_c1(l)), ("c2", lambda: self.phase_c2(l)),
                          ("d", lambda: self.phase_d(l, xin, xin_b, last)), ("e", lambda: self.phase_e(l, last))]
                stopped = False
                for pname, pf in phases:
                    pf()
                    if STOP_AFTER == pname:
                        stopped = True
                        break
                if stopped:
                    break
            S.finish()
        return nc

    def dump(self, name, ap_sb, bufs, shape, dt=F32):
        if name not in self.dbgnames:
            return
        t = self.nc.dram_tensor("dbg_" + name, list(shape), dt, kind="ExternalOutput").ap()
        self.dbg[name] = t
        self.st(t, ap_sb, r=bufs, w=[self.S.buf("dbgd_" + name)], sbuf=bufs[0])

    def rstd_from_ss(self, ss_ap, rs_ap, b_ss, b_rs, n, rows=128):
        nc = self.nc
        self.A(lambda: nc.scalar.activation(out=rs_ap, in_=ss_ap, func=AF.Sqrt, bias=self.epsc[0:rows, 0:1], scale=1.0 / n),
               [b_ss, self.b_epsc], [b_rs])
        self.V(lambda: nc.vector.reciprocal(out=rs_ap, in_=rs_ap), [b_rs], [b_rs])

    def phase_mod(self, l):
        nc, S, i = self.nc, self.S, self.inp
        with ExitStack() as es:
            npost, b_npost = self.sb(es, "npost", [128, 2, D])
            cc, bcc = self.sb(es, "cc", [128, 16])
            sc, bsc = self.sb(es, "sc", [128, 16])
            self.screp, self.b_screp = self.sb(es, "screp", [128, 16, 128])
            self.ld(cc[:], i["cc"], w=[bcc])
            self.A(lambda: nc.scalar.activation(out=sc[:], in_=cc[:], func=AF.Silu), [bcc], [bsc])
            self.V(lambda: nc.vector.tensor_copy(out=self.screp[:], in_=sc[:].unsqueeze(2).to_broadcast([128, 16, 128])),
                   [bsc], [self.b_screp])
            npre, b_npre = self.sb(es, "npre", [128, 2, 8])
            adab, b_adab = self.sb(es, "adab", [128, 512])
            wblk = [self.sb(es, "adaw%d" % k, [128, 8, 512]) for k in range(2)]
            mb = [self.sb(es, "mb%d" % k, [128, 512]) for k in range(2)]
            src = i["npost"][l]
            self.ld(npost[:], AP(src.tensor, src.offset, [[0, 128], [D, 2], [1, D]]), w=[b_npost])
            self.ld(npre[:], i["npre"][l], w=[b_npre])
            for cb in range(12):
                kind, half = cb // 2, cb % 2
                wt, bw = wblk[cb % 2]
                self.ld(wt[:], i["ada_w"][l, :, cb * 512:(cb + 1) * 512].rearrange("(k p) n -> p k n", p=128), w=[bw])
                src = i["ada_b"][l, cb * 512:(cb + 1) * 512]
                self.ld(adab[:], AP(src.tensor, src.offset, [[0, 128], [1, 512]]), w=[b_adab])
                for which in range(2):
                    pt, bp = self.ps[which]
                    for k in range(8):
                        self.M(lambda k=k: nc.tensor.matmul(pt[:], lhsT=self.screp[:, which * 8 + k, :], rhs=wt[:, k, :],
                                                           start=(k == 0), stop=(k == 7)),
                               [self.b_screp, bw], pw=[bp] if k else (), w=[bp] if k == 0 else ())
                    mt, bm = mb[which]
                    self.V(lambda: nc.vector.tensor_tensor(out=mt[:], in0=pt[:], in1=adab[:], op=ALU.add), [bp, b_adab], [bm])
                    if kind in (2, 5):
                        mf = 0 if kind == 2 else 1
                        self.V(lambda: nc.vector.tensor_tensor(out=self.GG[:, which, mf, half * 512:(half + 1) * 512], in0=mt[:],
                                                               in1=npost[:, mf, half * 512:(half + 1) * 512], op=ALU.mult),
                               [bm, b_npost], pw=[self.b_GG])
                    else:
                        p2, bp2 = self.ps[2 + which]
                        for j in range(4):
                            self.M(lambda j=j: nc.tensor.transpose(out=p2[:, j * 32:(j + 1) * 32], in_=mt[0:32, j * 128:(j + 1) * 128],
                                                                  identity=self.ident[0:32, 0:32]),
                                   [bm, self.b_ident], pw=[bp2] if j else (), w=[bp2] if j == 0 else ())
                        srcv = p2[:, 0:128].rearrange("p (j c) -> p j c", c=32)[:, :, 0]
                        self.V(lambda: nc.vector.tensor_copy(out=self.modc[:, which, kind, half * 4:(half + 1) * 4], in_=srcv),
                               [bp2], pw=[self.b_modc])
            for which in range(2):
                for mf in range(2):
                    ksh, ksc = (0, 1) if mf == 0 else (3, 4)
                    self.V(lambda: nc.vector.scalar_tensor_tensor(out=self.prec[:, which, mf, 0, :], in0=self.modc[:, which, ksc, :],
                                                                  scalar=1.0, in1=npre[:, mf, :], op0=ALU.add, op1=ALU.mult),
                           [self.b_modc, b_npre], pw=[self.b_prec])
                    self.V(lambda: nc.vector.tensor_copy(out=self.prec[:, which, mf, 1, :], in_=self.modc[:, which, ksh, :]),
                           [self.b_modc], pw=[self.b_prec])
            self.dump("GG", self.GG[:], [self.b_GG], [128, 2, 2, D])
            self.dump("prec", self.prec[:], [self.b_prec], [128, 2, 2, 2, 8])
            S.barrier()
            S.recycle(self.phase_bufs)
            self.phase_bufs = []

    def norm_tile(self, xt, bx, rows, which, mf, hT, bh, col0, tmp, pst):
        nc = self.nc
        (xn, bxn), (ss, bss), (rs, brs) = tmp
        self.A(lambda: nc.scalar.activation(out=xn[0:rows, :], in_=xt[0:rows, :], func=AF.Square, accum_out=ss[0:rows, 0:1]),
               [bx], [bxn, bss])
        self.rstd_from_ss(ss[0:rows, 0:1], rs[0:rows, 0:1], bss, brs, D, rows)
        self.A(lambda: nc.scalar.activation(out=xn[0:rows, :], in_=xt[0:rows, :], func=AF.Identity, scale=rs[0:rows, 0:1]),
               [bx, brs], [bxn])
        for hf in range(2):
            pt, bp = pst[hf]
            for kk in range(4):
                k = hf * 4 + kk
                self.M(lambda: nc.tensor.transpose(out=pt[:, kk * 128:kk * 128 + rows], in_=xn[0:rows, k * 128:(k + 1) * 128],
                                                   identity=self.ident[0:rows, 0:rows]),
                       [bxn, self.b_ident], pw=[bp] if kk else (), w=[bp] if kk == 0 else ())
            for kk in range(4):
                k = hf * 4 + kk
                self.A(lambda: nc.scalar.activation(out=hT[:, k, col0:col0 + rows], in_=pt[:, kk * 128:kk * 128 + rows],
                                                    func=AF.Identity, scale=self.prec[:, which, mf, 0, k:k + 1],
                                                    bias=self.prec[:, which, mf, 1, k:k + 1]),
                       [bp, self.b_prec], pw=[bh])

    def phase_a(self, l, xin, xin_b):
        nc, S, i, d = self.nc, self.S, self.inp, self.scr
        db = self.dbufs
        with ExitStack() as es:
            WA, bWA = self.sb(es, "WA", [128, 8, NA], BF16)
            for k in range(8):
                self.ldc(WA[:, k, :], i["w_in"][l, k * 128:(k + 1) * 128, 0:NA], pw=[bWA])
            gw, bgw = self.sb(es, "gw", [32, 2, 256], BF16)
            self.ldc(gw[:], i["gatew"][l], w=[bgw])
            gb, bgb = self.sb(es, "gb", [128, 4])
            self.ld(gb[:], i["gateb"][l], w=[bgb])
            self.V(lambda: nc.vector.tensor_scalar(out=gb[:], in0=gb[:], scalar1=-1.0, scalar2=None, op0=ALU.mult), [bgb], [bgb])
            qkg, bqkg = self.sb(es, "qkg", [128, 384])
            src = i["qkg"][l]
            self.ld(qkg[:], AP(src.tensor, src.offset, [[0, 128], [1, 384]]), w=[bqkg])
            xts = [self.sb(es, "xt%d" % k, [128, D]) for k in range(2)]
            tmp = (self.sb(es, "xn", [128, D]), self.sb(es, "ss", [128, 1]), self.sb(es, "rs", [128, 1]))
            hTs = [self.sb(es, "hT%d" % k, [128, 8, 512], BF16) for k in range(2)]
            sq, bsq = self.sb(es, "sq", [128, 384])
            ss6, bss6 = self.sb(es, "ss6", [128, 6])
            rs6, brs6 = self.sb(es, "rs6", [128, 6])
            qk1, bqk1 = self.sb(es, "qk1", [128, 384])
            qk2, bqk2 = self.sb(es, "qk2", [128, 384], BF16)
            ra, bra = self.sb(es, "ra", [128, 6, 2, 16])
            rb, brb = self.sb(es, "rb", [128, 6, 2, 16])
            rope, brope = self.sb(es, "rope", [128, 64])
            vbf, bvbf = self.sb(es, "vbf", [128, 128], BF16)
            tmA = [self.sb(es, "tmA%d" % k, [128, 512], BF16) for k in range(2)]
            qkT, bqkT = self.sb(es, "qkT", [128, 3, 128], BF16)
            lowT, blowT = self.sb(es, "lowT", [32, 512], BF16)
            fm = [self.sb(es, "fm%d" % k, [128, 512], BF16) for k in range(2)]
            e1, be1 = self.sb(es, "e1", [128, 512])
            l1, bl1 = self.sb(es, "l1", [128, 512])
            bp_, bbp = self.sb(es, "bpl", [128, 512])
            dd, bdd = self.sb(es, "dd", [128, 512])
            E = [self.sb(es, "E%d" % k, [128, 512]) for k in range(3)]
            qo = [self.sb(es, "qo%d" % k, [128, 512], BF16) for k in range(3)]
            keT, bkeT = self.sb(es, "keT", [128, 4, 128], BF16)
            pst = (self.ps[0], self.ps[1])
            tm_i = 0
            fm_i = 0
            for bi, (t0, n, isctx) in enumerate(self.blocks):
                which = 1 if isctx else 0
                nt = n // 128
                hT, bh = hTs[bi % 2]
                for j in range(nt):
                    xt, bx = xts[j % 2]
                    r0 = t0 + j * 128
                    self.ld(xt[:], xin[r0:r0 + 128, :], r=[xin_b], w=[bx])
                    self.norm_tile(xt, bx, 128, which, 0, hT, bh, j * 128, tmp, pst)
                self.st(d["HT"][:, t0:t0 + n].rearrange("(k p) t -> p k t", p=128), hT[:, :, 0:n], r=[bh], pw=[db["HT"]])
                if l == 0 and bi == 1:
                    self.dump("hT", hT[:], [bh], [128, 8, 512], BF16)
                for j in range(nt):
                    r0 = t0 + j * 128
                    pt, bp = self.ps[2]
                    for k in range(8):
                        self.M(lambda k=k: nc.tensor.matmul(pt[:], lhsT=hT[:, k, j * 128:(j + 1) * 128], rhs=WA[:, k, 0:512],
                                                           start=(k == 0), stop=(k == 7)),
                               [bh, bWA], pw=[bp] if k else (), w=[bp] if k == 0 else ())
                    self.A(lambda: nc.scalar.activation(out=sq[:], in_=pt[:, 0:384], func=AF.Square), [bp], [bsq])
                    self.V(lambda: nc.vector.tensor_reduce(out=ss6[:], in_=sq[:].rearrange("p (h e) -> p h e", e=64), axis=AX.X, op=ALU.add),
                           [bsq], [bss6])
                    self.rstd_from_ss(ss6[:], rs6[:], bss6, brs6, 64)
                    self.V(lambda: nc.vector.tensor_tensor(out=qk1[:].rearrange("p (h e) -> p h e", e=64),
                                                           in0=pt[:, 0:384].rearrange("p (h e) -> p h e", e=64),
                                                           in1=rs6[:].unsqueeze(2).to_broadcast([128, 6, 64]), op=ALU.mult),
                           [bp, brs6], [bqk1])
                    self.A(lambda: nc.scalar.copy(out=vbf[:], in_=pt[:, 384:512]), [bp], [bvbf])
                    if isctx:
                        self.st(d["VV"][r0:r0 + 128, :], vbf[:], r=[bvbf], pw=[db["VV"]])
                    else:
                        kvs = d["KVS"]
                        vdst = AP(kvs.tensor, kvs.offset + 128 * self.NL + (r0 - 256) * 128, [[128, 128], [1, 128]])
                        self.st(vdst, vbf[:], r=[bvbf], pw=[db["KVS"]])
                    if isctx:
                        self.V(lambda: nc.vector.tensor_tensor(out=qk2[:], in0=qk1[:], in1=qkg[:], op=ALU.mult), [bqk1, bqkg], [bqk2])
                    else:
                        self.V(lambda: nc.vector.tensor_tensor(out=qk1[:], in0=qk1[:], in1=qkg[:], op=ALU.mult), [bqk1, bqkg], [bqk1])
                        self.ld(rope[:], i["rope"][r0 - 256:r0 - 256 + 128, :], w=[brope])
                        v5 = qk1[:].rearrange("p (h a s f) -> p h a s f", h=6, a=2, s=2)
                        o5 = qk2[:].rearrange("p (h a s f) -> p h a s f", h=6, a=2, s=2)
                        x1, x2 = v5[:, :, :, 0, :], v5[:, :, :, 1, :]
                        cs = rope[:, 0:32].rearrange("p (a f) -> p a f", a=2).unsqueeze(1).to_broadcast([128, 6, 2, 16])
                        sn = rope[:, 32:64].rearrange("p (a f) -> p a f", a=2).unsqueeze(1).to_broadcast([128, 6, 2, 16])
                        self.V(lambda: nc.vector.tensor_tensor(out=ra[:], in0=x1, in1=cs, op=ALU.mult), [bqk1, brope], [bra])
                        self.V(lambda: nc.vector.tensor_tensor(out=rb[:], in0=x2, in1=sn, op=ALU.mult), [bqk1, brope], [brb])
                        self.V(lambda: nc.vector.tensor_tensor(out=o5[:, :, :, 0, :], in0=ra[:], in1=rb[:], op=ALU.subtract),
                               [bra, brb], pw=[bqk2])
                        self.V(lambda: nc.vector.tensor_tensor(out=ra[:], in0=x2, in1=cs, op=ALU.mult), [bqk1, brope], [bra])
                        self.V(lambda: nc.vector.tensor_tensor(out=rb[:], in0=x1, in1=sn, op=ALU.mult), [bqk1, brope], [brb])
                        self.V(lambda: nc.vector.tensor_tensor(out=o5[:, :, :, 1, :], in0=ra[:], in1=rb[:], op=ALU.add),
                               [bra, brb], pw=[bqk2])
                    p7, bp7 = self.ps[7]
                    p7b = p7[:].bitcast(BF16)
                    for c in range(3):
                        self.M(lambda c=c: nc.tensor.transpose(out=p7b[:, c * 128:(c + 1) * 128], in_=qk2[:, c * 128:(c + 1) * 128],
                                                              identity=self.identb[:]),
                               [bqk2, self.b_identb], pw=[bp7] if c else (), w=[bp7] if c == 0 else ())
                    self.A(lambda: nc.scalar.copy(out=qkT[:].rearrange("p c t -> p (c t)"), in_=p7b[:, 0:384]), [bp7], [bqkT])
                    self.st(d["QT"][:, r0:r0 + 128].rearrange("(c p) t -> p c t", p=128), qkT[:, 0:2, :], r=[bqkT], pw=[db["QT"]])
                    if isctx:
                        self.st(d["KT"][:, r0:r0 + 128], qkT[:, 2, :], r=[bqkT], pw=[db["KT"]])
                    else:
                        self.st(d["KVS"][0:128, r0 - 256:r0 - 256 + 128], qkT[:, 2, :], r=[bqkT], pw=[db["KVS"]])
                    for gi, (c0, dst) in enumerate(((1024, "GV"), (1536, "GR"))):
                        pt, bp = self.ps[3 + gi]
                        for k in range(8):
                            self.M(lambda k=k: nc.tensor.matmul(pt[:], lhsT=hT[:, k, j * 128:(j + 1) * 128], rhs=WA[:, k, c0:c0 + 512],
                                                               start=(k == 0), stop=(k == 7)),
                                   [bh, bWA], pw=[bp] if k else (), w=[bp] if k == 0 else ())
                        tt, bt = tmA[tm_i % 2]
                        tm_i += 1
                        if gi == 0:
                            self.A(lambda: nc.scalar.copy(out=tt[:], in_=pt[:]), [bp], [bt])
                        else:
                            self.A(lambda: nc.scalar.activation(out=tt[:], in_=pt[:], func=AF.Silu), [bp], [bt])
                        self.st(d[dst][r0:r0 + 128, :], tt[:], r=[bt], pw=[db[dst]])
                pt, bp = self.ps[2]
                for k in range(8):
                    self.M(lambda k=k: nc.tensor.matmul(pt[0:32, 0:n], lhsT=WA[:, k, 2048:2080], rhs=hT[:, k, 0:n], start=(k == 0), stop=(k == 7)),
                           [bh, bWA], pw=[bp] if k else (), w=[bp] if k == 0 else ())
                self.A(lambda: nc.scalar.copy(out=lowT[:, 0:n], in_=pt[0:32, 0:n]), [bp], [blowT])
                for c in range(2):
                    pt, bp = self.ps[3 + c]
                    for k in range(8):
                        self.M(lambda k=k: nc.tensor.matmul(pt[:, 0:n], lhsT=WA[:, k, 2080 + c * 128:2080 + (c + 1) * 128], rhs=hT[:, k, 0:n],
                                                           start=(k == 0), stop=(k == 7)),
                               [bh, bWA], pw=[bp] if k else (), w=[bp] if k == 0 else ())
                    ft, bf = fm[fm_i % 2]
                    fm_i += 1
                    self.A(lambda: nc.scalar.copy(out=ft[:, 0:n], in_=pt[:, 0:n]), [bp], [bf])
                    self.st(d["SU"][c * 128:(c + 1) * 128, t0:t0 + n], ft[:, 0:n], r=[bf], pw=[db["SU"]])
                nchb = n // 64
                ch0 = t0 // 64
                for c in range(2):
                    pq, bpq = self.ps[3]
                    pk, bpk = self.ps[4]
                    for (pp, bpp, c0) in ((pq, bpq, 512), (pk, bpk, 768)):
                        for k in range(8):
                            self.M(lambda k=k: nc.tensor.matmul(pp[:, 0:n], lhsT=WA[:, k, c0 + c * 128:c0 + (c + 1) * 128], rhs=hT[:, k, 0:n],
                                                               start=(k == 0), stop=(k == 7)),
                                   [bh, bWA], pw=[bpp] if k else (), w=[bpp] if k == 0 else ())
                    for dr in range(2):
                        pz, bpz = self.ps[5 + dr]
                        self.M(lambda: nc.tensor.matmul(pz[:, 0:n], lhsT=gw[:, dr, c * 128:(c + 1) * 128], rhs=lowT[:, 0:n], start=True, stop=True),
                               [bgw, blowT], [bpz])
                        self.A(lambda: nc.scalar.activation(out=e1[:, 0:n], in_=pz[:, 0:n], func=AF.Exp, scale=-1.0,
                                                            bias=gb[:, dr * 2 + c:dr * 2 + c + 1]), [bpz, bgb], [be1])
                        self.A(lambda: nc.scalar.activation(out=l1[:, 0:n], in_=e1[:, 0:n], func=AF.Ln, bias=self.ones[:, 0:1], scale=1.0),
                               [be1, self.b_ones], [bl1])
                        if dr == 0:
                            self.V(lambda: nc.vector.tensor_tensor_scan(out=bp_[:, 0:n], data0=self.scanmask[:, 0, 0:n], data1=l1[:, 0:n],
                                                                        initial=0.0, op0=ALU.mult, op1=ALU.add),
                                   [self.b_scanmask, bl1], [bbp])
                            endcol = 63
                        else:
                            self.V(lambda: nc.vector.tensor_tensor_scan(out=rev(bp_[:, 0:n]), data0=rev(self.scanmask[:, 1, 0:n]),
                                                                        data1=rev(l1[:, 0:n]), initial=0.0, op0=ALU.mult, op1=ALU.add),
                                   [self.b_scanmask, bl1], [bbp])
                            endcol = 0
                        b3 = bp_[:, 0:n].rearrange("p (c i) -> p c i", i=64)
                        bend = b3[:, :, endcol:endcol + 1]
                        self.V(lambda: nc.vector.tensor_tensor(out=dd[:, 0:n].rearrange("p (c i) -> p c i", i=64),
                                                               in0=bend.to_broadcast([128, nchb, 64]), in1=b3, op=ALU.subtract),
                               [bbp], [bdd])
                        self.A(lambda: nc.scalar.activation(out=E[0][0][:, 0:n], in_=bp_[:, 0:n], func=AF.Exp, scale=-1.0 / 16), [bbp], [E[0][1]])
                        self.A(lambda: nc.scalar.activation(out=E[1][0][:, 0:n], in_=bp_[:, 0:n], func=AF.Exp, scale=1.0 / 16), [bbp], [E[1][1]])
                        self.A(lambda: nc.scalar.activation(out=E[2][0][:, 0:n], in_=dd[:, 0:n], func=AF.Exp, scale=-1.0 / 16), [bdd], [E[2][1]])
                        self.A(lambda: nc.scalar.activation(out=self.dec[:, dr, c, ch0:ch0 + nchb], in_=bend.rearrange("p c o -> p (c o)"),
                                                            func=AF.Exp, scale=-1.0 / 16), [bbp], pw=[self.b_dec])
                        self.V(lambda: nc.vector.scalar_tensor_tensor(out=qo[0][0][:, 0:n], in0=pq[:, 0:n], scalar=0.125, in1=E[0][0][:, 0:n],
                                                                      op0=ALU.mult, op1=ALU.mult), [bpq, E[0][1]], [qo[0][1]])
                        self.V(lambda: nc.vector.tensor_tensor(out=qo[1][0][:, 0:n], in0=pk[:, 0:n], in1=E[1][0][:, 0:n], op=ALU.mult),
                               [bpk, E[1][1]], [qo[1][1]])
                        self.V(lambda: nc.vector.tensor_tensor(out=qo[2][0][:, 0:n], in0=pk[:, 0:n], in1=E[2][0][:, 0:n], op=ALU.mult),
                               [bpk, E[2][1]], [qo[2][1]])
                        self.st(d["GT"][dr * 2 + 0, c * 128:(c + 1) * 128, t0:t0 + n], qo[0][0][:, 0:n], r=[qo[0][1]], pw=[db["GT"]])
                        self.st(d["GT"][dr * 2 + 1, c * 128:(c + 1) * 128, t0:t0 + n], qo[1][0][:, 0:n], r=[qo[1][1]], pw=[db["GT"]])
                        p7, bp7 = self.ps[7]
                        p7b = p7[:].bitcast(BF16)
                        for j in range(nt):
                            self.M(lambda j=j: nc.tensor.transpose(out=p7b[:, j * 128:(j + 1) * 128], in_=qo[2][0][:, j * 128:(j + 1) * 128],
                                                                  identity=self.identb[:]),
                                   [qo[2][1], self.b_identb], pw=[bp7] if j else (), w=[bp7] if j == 0 else ())
                        self.A(lambda: nc.scalar.copy(out=keT[:, 0:nt, :].rearrange("p j f -> p (j f)"), in_=p7b[:, 0:nt * 128]), [bp7], [bkeT])
                        self.st(d["KE"][dr, t0:t0 + n, c * 128:(c + 1) * 128].rearrange("(j p) f -> p j f", p=128), keT[:, 0:nt, :],
                                r=[bkeT], pw=[db["KE"]])
            S.barrier()
            S.recycle(self.phase_bufs)
            self.phase_bufs = []

    def gen_b(self, l, es):
        nc, S, d, db = self.nc, self.S, self.scr, self.dbufs
        T, NT, NL = 256 + 2 * self.NL, self.NTK, self.NL
        S.collective("AllGather", ALU.bypass, self.groups, d["KVS"], d["KVG"], [db["KVS"]], [db["KVG"]], "kv")
        if True:
            KT, bKT = self.sb(es, "KTs", [128, T], BF16)
            V1, bV1 = self.sb(es, "V1", [128, NT, 2, 65], BF16)
            self.ld(KT[:, 0:256], d["KT"], r=[db["KT"]], w=[bKT])
            for rk in range(2):
                self.ld(KT[:, 256 + rk * NL:256 + (rk + 1) * NL], d["KVG"][rk * 256:rk * 256 + 128, :], r=[db["KVG"]], pw=[bKT])
            self.P(lambda: nc.gpsimd.memset(V1[:], 1.0), w=[bV1])
            for hh in range(2):
                self.ld(V1[:, 0:2, hh, 0:64], d["VV"][:, hh * 64:(hh + 1) * 64].rearrange("(t p) e -> p t e", p=128), r=[db["VV"]], pw=[bV1])
                for rk in range(2):
                    kvg = d["KVG"]
                    vsrc = AP(kvg.tensor, kvg.offset + (rk * 256 + 128) * NL + hh * 64, [[128, 128], [128 * 128, NL // 128], [1, 64]])
                    t_0 = 2 + rk * (NL // 128)
                    self.ld(V1[:, t_0:t_0 + NL // 128, hh, 0:64], vsrc, r=[db["KVG"]], pw=[bV1])
            Qs = [self.sb(es, "Qs%d" % k, [128, 512], BF16) for k in range(2)]
            PT = [self.sb(es, "PT%d" % k, [128, 512], BF16) for k in range(4)]
            rd, brd = self.sb(es, "rd", [128, 512])
            osb, bosb = self.sb(es, "osb", [128, 512])
            obf = [self.sb(es, "obf%d" % k, [128, 512], BF16) for k in range(2)]
            qi = 0
            pi = 0
            oi = 0
            for (q0, nq, isctx) in self.blocks:
                keys = [0, 1] if isctx else list(range(NT))
                for pair in range(2):
                    Q, bQ = Qs[qi % 2]
                    qi += 1
                    for hh in range(2):
                        h = pair + 2 * hh
                        self.ld(Q[hh * 64:(hh + 1) * 64, 0:nq], d["QT"][h * 64:(h + 1) * 64, q0:q0 + nq], r=[db["QT"]], pw=[bQ])
                    pts = {}

                    def stepA(ki, hh):
                        kt = keys[ki]
                        slot = (ki % 2) * 2 + hh
                        st_, bst = self.ps[slot]
                        self.M(lambda: nc.tensor.matmul(st_[:, 0:nq], lhsT=KT[hh * 64:(hh + 1) * 64, kt * 128:(kt + 1) * 128],
                                                        rhs=Q[hh * 64:(hh + 1) * 64, 0:nq], start=True, stop=True),
                               [bKT, bQ], [bst])
                        pt_, bpt = PT[slot]
                        self.A(lambda: nc.scalar.activation(out=pt_[:, 0:nq], in_=st_[:, 0:nq], func=AF.Exp, scale=0.125), [bst], [bpt])
                        pts[(ki, hh)] = (pt_, bpt)

                    def stepB(ki, hh):
                        kt = keys[ki]
                        pt_, bpt = pts.pop((ki, hh))
                        oa, boa = self.ps[4 + hh]
                        first, lastk = (ki == 0), (ki == len(keys) - 1)
                        self.M(lambda: nc.tensor.matmul(oa[0:65, 0:nq], lhsT=V1[:, kt, hh, :], rhs=pt_[:, 0:nq], start=first, stop=lastk),
                               [bV1, bpt], pw=[boa] if not first else (), w=[boa] if first else ())
                    for ki in range(len(keys) + 1):
                        if ki < len(keys):
                            stepA(ki, 0)
                            stepA(ki, 1)
                        if ki >= 1:
                            stepB(ki - 1, 0)
                            stepB(ki - 1, 1)
                        yield
                    for hh in range(2):
                        h = pair + 2 * hh
                        oa, boa = self.ps[4 + hh]
                        self.V(lambda: nc.vector.reciprocal(out=rd[64:65, 0:nq], in_=oa[64:65, 0:nq]), [boa], [brd])
                        bc, bbc = self.ps[3]
                        self.M(lambda: nc.tensor.matmul(bc[0:64, 0:nq], lhsT=self.ones[64:65, 0:64], rhs=rd[64:65, 0:nq], start=True, stop=True),
                               [self.b_ones, brd], [bbc])
                        self.A(lambda: nc.scalar.copy(out=osb[0:64, 0:nq], in_=oa[0:64, 0:nq]), [boa], [bosb])
                        ob, bob = obf[oi % 2]
                        oi += 1
                        self.V(lambda: nc.vector.tensor_tensor(out=ob[0:64, 0:nq], in0=osb[0:64, 0:nq], in1=bc[0:64, 0:nq], op=ALU.mult),
                               [bosb, bbc], [bob])
                        self.st(d["OATT"][h * 64:(h + 1) * 64, q0:q0 + nq], ob[0:64, 0:nq], r=[bob], pw=[db["OATT"]])
                        yield

    def phase_c1(self, l):
        nc, S, d, db = self.nc, self.S, self.scr, self.dbufs
        with ExitStack() as es:
            St, bSt = self.sb(es, "St", [128, 2, 128])
            sfg = [self.sb(es, "sfg%d" % k, [128, 2, 128]) for k in range(2)]
            kes = [self.sb(es, "ke%d" % k, [128, 4, 256], BF16) for k in range(2)]
            gvs = [self.sb(es, "gvs%d" % k, [128, 4, 512], BF16) for k in range(2)]
            hists = [self.sb(es, "hist%d" % k, [128, 8, 2, 128], BF16) for k in range(2)]
            it = 0
            for dr in range(2):
                self.V(lambda: nc.vector.memset(St[:], 0.0), w=[bSt])
                ctxb = [b for b in self.blocks if b[2]]
                latb = [b for b in self.blocks if not b[2]]
                order = ctxb + (latb if dr == 0 else latb[::-1])
                for (t0, n, isctx) in order:
                    if dr == 1 and (t0, n, isctx) == order[len(ctxb)]:
                        for rk in range(2):
                            self.ld(sfg[rk][0][:], d["SFG"][rk * 256:(rk + 1) * 256, :].rearrange("(c p) e -> p c e", p=128), r=[db["SFG"]], w=[sfg[rk][1]])
                        self.V(lambda: nc.vector.tensor_scalar(out=St[:], in0=sfg[0][0][:], scalar1=self.selw[:, 0:1], scalar2=None, op0=ALU.mult),
                               [sfg[0][1], self.b_selw], [bSt])
                        self.V(lambda: nc.vector.scalar_tensor_tensor(out=St[:], in0=sfg[1][0][:], scalar=self.selw[:, 1:2], in1=St[:], op0=ALU.mult, op1=ALU.add),
                               [sfg[1][1], self.b_selw, bSt], [bSt])
                    nt = n // 128
                    ke, bke = kes[it % 2]
                    gv, bgv = gvs[it % 2]
                    hist, bhist = hists[it % 2]
                    it += 1
                    self.ld(ke[:, 0:nt, :], d["KE"][dr, t0:t0 + n, :].rearrange("(j p) f -> p j f", p=128), r=[db["KE"]], w=[bke])
                    self.ld(gv[:, 0:nt, :], d["GV"][t0:t0 + n, :].rearrange("(j p) f -> p j f", p=128), r=[db["GV"]], w=[bgv])
                    nchb = n // 64
                    chs = list(range(nchb)) if dr == 0 else list(range(nchb))[::-1]
                    for ci, cl in enumerate(chs):
                        j, half = cl // 2, cl % 2
                        ng = t0 // 64 + cl
                        self.A(lambda: nc.scalar.copy(out=hist[:, cl, :, :], in_=St[:]), [bSt], pw=[bhist] if ci else (), w=[bhist] if ci == 0 else ())
                        for c in range(2):
                            kv, bkv = self.ps[c + 2 * (ci % 2)]
                            for hl in range(2):
                                h = c * 2 + hl
                                self.M(lambda: nc.tensor.matmul(kv[hl * 64:(hl + 1) * 64, 0:128],
                                                                lhsT=ke[half * 64:(half + 1) * 64, j, c * 128 + hl * 64:c * 128 + (hl + 1) * 64],
                                                                rhs=gv[half * 64:(half + 1) * 64, j, h * 128:(h + 1) * 128], start=True, stop=True),
                                       [bke, bgv], pw=[bkv] if hl else (), w=[bkv] if hl == 0 else ())
                            self.V(lambda: nc.vector.scalar_tensor_tensor(out=St[:, c, :], in0=St[:, c, :], scalar=self.dec[:, dr, c, ng:ng + 1],
                                                                          in1=kv[:, 0:128], op0=ALU.mult, op1=ALU.add),
                                   [bSt, self.b_dec, bkv], [bSt])
                    ch0 = t0 // 64
                    self.st(d["SST"][dr, ch0:ch0 + nchb].rearrange("n c p e -> p n c e"), hist[:, 0:nchb, :, :], r=[bhist], pw=[db["SST"]])
                if dr == 0:
                    self.st(d["SFS"].rearrange("(c p) e -> p c e", p=128), St[:], r=[bSt], w=[db["SFS"]])
                    S.collective("AllGather", ALU.bypass, self.groups, d["SFS"], d["SFG"], [db["SFS"]], [db["SFG"]], "gla")
            S.barrier()
            S.recycle(self.phase_bufs)
            self.phase_bufs = []

    def phase_c2(self, l):
        nc, S, i, d, db = self.nc, self.S, self.inp, self.scr, self.dbufs
        with ExitStack() as es:
            gn, bgn = self.sb(es, "gn", [128, 512])
            src = i["glanorm"][l]
            self.ld(gn[:], AP(src.tensor, src.offset, [[0, 128], [1, 512]]), w=[bgn])
            gts = [self.sb(es, "gt%d" % k, [64, 4, 4, 512], BF16) for k in range(2)]
            gvs = [self.sb(es, "gv2%d" % k, [64, 8, 512], BF16) for k in range(2)]
            grs = [self.sb(es, "gr2%d" % k, [128, 4, 512], BF16) for k in range(2)]
            ssts = [self.sb(es, "sst%d" % k, [64, 2, 4, 8, 128], BF16) for k in range(2)]
            atts = [self.sb(es, "att%d" % k, [64, 2, 64], BF16) for k in range(4)]
            sq, bsq = self.sb(es, "sq2", [128, 512])
            ss4, bss4 = self.sb(es, "ss4", [128, 4])
            rs4, brs4 = self.sb(es, "rs4", [128, 4])
            t1, bt1 = self.sb(es, "t1", [128, 512])
            t3, bt3 = self.sb(es, "t3", [128, 512], BF16)
            ogT, bogT = self.sb(es, "ogT", [128, 4, 128], BF16)
            ai = 0
            for bi, (t0, n, isctx) in enumerate(self.blocks):
                nt = n // 128
                nchb = n // 64
                ch0 = t0 // 64
                gt, bgt = gts[bi % 2]
                gv, bgv = gvs[bi % 2]
                gr, bgr = grs[bi % 2]
                sst, bsst = ssts[bi % 2]
                for kind in range(4):
                    self.ld(gt[:, kind, :, 0:n], d["GT"][kind, :, t0:t0 + n].rearrange("(h p) t -> p h t", p=64), r=[db["GT"]],
                            pw=[bgt] if kind else (), w=[bgt] if kind == 0 else ())
                self.ld(gv[:, 0:nchb, :], d["GV"][t0:t0 + n, :].rearrange("(c p) f -> p c f", p=64), r=[db["GV"]], w=[bgv])
                self.ld(gr[:, 0:nt, :], d["GR"][t0:t0 + n, :].rearrange("(j p) f -> p j f", p=128), r=[db["GR"]], w=[bgr])
                first_ld = True
                for dr in range(2):
                    for h in range(4):
                        c, hl = h // 2, h % 2
                        self.ld(sst[:, dr, h, 0:nchb, :], d["SST"][dr, ch0:ch0 + nchb, c, hl * 64:(hl + 1) * 64, :].rearrange("n p e -> p n e"),
                                r=[db["SST"]], pw=[bsst] if not first_ld else (), w=[bsst] if first_ld else ())
                        first_ld = False
                for j in range(nt):
                    ops_, bops = self.ps[4 + (j % 2)]
                    for h in range(4):
                        asb = []
                        for dr in range(2):
                            ap_, bap = self.ps[dr + 2 * (h % 2)]
                            for ch in range(2):
                                tk = j * 128 + ch * 64
                                self.M(lambda: nc.tensor.matmul(ap_[0:64, ch * 64:(ch + 1) * 64], lhsT=gt[:, dr * 2 + 1, h, tk:tk + 64],
                                                                rhs=gt[:, dr * 2 + 0, h, tk:tk + 64], start=True, stop=True),
                                       [bgt], pw=[bap] if ch else (), w=[bap] if ch == 0 else ())
                            at, bat = atts[ai % 4]
                            ai += 1
                            self.V(lambda: nc.vector.tensor_tensor(out=at[:], in0=ap_[0:64, 0:128].rearrange("p (c i) -> p c i", i=64),
                                                                   in1=self.gmask[0:64, dr, :].unsqueeze(1).to_broadcast([64, 2, 64]), op=ALU.mult),
                                   [bap, self.b_gmask], [bat])
                            asb.append((at, bat))
                        for ch in range(2):
                            tk = j * 128 + ch * 64
                            cl = 2 * j + ch
                            o_ = ops_[ch * 64:(ch + 1) * 64, h * 128:(h + 1) * 128]
                            first = (h == 0 and ch == 0)
                            for dr in range(2):
                                at, bat = asb[dr]
                                self.M(lambda: nc.tensor.matmul(o_, lhsT=at[:, ch, :], rhs=gv[:, cl, h * 128:(h + 1) * 128],
                                                                start=(dr == 0), stop=False),
                                       [bat, bgv], pw=[bops] if not (first and dr == 0) else (), w=[bops] if (first and dr == 0) else ())
                                self.M(lambda: nc.tensor.matmul(o_, lhsT=gt[:, dr * 2 + 0, h, tk:tk + 64], rhs=sst[:, dr, h, cl, :],
                                                                start=False, stop=(dr == 1)),
                                       [bgt, bsst], pw=[bops])
                    self.A(lambda: nc.scalar.activation(out=sq[:], in_=ops_[:], func=AF.Square), [bops], [bsq])
                    self.V(lambda: nc.vector.tensor_reduce(out=ss4[:], in_=sq[:].rearrange("p (h e) -> p h e", e=128), axis=AX.X, op=ALU.add),
                           [bsq], [bss4])
                    self.rstd_from_ss(ss4[:], rs4[:], bss4, brs4, 128)
                    self.V(lambda: nc.vector.tensor_tensor(out=t1[:].rearrange("p (h e) -> p h e", e=128),
                                                           in0=ops_[:].rearrange("p (h e) -> p h e", e=128),
                                                           in1=rs4[:].unsqueeze(2).to_broadcast([128, 4, 128]), op=ALU.mult),
                           [bops, brs4], [bt1])
                    self.P(lambda: nc.gpsimd.tensor_tensor(out=t1[:], in0=t1[:], in1=gn[:], op=ALU.mult), [bt1, bgn], [bt1])
                    self.V(lambda: nc.vector.tensor_tensor(out=t3[:], in0=t1[:], in1=gr[:, j, :], op=ALU.mult), [bt1, bgr], [bt3])
                    p7, bp7 = self.ps[7]
                    p7b = p7[:].bitcast(BF16)
                    for h in range(4):
                        self.M(lambda h=h: nc.tensor.transpose(out=p7b[:, h * 128:(h + 1) * 128], in_=t3[:, h * 128:(h + 1) * 128],
                                                              identity=self.identb[:]),
                               [bt3, self.b_identb], pw=[bp7] if h else (), w=[bp7] if h == 0 else ())
                    self.A(lambda: nc.scalar.copy(out=ogT[:].rearrange("p h t -> p (h t)"), in_=p7b[:, 0:512]), [bp7], [bogT])
                    r0t = t0 + j * 128
                    self.st(d["OGLA"][:, r0t:r0t + 128].rearrange("(h p) t -> p h t", p=128), ogT[:], r=[bogT], pw=[db["OGLA"]])
            S.barrier()
            S.recycle(self.phase_bufs)
            self.phase_bufs = []

    def gen_c3(self, l, es):
        nc, S, i, d, db = self.nc, self.S, self.inp, self.scr, self.dbufs
        V, A, P, M = self.V, self.A, self.P, self.M
        NB5 = 256
        if True:
            R, bR = self.sb(es, "Rtab", [128, 2, 2, 8, NB5])
            mag, bmag = self.sb(es, "mag", [128, 2, 8])
            Bsb, bB = self.sb(es, "Bsb", [32, 2, 2, 8, 128], BF16)
            Csb, bC = self.sb(es, "Csb", [128, 2, 2, 8, 64], BF16)
            dcol, bdcol = self.sb(es, "dcol", [128, 2])
            glub, bglub = self.sb(es, "glub", [128, 2])
            gluw, bgluw = self.sb(es, "gluw", [128, 2, 256], BF16)
            self.ld(dcol[:], i["s5d"][l], w=[bdcol])
            self.ld(glub[:], i["s5glub"][l], w=[bglub])
            self.ldc(gluw[:], i["s5gluw"][l].rearrange("(c p) n -> p c n", p=128), w=[bgluw])
            carry, bcarry = self.sb(es, "carry", [128, 2, 8])
            if True:
                es2 = es
                a3, ba3 = self.sb(es2, "a3", [128, 3, 8])
                w = [self.sb(es2, "w%d" % k, [128, 8]) for k in range(12)]
                wi, bwi = self.sb(es2, "wi", [128, 8], I32)
                cf, bcf = self.sb(es2, "cf", [128, 2, 8, 64])
                fb, bfb = self.sb(es2, "fb", [128, 2, 8])
                tt = [self.sb(es2, "ct%d" % k, [128, 8, 64]) for k in range(2)]
                tr = [self.sb(es2, "tr%d" % k, [128, 8, NB5 // 2]) for k in range(2)]

                def ts(o, a, s1, s2, op0, op1=None):
                    if op1 is None:
                        V(lambda: nc.vector.tensor_scalar(out=o[0][:], in0=a[0][:], scalar1=s1, scalar2=None, op0=op0), [a[1]], [o[1]])
                    else:
                        V(lambda: nc.vector.tensor_scalar(out=o[0][:], in0=a[0][:], scalar1=s1, scalar2=s2, op0=op0, op1=op1), [a[1]], [o[1]])

                def tt_(o, a, b, op):
                    V(lambda: nc.vector.tensor_tensor(out=o[0][:], in0=a[0][:], in1=b[0][:], op=op), [a[1], b[1]], [o[1]])

                for dr in range(2):
                    self.ld(a3[:], i["s5a"][l, dr], w=[ba3])
                    self.ldc(Bsb[:, dr], i["s5b"][l, dr], pw=[bB])
                    self.ld(cf[:], i["s5c"][l, dr], w=[bcf])
                    are, aim, ldt = (a3[:, 0, :], ba3), (a3[:, 1, :], ba3), (a3[:, 2, :], ba3)
                    dt_, adt, ang, kf, r_, m_, sn, cs, t_, den, fre, fim = w
                    A(lambda: nc.scalar.activation(out=dt_[0][:], in_=a3[:, 2, :], func=AF.Exp), [ba3], [dt_[1]])
                    V(lambda: nc.vector.tensor_tensor(out=adt[0][:], in0=a3[:, 0, :], in1=dt_[0][:], op=ALU.mult), [ba3, dt_[1]], [adt[1]])
                    A(lambda: nc.scalar.activation(out=mag[:, dr, :], in_=adt[0][:], func=AF.Exp), [adt[1]], pw=[bmag])
                    V(lambda: nc.vector.tensor_tensor(out=ang[0][:], in0=a3[:, 1, :], in1=dt_[0][:], op=ALU.mult), [ba3, dt_[1]], [ang[1]])
                    ts(kf, ang, 1.0 / (2 * PI), None, ALU.mult)
                    V(lambda: nc.vector.tensor_copy(out=wi[:], in_=kf[0][:]), [kf[1]], [bwi])
                    V(lambda: nc.vector.tensor_copy(out=kf[0][:], in_=wi[:]), [bwi], [kf[1]])
                    V(lambda: nc.vector.scalar_tensor_tensor(out=r_[0][:], in0=kf[0][:], scalar=-2 * PI, in1=ang[0][:], op0=ALU.mult, op1=ALU.add),
                      [kf[1], ang[1]], [r_[1]])
                    ts(m_, r_, PI, -2 * PI, ALU.is_gt, ALU.mult)
                    tt_(r_, r_, m_, ALU.add)
                    ts(m_, r_, -PI, 2 * PI, ALU.is_lt, ALU.mult)
                    tt_(r_, r_, m_, ALU.add)
                    A(lambda: nc.scalar.activation(out=sn[0][:], in_=r_[0][:], func=AF.Sin), [r_[1]], [sn[1]])
                    ts(t_, r_, -1.0, None, ALU.mult)
                    tt_(t_, t_, r_, ALU.max)
                    ts(t_, t_, -1.0, PI / 2, ALU.mult, ALU.add)
                    A(lambda: nc.scalar.activation(out=cs[0][:], in_=t_[0][:], func=AF.Sin), [t_[1]], [cs[1]])
                    V(lambda: nc.vector.tensor_copy(out=R[:, dr, 0, :, 0], in_=cs[0][:]), [cs[1]], pw=[bR])
                    V(lambda: nc.vector.tensor_copy(out=R[:, dr, 1, :, 0], in_=sn[0][:]), [sn[1]], pw=[bR])
                    abre, abim = kf, m_
                    V(lambda: nc.vector.tensor_tensor(out=abre[0][:], in0=mag[:, dr, :], in1=cs[0][:], op=ALU.mult), [bmag, cs[1]], [abre[1]])
                    V(lambda: nc.vector.tensor_tensor(out=abim[0][:], in0=mag[:, dr, :], in1=sn[0][:], op=ALU.mult), [bmag, sn[1]], [abim[1]])
                    ts(abre, abre, -1.0, None, ALU.add)
                    V(lambda: nc.vector.tensor_tensor(out=den[0][:], in0=a3[:, 0, :], in1=a3[:, 0, :], op=ALU.mult), [ba3], [den[1]])
                    V(lambda: nc.vector.tensor_tensor(out=t_[0][:], in0=a3[:, 1, :], in1=a3[:, 1, :], op=ALU.mult), [ba3], [t_[1]])
                    tt_(den, den, t_, ALU.add)
                    V(lambda: nc.vector.reciprocal(out=den[0][:], in_=den[0][:]), [den[1]], [den[1]])
                    V(lambda: nc.vector.tensor_tensor(out=fre[0][:], in0=abre[0][:], in1=a3[:, 0, :], op=ALU.mult), [abre[1], ba3], [fre[1]])
                    V(lambda: nc.vector.tensor_tensor(out=t_[0][:], in0=abim[0][:], in1=a3[:, 1, :], op=ALU.mult), [abim[1], ba3], [t_[1]])
                    tt_(fre, fre, t_, ALU.add)
                    V(lambda: nc.vector.tensor_tensor(out=fb[:, 0, :], in0=fre[0][:], in1=den[0][:], op=ALU.mult), [fre[1], den[1]], pw=[bfb])
                    V(lambda: nc.vector.tensor_tensor(out=fim[0][:], in0=abim[0][:], in1=a3[:, 0, :], op=ALU.mult), [abim[1], ba3], [fim[1]])
                    V(lambda: nc.vector.tensor_tensor(out=t_[0][:], in0=abre[0][:], in1=a3[:, 1, :], op=ALU.mult), [abre[1], ba3], [t_[1]])
                    tt_(fim, fim, t_, ALU.subtract)
                    V(lambda: nc.vector.tensor_tensor(out=fb[:, 1, :], in0=fim[0][:], in1=den[0][:], op=ALU.mult), [fim[1], den[1]], pw=[bfb])
                    frb = fb[:, 0, :].unsqueeze(2).to_broadcast([128, 8, 64])
                    fib = fb[:, 1, :].unsqueeze(2).to_broadcast([128, 8, 64])
                    V(lambda: nc.vector.tensor_tensor(out=tt[0][0][:], in0=cf[:, 0], in1=frb, op=ALU.mult), [bcf, bfb], [tt[0][1]])
                    V(lambda: nc.vector.tensor_tensor(out=tt[1][0][:], in0=cf[:, 1], in1=fib, op=ALU.mult), [bcf, bfb], [tt[1][1]])
                    V(lambda: nc.vector.tensor_tensor(out=Csb[:, dr, 0], in0=tt[0][0][:], in1=tt[1][0][:], op=ALU.subtract), [tt[0][1], tt[1][1]], pw=[bC])
                    V(lambda: nc.vector.tensor_tensor(out=tt[0][0][:], in0=cf[:, 0], in1=fib, op=ALU.mult), [bcf, bfb], [tt[0][1]])
                    V(lambda: nc.vector.tensor_tensor(out=tt[1][0][:], in0=cf[:, 1], in1=frb, op=ALU.mult), [bcf, bfb], [tt[1][1]])
                    V(lambda: nc.vector.tensor_tensor(out=tt[0][0][:], in0=tt[0][0][:], in1=tt[1][0][:], op=ALU.add), [tt[0][1], tt[1][1]], [tt[0][1]])
                    V(lambda: nc.vector.tensor_scalar(out=Csb[:, dr, 1], in0=tt[0][0][:], scalar1=-1.0, scalar2=None, op0=ALU.mult), [tt[0][1]], pw=[bC])
                    nn = 1
                    while nn < NB5:
                        cr = R[:, dr, 0, :, nn - 1:nn].to_broadcast([128, 8, nn])
                        ci = R[:, dr, 1, :, nn - 1:nn].to_broadcast([128, 8, nn])
                        sre, sim = R[:, dr, 0, :, 0:nn], R[:, dr, 1, :, 0:nn]
                        u0, u1 = tr[0][0][:, :, 0:nn], tr[1][0][:, :, 0:nn]
                        V(lambda: nc.vector.tensor_tensor(out=u0, in0=sre, in1=cr, op=ALU.mult), [bR], [tr[0][1]])
                        V(lambda: nc.vector.tensor_tensor(out=u1, in0=sim, in1=ci, op=ALU.mult), [bR], [tr[1][1]])
                        V(lambda: nc.vector.tensor_tensor(out=R[:, dr, 0, :, nn:2 * nn], in0=u0, in1=u1, op=ALU.subtract), [tr[0][1], tr[1][1]], pw=[bR])
                        V(lambda: nc.vector.tensor_tensor(out=u0, in0=sre, in1=ci, op=ALU.mult), [bR], [tr[0][1]])
                        V(lambda: nc.vector.tensor_tensor(out=u1, in0=sim, in1=cr, op=ALU.mult), [bR], [tr[1][1]])
                        V(lambda: nc.vector.tensor_tensor(out=R[:, dr, 1, :, nn:2 * nn], in0=u0, in1=u1, op=ALU.add), [tr[0][1], tr[1][1]], pw=[bR])
                        nn *= 2
                        yield
                    yield
            if l == 0:
                self.dump("Rtab", R[:], [bR], [128, 2, 2, 8, 512])
                self.dump("mag", mag[:], [bmag], [128, 2, 8])

            Rneg, bRneg = self.sb(es, "Rneg", [128, 2, 8])
            for dr in range(2):
                V(lambda: nc.vector.tensor_scalar(out=Rneg[:, dr, :], in0=R[:, dr, 1, :, NB5 - 1], scalar1=-1.0, scalar2=None, op0=ALU.mult),
                  [bR], pw=[bRneg])
            us = [self.sb(es, "us%d" % k, [32, 8, NB5], BF16) for k in range(2)]
            ufs = [self.sb(es, "uf%d" % k, [128, 2, NB5], BF16) for k in range(2)]
            yfls = [self.sb(es, "yfl%d" % k, [128, 2, NB5]) for k in range(2)]
            tqA = [[self.sb(es, "tqA%d_%d" % (q, k), [128, NB5]) for k in range(4)] for q in range(2)]
            tqB = [[self.sb(es, "tqB%d_%d" % (q, k), [128, NB5]) for k in range(4)] for q in range(2)]
            zz = [[self.sb(es, "z%d_%d" % (q, k), [128, NB5]) for k in range(2)] for q in range(2)]
            scs = [[self.sb(es, "sc%d_%d" % (q, k), [128, NB5]) for k in range(2)] for q in range(3)]
            sbfs = [[self.sb(es, "sbf%d_%d" % (q, k), [128, NB5], BF16) for k in range(2)] for q in range(2)]
            ct = [self.sb(es, "ct%d" % k, [128, 1]) for k in range(2)]
            cfg = [self.sb(es, "cfg%d" % k, [128, 2, 8]) for k in range(2)]
            carryb = [S.buf("carry_m%d" % m) for m in range(8)]
            ysbs = [self.sb(es, "ysb%d" % k, [128, 2, NB5]) for k in range(2)]
            yg, byg = self.sb(es, "yg", [128, 2, NB5], BF16)
            sg, bsg = self.sb(es, "sg", [128, NB5])
            os5, bos5 = self.sb(es, "os5", [128, 2, NB5], BF16)
            p6, b6 = self.ps[6]
            p7, b7 = self.ps[7]
            n = NB5
            ctxb = [(0, 256)]
            latb = [(256 + NB5 * j, NB5) for j in range(self.NL // NB5)]
            for dr in range(2):
                for m in range(8):
                    V(lambda: nc.vector.memset(carry[:, :, m:m + 1], 0.0), w=[carryb[m]])
                order = ctxb + (latb if dr == 0 else latb[::-1])
                items = [(bi, blk, m) for bi, blk in enumerate(order) for m in range(8)]
                blkst = {}
                epis = []

                def dirv(ap2):
                    return ap2 if dr == 0 else rev(ap2)

                def s_load(bi):
                    if bi >= len(order) or bi in blkst:
                        return
                    t0, _n = order[bi]
                    u, bu = us[bi % 2]
                    self.ld(u[:, :, 0:n], d["SU"][:, t0:t0 + n].rearrange("(m r) t -> r m t", r=32), r=[db["SU"]], w=[bu])
                    st_ = dict(u=u, bu=bu)
                    if dr == 1:
                        uf, buf_ = ufs[bi % 2]
                        yfl, byfl = yfls[bi % 2]
                        self.ld(uf[:, :, 0:n], d["SU"][:, t0:t0 + n].rearrange("(c p) t -> p c t", p=128), r=[db["SU"]], w=[buf_])
                        self.ld(yfl[:, :, 0:n], d["YF"][:, t0:t0 + n].rearrange("(c p) t -> p c t", p=128), r=[db["YF"]], w=[byfl])
                        st_.update(uf=uf, buf_=buf_, yfl=yfl, byfl=byfl)
                    blkst[bi] = st_

                def stage1(k):
                    bi, (t0, _n), m = items[k]
                    if m == 0:
                        s_load(bi)
                    if m == 4:
                        s_load(bi + 1)
                    u, bu = blkst[bi]["u"], blkst[bi]["bu"]
                    M(lambda: nc.tensor.matmul(p6[:, 0:n], lhsT=Bsb[:, dr, 0, m, :], rhs=u[:, m, 0:n], start=True, stop=True), [bB, bu], [b6])
                    M(lambda: nc.tensor.matmul(p6[:, n:2 * n], lhsT=Bsb[:, dr, 1, m, :], rhs=u[:, m, 0:n], start=True, stop=True), [bB, bu], pw=[b6])

                def stage2(k):
                    bi, (t0, _n), m = items[k]
                    pre, pim = p6[:, 0:n], p6[:, n:2 * n]
                    Rre = dirv(R[:, dr, 0, m, 0:n])
                    Rim = dirv(R[:, dr, 1, m, 0:n])
                    tq = tqA[k % 2]
                    V(lambda: nc.vector.tensor_tensor(out=tq[0][0][:, 0:n], in0=pre, in1=Rre, op=ALU.mult), [b6, bR], [tq[0][1]])
                    V(lambda: nc.vector.tensor_tensor(out=tq[1][0][:, 0:n], in0=pim, in1=Rim, op=ALU.mult), [b6, bR], [tq[1][1]])
                    V(lambda: nc.vector.tensor_tensor(out=tq[2][0][:, 0:n], in0=pim, in1=Rre, op=ALU.mult), [b6, bR], [tq[2][1]])
                    V(lambda: nc.vector.tensor_tensor(out=tq[3][0][:, 0:n], in0=pre, in1=Rim, op=ALU.mult), [b6, bR], [tq[3][1]])
                    z = zz[k % 2]
                    P(lambda: nc.gpsimd.tensor_tensor(out=z[0][0][:, 0:n], in0=tq[0][0][:, 0:n], in1=tq[1][0][:, 0:n], op=ALU.add),
                      [tq[0][1], tq[1][1]], [z[0][1]])
                    P(lambda: nc.gpsimd.tensor_tensor(out=z[1][0][:, 0:n], in0=tq[2][0][:, 0:n], in1=tq[3][0][:, 0:n], op=ALU.subtract),
                      [tq[2][1], tq[3][1]], [z[1][1]])

                def stage3(k):
                    bi, (t0, _n), m = items[k]
                    z = zz[k % 2]
                    sc_ = scs[k % 3]
                    magb = mag[:, dr, m:m + 1].to_broadcast([128, n])
                    for ri in range(2):
                        V(lambda ri=ri: nc.vector.tensor_tensor_scan(out=dirv(sc_[ri][0][:, 0:n]), data0=magb, data1=dirv(z[ri][0][:, 0:n]),
                                                                    initial=carry[:, ri, m:m + 1], op0=ALU.mult, op1=ALU.add),
                          [bmag, z[ri][1], carryb[m]], [sc_[ri][1]])
                    Rre = dirv(R[:, dr, 0, m, 0:n])
                    Rim = dirv(R[:, dr, 1, m, 0:n])
                    tq = tqB[k % 2]
                    P(lambda: nc.gpsimd.tensor_tensor(out=tq[0][0][:, 0:n], in0=sc_[0][0][:, 0:n], in1=Rre, op=ALU.mult), [sc_[0][1], bR], [tq[0][1]])
                    P(lambda: nc.gpsimd.tensor_tensor(out=tq[1][0][:, 0:n], in0=sc_[1][0][:, 0:n], in1=Rim, op=ALU.mult), [sc_[1][1], bR], [tq[1][1]])
                    P(lambda: nc.gpsimd.tensor_tensor(out=tq[2][0][:, 0:n], in0=sc_[0][0][:, 0:n], in1=Rim, op=ALU.mult), [sc_[0][1], bR], [tq[2][1]])
                    P(lambda: nc.gpsimd.tensor_tensor(out=tq[3][0][:, 0:n], in0=sc_[1][0][:, 0:n], in1=Rre, op=ALU.mult), [sc_[1][1], bR], [tq[3][1]])

                def stage4(k):
                    bi, (t0, _n), m = items[k]
                    sc_ = scs[k % 3]
                    lastc = n - 1 if dr == 0 else 0
                    rl_re, rl_im = R[:, dr, 0, m, n - 1:n], R[:, dr, 1, m, n - 1:n]
                    nrl_im = Rneg[:, dr, m:m + 1]
                    s_re, s_im = sc_[0][0][:, lastc:lastc + 1], sc_[1][0][:, lastc:lastc + 1]
                    A(lambda: nc.scalar.activation(out=ct[0][0][:], in_=s_im, func=AF.Identity, scale=nrl_im), [sc_[1][1], bRneg], [ct[0][1]])
                    A(lambda: nc.scalar.activation(out=carry[:, 0, m:m + 1], in_=s_re, func=AF.Identity, scale=rl_re, bias=ct[0][0][:]),
                      [sc_[0][1], bR, ct[0][1]], [carryb[m]])
                    A(lambda: nc.scalar.activation(out=ct[1][0][:], in_=s_re, func=AF.Identity, scale=rl_im), [sc_[0][1], bR], [ct[1][1]])
                    A(lambda: nc.scalar.activation(out=carry[:, 1, m:m + 1], in_=s_im, func=AF.Identity, scale=rl_re, bias=ct[1][0][:]),
                      [sc_[1][1], bR, ct[1][1]], pw=[carryb[m]])
                    tq = tqB[k % 2]
                    (sr, bsr), (sm, bsm) = sbfs[k % 2]
                    V(lambda: nc.vector.tensor_tensor(out=sr[:, 0:n], in0=tq[0][0][:, 0:n], in1=tq[1][0][:, 0:n], op=ALU.subtract),
                      [tq[0][1], tq[1][1]], [bsr])
                    V(lambda: nc.vector.tensor_tensor(out=sm[:, 0:n], in0=tq[2][0][:, 0:n], in1=tq[3][0][:, 0:n], op=ALU.add),
                      [tq[2][1], tq[3][1]], [bsm])

                def stage5(k):
                    bi, (t0, _n), m = items[k]
                    (sr, bsr), (sm, bsm) = sbfs[k % 2]
                    c = m // 4
                    mo = 64 * ((m // 2) % 2)
                    yo = p7[mo:mo + 64, c * n:(c + 1) * n]
                    M(lambda: nc.tensor.matmul(yo, lhsT=Csb[:, dr, 0, m, :], rhs=sr[:, 0:n], start=(m % 2 == 0), stop=False),
                      [bC, bsr], pw=[b7] if m else (), w=[b7] if m == 0 else ())
                    M(lambda: nc.tensor.matmul(yo, lhsT=Csb[:, dr, 1, m, :], rhs=sm[:, 0:n], start=False, stop=(m % 2 == 1)),
                      [bC, bsm], pw=[b7])
                    if m == 7:
                        epis.append(epilogue(bi, t0))

                def epilogue(bi, t0):
                    st_ = blkst.pop(bi)
                    ysb, bysb = ysbs[bi % 2]
                    if dr == 0:
                        V(lambda: nc.vector.tensor_copy(out=ysb[:].rearrange("p c t -> p (c t)"), in_=p7[:, 0:2 * n]), [b7], [bysb])
                        yield
                        self.st(d["YF"][:, t0:t0 + n].rearrange("(c p) t -> p c t", p=128), ysb[:, :, 0:n], r=[bysb], pw=[db["YF"]])
                        return
                    uf, buf_, yfl, byfl = st_["uf"], st_["buf_"], st_["yfl"], st_["byfl"]
                    V(lambda: nc.vector.tensor_tensor(out=ysb[:].rearrange("p c t -> p (c t)"), in0=p7[:, 0:2 * n],
                                                      in1=yfl[:].rearrange("p c t -> p (c t)"), op=ALU.add), [b7, byfl], [bysb])
                    for c in range(2):
                        V(lambda: nc.vector.scalar_tensor_tensor(out=ysb[:, c, 0:n], in0=uf[:, c, 0:n], scalar=dcol[:, c:c + 1], in1=ysb[:, c, 0:n],
                                                                 op0=ALU.mult, op1=ALU.add), [buf_, bdcol, bysb], [bysb])
                    yield
                    for c in range(2):
                        V(lambda: nc.vector.tensor_tensor(out=sg[:, 0:n], in0=ysb[:, c, 0:n], in1=ysb[:, c, 0:n], op=ALU.mult), [bysb], [bsg])
                        V(lambda: nc.vector.tensor_scalar(out=sg[:, 0:n], in0=sg[:, 0:n], scalar1=0.044715, scalar2=1.0, op0=ALU.mult, op1=ALU.add), [bsg], [bsg])
                        V(lambda: nc.vector.tensor_tensor(out=sg[:, 0:n], in0=sg[:, 0:n], in1=ysb[:, c, 0:n], op=ALU.mult), [bsg, bysb], [bsg])
                        A(lambda: nc.scalar.activation(out=sg[:, 0:n], in_=sg[:, 0:n], func=AF.Sigmoid, scale=2.0 * math.sqrt(2.0 / PI)), [bsg], [bsg])
                        V(lambda: nc.vector.tensor_tensor(out=yg[:, c, 0:n], in0=ysb[:, c, 0:n], in1=sg[:, 0:n], op=ALU.mult), [bysb, bsg],
                          pw=[byg] if c else (), w=[byg] if c == 0 else ())
                        yield
                    for c2 in range(2):
                        zp = p6[:, c2 * n:(c2 + 1) * n]
                        for c in range(2):
                            M(lambda c=c: nc.tensor.matmul(zp, lhsT=gluw[:, c, c2 * 128:(c2 + 1) * 128], rhs=yg[:, c, 0:n],
                                                           start=(c == 0), stop=(c == 1)), [bgluw, byg],
                              pw=[b6] if (c or c2) else (), w=[b6] if (c == 0 and c2 == 0) else ())
                    for c2 in range(2):
                        zp = p6[:, c2 * n:(c2 + 1) * n]
                        A(lambda: nc.scalar.activation(out=sg[:, 0:n], in_=zp, func=AF.Sigmoid, bias=glub[:, c2:c2 + 1], scale=1.0),
                          [b6, bglub], [bsg])
                        V(lambda: nc.vector.tensor_tensor(out=os5[:, c2, 0:n], in0=yg[:, c2, 0:n], in1=sg[:, 0:n], op=ALU.mult), [byg, bsg],
                          pw=[bos5] if c2 else (), w=[bos5] if c2 == 0 else ())
                    self.st(d["OS5"][:, t0:t0 + n].rearrange("(c p) t -> p c t", p=128), os5[:, :, 0:n], r=[bos5], pw=[db["OS5"]])

                def step_epis():
                    for g in list(epis):
                        try:
                            next(g)
                        except StopIteration:
                            epis.remove(g)

                def run_range(k0, k1):
                    stages = (stage1, stage2, stage3, stage4, stage5)
                    for k in range(k0, k1 + len(stages) - 1):
                        for si, fn in reversed(list(enumerate(stages))):
                            kk = k - si
                            if k0 <= kk < k1:
                                fn(kk)
                                if fn is stage2:
                                    step_epis()
                                yield
                        if not (k0 <= k - 1 < k1):
                            step_epis()
                    while epis:
                        step_epis()
                        yield
                NI = len(items)
                nctx_items = 8 * len(ctxb)
                if dr == 0:
                    yield from run_range(0, NI)
                    self.st(d["CFS"], carry[:].rearrange("p r m -> p (r m)"), r=carryb, w=[db["CFS"]], sbuf=bcarry)
                    S.collective("AllGather", ALU.bypass, self.groups, d["CFS"], d["CFG"], [db["CFS"]], [db["CFG"]], "s5")
                else:
                    yield from run_range(0, nctx_items)
                    for rk in range(2):
                        self.ld(cfg[rk][0][:], d["CFG"][rk * 128:(rk + 1) * 128, :].rearrange("p (r m) -> p r m", r=2), r=[db["CFG"]], w=[cfg[rk][1]])
                    V(lambda: nc.vector.tensor_scalar(out=carry[:], in0=cfg[0][0][:], scalar1=self.selw[:, 0:1], scalar2=None, op0=ALU.mult),
                      [cfg[0][1], self.b_selw], carryb)
                    V(lambda: nc.vector.scalar_tensor_tensor(out=carry[:], in0=cfg[1][0][:], scalar=self.selw[:, 1:2], in1=carry[:], op0=ALU.mult, op1=ALU.add),
                      [cfg[1][1], self.b_selw] + carryb, carryb)
                    yield from run_range(nctx_items, NI)

    def phase_bc3(self, l):
        with ExitStack() as es:
            gb = self.gen_b(l, es)
            gc = self.gen_c3(l, es)
            gens = [(gb, 1), (gc, 2)]
            while gens:
                for g, reps in list(gens):
                    try:
                        for _ in range(reps):
                            next(g)
                    except StopIteration:
                        gens.remove((g, reps))
            self.S.barrier()
            self.S.recycle(self.phase_bufs)
            self.phase_bufs = []

    def phase_d(self, l, xin, xin_b, last):
        nc, S, i, d, db = self.nc, self.S, self.inp, self.scr, self.dbufs
        V, A, P, M = self.V, self.A, self.P, self.M
        with ExitStack() as es:
            WG, bWG = self.sb(es, "WG", [128, 8, 3072], BF16)
            bWGg = [[S.buf("WG_%d" % f)] * 3 for f in range(8)]
            self.phase_bufs += [g[0] for g in bWGg]

            def load_wg(f):
                for br in range(3):
                    c0 = br * 1024 + f * 128
                    self.ldc(WG[:, :, c0:c0 + 128], i["w_in"][l, :, NA + c0:NA + c0 + 128].rearrange("(k p) n -> p k n", p=128),
                             w=[bWGg[f][br]] if br == 0 else (), pw=[bWGg[f][br]] if br else ())
            load_wg(0)
            Wba, bWba = self.sb(es, "Wba", [128, 2, D], BF16)
            Wbg, bWbg = self.sb(es, "Wbg", [128, 4, D], BF16)
            Wbs, bWbs = self.sb(es, "Wbs", [128, 2, D], BF16)
            Wo, bWo = self.sb(es, "Wo", [128, 8, D], BF16)
            self.ldc(Wba[:], i["w_br_att"][l].rearrange("(k p) n -> p k n", p=128), w=[bWba])
            self.ldc(Wbg[:], i["w_br_gla"][l].rearrange("(k p) n -> p k n", p=128), w=[bWbg])
            self.ldc(Wbs[:], i["w_br_s5"][l].rearrange("(k p) n -> p k n", p=128), w=[bWbs])
            for f in range(1, 8):
                load_wg(f)
            for k in range(8):
                self.ldc(Wo[:, k, :], i["w_out"][l, k * 128:(k + 1) * 128, :], pw=[bWo])
            hTs = [self.sb(es, "dhT%d" % k, [128, 8, 512], BF16) for k in range(2)]
            srcs = [self.sb(es, "dsrc%d" % k, [128, 8, 512], BF16) for k in range(2)]
            sgs = [self.sb(es, "dsg%d" % k, [128, 512]) for k in range(3)]
            macc, bmacc = self.sb(es, "macc", [128, 512])
            tacc, btacc = self.sb(es, "tacc", [128, 512])
            mT, bmT = self.sb(es, "mT", [128, 8, 512], BF16)
            xts = [self.sb(es, "dxt%d" % k, [128, D]) for k in range(2)]
            xos = [self.sb(es, "dxo%d" % k, [128, D]) for k in range(2)]
            junk, bjunk = self.sb(es, "djunk", [128, 512])
            ss2, bss2 = self.sb(es, "dss2", [128, 2])
            ss, bss = self.sb(es, "dss", [128, 1])
            rs, brs = self.sb(es, "drs", [128, 1])
            tt, btt = self.sb(es, "dtt", [128, D])
            branches = ((Wba, bWba, 0, 2), (Wbg, bWbg, 2, 4), (Wbs, bWbs, 6, 2))
            ti = 0
            for bi, (t0, n, isctx) in enumerate(self.blocks):
                if isctx and last:
                    continue
                which = 1 if isctx else 0
                nt = n // 128
                hT, bh = hTs[bi % 2]
                sr, bsr = srcs[bi % 2]
                self.ld(hT[:, :, 0:n], d["HT"][:, t0:t0 + n].rearrange("(k p) t -> p k t", p=128), r=[db["HT"]], w=[bh])
                self.ld(sr[:, 0:2, 0:n], d["OATT"][:, t0:t0 + n].rearrange("(k p) t -> p k t", p=128), r=[db["OATT"]], w=[bsr])
                self.ld(sr[:, 2:6, 0:n], d["OGLA"][:, t0:t0 + n].rearrange("(k p) t -> p k t", p=128), r=[db["OGLA"]], pw=[bsr])
                self.ld(sr[:, 6:8, 0:n], d["OS5"][:, t0:t0 + n].rearrange("(k p) t -> p k t", p=128), r=[db["OS5"]], pw=[bsr])
                for f in range(8):
                    for br, (W, bW, k0, nk) in enumerate(branches):
                        pb, bpb = self.ps[br]
                        pg, bpg = self.ps[3 + br]
                        for k in range(nk):
                            M(lambda k=k: nc.tensor.matmul(pb[:, 0:n], lhsT=W[:, k, f * 128:(f + 1) * 128], rhs=sr[:, k0 + k, 0:n],
                                                           start=(k == 0), stop=(k == nk - 1)), [bW, bsr], pw=[bpb] if k else (), w=[bpb] if k == 0 else ())
                        for k in range(8):
                            M(lambda k=k: nc.tensor.matmul(pg[:, 0:n], lhsT=WG[:, k, br * 1024 + f * 128:br * 1024 + (f + 1) * 128], rhs=hT[:, k, 0:n],
                                                           start=(k == 0), stop=(k == 7)), [bWGg[f][br], bh], pw=[bpg] if k else (), w=[bpg] if k == 0 else ())
                        sg, bsg = sgs[br]
                        A(lambda: nc.scalar.activation(out=sg[:, 0:n], in_=pg[:, 0:n], func=AF.Sigmoid), [bpg], [bsg])
                        if br == 0:
                            V(lambda: nc.vector.tensor_tensor(out=macc[:, 0:n], in0=sg[:, 0:n], in1=pb[:, 0:n], op=ALU.mult), [bsg, bpb], [bmacc])
                        else:
                            V(lambda: nc.vector.tensor_tensor(out=tacc[:, 0:n], in0=sg[:, 0:n], in1=pb[:, 0:n], op=ALU.mult), [bsg, bpb], [btacc])
                            if br == 1:
                                P(lambda: nc.gpsimd.tensor_tensor(out=macc[:, 0:n], in0=macc[:, 0:n], in1=tacc[:, 0:n], op=ALU.add), [bmacc, btacc], [bmacc])
                            else:
                                P(lambda: nc.gpsimd.tensor_tensor(out=mT[:, f, 0:n], in0=macc[:, 0:n], in1=tacc[:, 0:n], op=ALU.add), [bmacc, btacc],
                                  pw=[bmT] if f else (), w=[bmT] if f == 0 else ())
                for j in range(nt):
                    r0 = t0 + j * 128
                    xt, bx = xts[ti % 2]
                    xo, bxo = xos[ti % 2]
                    ti += 1
                    self.ld(xt[:], xin[r0:r0 + 128, :], r=[xin_b], w=[bx])
                    for hf in range(2):
                        py, bpy = self.ps[6 + hf]
                        for k in range(8):
                            M(lambda k=k: nc.tensor.matmul(py[:], lhsT=mT[:, k, j * 128:(j + 1) * 128], rhs=Wo[:, k, hf * 512:(hf + 1) * 512],
                                                           start=(k == 0), stop=(k == 7)), [bmT, bWo], pw=[bpy] if k else (), w=[bpy] if k == 0 else ())
                        if hf == 0:
                            A(lambda: nc.scalar.activation(out=junk[:], in_=py[:], func=AF.Square, accum_out=ss2[:, 0:1]), [bpy], [bjunk, bss2])
                        else:
                            A(lambda: nc.scalar.activation(out=junk[:], in_=py[:], func=AF.Square, accum_out=ss2[:, 1:2]), [bpy], [bjunk], pw=[bss2])
                    V(lambda: nc.vector.tensor_tensor(out=ss[:], in0=ss2[:, 0:1], in1=ss2[:, 1:2], op=ALU.add), [bss2], [bss])
                    self.rstd_from_ss(ss[:], rs[:], bss, brs, D)
                    for hf in range(2):
                        py, bpy = self.ps[6 + hf]
                        cs = slice(hf * 512, (hf + 1) * 512)
                        V(lambda: nc.vector.scalar_tensor_tensor(out=tt[:, cs], in0=py[:], scalar=rs[:, 0:1], in1=self.GG[:, which, 0, cs],
                                                                 op0=ALU.mult, op1=ALU.mult), [bpy, brs, self.b_GG], pw=[btt] if hf else (), w=[btt] if hf == 0 else ())
                    P(lambda: nc.gpsimd.tensor_tensor(out=xo[:], in0=tt[:], in1=xt[:], op=ALU.add), [btt, bx], [bxo])
                    self.st(d["X2"][r0:r0 + 128, :], xo[:], r=[bxo], pw=[db["X2"]])
            S.barrier()
            S.recycle(self.phase_bufs)
            self.phase_bufs = []

    def phase_e(self, l, last):
        nc, S, i, d, db = self.nc, self.S, self.inp, self.scr, self.dbufs
        V, A, P, M = self.V, self.A, self.P, self.M
        with ExitStack() as es:
            Wup, bWup = self.sb(es, "Wup", [128, 8, 2 * DFF], BF16)
            bWupg = [[S.buf("Wup_%d" % g)] * 2 for g in range(11)]
            self.phase_bufs += [g[0] for g in bWupg]
            for g in range(11):
                for av in range(2):
                    c0 = av * DFF + g * 256
                    self.ldc(Wup[:, :, c0:c0 + 256], i["ffn_up"][l, :, c0:c0 + 256].rearrange("(k p) n -> p k n", p=128),
                             w=[bWupg[g][av]] if av == 0 else (), pw=[bWupg[g][av]] if av else ())
            Wdn, bWdn = self.sb(es, "Wdn", [128, 22, D], BF16)
            bWdng = [S.buf("Wdn_%d" % g) for g in range(11)]
            self.phase_bufs += bWdng
            for g in range(11):
                self.ldc(Wdn[:, 2 * g:2 * g + 2, :], i["ffn_down"][l, g * 256:(g + 1) * 256, :].rearrange("(k p) n -> p k n", p=128), w=[bWdng[g]])
            cp, bcp = self.sb(es, "cp", [128, 4, 44])
            self.ld(cp[:], i["convp"][l], w=[bcp])
            xts = [self.sb(es, "ext%d" % k, [128, D]) for k in range(1)]
            tmp = (self.sb(es, "exn", [128, D]), self.sb(es, "ess", [128, 1]), self.sb(es, "ers", [128, 1]))
            hT, bh = self.sb(es, "ehT", [128, 8, 512], BF16)
            gT, bgT = self.sb(es, "gT", [128, 22, 512], BF16)
            ua, bua = self.sb(es, "ua", [128, 512])
            uv, buv = self.sb(es, "uv", [128, 512])
            us_, bus = self.sb(es, "usl", [128, 512])
            junk, bjunk = uv, buv
            ss2, bss2 = self.sb(es, "ess2", [128, 2])
            ss, bss = self.sb(es, "ess1", [128, 1])
            rs, brs = self.sb(es, "ers1", [128, 1])
            tt, btt = self.sb(es, "ett", [128, D])
            xo, bxo = tt, btt
            xm, bxm = xts[0]
            pst = (self.ps[0], self.ps[1])
            segs = [(0, 256, 1)] + [(256, self.T, 0)]
            T_ = self.T
            self.st(d["XHS"], d["X2"][T_ - 1:T_, :], r=[db["X2"]], w=[db["XHS"]], sbuf=S.buf("xh_dma"))
            S.collective("AllGather", ALU.bypass, self.groups, d["XHS"], d["XHG"], [db["XHS"]], [db["XHG"]], "halo")
            xn_t, bxn_t = tmp[0]
            self.ld(tt[0:1, :], d["XHG"][0:1, :], r=[db["XHG"]], w=[btt])
            self.ld(xn_t[0:1, :], d["XHG"][1:2, :], r=[db["XHG"]], w=[bxn_t])
            V(lambda: nc.vector.tensor_scalar(out=tt[0:1, :], in0=tt[0:1, :], scalar1=self.selw[0:1, 0:1], scalar2=None, op0=ALU.mult),
              [btt, self.b_selw], [btt])
            V(lambda: nc.vector.scalar_tensor_tensor(out=tt[0:1, :], in0=xn_t[0:1, :], scalar=self.selw[0:1, 1:2], in1=tt[0:1, :], op0=ALU.mult, op1=ALU.add),
              [bxn_t, self.b_selw, btt], [btt])
            self.st(d["XHX"], tt[0:1, :], r=[btt], w=[db["XHX"]])
            for (s0, s1, which) in segs:
                if which == 1 and last:
                    continue
                oa = s0
                while oa < s1:
                    ob = min(oa + 510, s1)
                    ra = oa - 1 if oa > s0 else oa
                    rb_ = ob + 1 if ob < s1 else ob
                    halo = (which == 0 and ob == s1)
                    nrows = rb_ - ra + (1 if halo else 0)
                    j = 0
                    r = ra
                    while r < rb_:
                        nr = min(128, rb_ - r)
                        xt, bx = xts[0]
                        self.ld(xt[0:nr, :], d["X2"][r:r + nr, :], r=[db["X2"]], w=[bx])
                        nr2 = nr
                        if halo and r + nr == rb_:
                            assert nr < 128
                            self.ld(xt[nr:nr + 1, :], d["XHX"], r=[db["XHX"]], pw=[bx])
                            nr2 = nr + 1
                        self.norm_tile(xt, bx, nr2, which, 1, hT, bh, r - ra, tmp, pst)
                        r += nr
                        j += 1
                    lo, hi = oa - ra, ob - ra
                    nout = hi - lo
                    l0 = 1 if lo == 0 else 0
                    r1 = 1 if hi == nrows else 0
                    for cf in range(22):
                        res = []
                        for av, (uu, buu) in enumerate(((ua, bua), (uv, buv))):
                            pz, bpz = self.ps[2 + av + 2 * (cf % 2)]
                            c0 = av * DFF + cf * 128
                            ci = av * 22 + cf
                            for k in range(8):
                                M(lambda k=k: nc.tensor.matmul(pz[:, 0:nrows], lhsT=Wup[:, k, c0:c0 + 128], rhs=hT[:, k, 0:nrows],
                                                               start=(k == 0), stop=(k == 7)), [bWupg[cf // 2][av], bh], pw=[bpz] if k else (), w=[bpz] if k == 0 else ())
                            V(lambda: nc.vector.tensor_scalar(out=uu[:, 0:nout], in0=pz[:, lo:hi], scalar1=cp[:, 1, ci:ci + 1], scalar2=cp[:, 3, ci:ci + 1],
                                                              op0=ALU.mult, op1=ALU.add), [bpz, bcp], [buu])
                            V(lambda: nc.vector.scalar_tensor_tensor(out=uu[:, l0:nout], in0=pz[:, lo + l0 - 1:hi - 1], scalar=cp[:, 0, ci:ci + 1],
                                                                     in1=uu[:, l0:nout], op0=ALU.mult, op1=ALU.add), [bpz, bcp, buu], [buu])
                            V(lambda: nc.vector.scalar_tensor_tensor(out=uu[:, 0:nout - r1], in0=pz[:, lo + 1:hi + 1 - r1], scalar=cp[:, 2, ci:ci + 1],
                                                                     in1=uu[:, 0:nout - r1], op0=ALU.mult, op1=ALU.add), [bpz, bcp, buu], [buu])
                        A(lambda: nc.scalar.activation(out=us_[:, 0:nout], in_=ua[:, 0:nout], func=AF.Silu), [bua], [bus])
                        P(lambda: nc.gpsimd.tensor_tensor(out=gT[:, cf, 0:nout], in0=us_[:, 0:nout], in1=uv[:, 0:nout], op=ALU.mult), [bus, buv],
                          pw=[bgT] if cf else (), w=[bgT] if cf == 0 else ())
                    jo = 0
                    while jo * 128 < nout:
                        no = min(128, nout - jo * 128)
                        r0 = oa + jo * 128
                        self.ld(xm[0:no, :], d["X2"][r0:r0 + no, :], r=[db["X2"]], w=[bxm])
                        for hf in range(2):
                            py, bpy = self.ps[6 + hf]
                            for k in range(22):
                                M(lambda k=k: nc.tensor.matmul(py[0:no, :], lhsT=gT[:, k, jo * 128:jo * 128 + no], rhs=Wdn[:, k, hf * 512:(hf + 1) * 512],
                                                               start=(k == 0), stop=(k == 21)), [bgT, bWdng[k // 2]], pw=[bpy] if k else (), w=[bpy] if k == 0 else ())
                            if hf == 0:
                                A(lambda: nc.scalar.activation(out=junk[0:no, :], in_=py[0:no, :], func=AF.Square, accum_out=ss2[0:no, 0:1]), [bpy], [bjunk, bss2])
                            else:
                                A(lambda: nc.scalar.activation(out=junk[0:no, :], in_=py[0:no, :], func=AF.Square, accum_out=ss2[0:no, 1:2]), [bpy], [bjunk], pw=[bss2])
                        V(lambda: nc.vector.tensor_tensor(out=ss[0:no, :], in0=ss2[0:no, 0:1], in1=ss2[0:no, 1:2], op=ALU.add), [bss2], [bss])
                        self.rstd_from_ss(ss[0:no, :], rs[0:no, :], bss, brs, D, no)
                        for hf in range(2):
                            py, bpy = self.ps[6 + hf]
                            cs = slice(hf * 512, (hf + 1) * 512)
                            V(lambda: nc.vector.scalar_tensor_tensor(out=tt[0:no, cs], in0=py[0:no, :], scalar=rs[0:no, 0:1], in1=self.GG[0:no, which, 1, cs],
                                                                     op0=ALU.mult, op1=ALU.mult), [bpy, brs, self.b_GG], pw=[btt] if hf else (), w=[btt] if hf == 0 else ())
                        P(lambda: nc.gpsimd.tensor_tensor(out=xo[0:no, :], in0=tt[0:no, :], in1=xm[0:no, :], op=ALU.add), [btt, bxm], [bxo])
                        if last:
                            self.st(self.out[r0 - 256:r0 - 256 + no, :], xo[0:no, :], r=[bxo], pw=[self.out_buf])
                        else:
                            self.st(d["X1"][r0:r0 + no, :], xo[0:no, :], r=[bxo], pw=[db["X1"]])
                        jo += 1
                    oa = ob
            S.barrier()
            S.recycle(self.phase_bufs)
            self.phase_bufs = []


def host_consts(n_lat):
    rows = n_lat // 64
    row = np.repeat(np.arange(rows, dtype=np.float32), 64)
    col = np.tile(np.arange(64, dtype=np.float32), rows)
    n_freq = 16
    inv_freq = (np.float32(10000.0) ** (-np.arange(n_freq, dtype=np.float32) / n_freq)).astype(np.float32)
    ang = np.stack([row[:, None] * inv_freq, col[:, None] * inv_freq], axis=1)
    rope = np.concatenate([np.cos(ang).reshape(n_lat, 32), np.sin(ang).reshape(n_lat, 32)], axis=1).astype(np.float32)
    jj = np.arange(128) % 64
    ii = np.arange(64)
    gmask = np.stack([(jj[:, None] <= ii[None, :]), (jj[:, None] >= ii[None, :])], axis=1).astype(np.float32)
    scanmask = np.ones((128, 2, 512), np.float32)
    scanmask[:, 0, ::64] = 0.0
    scanmask[:, 1, 63::64] = 0.0
    return dict(ident=np.eye(128, dtype=np.float32), rope=rope, gmask=gmask, scanmask=scanmask)


def host_layout(inputs, L):
    f = lambda a: np.ascontiguousarray(np.asarray(a, dtype=np.float32))
    p = {}
    p["ada_w"] = f(inputs["ada_w"])[:L]
    p["ada_b"] = f(inputs["ada_b"])[:L]
    colz = lambda v: v.reshape(L, -1, 128).transpose(0, 2, 1)
    p["npre"] = f(np.stack([colz(f(inputs["norm_mix_pre"])[:L]), colz(f(inputs["norm_ffn_pre"])[:L])], axis=2))
    p["npost"] = f(np.stack([f(inputs["norm_mix_post"])[:L], f(inputs["norm_ffn_post"])[:L]], axis=1))
    p["w_in"] = f(inputs["w_in"])[:L]
    qn, kn = f(inputs["q_norm"])[:L], f(inputs["k_norm"])[:L]
    p["qkg"] = f(np.concatenate([np.tile(qn, (1, 4)), np.tile(kn, (1, 2))], axis=1))
    gw = f(inputs["gla_gate_w"])[:L]
    gwp = np.zeros((L, 32, 2, 256), np.float32)
    gwp[:, 0:16, 0, :] = gw[:, 0]
    gwp[:, 16:32, 1, :] = gw[:, 1]
    p["gatew"] = gwp
    gb = f(inputs["gla_gate_b"])[:L]
    p["gateb"] = f(gb.reshape(L, 2, 2, 128).transpose(0, 3, 1, 2).reshape(L, 128, 4))
    p["glanorm"] = f(np.tile(f(inputs["gla_out_norm"])[:L], (1, 4)))
    sm = lambda a: a.reshape(L, 2, 8, 2, 64).transpose(0, 1, 3, 4, 2).reshape(L, 2, 128, 8)
    are, aim = f(inputs["s5_a_re"])[:L], f(inputs["s5_a_im"])[:L]
    ldt = np.broadcast_to(f(inputs["s5_log_dt"])[:L][..., None], (L, 2, 16, 64))
    p["s5a"] = f(np.stack([sm(are), sm(aim), sm(f(ldt))], axis=3))
    bre, bim = f(inputs["s5_b_re"])[:L], f(inputs["s5_b_im"])[:L]
    s5b = np.zeros((L, 2, 32, 2, 8, 128), np.float32)
    cre, cim = f(inputs["s5_c_re"])[:L], f(inputs["s5_c_im"])[:L]
    s5c = np.zeros((L, 2, 128, 2, 8, 64), np.float32)
    for m in range(8):
        for gl in range(2):
            g = 2 * m + gl
            for ri, (bb, cc_) in enumerate(((bre, cre), (bim, cim))):
                s5b[:, :, gl * 16:(gl + 1) * 16, ri, m, gl * 64:(gl + 1) * 64] = bb[:, :, g].transpose(0, 1, 3, 2)
                s5c[:, :, gl * 64:(gl + 1) * 64, ri, m, (m % 2) * 32 + gl * 16:(m % 2) * 32 + (gl + 1) * 16] = cc_[:, :, g].transpose(0, 1, 3, 2)
    p["s5b"], p["s5c"] = s5b, s5c
    p["s5d"] = f(colz(f(inputs["s5_d"])[:L]))
    p["s5glub"] = f(colz(f(inputs["s5_glu_b"])[:L]))
    p["s5gluw"] = f(inputs["s5_glu_w"])[:L]
    for k in ("w_br_att", "w_br_gla", "w_br_s5", "w_out", "ffn_up", "ffn_down"):
        p[k] = f(inputs[k])[:L]
    cw = f(inputs["ffn_conv_w"])[:L]
    cb = f(inputs["ffn_conv_b"])[:L]
    c4 = np.concatenate([cw, cb[:, None, :]], axis=1)
    p["convp"] = f(c4.reshape(L, 4, 44, 128).transpose(0, 3, 1, 2))
    return p


_CACHE = {}

N_LAT_FULL = 8192


def swap_dirs(p):
    q = dict(p)
    gw = np.zeros_like(p["gatew"])
    gw[:, 16:32, 0, :] = p["gatew"][:, 16:32, 1, :]
    gw[:, 0:16, 1, :] = p["gatew"][:, 0:16, 0, :]
    q["gatew"] = gw
    gb = p["gateb"].reshape(-1, 128, 2, 2)
    q["gateb"] = np.ascontiguousarray(gb[:, :, ::-1, :]).reshape(-1, 128, 4)
    for k in ("s5a", "s5b", "s5c"):
        q[k] = np.ascontiguousarray(p[k][:, ::-1])
    cp = p["convp"]
    q["convp"] = np.ascontiguousarray(np.stack([cp[:, :, 2], cp[:, :, 1], cp[:, :, 0], cp[:, :, 3]], axis=2))
    return q


def run(inputs, n_lat, depth, n_batch, dbg=()):
    x = np.asarray(inputs["x"], dtype=np.float32)
    ctx = np.asarray(inputs["ctx"], dtype=np.float32)
    c = np.asarray(inputs["c"], dtype=np.float32)
    c_ctx = np.asarray(inputs["c_ctx"], dtype=np.float32)
    nl = n_lat // 2
    key = (nl, depth, tuple(dbg))
    if key not in _CACHE:
        b = Builder(nl, depth, dbg)
        b.build()
        _CACHE[key] = b
    b = _CACHE[key]
    p0 = host_layout(inputs, depth)
    cst = host_consts(n_lat)
    rope_full = cst.pop("rope")
    p0.update(cst)
    p1 = swap_dirs(p0)
    in_maps = []
    for core in range(8):
        bi = (core // 2) % n_batch
        r = core % 2
        m = dict(p0 if r == 0 else p1)
        if r == 0:
            m["xcat"] = np.ascontiguousarray(np.concatenate([ctx[bi], x[bi, :nl]], axis=0))
            m["rope"] = np.ascontiguousarray(rope_full[:nl])
            m["selw"] = np.ascontiguousarray(np.tile(np.array([[0.0, 1.0]], np.float32), (128, 1)))
        else:
            m["xcat"] = np.ascontiguousarray(np.concatenate([ctx[bi][::-1], x[bi, nl:n_lat][::-1]], axis=0))
            m["rope"] = np.ascontiguousarray(rope_full[nl:n_lat][::-1])
            m["selw"] = np.ascontiguousarray(np.tile(np.array([[1.0, 0.0]], np.float32), (128, 1)))
        cc = np.concatenate([c[bi].reshape(8, 128).T, c_ctx.reshape(8, 128).T], axis=1)
        m["cc"] = np.ascontiguousarray(cc.astype(np.float32))
        in_maps.append(m)
    res = run_bass_kernel_spmd(b.nc, in_maps, core_ids=list(range(8)))
    outs = []
    for bi in range(n_batch):
        o0 = np.asarray(res.results[2 * bi]["out"], dtype=np.float32)
        o1 = np.asarray(res.results[2 * bi + 1]["out"], dtype=np.float32)[::-1]
        outs.append(np.concatenate([o0, o1], axis=0))
    return np.stack(outs, axis=0), res.results


def kernel(**inputs):
    n_b = np.asarray(inputs["x"]).shape[0]
    out, _ = run(inputs, N_LAT_FULL, 4, n_b)
    return out
```

```python
import math
from contextlib import ExitStack
import numpy as np
import ml_dtypes
import concourse.bass as bass
import concourse.mybir as mybir
from concourse.ap import AP
from concourse.bass_utils import run_bass_kernel_spmd

F32 = mybir.dt.float32
BF16 = mybir.dt.bfloat16
I32 = mybir.dt.int32
AF = mybir.ActivationFunctionType
ALU = mybir.AluOpType
AX = mybir.AxisListType

D = 1024
KC = 8
DIN = 5408
DFF = 2816
NA = 2336
EPS = 1e-6
PI = math.pi
STOP_AFTER = ''


class Buf:
    __slots__ = ("name", "writers", "pws", "readers", "dsem", "dcount", "key")

    def __init__(self, name):
        self.name = name
        self.writers = {}
        self.pws = {}
        self.readers = {}
        self.dsem = None
        self.dcount = 0
        self.key = None


class DSem:
    __slots__ = ("sem", "key", "count", "sw")

    def __init__(self, sem, key, sw):
        self.sem, self.key, self.count, self.sw = sem, key, 0, sw


class Eng:
    def __init__(self, name, eng, sem, is_pe=False):
        self.name, self.eng, self.sem, self.count, self.seen, self.is_pe = name, eng, sem, 0, {}, is_pe


class Sched:
    def __init__(self, nc):
        self.nc = nc
        self.sems = {}
        self.pe = Eng("pe", nc.tensor, nc.alloc_semaphore("s_pe"), True)
        self.dve = Eng("dve", nc.vector, nc.alloc_semaphore("s_dve"))
        self.act = Eng("act", nc.scalar, nc.alloc_semaphore("s_act"))
        self.pool = Eng("pool", nc.gpsimd, nc.alloc_semaphore("s_pool"))
        self.sp = Eng("sp", nc.sync, nc.alloc_semaphore("s_sp"))
        self.engs = [self.pe, self.dve, self.act, self.pool, self.sp]
        for e in self.engs:
            self.sems[e.name] = e.sem
        self.bufs = {}
        self.dkeys = {}
        self.free_dsems = {True: [], False: []}
        self.ccount = {}
        self.ninst = 0
        self.nwait = 0

    def buf(self, name):
        b = self.bufs.get(name)
        if b is None:
            b = Buf(name)
            self.bufs[name] = b
        return b

    def _deps(self, reads, writes, pwrites):
        deps = {}

        def add(dd):
            for k, v in dd.items():
                if deps.get(k, 0) < v:
                    deps[k] = v
        for b in reads:
            add(b.writers)
            add(b.pws)
        for b in writes:
            add(b.writers)
            add(b.pws)
            add(b.readers)
        for b in pwrites:
            add(b.writers)
            add(b.readers)
        return deps

    def _wait(self, e, deps):
        for k, v in deps.items():
            if k == e.name and e.is_pe:
                continue
            if e.seen.get(k, 0) < v and k in self.dkeys:
                v = self.dkeys[k].count
            if e.seen.get(k, 0) < v:
                e.eng.wait_ge(self.sems[k], v)
                e.seen[k] = v
                self.nwait += 1

    def _post(self, ev, reads, writes, pwrites):
        for b in reads:
            if b.readers.get(ev[0], 0) < ev[1]:
                b.readers[ev[0]] = ev[1]
        for b in writes:
            b.writers = {ev[0]: ev[1]}
            b.pws = {}
            b.readers = {}
        for b in pwrites:
            b.pws[ev[0]] = ev[1]

    def op(self, e, fn, reads=(), writes=(), pwrites=()):
        self._wait(e, self._deps(reads, writes, pwrites))
        inst = fn()
        e.count += 1
        inst.then_inc(e.sem, 1)
        self.ninst += 1
        self._post((e.name, e.count), reads, writes, pwrites)
        return inst

    def dma(self, e, out, in_, reads=(), writes=(), pwrites=(), sbuf=None, **kw):
        if sbuf is None:
            sbuf = (list(writes) + list(pwrites) + list(reads))[0]
        sw = e is self.pool
        if sbuf.dsem is None:
            if self.free_dsems[sw]:
                sbuf.dsem = self.free_dsems[sw].pop()
            else:
                key = "D%d" % len(self.sems)
                sbuf.dsem = DSem(self.nc.alloc_semaphore("d%d" % len(self.sems)), key, sw)
                self.sems[key] = sbuf.dsem.sem
                self.dkeys[key] = sbuf.dsem
        ds = sbuf.dsem
        assert ds.sw == sw, ("mixed SW/HW DGE on one buffer semaphore", sbuf.name)
        self._wait(e, self._deps(reads, writes, pwrites))
        inst = e.eng.dma_start(out=out, in_=in_, **kw)
        ds.count += 16
        inst.then_inc(ds.sem, 16)
        self.ninst += 1
        self._post((ds.key, ds.count), reads, writes, pwrites)
        return inst

    def recycle(self, bufs):
        for b in bufs:
            if b.dsem is not None:
                self.free_dsems[b.dsem.sw].append(b.dsem)
                b.dsem = None

    def collective(self, kind, op, groups, src, dst, reads, writes, site):
        key = "C_" + site
        if key not in self.sems:
            self.sems[key] = self.nc.alloc_semaphore("c_" + site)
            self.ccount[key] = 0
        e = self.pool
        self._wait(e, self._deps(reads, writes, ()))
        inst = self.nc.gpsimd.collective_compute(kind, op, replica_groups=groups, ins=[src], outs=[dst])
        self.ccount[key] += 1
        inst.then_inc(self.sems[key], 1)
        self.ninst += 1
        self._post((key, self.ccount[key]), reads, writes, ())

    def all_events(self):
        deps = {e.name: e.count for e in self.engs if e.count > 0}
        for k, ds in self.dkeys.items():
            if ds.count > 0:
                deps[k] = ds.count
        for k, v in self.ccount.items():
            if v > 0:
                deps[k] = v
        return deps

    def barrier(self):
        deps = self.all_events()
        for e in self.engs:
            d = dict(deps)
            d.pop(e.name, None)
            self._wait(e, d)

    def finish(self):
        self._wait(self.sp, self.all_events())


def rev(ap2d):
    a = ap2d.ap
    assert len(a) == 2, a
    n = a[1][1]
    st = a[1][0]
    return AP(ap2d.tensor, ap2d.offset + (n - 1) * st, [list(a[0]), [-st, n]])


class Builder:
    def __init__(self, n_lat, depth, dbg=()):
        assert n_lat % 512 == 0
        self.n_lat, self.depth, self.dbgnames = n_lat, depth, tuple(dbg)
        self.T = 256 + n_lat
        self.NT = self.T // 128
        self.NL = n_lat
        self.NTK = (256 + 2 * n_lat) // 128
        self.groups = [[0, 1], [2, 3], [4, 5], [6, 7]]
        self.NCH = self.T // 64
        self.nc = bass.Bass("TRN2", target_bir_lowering=False)
        self.S = Sched(self.nc)
        self.blocks = [(0, 256, True)] + [(256 + 512 * j, 512, False) for j in range(n_lat // 512)]
        self.uid = 0
        self.phase_bufs = []

    def din(self, name, shape, dt=F32):
        return self.nc.dram_tensor(name, list(shape), dt, kind="ExternalInput").ap()

    def dscr(self, name, shape, dt):
        return self.nc.dram_tensor(name, list(shape), dt, kind="Internal").ap()

    def sb(self, es, name, shape, dt=F32):
        self.uid += 1
        t = es.enter_context(self.nc.sbuf_tensor("%s_%d" % (name, self.uid), list(shape), dt))
        b = self.S.buf(name)
        if es is not getattr(self, "ges", None):
            self.phase_bufs.append(b)
        return t, b

    def V(self, fn, r=(), w=(), pw=()):
        return self.S.op(self.S.dve, fn, r, w, pw)

    def A(self, fn, r=(), w=(), pw=()):
        return self.S.op(self.S.act, fn, r, w, pw)

    def P(self, fn, r=(), w=(), pw=()):
        return self.S.op(self.S.pool, fn, r, w, pw)

    def M(self, fn, r=(), w=(), pw=()):
        return self.S.op(self.S.pe, fn, r, w, pw)

    def ld(self, out, in_, r=(), w=(), pw=(), sbuf=None, **kw):
        return self.S.dma(self.S.sp, out, in_, r, w, pw, sbuf, **kw)

    def ldc(self, out, in_, r=(), w=(), pw=(), sbuf=None):
        return self.S.dma(self.S.pool, out, in_, r, w, pw, sbuf, max_dma_last_dim=4096)

    def st(self, out, in_, r=(), w=(), pw=(), sbuf=None, **kw):
        return self.S.dma(self.S.sp, out, in_, r, w, pw, sbuf, **kw)

    def build(self):
        nc, S = self.nc, self.S
        T, L, n_lat = self.T, self.depth, self.n_lat
        i = self.inp = {}
        i["xcat"] = self.din("xcat", [T, D])
        i["cc"] = self.din("cc", [128, 16])
        i["ada_w"] = self.din("ada_w", [L, D, 6 * D])
        i["ada_b"] = self.din("ada_b", [L, 6 * D])
        i["npre"] = self.din("npre", [L, 128, 2, 8])
        i["npost"] = self.din("npost", [L, 2, D])
        i["w_in"] = self.din("w_in", [L, D, DIN])
        i["qkg"] = self.din("qkg", [L, 384])
        i["gatew"] = self.din("gatew", [L, 32, 2, 256])
        i["gateb"] = self.din("gateb", [L, 128, 4])
        i["glanorm"] = self.din("glanorm", [L, 512])
        i["s5a"] = self.din("s5a", [L, 2, 128, 3, 8])
        i["s5b"] = self.din("s5b", [L, 2, 32, 2, 8, 128])
        i["s5c"] = self.din("s5c", [L, 2, 128, 2, 8, 64])
        i["s5d"] = self.din("s5d", [L, 128, 2])
        i["s5glub"] = self.din("s5glub", [L, 128, 2])
        i["s5gluw"] = self.din("s5gluw", [L, 256, 256])
        i["w_br_att"] = self.din("w_br_att", [L, 256, D])
        i["w_br_gla"] = self.din("w_br_gla", [L, 512, D])
        i["w_br_s5"] = self.din("w_br_s5", [L, 256, D])
        i["w_out"] = self.din("w_out", [L, D, D])
        i["ffn_up"] = self.din("ffn_up", [L, D, 2 * DFF])
        i["ffn_down"] = self.din("ffn_down", [L, DFF, D])
        i["convp"] = self.din("convp", [L, 128, 4, 44])
        i["ident"] = self.din("ident", [128, 128])
        i["rope"] = self.din("rope", [n_lat, 64])
        i["gmask"] = self.din("gmask", [128, 2, 64])
        i["scanmask"] = self.din("scanmask", [128, 2, 512])
        i["selw"] = self.din("selw", [128, 2])
        self.out = nc.dram_tensor("out", [n_lat, D], F32, kind="ExternalOutput").ap()
        self.dbg = {}

        d = self.scr = {}
        d["X1"] = self.dscr("X1", [T, D], F32)
        d["X2"] = self.dscr("X2", [T, D], F32)
        d["HT"] = self.dscr("HT", [D, T], BF16)
        d["QT"] = self.dscr("QT", [256, T], BF16)
        d["KT"] = self.dscr("KT", [128, 256], BF16)
        d["VV"] = self.dscr("VV", [256, 128], BF16)
        d["KVS"] = self.dscr("KVS", [256, n_lat], BF16)
        d["KVG"] = self.dscr("KVG", [512, n_lat], BF16)
        d["SFS"] = self.dscr("SFS", [256, 128], F32)
        d["SFG"] = self.dscr("SFG", [512, 128], F32)
        d["CFS"] = self.dscr("CFS", [128, 16], F32)
        d["CFG"] = self.dscr("CFG", [256, 16], F32)
        d["XHS"] = self.dscr("XHS", [1, D], F32)
        d["XHG"] = self.dscr("XHG", [2, D], F32)
        d["XHX"] = self.dscr("XHX", [1, D], F32)
        d["GT"] = self.dscr("GT", [4, 256, T], BF16)
        d["KE"] = self.dscr("KE", [2, T, 256], BF16)
        d["GV"] = self.dscr("GV", [T, 512], BF16)
        d["GR"] = self.dscr("GR", [T, 512], BF16)
        d["SU"] = self.dscr("SU", [256, T], BF16)
        d["OATT"] = self.dscr("OATT", [256, T], BF16)
        d["OGLA"] = self.dscr("OGLA", [512, T], BF16)
        d["OS5"] = self.dscr("OS5", [256, T], BF16)
        d["SST"] = self.dscr("SST", [2, self.NCH, 2, 128, 128], BF16)
        d["YF"] = self.dscr("YF", [256, T], F32)
        self.dbufs = {k: S.buf("dram_" + k) for k in d}
        self.xin_buf = S.buf("dram_xcat")
        self.out_buf = S.buf("dram_out")

        with ExitStack() as ges:
            self.ges = ges
            self.ps = []
            for k in range(8):
                t = ges.enter_context(nc.psum_tensor("ps%d" % k, [128, 512], F32))
                self.ps.append((t, S.buf("ps%d" % k)))
            self.ident, self.b_ident = self.sb(ges, "ident", [128, 128])
            self.identb, self.b_identb = self.sb(ges, "identb", [128, 128], BF16)
            self.ones, self.b_ones = self.sb(ges, "ones", [128, 128])
            self.gmask, self.b_gmask = self.sb(ges, "gmask", [128, 2, 64])
            self.scanmask, self.b_scanmask = self.sb(ges, "scanmask", [128, 2, 512])
            self.GG, self.b_GG = self.sb(ges, "GG", [128, 2, 2, D])
            self.modc, self.b_modc = self.sb(ges, "modc", [128, 2, 6, 8])
            self.prec, self.b_prec = self.sb(ges, "prec", [128, 2, 2, 2, 8])
            self.dec, self.b_dec = self.sb(ges, "dec", [128, 2, 2, self.NCH])
            self.epsc, self.b_epsc = self.sb(ges, "epsc", [128, 1])

            self.ld(self.ident[:], i["ident"], w=[self.b_ident])
            self.V(lambda: nc.vector.tensor_copy(out=self.identb[:], in_=self.ident[:]), [self.b_ident], [self.b_identb])
            self.V(lambda: nc.vector.memset(self.ones[:], 1.0), w=[self.b_ones])
            self.V(lambda: nc.vector.memset(self.epsc[:], EPS), w=[self.b_epsc])
            self.ld(self.gmask[:], i["gmask"], w=[self.b_gmask])
            self.ld(self.scanmask[:], i["scanmask"], w=[self.b_scanmask])
            self.selw, self.b_selw = self.sb(ges, "selw", [128, 2])
            self.ld(self.selw[:], i["selw"], w=[self.b_selw])
            S.barrier()
            S.recycle(self.phase_bufs)
            self.phase_bufs = []

            for l in range(L):
                last = (l == L - 1)
                xin = i["xcat"] if l == 0 else d["X1"]
                xin_b = self.xin_buf if l == 0 else self.dbufs["X1"]
                phases = [("mod", lambda: self.phase_mod(l)), ("a", lambda: self.phase_a(l, xin, xin_b)), ("bc3", lambda: self.phase_bc3(l)),
                          ("c1", lambda: self.phase_c1(l)), ("c2", lambda: self.phase_c2(l)),
                          ("d", lambda: self.phase_d(l, xin, xin_b, last)), ("e", lambda: self.phase_e(l, last))]
                stopped = False
                for pname, pf in phases:
                    pf()
                    if STOP_AFTER == pname:
                        stopped = True
                        break
                if stopped:
                    break
            S.finish()
        return nc

    def dump(self, name, ap_sb, bufs, shape, dt=F32):
        if name not in self.dbgnames:
            return
        t = self.nc.dram_tensor("dbg_" + name, list(shape), dt, kind="ExternalOutput").ap()
        self.dbg[name] = t
        self.st(t, ap_sb, r=bufs, w=[self.S.buf("dbgd_" + name)], sbuf=bufs[0])

    def rstd_from_ss(self, ss_ap, rs_ap, b_ss, b_rs, n, rows=128):
        nc = self.nc
        self.A(lambda: nc.scalar.activation(out=rs_ap, in_=ss_ap, func=AF.Sqrt, bias=self.epsc[0:rows, 0:1], scale=1.0 / n),
               [b_ss, self.b_epsc], [b_rs])
        self.V(lambda: nc.vector.reciprocal(out=rs_ap, in_=rs_ap), [b_rs], [b_rs])

    def phase_mod(self, l):
        nc, S, i = self.nc, self.S, self.inp
        with ExitStack() as es:
            npost, b_npost = self.sb(es, "npost", [128, 2, D])
            cc, bcc = self.sb(es, "cc", [128, 16])
            sc, bsc = self.sb(es, "sc", [128, 16])
            self.screp, self.b_screp = self.sb(es, "screp", [128, 16, 128])
            self.ld(cc[:], i["cc"], w=[bcc])
            self.A(lambda: nc.scalar.activation(out=sc[:], in_=cc[:], func=AF.Silu), [bcc], [bsc])
            self.V(lambda: nc.vector.tensor_copy(out=self.screp[:], in_=sc[:].unsqueeze(2).to_broadcast([128, 16, 128])),
                   [bsc], [self.b_screp])
            npre, b_npre = self.sb(es, "npre", [128, 2, 8])
            adab, b_adab = self.sb(es, "adab", [128, 512])
            wblk = [self.sb(es, "adaw%d" % k, [128, 8, 512]) for k in range(2)]
            mb = [self.sb(es, "mb%d" % k, [128, 512]) for k in range(2)]
            src = i["npost"][l]
            self.ld(npost[:], AP(src.tensor, src.offset, [[0, 128], [D, 2], [1, D]]), w=[b_npost])
            self.ld(npre[:], i["npre"][l], w=[b_npre])
            for cb in range(12):
                kind, half = cb // 2, cb % 2
                wt, bw = wblk[cb % 2]
                self.ld(wt[:], i["ada_w"][l, :, cb * 512:(cb + 1) * 512].rearrange("(k p) n -> p k n", p=128), w=[bw])
                src = i["ada_b"][l, cb * 512:(cb + 1) * 512]
                self.ld(adab[:], AP(src.tensor, src.offset, [[0, 128], [1, 512]]), w=[b_adab])
                for which in range(2):
                    pt, bp = self.ps[which]
                    for k in range(8):
                        self.M(lambda k=k: nc.tensor.matmul(pt[:], lhsT=self.screp[:, which * 8 + k, :], rhs=wt[:, k, :],
                                                           start=(k == 0), stop=(k == 7)),
                               [self.b_screp, bw], pw=[bp] if k else (), w=[bp] if k == 0 else ())
                    mt, bm = mb[which]
                    self.V(lambda: nc.vector.tensor_tensor(out=mt[:], in0=pt[:], in1=adab[:], op=ALU.add), [bp, b_adab], [bm])
                    if kind in (2, 5):
                        mf = 0 if kind == 2 else 1
                        self.V(lambda: nc.vector.tensor_tensor(out=self.GG[:, which, mf, half * 512:(half + 1) * 512], in0=mt[:],
                                                               in1=npost[:, mf, half * 512:(half + 1) * 512], op=ALU.mult),
                               [bm, b_npost], pw=[self.b_GG])
                    else:
                        p2, bp2 = self.ps[2 + which]
                        for j in range(4):
                            self.M(lambda j=j: nc.tensor.transpose(out=p2[:, j * 32:(j + 1) * 32], in_=mt[0:32, j * 128:(j + 1) * 128],
                                                                  identity=self.ident[0:32, 0:32]),
                                   [bm, self.b_ident], pw=[bp2] if j else (), w=[bp2] if j == 0 else ())
                        srcv = p2[:, 0:128].rearrange("p (j c) -> p j c", c=32)[:, :, 0]
                        self.V(lambda: nc.vector.tensor_copy(out=self.modc[:, which, kind, half * 4:(half + 1) * 4], in_=srcv),
                               [bp2], pw=[self.b_modc])
            for which in range(2):
                for mf in range(2):
                    ksh, ksc = (0, 1) if mf == 0 else (3, 4)
                    self.V(lambda: nc.vector.scalar_tensor_tensor(out=self.prec[:, which, mf, 0, :], in0=self.modc[:, which, ksc, :],
                                                                  scalar=1.0, in1=npre[:, mf, :], op0=ALU.add, op1=ALU.mult),
                           [self.b_modc, b_npre], pw=[self.b_prec])
                    self.V(lambda: nc.vector.tensor_copy(out=self.prec[:, which, mf, 1, :], in_=self.modc[:, which, ksh, :]),
                           [self.b_modc], pw=[self.b_prec])
            self.dump("GG", self.GG[:], [self.b_GG], [128, 2, 2, D])
            self.dump("prec", self.prec[:], [self.b_prec], [128, 2, 2, 2, 8])
            S.barrier()
            S.recycle(self.phase_bufs)
            self.phase_bufs = []

    def norm_tile(self, xt, bx, rows, which, mf, hT, bh, col0, tmp, pst):
        nc = self.nc
        (xn, bxn), (ss, bss), (rs, brs) = tmp
        self.A(lambda: nc.scalar.activation(out=xn[0:rows, :], in_=xt[0:rows, :], func=AF.Square, accum_out=ss[0:rows, 0:1]),
               [bx], [bxn, bss])
        self.rstd_from_ss(ss[0:rows, 0:1], rs[0:rows, 0:1], bss, brs, D, rows)
        self.A(lambda: nc.scalar.activation(out=xn[0:rows, :], in_=xt[0:rows, :], func=AF.Identity, scale=rs[0:rows, 0:1]),
               [bx, brs], [bxn])
        for hf in range(2):
            pt, bp = pst[hf]
            for kk in range(4):
                k = hf * 4 + kk
                self.M(lambda: nc.tensor.transpose(out=pt[:, kk * 128:kk * 128 + rows], in_=xn[0:rows, k * 128:(k + 1) * 128],
                                                   identity=self.ident[0:rows, 0:rows]),
                       [bxn, self.b_ident], pw=[bp] if kk else (), w=[bp] if kk == 0 else ())
            for kk in range(4):
                k = hf * 4 + kk
                self.A(lambda: nc.scalar.activation(out=hT[:, k, col0:col0 + rows], in_=pt[:, kk * 128:kk * 128 + rows],
                                                    func=AF.Identity, scale=self.prec[:, which, mf, 0, k:k + 1],
                                                    bias=self.prec[:, which, mf, 1, k:k + 1]),
                       [bp, self.b_prec], pw=[bh])

    def phase_a(self, l, xin, xin_b):
        nc, S, i, d = self.nc, self.S, self.inp, self.scr
        db = self.dbufs
        with ExitStack() as es:
            WA, bWA = self.sb(es, "WA", [128, 8, NA], BF16)
            for k in range(8):
                self.ldc(WA[:, k, :], i["w_in"][l, k * 128:(k + 1) * 128, 0:NA], pw=[bWA])
            gw, bgw = self.sb(es, "gw", [32, 2, 256], BF16)
            self.ldc(gw[:], i["gatew"][l], w=[bgw])
            gb, bgb = self.sb(es, "gb", [128, 4])
            self.ld(gb[:], i["gateb"][l], w=[bgb])
            self.V(lambda: nc.vector.tensor_scalar(out=gb[:], in0=gb[:], scalar1=-1.0, scalar2=None, op0=ALU.mult), [bgb], [bgb])
            qkg, bqkg = self.sb(es, "qkg", [128, 384])
            src = i["qkg"][l]
            self.ld(qkg[:], AP(src.tensor, src.offset, [[0, 128], [1, 384]]), w=[bqkg])
            xts = [self.sb(es, "xt%d" % k, [128, D]) for k in range(2)]
            tmp = (self.sb(es, "xn", [128, D]), self.sb(es, "ss", [128, 1]), self.sb(es, "rs", [128, 1]))
            hTs = [self.sb(es, "hT%d" % k, [128, 8, 512], BF16) for k in range(2)]
            qk0 = [self.sb(es, "qk0_%d" % k, [128, 384]) for k in range(4)]
            sqs = [self.sb(es, "sq_%d" % k, [128, 384]) for k in range(4)]
            ss6s = [self.sb(es, "ss6_%d" % k, [128, 6]) for k in range(4)]
            rs6s = [self.sb(es, "rs6_%d" % k, [128, 6]) for k in range(4)]
            qk1s = [self.sb(es, "qk1_%d" % k, [128, 384]) for k in range(4)]
            qk2s = [self.sb(es, "qk2_%d" % k, [128, 384], BF16) for k in range(4)]
            ras = [self.sb(es, "ra_%d" % k, [128, 6, 2, 16]) for k in range(4)]
            rbs = [self.sb(es, "rb_%d" % k, [128, 6, 2, 16]) for k in range(4)]
            ropes = [self.sb(es, "rope_%d" % k, [128, 64]) for k in range(4)]
            vbfs = [self.sb(es, "vbf_%d" % k, [128, 128], BF16) for k in range(4)]
            qkTs = [self.sb(es, "qkT_%d" % k, [128, 3, 128], BF16) for k in range(4)]
            tmA = [self.sb(es, "tmA%d" % k, [128, 512], BF16) for k in range(2)]
            lowT, blowT = self.sb(es, "lowT", [32, 512], BF16)
            fm = [self.sb(es, "fm%d" % k, [128, 512], BF16) for k in range(2)]
            e1, be1 = self.sb(es, "e1", [128, 512])
            l1, bl1 = self.sb(es, "l1", [128, 512])
            bp_, bbp = self.sb(es, "bpl", [128, 512])
            dd, bdd = self.sb(es, "dd", [128, 512])
            E = [self.sb(es, "E%d" % k, [128, 512]) for k in range(3)]
            qo = [self.sb(es, "qo%d" % k, [128, 512], BF16) for k in range(3)]
            keT, bkeT = self.sb(es, "keT", [128, 4, 128], BF16)
            pst = (self.ps[0], self.ps[1])
            tm_i = 0
            fm_i = 0
            for bi, (t0, n, isctx) in enumerate(self.blocks):
                which = 1 if isctx else 0
                nt = n // 128
                hT, bh = hTs[bi % 2]
                for j in range(nt):
                    xt, bx = xts[j % 2]
                    r0 = t0 + j * 128
                    self.ld(xt[:], xin[r0:r0 + 128, :], r=[xin_b], w=[bx])
                    self.norm_tile(xt, bx, 128, which, 0, hT, bh, j * 128, tmp, pst)
                self.st(d["HT"][:, t0:t0 + n].rearrange("(k p) t -> p k t", p=128), hT[:, :, 0:n], r=[bh], pw=[db["HT"]])
                if l == 0 and bi == 1:
                    self.dump("hT", hT[:], [bh], [128, 8, 512], BF16)
                TJ = list(range(nt))
                for j in TJ:
                    r0 = t0 + j * 128
                    pt, bp = self.ps[2]
                    for k in range(8):
                        self.M(lambda k=k: nc.tensor.matmul(pt[:], lhsT=hT[:, k, j * 128:(j + 1) * 128], rhs=WA[:, k, 0:512],
                                                           start=(k == 0), stop=(k == 7)),
                               [bh, bWA], pw=[bp] if k else (), w=[bp] if k == 0 else ())
                    self.A(lambda: nc.scalar.copy(out=qk0[j][0][:], in_=pt[:, 0:384]), [bp], [qk0[j][1]])
                    self.A(lambda: nc.scalar.copy(out=vbfs[j][0][:], in_=pt[:, 384:512]), [bp], [vbfs[j][1]])
                    if isctx:
                        self.st(d["VV"][r0:r0 + 128, :], vbfs[j][0][:], r=[vbfs[j][1]], pw=[db["VV"]])
                    else:
                        kvs = d["KVS"]
                        vdst = AP(kvs.tensor, kvs.offset + 128 * self.NL + (r0 - 256) * 128, [[128, 128], [1, 128]])
                        self.st(vdst, vbfs[j][0][:], r=[vbfs[j][1]], pw=[db["KVS"]])
                        self.ld(ropes[j][0][:], i["rope"][r0 - 256:r0 - 256 + 128, :], w=[ropes[j][1]])
                    for gi, (c0, dst) in enumerate(((1024, "GV"), (1536, "GR"))):
                        pt, bp = self.ps[3 + gi]
                        for k in range(8):
                            self.M(lambda k=k: nc.tensor.matmul(pt[:], lhsT=hT[:, k, j * 128:(j + 1) * 128], rhs=WA[:, k, c0:c0 + 512],
                                                               start=(k == 0), stop=(k == 7)),
                                   [bh, bWA], pw=[bp] if k else (), w=[bp] if k == 0 else ())
                        tt, bt = tmA[tm_i % 2]
                        tm_i += 1
                        if gi == 0:
                            self.A(lambda: nc.scalar.copy(out=tt[:], in_=pt[:]), [bp], [bt])
                        else:
                            self.A(lambda: nc.scalar.activation(out=tt[:], in_=pt[:], func=AF.Silu), [bp], [bt])
                        self.st(d[dst][r0:r0 + 128, :], tt[:], r=[bt], pw=[db[dst]])
                for j in TJ:
                    self.A(lambda: nc.scalar.activation(out=sqs[j][0][:], in_=qk0[j][0][:], func=AF.Square), [qk0[j][1]], [sqs[j][1]])
                for j in TJ:
                    self.V(lambda: nc.vector.tensor_reduce(out=ss6s[j][0][:], in_=sqs[j][0][:].rearrange("p (h e) -> p h e", e=64), axis=AX.X, op=ALU.add),
                           [sqs[j][1]], [ss6s[j][1]])
                for j in TJ:
                    self.A(lambda: nc.scalar.activation(out=rs6s[j][0][:], in_=ss6s[j][0][:], func=AF.Sqrt, bias=self.epsc[:, 0:1], scale=1.0 / 64),
                           [ss6s[j][1], self.b_epsc], [rs6s[j][1]])
                for j in TJ:
                    self.V(lambda: nc.vector.reciprocal(out=rs6s[j][0][:], in_=rs6s[j][0][:]), [rs6s[j][1]], [rs6s[j][1]])
                for j in TJ:
                    self.V(lambda: nc.vector.tensor_tensor(out=qk1s[j][0][:].rearrange("p (h e) -> p h e", e=64),
                                                           in0=qk0[j][0][:].rearrange("p (h e) -> p h e", e=64),
                                                           in1=rs6s[j][0][:].unsqueeze(2).to_broadcast([128, 6, 64]), op=ALU.mult),
                           [qk0[j][1], rs6s[j][1]], [qk1s[j][1]])
                if isctx:
                    for j in TJ:
                        self.V(lambda: nc.vector.tensor_tensor(out=qk2s[j][0][:], in0=qk1s[j][0][:], in1=qkg[:], op=ALU.mult),
                               [qk1s[j][1], bqkg], [qk2s[j][1]])
                else:
                    for j in TJ:
                        self.V(lambda: nc.vector.tensor_tensor(out=qk1s[j][0][:], in0=qk1s[j][0][:], in1=qkg[:], op=ALU.mult),
                               [qk1s[j][1], bqkg], [qk1s[j][1]])

                    def rviews(j):
                        v5 = qk1s[j][0][:].rearrange("p (h a s f) -> p h a s f", h=6, a=2, s=2)
                        o5 = qk2s[j][0][:].rearrange("p (h a s f) -> p h a s f", h=6, a=2, s=2)
                        rp = ropes[j][0]
                        cs = rp[:, 0:32].rearrange("p (a f) -> p a f", a=2).unsqueeze(1).to_broadcast([128, 6, 2, 16])
                        sn = rp[:, 32:64].rearrange("p (a f) -> p a f", a=2).unsqueeze(1).to_broadcast([128, 6, 2, 16])
                        return v5[:, :, :, 0, :], v5[:, :, :, 1, :], o5, cs, sn
                    for half in range(2):
                        for j in TJ:
                            x1, x2, o5, cs, sn = rviews(j)
                            xa, xb = (x1, x2) if half == 0 else (x2, x1)
                            self.V(lambda: nc.vector.tensor_tensor(out=ras[j][0][:], in0=xa, in1=cs, op=ALU.mult), [qk1s[j][1], ropes[j][1]], [ras[j][1]])
                            self.V(lambda: nc.vector.tensor_tensor(out=rbs[j][0][:], in0=xb, in1=sn, op=ALU.mult), [qk1s[j][1], ropes[j][1]], [rbs[j][1]])
                        for j in TJ:
                            x1, x2, o5, cs, sn = rviews(j)
                            self.V(lambda: nc.vector.tensor_tensor(out=o5[:, :, :, half, :], in0=ras[j][0][:], in1=rbs[j][0][:],
                                                                   op=ALU.subtract if half == 0 else ALU.add),
                                   [ras[j][1], rbs[j][1]], pw=[qk2s[j][1]])
                for j in TJ:
                    r0 = t0 + j * 128
                    p7, bp7 = self.ps[7]
                    p7b = p7[:].bitcast(BF16)
                    for c in range(3):
                        self.M(lambda c=c: nc.tensor.transpose(out=p7b[:, c * 128:(c + 1) * 128], in_=qk2s[j][0][:, c * 128:(c + 1) * 128],
                                                              identity=self.identb[:]),
                               [qk2s[j][1], self.b_identb], pw=[bp7] if c else (), w=[bp7] if c == 0 else ())
                    qkT, bqkT = qkTs[j]
                    self.A(lambda: nc.scalar.copy(out=qkT[:].rearrange("p c t -> p (c t)"), in_=p7b[:, 0:384]), [bp7], [bqkT])
                    self.st(d["QT"][:, r0:r0 + 128].rearrange("(c p) t -> p c t", p=128), qkT[:, 0:2, :], r=[bqkT], pw=[db["QT"]])
                    if isctx:
                        self.st(d["KT"][:, r0:r0 + 128], qkT[:, 2, :], r=[bqkT], pw=[db["KT"]])
                    else:
                        self.st(d["KVS"][0:128, r0 - 256:r0 - 256 + 128], qkT[:, 2, :], r=[bqkT], pw=[db["KVS"]])
                pt, bp = self.ps[2]
                for k in range(8):
                    self.M(lambda k=k: nc.tensor.matmul(pt[0:32, 0:n], lhsT=WA[:, k, 2048:2080], rhs=hT[:, k, 0:n], start=(k == 0), stop=(k == 7)),
                           [bh, bWA], pw=[bp] if k else (), w=[bp] if k == 0 else ())
                self.A(lambda: nc.scalar.copy(out=lowT[:, 0:n], in_=pt[0:32, 0:n]), [bp], [blowT])
                for c in range(2):
                    pt, bp = self.ps[3 + c]
                    for k in range(8):
                        self.M(lambda k=k: nc.tensor.matmul(pt[:, 0:n], lhsT=WA[:, k, 2080 + c * 128:2080 + (c + 1) * 128], rhs=hT[:, k, 0:n],
                                                           start=(k == 0), stop=(k == 7)),
                               [bh, bWA], pw=[bp] if k else (), w=[bp] if k == 0 else ())
                    ft, bf = fm[fm_i % 2]
                    fm_i += 1
                    self.A(lambda: nc.scalar.copy(out=ft[:, 0:n], in_=pt[:, 0:n]), [bp], [bf])
                    self.st(d["SU"][c * 128:(c + 1) * 128, t0:t0 + n], ft[:, 0:n], r=[bf], pw=[db["SU"]])
                nchb = n // 64
                ch0 = t0 // 64
                for c in range(2):
                    pq, bpq = self.ps[3]
                    pk, bpk = self.ps[4]
                    for (pp, bpp, c0) in ((pq, bpq, 512), (pk, bpk, 768)):
                        for k in range(8):
                            self.M(lambda k=k: nc.tensor.matmul(pp[:, 0:n], lhsT=WA[:, k, c0 + c * 128:c0 + (c + 1) * 128], rhs=hT[:, k, 0:n],
                                                               start=(k == 0), stop=(k == 7)),
                                   [bh, bWA], pw=[bpp] if k else (), w=[bpp] if k == 0 else ())
                    for dr in range(2):
                        pz, bpz = self.ps[5 + dr]
                        self.M(lambda: nc.tensor.matmul(pz[:, 0:n], lhsT=gw[:, dr, c * 128:(c + 1) * 128], rhs=lowT[:, 0:n], start=True, stop=True),
                               [bgw, blowT], [bpz])
                        self.A(lambda: nc.scalar.activation(out=e1[:, 0:n], in_=pz[:, 0:n], func=AF.Exp, scale=-1.0,
                                                            bias=gb[:, dr * 2 + c:dr * 2 + c + 1]), [bpz, bgb], [be1])
                        self.A(lambda: nc.scalar.activation(out=l1[:, 0:n], in_=e1[:, 0:n], func=AF.Ln, bias=self.ones[:, 0:1], scale=1.0),
                               [be1, self.b_ones], [bl1])
                        if dr == 0:
                            self.V(lambda: nc.vector.tensor_tensor_scan(out=bp_[:, 0:n], data0=self.scanmask[:, 0, 0:n], data1=l1[:, 0:n],
                                                                        initial=0.0, op0=ALU.mult, op1=ALU.add),
                                   [self.b_scanmask, bl1], [bbp])
                            endcol = 63
                        else:
                            self.V(lambda: nc.vector.tensor_tensor_scan(out=rev(bp_[:, 0:n]), data0=rev(self.scanmask[:, 1, 0:n]),
                                                                        data1=rev(l1[:, 0:n]), initial=0.0, op0=ALU.mult, op1=ALU.add),
                                   [self.b_scanmask, bl1], [bbp])
                            endcol = 0
                        b3 = bp_[:, 0:n].rearrange("p (c i) -> p c i", i=64)
                        bend = b3[:, :, endcol:endcol + 1]
                        self.V(lambda: nc.vector.tensor_tensor(out=dd[:, 0:n].rearrange("p (c i) -> p c i", i=64),
                                                               in0=bend.to_broadcast([128, nchb, 64]), in1=b3, op=ALU.subtract),
                               [bbp], [bdd])
                        self.A(lambda: nc.scalar.activation(out=E[0][0][:, 0:n], in_=bp_[:, 0:n], func=AF.Exp, scale=-1.0 / 16), [bbp], [E[0][1]])
                        self.A(lambda: nc.scalar.activation(out=E[1][0][:, 0:n], in_=bp_[:, 0:n], func=AF.Exp, scale=1.0 / 16), [bbp], [E[1][1]])
                        self.A(lambda: nc.scalar.activation(out=E[2][0][:, 0:n], in_=dd[:, 0:n], func=AF.Exp, scale=-1.0 / 16), [bdd], [E[2][1]])
                        self.A(lambda: nc.scalar.activation(out=self.dec[:, dr, c, ch0:ch0 + nchb], in_=bend.rearrange("p c o -> p (c o)"),
                                                            func=AF.Exp, scale=-1.0 / 16), [bbp], pw=[self.b_dec])
                        self.V(lambda: nc.vector.scalar_tensor_tensor(out=qo[0][0][:, 0:n], in0=pq[:, 0:n], scalar=0.125, in1=E[0][0][:, 0:n],
                                                                      op0=ALU.mult, op1=ALU.mult), [bpq, E[0][1]], [qo[0][1]])
                        self.V(lambda: nc.vector.tensor_tensor(out=qo[1][0][:, 0:n], in0=pk[:, 0:n], in1=E[1][0][:, 0:n], op=ALU.mult),
                               [bpk, E[1][1]], [qo[1][1]])
                        self.V(lambda: nc.vector.tensor_tensor(out=qo[2][0][:, 0:n], in0=pk[:, 0:n], in1=E[2][0][:, 0:n], op=ALU.mult),
                               [bpk, E[2][1]], [qo[2][1]])
                        self.st(d["GT"][dr * 2 + 0, c * 128:(c + 1) * 128, t0:t0 + n], qo[0][0][:, 0:n], r=[qo[0][1]], pw=[db["GT"]])
                        self.st(d["GT"][dr * 2 + 1, c * 128:(c + 1) * 128, t0:t0 + n], qo[1][0][:, 0:n], r=[qo[1][1]], pw=[db["GT"]])
                        p7, bp7 = self.ps[7]
                        p7b = p7[:].bitcast(BF16)
                        for j in range(nt):
                            self.M(lambda j=j: nc.tensor.transpose(out=p7b[:, j * 128:(j + 1) * 128], in_=qo[2][0][:, j * 128:(j + 1) * 128],
                                                                  identity=self.identb[:]),
                                   [qo[2][1], self.b_identb], pw=[bp7] if j else (), w=[bp7] if j == 0 else ())
                        self.A(lambda: nc.scalar.copy(out=keT[:, 0:nt, :].rearrange("p j f -> p (j f)"), in_=p7b[:, 0:nt * 128]), [bp7], [bkeT])
                        self.st(d["KE"][dr, t0:t0 + n, c * 128:(c + 1) * 128].rearrange("(j p) f -> p j f", p=128), keT[:, 0:nt, :],
                                r=[bkeT], pw=[db["KE"]])
            S.barrier()
            S.recycle(self.phase_bufs)
            self.phase_bufs = []

    def gen_b(self, l, es):
        nc, S, d, db = self.nc, self.S, self.scr, self.dbufs
        T, NT, NL = 256 + 2 * self.NL, self.NTK, self.NL
        S.collective("AllGather", ALU.bypass, self.groups, d["KVS"], d["KVG"], [db["KVS"]], [db["KVG"]], "kv")
        if True:
            KT, bKT = self.sb(es, "KTs", [128, T], BF16)
            V1, bV1 = self.sb(es, "V1", [128, NT, 2, 65], BF16)
            self.ld(KT[:, 0:256], d["KT"], r=[db["KT"]], w=[bKT])
            for rk in range(2):
                self.ld(KT[:, 256 + rk * NL:256 + (rk + 1) * NL], d["KVG"][rk * 256:rk * 256 + 128, :], r=[db["KVG"]], pw=[bKT])
            self.P(lambda: nc.gpsimd.memset(V1[:], 1.0), w=[bV1])
            for hh in range(2):
                self.ld(V1[:, 0:2, hh, 0:64], d["VV"][:, hh * 64:(hh + 1) * 64].rearrange("(t p) e -> p t e", p=128), r=[db["VV"]], pw=[bV1])
                for rk in range(2):
                    kvg = d["KVG"]
                    vsrc = AP(kvg.tensor, kvg.offset + (rk * 256 + 128) * NL + hh * 64, [[128, 128], [128 * 128, NL // 128], [1, 64]])
                    t_0 = 2 + rk * (NL // 128)
                    self.ld(V1[:, t_0:t_0 + NL // 128, hh, 0:64], vsrc, r=[db["KVG"]], pw=[bV1])
            Qs = [self.sb(es, "Qs%d" % k, [128, 512], BF16) for k in range(2)]
            PT = [self.sb(es, "PT%d" % k, [128, 512], BF16) for k in range(4)]
            rd, brd = self.sb(es, "rd", [128, 512])
            osb, bosb = self.sb(es, "osb", [128, 512])
            obf = [self.sb(es, "obf%d" % k, [128, 512], BF16) for k in range(2)]
            qi = 0
            pi = 0
            oi = 0
            for (q0, nq, isctx) in self.blocks:
                keys = [0, 1] if isctx else list(range(NT))
                for pair in range(2):
                    Q, bQ = Qs[qi % 2]
                    qi += 1
                    for hh in range(2):
                        h = pair + 2 * hh
                        self.ld(Q[hh * 64:(hh + 1) * 64, 0:nq], d["QT"][h * 64:(h + 1) * 64, q0:q0 + nq], r=[db["QT"]], pw=[bQ])
                    pts = {}

                    def stepA(ki, hh):
                        kt = keys[ki]
                        slot = (ki % 2) * 2 + hh
                        st_, bst = self.ps[slot]
                        self.M(lambda: nc.tensor.matmul(st_[:, 0:nq], lhsT=KT[hh * 64:(hh + 1) * 64, kt * 128:(kt + 1) * 128],
                                                        rhs=Q[hh * 64:(hh + 1) * 64, 0:nq], start=True, stop=True),
                               [bKT, bQ], [bst])
                        pt_, bpt = PT[slot]
                        self.A(lambda: nc.scalar.activation(out=pt_[:, 0:nq], in_=st_[:, 0:nq], func=AF.Exp, scale=0.125), [bst], [bpt])
                        pts[(ki, hh)] = (pt_, bpt)

                    def stepB(ki, hh):
                        kt = keys[ki]
                        pt_, bpt = pts.pop((ki, hh))
                        oa, boa = self.ps[4 + hh]
                        first, lastk = (ki == 0), (ki == len(keys) - 1)
                        self.M(lambda: nc.tensor.matmul(oa[0:65, 0:nq], lhsT=V1[:, kt, hh, :], rhs=pt_[:, 0:nq], start=first, stop=lastk),
                               [bV1, bpt], pw=[boa] if not first else (), w=[boa] if first else ())
                    for ki in range(len(keys) + 1):
                        if ki < len(keys):
                            stepA(ki, 0)
                            stepA(ki, 1)
                        if ki >= 1:
                            stepB(ki - 1, 0)
                            stepB(ki - 1, 1)
                        yield
                    for hh in range(2):
                        h = pair + 2 * hh
                        oa, boa = self.ps[4 + hh]
                        self.V(lambda: nc.vector.reciprocal(out=rd[64:65, 0:nq], in_=oa[64:65, 0:nq]), [boa], [brd])
                        bc, bbc = self.ps[3]
                        self.M(lambda: nc.tensor.matmul(bc[0:64, 0:nq], lhsT=self.ones[64:65, 0:64], rhs=rd[64:65, 0:nq], start=True, stop=True),
                               [self.b_ones, brd], [bbc])
                        self.A(lambda: nc.scalar.copy(out=osb[0:64, 0:nq], in_=oa[0:64, 0:nq]), [boa], [bosb])
                        ob, bob = obf[oi % 2]
                        oi += 1
                        self.V(lambda: nc.vector.tensor_tensor(out=ob[0:64, 0:nq], in0=osb[0:64, 0:nq], in1=bc[0:64, 0:nq], op=ALU.mult),
                               [bosb, bbc], [bob])
                        self.st(d["OATT"][h * 64:(h + 1) * 64, q0:q0 + nq], ob[0:64, 0:nq], r=[bob], pw=[db["OATT"]])
                        yield

    def phase_c1(self, l):
        nc, S, d, db = self.nc, self.S, self.scr, self.dbufs
        with ExitStack() as es:
            St, bSt = self.sb(es, "St", [128, 2, 128])
            sfg = [self.sb(es, "sfg%d" % k, [128, 2, 128]) for k in range(2)]
            kes = [self.sb(es, "ke%d" % k, [128, 4, 256], BF16) for k in range(2)]
            gvs = [self.sb(es, "gvs%d" % k, [128, 4, 512], BF16) for k in range(2)]
            hists = [self.sb(es, "hist%d" % k, [128, 8, 2, 128], BF16) for k in range(2)]
            it = 0
            for dr in range(2):
                self.V(lambda: nc.vector.memset(St[:], 0.0), w=[bSt])
                ctxb = [b for b in self.blocks if b[2]]
                latb = [b for b in self.blocks if not b[2]]
                order = ctxb + (latb if dr == 0 else latb[::-1])
                for (t0, n, isctx) in order:
                    if dr == 1 and (t0, n, isctx) == order[len(ctxb)]:
                        for rk in range(2):
                            self.ld(sfg[rk][0][:], d["SFG"][rk * 256:(rk + 1) * 256, :].rearrange("(c p) e -> p c e", p=128), r=[db["SFG"]], w=[sfg[rk][1]])
                        self.V(lambda: nc.vector.tensor_scalar(out=St[:], in0=sfg[0][0][:], scalar1=self.selw[:, 0:1], scalar2=None, op0=ALU.mult),
                               [sfg[0][1], self.b_selw], [bSt])
                        self.V(lambda: nc.vector.scalar_tensor_tensor(out=St[:], in0=sfg[1][0][:], scalar=self.selw[:, 1:2], in1=St[:], op0=ALU.mult, op1=ALU.add),
                               [sfg[1][1], self.b_selw, bSt], [bSt])
                    nt = n // 128
                    ke, bke = kes[it % 2]
                    gv, bgv = gvs[it % 2]
                    hist, bhist = hists[it % 2]
                    it += 1
                    self.ld(ke[:, 0:nt, :], d["KE"][dr, t0:t0 + n, :].rearrange("(j p) f -> p j f", p=128), r=[db["KE"]], w=[bke])
                    self.ld(gv[:, 0:nt, :], d["GV"][t0:t0 + n, :].rearrange("(j p) f -> p j f", p=128), r=[db["GV"]], w=[bgv])
                    nchb = n // 64
                    chs = list(range(nchb)) if dr == 0 else list(range(nchb))[::-1]
                    for ci, cl in enumerate(chs):
                        j, half = cl // 2, cl % 2
                        ng = t0 // 64 + cl
                        self.A(lambda: nc.scalar.copy(out=hist[:, cl, :, :], in_=St[:]), [bSt], pw=[bhist] if ci else (), w=[bhist] if ci == 0 else ())
                        for c in range(2):
                            kv, bkv = self.ps[c + 2 * (ci % 2)]
                            for hl in range(2):
                                h = c * 2 + hl
                                self.M(lambda: nc.tensor.matmul(kv[hl * 64:(hl + 1) * 64, 0:128],
                                                                lhsT=ke[half * 64:(half + 1) * 64, j, c * 128 + hl * 64:c * 128 + (hl + 1) * 64],
                                                                rhs=gv[half * 64:(half + 1) * 64, j, h * 128:(h + 1) * 128], start=True, stop=True),
                                       [bke, bgv], pw=[bkv] if hl else (), w=[bkv] if hl == 0 else ())
                            self.V(lambda: nc.vector.scalar_tensor_tensor(out=St[:, c, :], in0=St[:, c, :], scalar=self.dec[:, dr, c, ng:ng + 1],
                                                                          in1=kv[:, 0:128], op0=ALU.mult, op1=ALU.add),
                                   [bSt, self.b_dec, bkv], [bSt])
                    ch0 = t0 // 64
                    self.st(d["SST"][dr, ch0:ch0 + nchb].rearrange("n c p e -> p n c e"), hist[:, 0:nchb, :, :], r=[bhist], pw=[db["SST"]])
                if dr == 0:
                    self.st(d["SFS"].rearrange("(c p) e -> p c e", p=128), St[:], r=[bSt], w=[db["SFS"]])
                    S.collective("AllGather", ALU.bypass, self.groups, d["SFS"], d["SFG"], [db["SFS"]], [db["SFG"]], "gla")
            S.barrier()
            S.recycle(self.phase_bufs)
            self.phase_bufs = []

    def phase_c2(self, l):
        nc, S, i, d, db = self.nc, self.S, self.inp, self.scr, self.dbufs
        with ExitStack() as es:
            gn, bgn = self.sb(es, "gn", [128, 512])
            src = i["glanorm"][l]
            self.ld(gn[:], AP(src.tensor, src.offset, [[0, 128], [1, 512]]), w=[bgn])
            gts = [self.sb(es, "gt%d" % k, [64, 4, 4, 512], BF16) for k in range(2)]
            gvs = [self.sb(es, "gv2%d" % k, [64, 8, 512], BF16) for k in range(2)]
            grs = [self.sb(es, "gr2%d" % k, [128, 4, 512], BF16) for k in range(2)]
            ssts = [self.sb(es, "sst%d" % k, [64, 2, 4, 8, 128], BF16) for k in range(2)]
            atts = [self.sb(es, "att%d" % k, [64, 2, 64], BF16) for k in range(4)]
            sq, bsq = self.sb(es, "sq2", [128, 512])
            ss4, bss4 = self.sb(es, "ss4", [128, 4])
            rs4, brs4 = self.sb(es, "rs4", [128, 4])
            t1, bt1 = self.sb(es, "t1", [128, 512])
            t3, bt3 = self.sb(es, "t3", [128, 512], BF16)
            ogT, bogT = self.sb(es, "ogT", [128, 4, 128], BF16)
            ai = 0
            for bi, (t0, n, isctx) in enumerate(self.blocks):
                nt = n // 128
                nchb = n // 64
                ch0 = t0 // 64
                gt, bgt = gts[bi % 2]
                gv, bgv = gvs[bi % 2]
                gr, bgr = grs[bi % 2]
                sst, bsst = ssts[bi % 2]
                for kind in range(4):
                    self.ld(gt[:, kind, :, 0:n], d["GT"][kind, :, t0:t0 + n].rearrange("(h p) t -> p h t", p=64), r=[db["GT"]],
                            pw=[bgt] if kind else (), w=[bgt] if kind == 0 else ())
                self.ld(gv[:, 0:nchb, :], d["GV"][t0:t0 + n, :].rearrange("(c p) f -> p c f", p=64), r=[db["GV"]], w=[bgv])
                self.ld(gr[:, 0:nt, :], d["GR"][t0:t0 + n, :].rearrange("(j p) f -> p j f", p=128), r=[db["GR"]], w=[bgr])
                first_ld = True
                for dr in range(2):
                    for h in range(4):
                        c, hl = h // 2, h % 2
                        self.ld(sst[:, dr, h, 0:nchb, :], d["SST"][dr, ch0:ch0 + nchb, c, hl * 64:(hl + 1) * 64, :].rearrange("n p e -> p n e"),
                                r=[db["SST"]], pw=[bsst] if not first_ld else (), w=[bsst] if first_ld else ())
                        first_ld = False
                for j in range(nt):
                    ops_, bops = self.ps[4 + (j % 2)]
                    for h in range(4):
                        asb = []
                        for dr in range(2):
                            ap_, bap = self.ps[dr + 2 * (h % 2)]
                            for ch in range(2):
                                tk = j * 128 + ch * 64
                                self.M(lambda: nc.tensor.matmul(ap_[0:64, ch * 64:(ch + 1) * 64], lhsT=gt[:, dr * 2 + 1, h, tk:tk + 64],
                                                                rhs=gt[:, dr * 2 + 0, h, tk:tk + 64], start=True, stop=True),
                                       [bgt], pw=[bap] if ch else (), w=[bap] if ch == 0 else ())
                            at, bat = atts[ai % 4]
                            ai += 1
                            self.V(lambda: nc.vector.tensor_tensor(out=at[:], in0=ap_[0:64, 0:128].rearrange("p (c i) -> p c i", i=64),
                                                                   in1=self.gmask[0:64, dr, :].unsqueeze(1).to_broadcast([64, 2, 64]), op=ALU.mult),
                                   [bap, self.b_gmask], [bat])
                            asb.append((at, bat))
                        for ch in range(2):
                            tk = j * 128 + ch * 64
                            cl = 2 * j + ch
                            o_ = ops_[ch * 64:(ch + 1) * 64, h * 128:(h + 1) * 128]
                            first = (h == 0 and ch == 0)
                            for dr in range(2):
                                at, bat = asb[dr]
                                self.M(lambda: nc.tensor.matmul(o_, lhsT=at[:, ch, :], rhs=gv[:, cl, h * 128:(h + 1) * 128],
                                                                start=(dr == 0), stop=False),
                                       [bat, bgv], pw=[bops] if not (first and dr == 0) else (), w=[bops] if (first and dr == 0) else ())
                                self.M(lambda: nc.tensor.matmul(o_, lhsT=gt[:, dr * 2 + 0, h, tk:tk + 64], rhs=sst[:, dr, h, cl, :],
                                                                start=False, stop=(dr == 1)),
                                       [bgt, bsst], pw=[bops])
                    self.A(lambda: nc.scalar.activation(out=sq[:], in_=ops_[:], func=AF.Square), [bops], [bsq])
                    self.V(lambda: nc.vector.tensor_reduce(out=ss4[:], in_=sq[:].rearrange("p (h e) -> p h e", e=128), axis=AX.X, op=ALU.add),
                           [bsq], [bss4])
                    self.rstd_from_ss(ss4[:], rs4[:], bss4, brs4, 128)
                    self.V(lambda: nc.vector.tensor_tensor(out=t1[:].rearrange("p (h e) -> p h e", e=128),
                                                           in0=ops_[:].rearrange("p (h e) -> p h e", e=128),
                                                           in1=rs4[:].unsqueeze(2).to_broadcast([128, 4, 128]), op=ALU.mult),
                           [bops, brs4], [bt1])
                    self.P(lambda: nc.gpsimd.tensor_tensor(out=t1[:], in0=t1[:], in1=gn[:], op=ALU.mult), [bt1, bgn], [bt1])
                    self.V(lambda: nc.vector.tensor_tensor(out=t3[:], in0=t1[:], in1=gr[:, j, :], op=ALU.mult), [bt1, bgr], [bt3])
                    p7, bp7 = self.ps[7]
                    p7b = p7[:].bitcast(BF16)
                    for h in range(4):
                        self.M(lambda h=h: nc.tensor.transpose(out=p7b[:, h * 128:(h + 1) * 128], in_=t3[:, h * 128:(h + 1) * 128],
                                                              identity=self.identb[:]),
                               [bt3, self.b_identb], pw=[bp7] if h else (), w=[bp7] if h == 0 else ())
                    self.A(lambda: nc.scalar.copy(out=ogT[:].rearrange("p h t -> p (h t)"), in_=p7b[:, 0:512]), [bp7], [bogT])
                    r0t = t0 + j * 128
                    self.st(d["OGLA"][:, r0t:r0t + 128].rearrange("(h p) t -> p h t", p=128), ogT[:], r=[bogT], pw=[db["OGLA"]])
            S.barrier()
            S.recycle(self.phase_bufs)
            self.phase_bufs = []

    def gen_c3(self, l, es):
        nc, S, i, d, db = self.nc, self.S, self.inp, self.scr, self.dbufs
        V, A, P, M = self.V, self.A, self.P, self.M
        NB5 = 256
        if True:
            R, bR = self.sb(es, "Rtab", [128, 2, 2, 8, NB5])
            mag, bmag = self.sb(es, "mag", [128, 2, 8])
            Bsb, bB = self.sb(es, "Bsb", [32, 2, 2, 8, 128], BF16)
            Csb, bC = self.sb(es, "Csb", [128, 2, 2, 8, 64], BF16)
            dcol, bdcol = self.sb(es, "dcol", [128, 2])
            glub, bglub = self.sb(es, "glub", [128, 2])
            gluw, bgluw = self.sb(es, "gluw", [128, 2, 256], BF16)
            self.ld(dcol[:], i["s5d"][l], w=[bdcol])
            self.ld(glub[:], i["s5glub"][l], w=[bglub])
            self.ldc(gluw[:], i["s5gluw"][l].rearrange("(c p) n -> p c n", p=128), w=[bgluw])
            carry, bcarry = self.sb(es, "carry", [128, 2, 8])
            if True:
                es2 = es
                a3, ba3 = self.sb(es2, "a3", [128, 3, 8])
                w = [self.sb(es2, "w%d" % k, [128, 8]) for k in range(12)]
                wi, bwi = self.sb(es2, "wi", [128, 8], I32)
                cf, bcf = self.sb(es2, "cf", [128, 2, 8, 64])
                fb, bfb = self.sb(es2, "fb", [128, 2, 8])
                tt = [self.sb(es2, "ct%d" % k, [128, 8, 64]) for k in range(2)]
                tr = [self.sb(es2, "tr%d" % k, [128, 8, NB5 // 2]) for k in range(2)]

                def ts(o, a, s1, s2, op0, op1=None):
                    if op1 is None:
                        V(lambda: nc.vector.tensor_scalar(out=o[0][:], in0=a[0][:], scalar1=s1, scalar2=None, op0=op0), [a[1]], [o[1]])
                    else:
                        V(lambda: nc.vector.tensor_scalar(out=o[0][:], in0=a[0][:], scalar1=s1, scalar2=s2, op0=op0, op1=op1), [a[1]], [o[1]])

                def tt_(o, a, b, op):
                    V(lambda: nc.vector.tensor_tensor(out=o[0][:], in0=a[0][:], in1=b[0][:], op=op), [a[1], b[1]], [o[1]])

                for dr in range(2):
                    self.ld(a3[:], i["s5a"][l, dr], w=[ba3])
                    self.ldc(Bsb[:, dr], i["s5b"][l, dr], pw=[bB])
                    self.ld(cf[:], i["s5c"][l, dr], w=[bcf])
                    are, aim, ldt = (a3[:, 0, :], ba3), (a3[:, 1, :], ba3), (a3[:, 2, :], ba3)
                    dt_, adt, ang, kf, r_, m_, sn, cs, t_, den, fre, fim = w
                    A(lambda: nc.scalar.activation(out=dt_[0][:], in_=a3[:, 2, :], func=AF.Exp), [ba3], [dt_[1]])
                    V(lambda: nc.vector.tensor_tensor(out=adt[0][:], in0=a3[:, 0, :], in1=dt_[0][:], op=ALU.mult), [ba3, dt_[1]], [adt[1]])
                    A(lambda: nc.scalar.activation(out=mag[:, dr, :], in_=adt[0][:], func=AF.Exp), [adt[1]], pw=[bmag])
                    V(lambda: nc.vector.tensor_tensor(out=ang[0][:], in0=a3[:, 1, :], in1=dt_[0][:], op=ALU.mult), [ba3, dt_[1]], [ang[1]])
                    ts(kf, ang, 1.0 / (2 * PI), None, ALU.mult)
                    V(lambda: nc.vector.tensor_copy(out=wi[:], in_=kf[0][:]), [kf[1]], [bwi])
                    V(lambda: nc.vector.tensor_copy(out=kf[0][:], in_=wi[:]), [bwi], [kf[1]])
                    V(lambda: nc.vector.scalar_tensor_tensor(out=r_[0][:], in0=kf[0][:], scalar=-2 * PI, in1=ang[0][:], op0=ALU.mult, op1=ALU.add),
                      [kf[1], ang[1]], [r_[1]])
                    ts(m_, r_, PI, -2 * PI, ALU.is_gt, ALU.mult)
                    tt_(r_, r_, m_, ALU.add)
                    ts(m_, r_, -PI, 2 * PI, ALU.is_lt, ALU.mult)
                    tt_(r_, r_, m_, ALU.add)
                    A(lambda: nc.scalar.activation(out=sn[0][:], in_=r_[0][:], func=AF.Sin), [r_[1]], [sn[1]])
                    ts(t_, r_, -1.0, None, ALU.mult)
                    tt_(t_, t_, r_, ALU.max)
                    ts(t_, t_, -1.0, PI / 2, ALU.mult, ALU.add)
                    A(lambda: nc.scalar.activation(out=cs[0][:], in_=t_[0][:], func=AF.Sin), [t_[1]], [cs[1]])
                    V(lambda: nc.vector.tensor_copy(out=R[:, dr, 0, :, 0], in_=cs[0][:]), [cs[1]], pw=[bR])
                    V(lambda: nc.vector.tensor_copy(out=R[:, dr, 1, :, 0], in_=sn[0][:]), [sn[1]], pw=[bR])
                    abre, abim = kf, m_
                    V(lambda: nc.vector.tensor_tensor(out=abre[0][:], in0=mag[:, dr, :], in1=cs[0][:], op=ALU.mult), [bmag, cs[1]], [abre[1]])
                    V(lambda: nc.vector.tensor_tensor(out=abim[0][:], in0=mag[:, dr, :], in1=sn[0][:], op=ALU.mult), [bmag, sn[1]], [abim[1]])
                    ts(abre, abre, -1.0, None, ALU.add)
                    V(lambda: nc.vector.tensor_tensor(out=den[0][:], in0=a3[:, 0, :], in1=a3[:, 0, :], op=ALU.mult), [ba3], [den[1]])
                    V(lambda: nc.vector.tensor_tensor(out=t_[0][:], in0=a3[:, 1, :], in1=a3[:, 1, :], op=ALU.mult), [ba3], [t_[1]])
                    tt_(den, den, t_, ALU.add)
                    V(lambda: nc.vector.reciprocal(out=den[0][:], in_=den[0][:]), [den[1]], [den[1]])
                    V(lambda: nc.vector.tensor_tensor(out=fre[0][:], in0=abre[0][:], in1=a3[:, 0, :], op=ALU.mult), [abre[1], ba3], [fre[1]])
                    V(lambda: nc.vector.tensor_tensor(out=t_[0][:], in0=abim[0][:], in1=a3[:, 1, :], op=ALU.mult), [abim[1], ba3], [t_[1]])
                    tt_(fre, fre, t_, ALU.add)
                    V(lambda: nc.vector.tensor_tensor(out=fb[:, 0, :], in0=fre[0][:], in1=den[0][:], op=ALU.mult), [fre[1], den[1]], pw=[bfb])
                    V(lambda: nc.vector.tensor_tensor(out=fim[0][:], in0=abim[0][:], in1=a3[:, 0, :], op=ALU.mult), [abim[1], ba3], [fim[1]])
                    V(lambda: nc.vector.tensor_tensor(out=t_[0][:], in0=abre[0][:], in1=a3[:, 1, :], op=ALU.mult), [abre[1], ba3], [t_[1]])
                    tt_(fim, fim, t_, ALU.subtract)
                    V(lambda: nc.vector.tensor_tensor(out=fb[:, 1, :], in0=fim[0][:], in1=den[0][:], op=ALU.mult), [fim[1], den[1]], pw=[bfb])
                    frb = fb[:, 0, :].unsqueeze(2).to_broadcast([128, 8, 64])
                    fib = fb[:, 1, :].unsqueeze(2).to_broadcast([128, 8, 64])
                    V(lambda: nc.vector.tensor_tensor(out=tt[0][0][:], in0=cf[:, 0], in1=frb, op=ALU.mult), [bcf, bfb], [tt[0][1]])
                    V(lambda: nc.vector.tensor_tensor(out=tt[1][0][:], in0=cf[:, 1], in1=fib, op=ALU.mult), [bcf, bfb], [tt[1][1]])
                    V(lambda: nc.vector.tensor_tensor(out=Csb[:, dr, 0], in0=tt[0][0][:], in1=tt[1][0][:], op=ALU.subtract), [tt[0][1], tt[1][1]], pw=[bC])
                    V(lambda: nc.vector.tensor_tensor(out=tt[0][0][:], in0=cf[:, 0], in1=fib, op=ALU.mult), [bcf, bfb], [tt[0][1]])
                    V(lambda: nc.vector.tensor_tensor(out=tt[1][0][:], in0=cf[:, 1], in1=frb, op=ALU.mult), [bcf, bfb], [tt[1][1]])
                    V(lambda: nc.vector.tensor_tensor(out=tt[0][0][:], in0=tt[0][0][:], in1=tt[1][0][:], op=ALU.add), [tt[0][1], tt[1][1]], [tt[0][1]])
                    V(lambda: nc.vector.tensor_scalar(out=Csb[:, dr, 1], in0=tt[0][0][:], scalar1=-1.0, scalar2=None, op0=ALU.mult), [tt[0][1]], pw=[bC])
                    nn = 1
                    while nn < NB5:
                        cr = R[:, dr, 0, :, nn - 1:nn].to_broadcast([128, 8, nn])
                        ci = R[:, dr, 1, :, nn - 1:nn].to_broadcast([128, 8, nn])
                        sre, sim = R[:, dr, 0, :, 0:nn], R[:, dr, 1, :, 0:nn]
                        u0, u1 = tr[0][0][:, :, 0:nn], tr[1][0][:, :, 0:nn]
                        V(lambda: nc.vector.tensor_tensor(out=u0, in0=sre, in1=cr, op=ALU.mult), [bR], [tr[0][1]])
                        V(lambda: nc.vector.tensor_tensor(out=u1, in0=sim, in1=ci, op=ALU.mult), [bR], [tr[1][1]])
                        V(lambda: nc.vector.tensor_tensor(out=R[:, dr, 0, :, nn:2 * nn], in0=u0, in1=u1, op=ALU.subtract), [tr[0][1], tr[1][1]], pw=[bR])
                        V(lambda: nc.vector.tensor_tensor(out=u0, in0=sre, in1=ci, op=ALU.mult), [bR], [tr[0][1]])
                        V(lambda: nc.vector.tensor_tensor(out=u1, in0=sim, in1=cr, op=ALU.mult), [bR], [tr[1][1]])
                        V(lambda: nc.vector.tensor_tensor(out=R[:, dr, 1, :, nn:2 * nn], in0=u0, in1=u1, op=ALU.add), [tr[0][1], tr[1][1]], pw=[bR])
                        nn *= 2
                        yield
                    yield
            if l == 0:
                self.dump("Rtab", R[:], [bR], [128, 2, 2, 8, 512])
                self.dump("mag", mag[:], [bmag], [128, 2, 8])

            Rneg, bRneg = self.sb(es, "Rneg", [128, 2, 8])
            for dr in range(2):
                V(lambda: nc.vector.tensor_scalar(out=Rneg[:, dr, :], in0=R[:, dr, 1, :, NB5 - 1], scalar1=-1.0, scalar2=None, op0=ALU.mult),
                  [bR], pw=[bRneg])
            us = [self.sb(es, "us%d" % k, [32, 8, NB5], BF16) for k in range(2)]
            ufs = [self.sb(es, "uf%d" % k, [128, 2, NB5], BF16) for k in range(2)]
            yfls = [self.sb(es, "yfl%d" % k, [128, 2, NB5]) for k in range(2)]
            tqA = [[self.sb(es, "tqA%d_%d" % (q, k), [128, NB5]) for k in range(4)] for q in range(2)]
            tqB = [[self.sb(es, "tqB%d_%d" % (q, k), [128, NB5]) for k in range(4)] for q in range(2)]
            zz = [[self.sb(es, "z%d_%d" % (q, k), [128, NB5]) for k in range(2)] for q in range(2)]
            scs = [[self.sb(es, "sc%d_%d" % (q, k), [128, NB5]) for k in range(2)] for q in range(3)]
            sbfs = [[self.sb(es, "sbf%d_%d" % (q, k), [128, NB5], BF16) for k in range(2)] for q in range(2)]
            ct = [self.sb(es, "ct%d" % k, [128, 1]) for k in range(2)]
            cfg = [self.sb(es, "cfg%d" % k, [128, 2, 8]) for k in range(2)]
            carryb = [S.buf("carry_m%d" % m) for m in range(8)]
            ysbs = [self.sb(es, "ysb%d" % k, [128, 2, NB5]) for k in range(2)]
            yg, byg = self.sb(es, "yg", [128, 2, NB5], BF16)
            sg, bsg = self.sb(es, "sg", [128, NB5])
            os5, bos5 = self.sb(es, "os5", [128, 2, NB5], BF16)
            p6, b6 = self.ps[6]
            p7, b7 = self.ps[7]
            n = NB5
            ctxb = [(0, 256)]
            latb = [(256 + NB5 * j, NB5) for j in range(self.NL // NB5)]
            for dr in range(2):
                for m in range(8):
                    V(lambda: nc.vector.memset(carry[:, :, m:m + 1], 0.0), w=[carryb[m]])
                order = ctxb + (latb if dr == 0 else latb[::-1])
                items = [(bi, blk, m) for bi, blk in enumerate(order) for m in range(8)]
                blkst = {}
                epis = []

                def dirv(ap2):
                    return ap2 if dr == 0 else rev(ap2)

                def s_load(bi):
                    if bi >= len(order) or bi in blkst:
                        return
                    t0, _n = order[bi]
                    u, bu = us[bi % 2]
                    self.ld(u[:, :, 0:n], d["SU"][:, t0:t0 + n].rearrange("(m r) t -> r m t", r=32), r=[db["SU"]], w=[bu])
                    st_ = dict(u=u, bu=bu)
                    if dr == 1:
                        uf, buf_ = ufs[bi % 2]
                        yfl, byfl = yfls[bi % 2]
                        self.ld(uf[:, :, 0:n], d["SU"][:, t0:t0 + n].rearrange("(c p) t -> p c t", p=128), r=[db["SU"]], w=[buf_])
                        self.ld(yfl[:, :, 0:n], d["YF"][:, t0:t0 + n].rearrange("(c p) t -> p c t", p=128), r=[db["YF"]], w=[byfl])
                        st_.update(uf=uf, buf_=buf_, yfl=yfl, byfl=byfl)
                    blkst[bi] = st_

                def stage1(k):
                    bi, (t0, _n), m = items[k]
                    if m == 0:
                        s_load(bi)
                    if m == 4:
                        s_load(bi + 1)
                    u, bu = blkst[bi]["u"], blkst[bi]["bu"]
                    M(lambda: nc.tensor.matmul(p6[:, 0:n], lhsT=Bsb[:, dr, 0, m, :], rhs=u[:, m, 0:n], start=True, stop=True), [bB, bu], [b6])
                    M(lambda: nc.tensor.matmul(p6[:, n:2 * n], lhsT=Bsb[:, dr, 1, m, :], rhs=u[:, m, 0:n], start=True, stop=True), [bB, bu], pw=[b6])

                def stage2(k):
                    bi, (t0, _n), m = items[k]
                    pre, pim = p6[:, 0:n], p6[:, n:2 * n]
                    Rre = dirv(R[:, dr, 0, m, 0:n])
                    Rim = dirv(R[:, dr, 1, m, 0:n])
                    tq = tqA[k % 2]
                    V(lambda: nc.vector.tensor_tensor(out=tq[0][0][:, 0:n], in0=pre, in1=Rre, op=ALU.mult), [b6, bR], [tq[0][1]])
                    V(lambda: nc.vector.tensor_tensor(out=tq[1][0][:, 0:n], in0=pim, in1=Rim, op=ALU.mult), [b6, bR], [tq[1][1]])
                    V(lambda: nc.vector.tensor_tensor(out=tq[2][0][:, 0:n], in0=pim, in1=Rre, op=ALU.mult), [b6, bR], [tq[2][1]])
                    V(lambda: nc.vector.tensor_tensor(out=tq[3][0][:, 0:n], in0=pre, in1=Rim, op=ALU.mult), [b6, bR], [tq[3][1]])
                    z = zz[k % 2]
                    P(lambda: nc.gpsimd.tensor_tensor(out=z[0][0][:, 0:n], in0=tq[0][0][:, 0:n], in1=tq[1][0][:, 0:n], op=ALU.add),
                      [tq[0][1], tq[1][1]], [z[0][1]])
                    P(lambda: nc.gpsimd.tensor_tensor(out=z[1][0][:, 0:n], in0=tq[2][0][:, 0:n], in1=tq[3][0][:, 0:n], op=ALU.subtract),
                      [tq[2][1], tq[3][1]], [z[1][1]])

                def stage3(k):
                    bi, (t0, _n), m = items[k]
                    z = zz[k % 2]
                    sc_ = scs[k % 3]
                    magb = mag[:, dr, m:m + 1].to_broadcast([128, n])
                    for ri in range(2):
                        V(lambda ri=ri: nc.vector.tensor_tensor_scan(out=dirv(sc_[ri][0][:, 0:n]), data0=magb, data1=dirv(z[ri][0][:, 0:n]),
                                                                    initial=carry[:, ri, m:m + 1], op0=ALU.mult, op1=ALU.add),
                          [bmag, z[ri][1], carryb[m]], [sc_[ri][1]])
                    Rre = dirv(R[:, dr, 0, m, 0:n])
                    Rim = dirv(R[:, dr, 1, m, 0:n])
                    tq = tqB[k % 2]
                    P(lambda: nc.gpsimd.tensor_tensor(out=tq[0][0][:, 0:n], in0=sc_[0][0][:, 0:n], in1=Rre, op=ALU.mult), [sc_[0][1], bR], [tq[0][1]])
                    P(lambda: nc.gpsimd.tensor_tensor(out=tq[1][0][:, 0:n], in0=sc_[1][0][:, 0:n], in1=Rim, op=ALU.mult), [sc_[1][1], bR], [tq[1][1]])
                    P(lambda: nc.gpsimd.tensor_tensor(out=tq[2][0][:, 0:n], in0=sc_[0][0][:, 0:n], in1=Rim, op=ALU.mult), [sc_[0][1], bR], [tq[2][1]])
                    P(lambda: nc.gpsimd.tensor_tensor(out=tq[3][0][:, 0:n], in0=sc_[1][0][:, 0:n], in1=Rre, op=ALU.mult), [sc_[1][1], bR], [tq[3][1]])

                def stage4(k):
                    bi, (t0, _n), m = items[k]
                    sc_ = scs[k % 3]
                    lastc = n - 1 if dr == 0 else 0
                    rl_re, rl_im = R[:, dr, 0, m, n - 1:n], R[:, dr, 1, m, n - 1:n]
                    nrl_im = Rneg[:, dr, m:m + 1]
                    s_re, s_im = sc_[0][0][:, lastc:lastc + 1], sc_[1][0][:, lastc:lastc + 1]
                    A(lambda: nc.scalar.activation(out=ct[0][0][:], in_=s_im, func=AF.Identity, scale=nrl_im), [sc_[1][1], bRneg], [ct[0][1]])
                    A(lambda: nc.scalar.activation(out=carry[:, 0, m:m + 1], in_=s_re, func=AF.Identity, scale=rl_re, bias=ct[0][0][:]),
                      [sc_[0][1], bR, ct[0][1]], [carryb[m]])
                    A(lambda: nc.scalar.activation(out=ct[1][0][:], in_=s_re, func=AF.Identity, scale=rl_im), [sc_[0][1], bR], [ct[1][1]])
                    A(lambda: nc.scalar.activation(out=carry[:, 1, m:m + 1], in_=s_im, func=AF.Identity, scale=rl_re, bias=ct[1][0][:]),
                      [sc_[1][1], bR, ct[1][1]], pw=[carryb[m]])
                    tq = tqB[k % 2]
                    (sr, bsr), (sm, bsm) = sbfs[k % 2]
                    V(lambda: nc.vector.tensor_tensor(out=sr[:, 0:n], in0=tq[0][0][:, 0:n], in1=tq[1][0][:, 0:n], op=ALU.subtract),
                      [tq[0][1], tq[1][1]], [bsr])
                    V(lambda: nc.vector.tensor_tensor(out=sm[:, 0:n], in0=tq[2][0][:, 0:n], in1=tq[3][0][:, 0:n], op=ALU.add),
                      [tq[2][1], tq[3][1]], [bsm])

                def stage5(k):
                    bi, (t0, _n), m = items[k]
                    (sr, bsr), (sm, bsm) = sbfs[k % 2]
                    c = m // 4
                    mo = 64 * ((m // 2) % 2)
                    yo = p7[mo:mo + 64, c * n:(c + 1) * n]
                    M(lambda: nc.tensor.matmul(yo, lhsT=Csb[:, dr, 0, m, :], rhs=sr[:, 0:n], start=(m % 2 == 0), stop=False),
                      [bC, bsr], pw=[b7] if m else (), w=[b7] if m == 0 else ())
                    M(lambda: nc.tensor.matmul(yo, lhsT=Csb[:, dr, 1, m, :], rhs=sm[:, 0:n], start=False, stop=(m % 2 == 1)),
                      [bC, bsm], pw=[b7])
                    if m == 7:
                        epis.append(epilogue(bi, t0))

                def epilogue(bi, t0):
                    st_ = blkst.pop(bi)
                    ysb, bysb = ysbs[bi % 2]
                    if dr == 0:
                        V(lambda: nc.vector.tensor_copy(out=ysb[:].rearrange("p c t -> p (c t)"), in_=p7[:, 0:2 * n]), [b7], [bysb])
                        yield
                        self.st(d["YF"][:, t0:t0 + n].rearrange("(c p) t -> p c t", p=128), ysb[:, :, 0:n], r=[bysb], pw=[db["YF"]])
                        return
                    uf, buf_, yfl, byfl = st_["uf"], st_["buf_"], st_["yfl"], st_["byfl"]
                    V(lambda: nc.vector.tensor_tensor(out=ysb[:].rearrange("p c t -> p (c t)"), in0=p7[:, 0:2 * n],
                                                      in1=yfl[:].rearrange("p c t -> p (c t)"), op=ALU.add), [b7, byfl], [bysb])
                    for c in range(2):
                        V(lambda: nc.vector.scalar_tensor_tensor(out=ysb[:, c, 0:n], in0=uf[:, c, 0:n], scalar=dcol[:, c:c + 1], in1=ysb[:, c, 0:n],
                                                                 op0=ALU.mult, op1=ALU.add), [buf_, bdcol, bysb], [bysb])
                    yield
                    for c in range(2):
                        V(lambda: nc.vector.tensor_tensor(out=sg[:, 0:n], in0=ysb[:, c, 0:n], in1=ysb[:, c, 0:n], op=ALU.mult), [bysb], [bsg])
                        V(lambda: nc.vector.tensor_scalar(out=sg[:, 0:n], in0=sg[:, 0:n], scalar1=0.044715, scalar2=1.0, op0=ALU.mult, op1=ALU.add), [bsg], [bsg])
                        V(lambda: nc.vector.tensor_tensor(out=sg[:, 0:n], in0=sg[:, 0:n], in1=ysb[:, c, 0:n], op=ALU.mult), [bsg, bysb], [bsg])
                        A(lambda: nc.scalar.activation(out=sg[:, 0:n], in_=sg[:, 0:n], func=AF.Sigmoid, scale=2.0 * math.sqrt(2.0 / PI)), [bsg], [bsg])
                        V(lambda: nc.vector.tensor_tensor(out=yg[:, c, 0:n], in0=ysb[:, c, 0:n], in1=sg[:, 0:n], op=ALU.mult), [bysb, bsg],
                          pw=[byg] if c else (), w=[byg] if c == 0 else ())
                        yield
                    for c2 in range(2):
                        zp = p6[:, c2 * n:(c2 + 1) * n]
                        for c in range(2):
                            M(lambda c=c: nc.tensor.matmul(zp, lhsT=gluw[:, c, c2 * 128:(c2 + 1) * 128], rhs=yg[:, c, 0:n],
                                                           start=(c == 0), stop=(c == 1)), [bgluw, byg],
                              pw=[b6] if (c or c2) else (), w=[b6] if (c == 0 and c2 == 0) else ())
                    for c2 in range(2):
                        zp = p6[:, c2 * n:(c2 + 1) * n]
                        A(lambda: nc.scalar.activation(out=sg[:, 0:n], in_=zp, func=AF.Sigmoid, bias=glub[:, c2:c2 + 1], scale=1.0),
                          [b6, bglub], [bsg])
                        V(lambda: nc.vector.tensor_tensor(out=os5[:, c2, 0:n], in0=yg[:, c2, 0:n], in1=sg[:, 0:n], op=ALU.mult), [byg, bsg],
                          pw=[bos5] if c2 else (), w=[bos5] if c2 == 0 else ())
                    self.st(d["OS5"][:, t0:t0 + n].rearrange("(c p) t -> p c t", p=128), os5[:, :, 0:n], r=[bos5], pw=[db["OS5"]])

                def step_epis():
                    for g in list(epis):
                        try:
                            next(g)
                        except StopIteration:
                            epis.remove(g)

                def run_range(k0, k1):
                    stages = (stage1, stage2, stage3, stage4, stage5)
                    for k in range(k0, k1 + len(stages) - 1):
                        for si, fn in reversed(list(enumerate(stages))):
                            kk = k - si
                            if k0 <= kk < k1:
                                fn(kk)
                                if fn is stage2:
                                    step_epis()
                                yield
                        if not (k0 <= k - 1 < k1):
                            step_epis()
                    while epis:
                        step_epis()
                        yield
                NI = len(items)
                nctx_items = 8 * len(ctxb)
                if dr == 0:
                    yield from run_range(0, NI)
                    self.st(d["CFS"], carry[:].rearrange("p r m -> p (r m)"), r=carryb, w=[db["CFS"]], sbuf=bcarry)
                    S.collective("AllGather", ALU.bypass, self.groups, d["CFS"], d["CFG"], [db["CFS"]], [db["CFG"]], "s5")
                else:
                    yield from run_range(0, nctx_items)
                    for rk in range(2):
                        self.ld(cfg[rk][0][:], d["CFG"][rk * 128:(rk + 1) * 128, :].rearrange("p (r m) -> p r m", r=2), r=[db["CFG"]], w=[cfg[rk][1]])
                    V(lambda: nc.vector.tensor_scalar(out=carry[:], in0=cfg[0][0][:], scalar1=self.selw[:, 0:1], scalar2=None, op0=ALU.mult),
                      [cfg[0][1], self.b_selw], carryb)
                    V(lambda: nc.vector.scalar_tensor_tensor(out=carry[:], in0=cfg[1][0][:], scalar=self.selw[:, 1:2], in1=carry[:], op0=ALU.mult, op1=ALU.add),
                      [cfg[1][1], self.b_selw] + carryb, carryb)
                    yield from run_range(nctx_items, NI)

    def phase_bc3(self, l):
        with ExitStack() as es:
            gb = self.gen_b(l, es)
            gc = self.gen_c3(l, es)
            gens = [gb, gc]
            while gens:
                for g in list(gens):
                    try:
                        next(g)
                    except StopIteration:
                        gens.remove(g)
            self.S.barrier()
            self.S.recycle(self.phase_bufs)
            self.phase_bufs = []

    def phase_d(self, l, xin, xin_b, last):
        nc, S, i, d, db = self.nc, self.S, self.inp, self.scr, self.dbufs
        V, A, P, M = self.V, self.A, self.P, self.M
        with ExitStack() as es:
            WG, bWG = self.sb(es, "WG", [128, 8, 3072], BF16)
            bWGg = [[S.buf("WG_%d" % f)] * 3 for f in range(8)]
            self.phase_bufs += [g[0] for g in bWGg]

            def load_wg(f):
                for br in range(3):
                    c0 = br * 1024 + f * 128
                    self.ldc(WG[:, :, c0:c0 + 128], i["w_in"][l, :, NA + c0:NA + c0 + 128].rearrange("(k p) n -> p k n", p=128),
                             w=[bWGg[f][br]] if br == 0 else (), pw=[bWGg[f][br]] if br else ())
            load_wg(0)
            Wba, bWba = self.sb(es, "Wba", [128, 2, D], BF16)
            Wbg, bWbg = self.sb(es, "Wbg", [128, 4, D], BF16)
            Wbs, bWbs = self.sb(es, "Wbs", [128, 2, D], BF16)
            Wo, bWo = self.sb(es, "Wo", [128, 8, D], BF16)
            self.ldc(Wba[:], i["w_br_att"][l].rearrange("(k p) n -> p k n", p=128), w=[bWba])
            self.ldc(Wbg[:], i["w_br_gla"][l].rearrange("(k p) n -> p k n", p=128), w=[bWbg])
            self.ldc(Wbs[:], i["w_br_s5"][l].rearrange("(k p) n -> p k n", p=128), w=[bWbs])
            for f in range(1, 8):
                load_wg(f)
            for k in range(8):
                self.ldc(Wo[:, k, :], i["w_out"][l, k * 128:(k + 1) * 128, :], pw=[bWo])
            hTs = [self.sb(es, "dhT%d" % k, [128, 8, 512], BF16) for k in range(2)]
            srcs = [self.sb(es, "dsrc%d" % k, [128, 8, 512], BF16) for k in range(2)]
            sgs = [self.sb(es, "dsg%d" % k, [128, 512]) for k in range(3)]
            macc, bmacc = self.sb(es, "macc", [128, 512])
            tacc, btacc = self.sb(es, "tacc", [128, 512])
            mT, bmT = self.sb(es, "mT", [128, 8, 512], BF16)
            xts = [self.sb(es, "dxt%d" % k, [128, D]) for k in range(2)]
            xos = [self.sb(es, "dxo%d" % k, [128, D]) for k in range(2)]
            junk, bjunk = self.sb(es, "djunk", [128, 512])
            ss2, bss2 = self.sb(es, "dss2", [128, 2])
            ss, bss = self.sb(es, "dss", [128, 1])
            rs, brs = self.sb(es, "drs", [128, 1])
            tt, btt = self.sb(es, "dtt", [128, D])
            branches = ((Wba, bWba, 0, 2), (Wbg, bWbg, 2, 4), (Wbs, bWbs, 6, 2))
            ti = 0
            for bi, (t0, n, isctx) in enumerate(self.blocks):
                if isctx and last:
                    continue
                which = 1 if isctx else 0
                nt = n // 128
                hT, bh = hTs[bi % 2]
                sr, bsr = srcs[bi % 2]
                self.ld(hT[:, :, 0:n], d["HT"][:, t0:t0 + n].rearrange("(k p) t -> p k t", p=128), r=[db["HT"]], w=[bh])
                self.ld(sr[:, 0:2, 0:n], d["OATT"][:, t0:t0 + n].rearrange("(k p) t -> p k t", p=128), r=[db["OATT"]], w=[bsr])
                self.ld(sr[:, 2:6, 0:n], d["OGLA"][:, t0:t0 + n].rearrange("(k p) t -> p k t", p=128), r=[db["OGLA"]], pw=[bsr])
                self.ld(sr[:, 6:8, 0:n], d["OS5"][:, t0:t0 + n].rearrange("(k p) t -> p k t", p=128), r=[db["OS5"]], pw=[bsr])
                for f in range(8):
                    for br, (W, bW, k0, nk) in enumerate(branches):
                        pb, bpb = self.ps[br]
                        pg, bpg = self.ps[3 + br]
                        for k in range(nk):
                            M(lambda k=k: nc.tensor.matmul(pb[:, 0:n], lhsT=W[:, k, f * 128:(f + 1) * 128], rhs=sr[:, k0 + k, 0:n],
                                                           start=(k == 0), stop=(k == nk - 1)), [bW, bsr], pw=[bpb] if k else (), w=[bpb] if k == 0 else ())
                        for k in range(8):
                            M(lambda k=k: nc.tensor.matmul(pg[:, 0:n], lhsT=WG[:, k, br * 1024 + f * 128:br * 1024 + (f + 1) * 128], rhs=hT[:, k, 0:n],
                                                           start=(k == 0), stop=(k == 7)), [bWGg[f][br], bh], pw=[bpg] if k else (), w=[bpg] if k == 0 else ())
                        sg, bsg = sgs[br]
                        A(lambda: nc.scalar.activation(out=sg[:, 0:n], in_=pg[:, 0:n], func=AF.Sigmoid), [bpg], [bsg])
                        if br == 0:
                            V(lambda: nc.vector.tensor_tensor(out=macc[:, 0:n], in0=sg[:, 0:n], in1=pb[:, 0:n], op=ALU.mult), [bsg, bpb], [bmacc])
                        else:
                            V(lambda: nc.vector.tensor_tensor(out=tacc[:, 0:n], in0=sg[:, 0:n], in1=pb[:, 0:n], op=ALU.mult), [bsg, bpb], [btacc])
                            if br == 1:
                                P(lambda: nc.gpsimd.tensor_tensor(out=macc[:, 0:n], in0=macc[:, 0:n], in1=tacc[:, 0:n], op=ALU.add), [bmacc, btacc], [bmacc])
                            else:
                                P(lambda: nc.gpsimd.tensor_tensor(out=mT[:, f, 0:n], in0=macc[:, 0:n], in1=tacc[:, 0:n], op=ALU.add), [bmacc, btacc],
                                  pw=[bmT] if f else (), w=[bmT] if f == 0 else ())
                for j in range(nt):
                    r0 = t0 + j * 128
                    xt, bx = xts[ti % 2]
                    xo, bxo = xos[ti % 2]
                    ti += 1
                    self.ld(xt[:], xin[r0:r0 + 128, :], r=[xin_b], w=[bx])
                    for hf in range(2):
                        py, bpy = self.ps[6 + hf]
                        for k in range(8):
                            M(lambda k=k: nc.tensor.matmul(py[:], lhsT=mT[:, k, j * 128:(j + 1) * 128], rhs=Wo[:, k, hf * 512:(hf + 1) * 512],
                                                           start=(k == 0), stop=(k == 7)), [bmT, bWo], pw=[bpy] if k else (), w=[bpy] if k == 0 else ())
                        if hf == 0:
                            A(lambda: nc.scalar.activation(out=junk[:], in_=py[:], func=AF.Square, accum_out=ss2[:, 0:1]), [bpy], [bjunk, bss2])
                        else:
                            A(lambda: nc.scalar.activation(out=junk[:], in_=py[:], func=AF.Square, accum_out=ss2[:, 1:2]), [bpy], [bjunk], pw=[bss2])
                    V(lambda: nc.vector.tensor_tensor(out=ss[:], in0=ss2[:, 0:1], in1=ss2[:, 1:2], op=ALU.add), [bss2], [bss])
                    self.rstd_from_ss(ss[:], rs[:], bss, brs, D)
                    for hf in range(2):
                        py, bpy = self.ps[6 + hf]
                        cs = slice(hf * 512, (hf + 1) * 512)
                        V(lambda: nc.vector.scalar_tensor_tensor(out=tt[:, cs], in0=py[:], scalar=rs[:, 0:1], in1=self.GG[:, which, 0, cs],
                                                                 op0=ALU.mult, op1=ALU.mult), [bpy, brs, self.b_GG], pw=[btt] if hf else (), w=[btt] if hf == 0 else ())
                    P(lambda: nc.gpsimd.tensor_tensor(out=xo[:], in0=tt[:], in1=xt[:], op=ALU.add), [btt, bx], [bxo])
                    self.st(d["X2"][r0:r0 + 128, :], xo[:], r=[bxo], pw=[db["X2"]])
            S.barrier()
            S.recycle(self.phase_bufs)
            self.phase_bufs = []

    def phase_e(self, l, last):
        nc, S, i, d, db = self.nc, self.S, self.inp, self.scr, self.dbufs
        V, A, P, M = self.V, self.A, self.P, self.M
        with ExitStack() as es:
            Wup, bWup = self.sb(es, "Wup", [128, 8, 2 * DFF], BF16)
            bWupg = [[S.buf("Wup_%d" % g)] * 2 for g in range(11)]
            self.phase_bufs += [g[0] for g in bWupg]
            for g in range(11):
                for av in range(2):
                    c0 = av * DFF + g * 256
                    self.ldc(Wup[:, :, c0:c0 + 256], i["ffn_up"][l, :, c0:c0 + 256].rearrange("(k p) n -> p k n", p=128),
                             w=[bWupg[g][av]] if av == 0 else (), pw=[bWupg[g][av]] if av else ())
            Wdn, bWdn = self.sb(es, "Wdn", [128, 22, D], BF16)
            bWdng = [S.buf("Wdn_%d" % g) for g in range(11)]
            self.phase_bufs += bWdng
            for g in range(11):
                self.ldc(Wdn[:, 2 * g:2 * g + 2, :], i["ffn_down"][l, g * 256:(g + 1) * 256, :].rearrange("(k p) n -> p k n", p=128), w=[bWdng[g]])
            cp, bcp = self.sb(es, "cp", [128, 4, 44])
            self.ld(cp[:], i["convp"][l], w=[bcp])
            xts = [self.sb(es, "ext%d" % k, [128, D]) for k in range(1)]
            tmp = (self.sb(es, "exn", [128, D]), self.sb(es, "ess", [128, 1]), self.sb(es, "ers", [128, 1]))
            hT, bh = self.sb(es, "ehT", [128, 8, 512], BF16)
            gT, bgT = self.sb(es, "gT", [128, 22, 512], BF16)
            ua, bua = self.sb(es, "ua", [128, 512])
            uv, buv = self.sb(es, "uv", [128, 512])
            us_, bus = self.sb(es, "usl", [128, 512])
            junk, bjunk = uv, buv
            ss2, bss2 = self.sb(es, "ess2", [128, 2])
            ss, bss = self.sb(es, "ess1", [128, 1])
            rs, brs = self.sb(es, "ers1", [128, 1])
            tt, btt = self.sb(es, "ett", [128, D])
            xo, bxo = tt, btt
            xm, bxm = xts[0]
            pst = (self.ps[0], self.ps[1])
            segs = [(0, 256, 1)] + [(256, self.T, 0)]
            T_ = self.T
            self.st(d["XHS"], d["X2"][T_ - 1:T_, :], r=[db["X2"]], w=[db["XHS"]], sbuf=S.buf("xh_dma"))
            S.collective("AllGather", ALU.bypass, self.groups, d["XHS"], d["XHG"], [db["XHS"]], [db["XHG"]], "halo")
            xn_t, bxn_t = tmp[0]
            self.ld(tt[0:1, :], d["XHG"][0:1, :], r=[db["XHG"]], w=[btt])
            self.ld(xn_t[0:1, :], d["XHG"][1:2, :], r=[db["XHG"]], w=[bxn_t])
            V(lambda: nc.vector.tensor_scalar(out=tt[0:1, :], in0=tt[0:1, :], scalar1=self.selw[0:1, 0:1], scalar2=None, op0=ALU.mult),
              [btt, self.b_selw], [btt])
            V(lambda: nc.vector.scalar_tensor_tensor(out=tt[0:1, :], in0=xn_t[0:1, :], scalar=self.selw[0:1, 1:2], in1=tt[0:1, :], op0=ALU.mult, op1=ALU.add),
              [bxn_t, self.b_selw, btt], [btt])
            self.st(d["XHX"], tt[0:1, :], r=[btt], w=[db["XHX"]])
            for (s0, s1, which) in segs:
                if which == 1 and last:
                    continue
                oa = s0
                while oa < s1:
                    ob = min(oa + 510, s1)
                    ra = oa - 1 if oa > s0 else oa
                    rb_ = ob + 1 if ob < s1 else ob
                    halo = (which == 0 and ob == s1)
                    nrows = rb_ - ra + (1 if halo else 0)
                    j = 0
                    r = ra
                    while r < rb_:
                        nr = min(128, rb_ - r)
                        xt, bx = xts[0]
                        self.ld(xt[0:nr, :], d["X2"][r:r + nr, :], r=[db["X2"]], w=[bx])
                        nr2 = nr
                        if halo and r + nr == rb_:
                            assert nr < 128
                            self.ld(xt[nr:nr + 1, :], d["XHX"], r=[db["XHX"]], pw=[bx])
                            nr2 = nr + 1
                        self.norm_tile(xt, bx, nr2, which, 1, hT, bh, r - ra, tmp, pst)
                        r += nr
                        j += 1
                    lo, hi = oa - ra, ob - ra
                    nout = hi - lo
                    l0 = 1 if lo == 0 else 0
                    r1 = 1 if hi == nrows else 0
                    for cf in range(22):
                        res = []
                        for av, (uu, buu) in enumerate(((ua, bua), (uv, buv))):
                            pz, bpz = self.ps[2 + av + 2 * (cf % 2)]
                            c0 = av * DFF + cf * 128
                            ci = av * 22 + cf
                            for k in range(8):
                                M(lambda k=k: nc.tensor.matmul(pz[:, 0:nrows], lhsT=Wup[:, k, c0:c0 + 128], rhs=hT[:, k, 0:nrows],
                                                               start=(k == 0), stop=(k == 7)), [bWupg[cf // 2][av], bh], pw=[bpz] if k else (), w=[bpz] if k == 0 else ())
                            V(lambda: nc.vector.tensor_scalar(out=uu[:, 0:nout], in0=pz[:, lo:hi], scalar1=cp[:, 1, ci:ci + 1], scalar2=cp[:, 3, ci:ci + 1],
                                                              op0=ALU.mult, op1=ALU.add), [bpz, bcp], [buu])
                            V(lambda: nc.vector.scalar_tensor_tensor(out=uu[:, l0:nout], in0=pz[:, lo + l0 - 1:hi - 1], scalar=cp[:, 0, ci:ci + 1],
                                                                     in1=uu[:, l0:nout], op0=ALU.mult, op1=ALU.add), [bpz, bcp, buu], [buu])
                            V(lambda: nc.vector.scalar_tensor_tensor(out=uu[:, 0:nout - r1], in0=pz[:, lo + 1:hi + 1 - r1], scalar=cp[:, 2, ci:ci + 1],
                                                                     in1=uu[:, 0:nout - r1], op0=ALU.mult, op1=ALU.add), [bpz, bcp, buu], [buu])
                        A(lambda: nc.scalar.activation(out=us_[:, 0:nout], in_=ua[:, 0:nout], func=AF.Silu), [bua], [bus])
                        P(lambda: nc.gpsimd.tensor_tensor(out=gT[:, cf, 0:nout], in0=us_[:, 0:nout], in1=uv[:, 0:nout], op=ALU.mult), [bus, buv],
                          pw=[bgT] if cf else (), w=[bgT] if cf == 0 else ())
                    jo = 0
                    while jo * 128 < nout:
                        no = min(128, nout - jo * 128)
                        r0 = oa + jo * 128
                        self.ld(xm[0:no, :], d["X2"][r0:r0 + no, :], r=[db["X2"]], w=[bxm])
                        for hf in range(2):
                            py, bpy = self.ps[6 + hf]
                            for k in range(22):
                                M(lambda k=k: nc.tensor.matmul(py[0:no, :], lhsT=gT[:, k, jo * 128:jo * 128 + no], rhs=Wdn[:, k, hf * 512:(hf + 1) * 512],
                                                               start=(k == 0), stop=(k == 21)), [bgT, bWdng[k // 2]], pw=[bpy] if k else (), w=[bpy] if k == 0 else ())
                            if hf == 0:
                                A(lambda: nc.scalar.activation(out=junk[0:no, :], in_=py[0:no, :], func=AF.Square, accum_out=ss2[0:no, 0:1]), [bpy], [bjunk, bss2])
                            else:
                                A(lambda: nc.scalar.activation(out=junk[0:no, :], in_=py[0:no, :], func=AF.Square, accum_out=ss2[0:no, 1:2]), [bpy], [bjunk], pw=[bss2])
                        V(lambda: nc.vector.tensor_tensor(out=ss[0:no, :], in0=ss2[0:no, 0:1], in1=ss2[0:no, 1:2], op=ALU.add), [bss2], [bss])
                        self.rstd_from_ss(ss[0:no, :], rs[0:no, :], bss, brs, D, no)
                        for hf in range(2):
                            py, bpy = self.ps[6 + hf]
                            cs = slice(hf * 512, (hf + 1) * 512)
                            V(lambda: nc.vector.scalar_tensor_tensor(out=tt[0:no, cs], in0=py[0:no, :], scalar=rs[0:no, 0:1], in1=self.GG[0:no, which, 1, cs],
                                                                     op0=ALU.mult, op1=ALU.mult), [bpy, brs, self.b_GG], pw=[btt] if hf else (), w=[btt] if hf == 0 else ())
                        P(lambda: nc.gpsimd.tensor_tensor(out=xo[0:no, :], in0=tt[0:no, :], in1=xm[0:no, :], op=ALU.add), [btt, bxm], [bxo])
                        if last:
                            self.st(self.out[r0 - 256:r0 - 256 + no, :], xo[0:no, :], r=[bxo], pw=[self.out_buf])
                        else:
                            self.st(d["X1"][r0:r0 + no, :], xo[0:no, :], r=[bxo], pw=[db["X1"]])
                        jo += 1
                    oa = ob
            S.barrier()
            S.recycle(self.phase_bufs)
            self.phase_bufs = []


def host_consts(n_lat):
    rows = n_lat // 64
    row = np.repeat(np.arange(rows, dtype=np.float32), 64)
    col = np.tile(np.arange(64, dtype=np.float32), rows)
    n_freq = 16
    inv_freq = (np.float32(10000.0) ** (-np.arange(n_freq, dtype=np.float32) / n_freq)).astype(np.float32)
    ang = np.stack([row[:, None] * inv_freq, col[:, None] * inv_freq], axis=1)
    rope = np.concatenate([np.cos(ang).reshape(n_lat, 32), np.sin(ang).reshape(n_lat, 32)], axis=1).astype(np.float32)
    jj = np.arange(128) % 64
    ii = np.arange(64)
    gmask = np.stack([(jj[:, None] <= ii[None, :]), (jj[:, None] >= ii[None, :])], axis=1).astype(np.float32)
    scanmask = np.ones((128, 2, 512), np.float32)
    scanmask[:, 0, ::64] = 0.0
    scanmask[:, 1, 63::64] = 0.0
    return dict(ident=np.eye(128, dtype=np.float32), rope=rope, gmask=gmask, scanmask=scanmask)


def host_layout(inputs, L):
    f = lambda a: np.ascontiguousarray(np.asarray(a, dtype=np.float32))
    p = {}
    p["ada_w"] = f(inputs["ada_w"])[:L]
    p["ada_b"] = f(inputs["ada_b"])[:L]
    colz = lambda v: v.reshape(L, -1, 128).transpose(0, 2, 1)
    p["npre"] = f(np.stack([colz(f(inputs["norm_mix_pre"])[:L]), colz(f(inputs["norm_ffn_pre"])[:L])], axis=2))
    p["npost"] = f(np.stack([f(inputs["norm_mix_post"])[:L], f(inputs["norm_ffn_post"])[:L]], axis=1))
    p["w_in"] = f(inputs["w_in"])[:L]
    qn, kn = f(inputs["q_norm"])[:L], f(inputs["k_norm"])[:L]
    p["qkg"] = f(np.concatenate([np.tile(qn, (1, 4)), np.tile(kn, (1, 2))], axis=1))
    gw = f(inputs["gla_gate_w"])[:L]
    gwp = np.zeros((L, 32, 2, 256), np.float32)
    gwp[:, 0:16, 0, :] = gw[:, 0]
    gwp[:, 16:32, 1, :] = gw[:, 1]
    p["gatew"] = gwp
    gb = f(inputs["gla_gate_b"])[:L]
    p["gateb"] = f(gb.reshape(L, 2, 2, 128).transpose(0, 3, 1, 2).reshape(L, 128, 4))
    p["glanorm"] = f(np.tile(f(inputs["gla_out_norm"])[:L], (1, 4)))
    sm = lambda a: a.reshape(L, 2, 8, 2, 64).transpose(0, 1, 3, 4, 2).reshape(L, 2, 128, 8)
    are, aim = f(inputs["s5_a_re"])[:L], f(inputs["s5_a_im"])[:L]
    ldt = np.broadcast_to(f(inputs["s5_log_dt"])[:L][..., None], (L, 2, 16, 64))
    p["s5a"] = f(np.stack([sm(are), sm(aim), sm(f(ldt))], axis=3))
    bre, bim = f(inputs["s5_b_re"])[:L], f(inputs["s5_b_im"])[:L]
    s5b = np.zeros((L, 2, 32, 2, 8, 128), np.float32)
    cre, cim = f(inputs["s5_c_re"])[:L], f(inputs["s5_c_im"])[:L]
    s5c = np.zeros((L, 2, 128, 2, 8, 64), np.float32)
    for m in range(8):
        for gl in range(2):
            g = 2 * m + gl
            for ri, (bb, cc_) in enumerate(((bre, cre), (bim, cim))):
                s5b[:, :, gl * 16:(gl + 1) * 16, ri, m, gl * 64:(gl + 1) * 64] = bb[:, :, g].transpose(0, 1, 3, 2)
                s5c[:, :, gl * 64:(gl + 1) * 64, ri, m, (m % 2) * 32 + gl * 16:(m % 2) * 32 + (gl + 1) * 16] = cc_[:, :, g].transpose(0, 1, 3, 2)
    p["s5b"], p["s5c"] = s5b, s5c
    p["s5d"] = f(colz(f(inputs["s5_d"])[:L]))
    p["s5glub"] = f(colz(f(inputs["s5_glu_b"])[:L]))
    p["s5gluw"] = f(inputs["s5_glu_w"])[:L]
    for k in ("w_br_att", "w_br_gla", "w_br_s5", "w_out", "ffn_up", "ffn_down"):
        p[k] = f(inputs[k])[:L]
    cw = f(inputs["ffn_conv_w"])[:L]
    cb = f(inputs["ffn_conv_b"])[:L]
    c4 = np.concatenate([cw, cb[:, None, :]], axis=1)
    p["convp"] = f(c4.reshape(L, 4, 44, 128).transpose(0, 3, 1, 2))
    return p


_CACHE = {}

N_LAT_FULL = 8192


def swap_dirs(p):
    q = dict(p)
    gw = np.zeros_like(p["gatew"])
    gw[:, 16:32, 0, :] = p["gatew"][:, 16:32, 1, :]
    gw[:, 0:16, 1, :] = p["gatew"][:, 0:16, 0, :]
    q["gatew"] = gw
    gb = p["gateb"].reshape(-1, 128, 2, 2)
    q["gateb"] = np.ascontiguousarray(gb[:, :, ::-1, :]).reshape(-1, 128, 4)
    for k in ("s5a", "s5b", "s5c"):
        q[k] = np.ascontiguousarray(p[k][:, ::-1])
    cp = p["convp"]
    q["convp"] = np.ascontiguousarray(np.stack([cp[:, :, 2], cp[:, :, 1], cp[:, :, 0], cp[:, :, 3]], axis=2))
    return q


def run(inputs, n_lat, depth, n_batch, dbg=()):
    x = np.asarray(inputs["x"], dtype=np.float32)
    ctx = np.asarray(inputs["ctx"], dtype=np.float32)
    c = np.asarray(inputs["c"], dtype=np.float32)
    c_ctx = np.asarray(inputs["c_ctx"], dtype=np.float32)
    nl = n_lat // 2
    key = (nl, depth, tuple(dbg))
    if key not in _CACHE:
        b = Builder(nl, depth, dbg)
        b.build()
        _CACHE[key] = b
    b = _CACHE[key]
    p0 = host_layout(inputs, depth)
    cst = host_consts(n_lat)
    rope_full = cst.pop("rope")
    p0.update(cst)
    p1 = swap_dirs(p0)
    in_maps = []
    for core in range(8):
        bi = (core // 2) % n_batch
        r = core % 2
        m = dict(p0 if r == 0 else p1)
        if r == 0:
            m["xcat"] = np.ascontiguousarray(np.concatenate([ctx[bi], x[bi, :nl]], axis=0))
            m["rope"] = np.ascontiguousarray(rope_full[:nl])
            m["selw"] = np.ascontiguousarray(np.tile(np.array([[0.0, 1.0]], np.float32), (128, 1)))
        else:
            m["xcat"] = np.ascontiguousarray(np.concatenate([ctx[bi][::-1], x[bi, nl:n_lat][::-1]], axis=0))
            m["rope"] = np.ascontiguousarray(rope_full[nl:n_lat][::-1])
            m["selw"] = np.ascontiguousarray(np.tile(np.array([[1.0, 0.0]], np.float32), (128, 1)))
        cc = np.concatenate([c[bi].reshape(8, 128).T, c_ctx.reshape(8, 128).T], axis=1)
        m["cc"] = np.ascontiguousarray(cc.astype(np.float32))
        in_maps.append(m)
    res = run_bass_kernel_spmd(b.nc, in_maps, core_ids=list(range(8)))
    outs = []
    for bi in range(n_batch):
        o0 = np.asarray(res.results[2 * bi]["out"], dtype=np.float32)
        o1 = np.asarray(res.results[2 * bi + 1]["out"], dtype=np.float32)[::-1]
        outs.append(np.concatenate([o0, o1], axis=0))
    return np.stack(outs, axis=0), res.results


def kernel(**inputs):
    n_b = np.asarray(inputs["x"]).shape[0]
    out, _ = run(inputs, N_LAT_FULL, 4, n_b)
    return out
```

```python
import math
from contextlib import ExitStack
import numpy as np
import ml_dtypes
import concourse.bass as bass
import concourse.mybir as mybir
from concourse.ap import AP
from concourse.bass_utils import run_bass_kernel_spmd

F32 = mybir.dt.float32
BF16 = mybir.dt.bfloat16
I32 = mybir.dt.int32
AF = mybir.ActivationFunctionType
ALU = mybir.AluOpType
AX = mybir.AxisListType

D = 1024
KC = 8
DIN = 5408
DFF = 2816
NA = 2336
EPS = 1e-6
PI = math.pi
STOP_AFTER = ''


class Buf:
    __slots__ = ("name", "writers", "pws", "readers", "dsem", "dcount", "key")

    def __init__(self, name):
        self.name = name
        self.writers = {}
        self.pws = {}
        self.readers = {}
        self.dsem = None
        self.dcount = 0
        self.key = None


class DSem:
    __slots__ = ("sem", "key", "count", "sw")

    def __init__(self, sem, key, sw):
        self.sem, self.key, self.count, self.sw = sem, key, 0, sw


class Eng:
    def __init__(self, name, eng, sem, is_pe=False):
        self.name, self.eng, self.sem, self.count, self.seen, self.is_pe = name, eng, sem, 0, {}, is_pe


class Sched:
    def __init__(self, nc):
        self.nc = nc
        self.sems = {}
        self.pe = Eng("pe", nc.tensor, nc.alloc_semaphore("s_pe"), True)
        self.dve = Eng("dve", nc.vector, nc.alloc_semaphore("s_dve"))
        self.act = Eng("act", nc.scalar, nc.alloc_semaphore("s_act"))
        self.pool = Eng("pool", nc.gpsimd, nc.alloc_semaphore("s_pool"))
        self.sp = Eng("sp", nc.sync, nc.alloc_semaphore("s_sp"))
        self.engs = [self.pe, self.dve, self.act, self.pool, self.sp]
        for e in self.engs:
            self.sems[e.name] = e.sem
        self.bufs = {}
        self.dkeys = {}
        self.free_dsems = {True: [], False: []}
        self.ccount = {}
        self.ninst = 0
        self.nwait = 0

    def buf(self, name):
        b = self.bufs.get(name)
        if b is None:
            b = Buf(name)
            self.bufs[name] = b
        return b

    def _deps(self, reads, writes, pwrites):
        deps = {}

        def add(dd):
            for k, v in dd.items():
                if deps.get(k, 0) < v:
                    deps[k] = v
        for b in reads:
            add(b.writers)
            add(b.pws)
        for b in writes:
            add(b.writers)
            add(b.pws)
            add(b.readers)
        for b in pwrites:
            add(b.writers)
            add(b.readers)
        return deps

    def _wait(self, e, deps):
        for k, v in deps.items():
            if k == e.name and e.is_pe:
                continue
            if e.seen.get(k, 0) < v and k in self.dkeys:
                v = self.dkeys[k].count
            if e.seen.get(k, 0) < v:
                e.eng.wait_ge(self.sems[k], v)
                e.seen[k] = v
                self.nwait += 1

    def _post(self, ev, reads, writes, pwrites):
        for b in reads:
            if b.readers.get(ev[0], 0) < ev[1]:
                b.readers[ev[0]] = ev[1]
        for b in writes:
            b.writers = {ev[0]: ev[1]}
            b.pws = {}
            b.readers = {}
        for b in pwrites:
            b.pws[ev[0]] = ev[1]

    def op(self, e, fn, reads=(), writes=(), pwrites=()):
        self._wait(e, self._deps(reads, writes, pwrites))
        inst = fn()
        e.count += 1
        inst.then_inc(e.sem, 1)
        self.ninst += 1
        self._post((e.name, e.count), reads, writes, pwrites)
        return inst

    def dma(self, e, out, in_, reads=(), writes=(), pwrites=(), sbuf=None, **kw):
        if sbuf is None:
            sbuf = (list(writes) + list(pwrites) + list(reads))[0]
        sw = e is self.pool
        if sbuf.dsem is None:
            if self.free_dsems[sw]:
                sbuf.dsem = self.free_dsems[sw].pop()
            else:
                key = "D%d" % len(self.sems)
                sbuf.dsem = DSem(self.nc.alloc_semaphore("d%d" % len(self.sems)), key, sw)
                self.sems[key] = sbuf.dsem.sem
                self.dkeys[key] = sbuf.dsem
        ds = sbuf.dsem
        assert ds.sw == sw, ("mixed SW/HW DGE on one buffer semaphore", sbuf.name)
        self._wait(e, self._deps(reads, writes, pwrites))
        inst = e.eng.dma_start(out=out, in_=in_, **kw)
        ds.count += 16
        inst.then_inc(ds.sem, 16)
        self.ninst += 1
        self._post((ds.key, ds.count), reads, writes, pwrites)
        return inst

    def recycle(self, bufs):
        for b in bufs:
            if b.dsem is not None:
                self.free_dsems[b.dsem.sw].append(b.dsem)
                b.dsem = None

    def collective(self, kind, op, groups, src, dst, reads, writes, site):
        key = "C_" + site
        if key not in self.sems:
            self.sems[key] = self.nc.alloc_semaphore("c_" + site)
            self.ccount[key] = 0
        e = self.pool
        self._wait(e, self._deps(reads, writes, ()))
        inst = self.nc.gpsimd.collective_compute(kind, op, replica_groups=groups, ins=[src], outs=[dst])
        self.ccount[key] += 1
        inst.then_inc(self.sems[key], 1)
        self.ninst += 1
        self._post((key, self.ccount[key]), reads, writes, ())

    def all_events(self):
        deps = {e.name: e.count for e in self.engs if e.count > 0}
        for k, ds in self.dkeys.items():
            if ds.count > 0:
                deps[k] = ds.count
        for k, v in self.ccount.items():
            if v > 0:
                deps[k] = v
        return deps

    def barrier(self):
        deps = self.all_events()
        for e in self.engs:
            d = dict(deps)
            d.pop(e.name, None)
            self._wait(e, d)

    def finish(self):
        self._wait(self.sp, self.all_events())


def rev(ap2d):
    a = ap2d.ap
    assert len(a) == 2, a
    n = a[1][1]
    st = a[1][0]
    return AP(ap2d.tensor, ap2d.offset + (n - 1) * st, [list(a[0]), [-st, n]])


class Builder:
    def __init__(self, n_lat, depth, dbg=()):
        assert n_lat % 512 == 0
        self.n_lat, self.depth, self.dbgnames = n_lat, depth, tuple(dbg)
        self.T = 256 + n_lat
        self.NT = self.T // 128
        self.NL = n_lat
        self.NTK = (256 + 2 * n_lat) // 128
        self.groups = [[0, 1], [2, 3], [4, 5], [6, 7]]
        self.NCH = self.T // 64
        self.nc = bass.Bass("TRN2", target_bir_lowering=False)
        self.S = Sched(self.nc)
        self.blocks = [(0, 256, True)] + [(256 + 512 * j, 512, False) for j in range(n_lat // 512)]
        self.uid = 0
        self.phase_bufs = []

    def din(self, name, shape, dt=F32):
        return self.nc.dram_tensor(name, list(shape), dt, kind="ExternalInput").ap()

    def dscr(self, name, shape, dt):
        return self.nc.dram_tensor(name, list(shape), dt, kind="Internal").ap()

    def sb(self, es, name, shape, dt=F32):
        self.uid += 1
        t = es.enter_context(self.nc.sbuf_tensor("%s_%d" % (name, self.uid), list(shape), dt))
        b = self.S.buf(name)
        if es is not getattr(self, "ges", None):
            self.phase_bufs.append(b)
        return t, b

    def V(self, fn, r=(), w=(), pw=()):
        return self.S.op(self.S.dve, fn, r, w, pw)

    def A(self, fn, r=(), w=(), pw=()):
        return self.S.op(self.S.act, fn, r, w, pw)

    def P(self, fn, r=(), w=(), pw=()):
        return self.S.op(self.S.pool, fn, r, w, pw)

    def M(self, fn, r=(), w=(), pw=()):
        return self.S.op(self.S.pe, fn, r, w, pw)

    def ld(self, out, in_, r=(), w=(), pw=(), sbuf=None, **kw):
        return self.S.dma(self.S.sp, out, in_, r, w, pw, sbuf, **kw)

    def ldc(self, out, in_, r=(), w=(), pw=(), sbuf=None):
        return self.S.dma(self.S.pool, out, in_, r, w, pw, sbuf, max_dma_last_dim=4096)

    def st(self, out, in_, r=(), w=(), pw=(), sbuf=None, **kw):
        return self.S.dma(self.S.sp, out, in_, r, w, pw, sbuf, **kw)

    def build(self):
        nc, S = self.nc, self.S
        T, L, n_lat = self.T, self.depth, self.n_lat
        i = self.inp = {}
        i["xcat"] = self.din("xcat", [T, D])
        i["cc"] = self.din("cc", [128, 16])
        i["ada_w"] = self.din("ada_w", [L, D, 6 * D])
        i["ada_b"] = self.din("ada_b", [L, 6 * D])
        i["npre"] = self.din("npre", [L, 128, 2, 8])
        i["npost"] = self.din("npost", [L, 2, D])
        i["w_in"] = self.din("w_in", [L, D, DIN])
        i["qkg"] = self.din("qkg", [L, 384])
        i["gatew"] = self.din("gatew", [L, 32, 2, 256])
        i["gateb"] = self.din("gateb", [L, 128, 4])
        i["glanorm"] = self.din("glanorm", [L, 512])
        i["s5a"] = self.din("s5a", [L, 2, 128, 3, 8])
        i["s5b"] = self.din("s5b", [L, 2, 32, 2, 8, 128])
        i["s5c"] = self.din("s5c", [L, 2, 128, 2, 8, 64])
        i["s5d"] = self.din("s5d", [L, 128, 2])
        i["s5glub"] = self.din("s5glub", [L, 128, 2])
        i["s5gluw"] = self.din("s5gluw", [L, 256, 256])
        i["w_br_att"] = self.din("w_br_att", [L, 256, D])
        i["w_br_gla"] = self.din("w_br_gla", [L, 512, D])
        i["w_br_s5"] = self.din("w_br_s5", [L, 256, D])
        i["w_out"] = self.din("w_out", [L, D, D])
        i["ffn_up"] = self.din("ffn_up", [L, D, 2 * DFF])
        i["ffn_down"] = self.din("ffn_down", [L, DFF, D])
        i["convp"] = self.din("convp", [L, 128, 4, 44])
        i["ident"] = self.din("ident", [128, 128])
        i["rope"] = self.din("rope", [n_lat, 64])
        i["gmask"] = self.din("gmask", [128, 2, 64])
        i["scanmask"] = self.din("scanmask", [128, 2, 512])
        i["selw"] = self.din("selw", [128, 2])
        self.out = nc.dram_tensor("out", [n_lat, D], F32, kind="ExternalOutput").ap()
        self.dbg = {}

        d = self.scr = {}
        d["X1"] = self.dscr("X1", [T, D], F32)
        d["X2"] = self.dscr("X2", [T, D], F32)
        d["HT"] = self.dscr("HT", [D, T], BF16)
        d["QT"] = self.dscr("QT", [256, T], BF16)
        d["KT"] = self.dscr("KT", [128, 256], BF16)
        d["VV"] = self.dscr("VV", [256, 128], BF16)
        d["KVS"] = self.dscr("KVS", [256, n_lat], BF16)
        d["KVG"] = self.dscr("KVG", [512, n_lat], BF16)
        d["SFS"] = self.dscr("SFS", [256, 128], F32)
        d["SFG"] = self.dscr("SFG", [512, 128], F32)
        d["CFS"] = self.dscr("CFS", [128, 16], F32)
        d["CFG"] = self.dscr("CFG", [256, 16], F32)
        d["XHS"] = self.dscr("XHS", [1, D], F32)
        d["XHG"] = self.dscr("XHG", [2, D], F32)
        d["XHX"] = self.dscr("XHX", [1, D], F32)
        d["GT"] = self.dscr("GT", [4, 256, T], BF16)
        d["KE"] = self.dscr("KE", [2, T, 256], BF16)
        d["GV"] = self.dscr("GV", [T, 512], BF16)
        d["GR"] = self.dscr("GR", [T, 512], BF16)
        d["SU"] = self.dscr("SU", [256, T], BF16)
        d["OATT"] = self.dscr("OATT", [256, T], BF16)
        d["OGLA"] = self.dscr("OGLA", [512, T], BF16)
        d["OS5"] = self.dscr("OS5", [256, T], BF16)
        d["SST"] = self.dscr("SST", [2, self.NCH, 2, 128, 128], BF16)
        d["YF"] = self.dscr("YF", [256, T], F32)
        self.dbufs = {k: S.buf("dram_" + k) for k in d}
        self.xin_buf = S.buf("dram_xcat")
        self.out_buf = S.buf("dram_out")

        with ExitStack() as ges:
            self.ges = ges
            self.ps = []
            for k in range(8):
                t = ges.enter_context(nc.psum_tensor("ps%d" % k, [128, 512], F32))
                self.ps.append((t, S.buf("ps%d" % k)))
            self.ident, self.b_ident = self.sb(ges, "ident", [128, 128])
            self.identb, self.b_identb = self.sb(ges, "identb", [128, 128], BF16)
            self.ones, self.b_ones = self.sb(ges, "ones", [128, 128])
            self.gmask, self.b_gmask = self.sb(ges, "gmask", [128, 2, 64])
            self.scanmask, self.b_scanmask = self.sb(ges, "scanmask", [128, 2, 512])
            self.GG, self.b_GG = self.sb(ges, "GG", [128, 2, 2, D])
            self.modc, self.b_modc = self.sb(ges, "modc", [128, 2, 6, 8])
            self.prec, self.b_prec = self.sb(ges, "prec", [128, 2, 2, 2, 8])
            self.dec, self.b_dec = self.sb(ges, "dec", [128, 2, 2, self.NCH])
            self.epsc, self.b_epsc = self.sb(ges, "epsc", [128, 1])

            self.ld(self.ident[:], i["ident"], w=[self.b_ident])
            self.V(lambda: nc.vector.tensor_copy(out=self.identb[:], in_=self.ident[:]), [self.b_ident], [self.b_identb])
            self.V(lambda: nc.vector.memset(self.ones[:], 1.0), w=[self.b_ones])
            self.V(lambda: nc.vector.memset(self.epsc[:], EPS), w=[self.b_epsc])
            self.ld(self.gmask[:], i["gmask"], w=[self.b_gmask])
            self.ld(self.scanmask[:], i["scanmask"], w=[self.b_scanmask])
            self.selw, self.b_selw = self.sb(ges, "selw", [128, 2])
            self.ld(self.selw[:], i["selw"], w=[self.b_selw])
            S.barrier()
            S.recycle(self.phase_bufs)
            self.phase_bufs = []

            for l in range(L):
                last = (l == L - 1)
                xin = i["xcat"] if l == 0 else d["X1"]
                xin_b = self.xin_buf if l == 0 else self.dbufs["X1"]
                phases = [("mod", lambda: self.phase_mod(l)), ("a", lambda: self.phase_a(l, xin, xin_b)), ("bc3", lambda: self.phase_bc3(l)),
                          ("c1", lambda: self.phase_c1(l)), ("c2", lambda: self.phase_c2(l)),
                          ("d", lambda: self.phase_d(l, xin, xin_b, last)), ("e", lambda: self.phase_e(l, last))]
                stopped = False
                for pname, pf in phases:
                    pf()
                    if STOP_AFTER == pname:
                        stopped = True
                        break
                if stopped:
                    break
            S.finish()
        return nc

    def dump(self, name, ap_sb, bufs, shape, dt=F32):
        if name not in self.dbgnames:
            return
        t = self.nc.dram_tensor("dbg_" + name, list(shape), dt, kind="ExternalOutput").ap()
        self.dbg[name] = t
        self.st(t, ap_sb, r=bufs, w=[self.S.buf("dbgd_" + name)], sbuf=bufs[0])

    def rstd_from_ss(self, ss_ap, rs_ap, b_ss, b_rs, n, rows=128):
        nc = self.nc
        self.A(lambda: nc.scalar.activation(out=rs_ap, in_=ss_ap, func=AF.Sqrt, bias=self.epsc[0:rows, 0:1], scale=1.0 / n),
               [b_ss, self.b_epsc], [b_rs])
        self.V(lambda: nc.vector.reciprocal(out=rs_ap, in_=rs_ap), [b_rs], [b_rs])

    def phase_mod(self, l):
        nc, S, i = self.nc, self.S, self.inp
        with ExitStack() as es:
            npost, b_npost = self.sb(es, "npost", [128, 2, D])
            cc, bcc = self.sb(es, "cc", [128, 16])
            sc, bsc = self.sb(es, "sc", [128, 16])
            self.screp, self.b_screp = self.sb(es, "screp", [128, 16, 128])
            self.ld(cc[:], i["cc"], w=[bcc])
            self.A(lambda: nc.scalar.activation(out=sc[:], in_=cc[:], func=AF.Silu), [bcc], [bsc])
            self.V(lambda: nc.vector.tensor_copy(out=self.screp[:], in_=sc[:].unsqueeze(2).to_broadcast([128, 16, 128])),
                   [bsc], [self.b_screp])
            npre, b_npre = self.sb(es, "npre", [128, 2, 8])
            adab, b_adab = self.sb(es, "adab", [128, 512])
            wblk = [self.sb(es, "adaw%d" % k, [128, 8, 512]) for k in range(2)]
            mb = [self.sb(es, "mb%d" % k, [128, 512]) for k in range(2)]
            src = i["npost"][l]
            self.ld(npost[:], AP(src.tensor, src.offset, [[0, 128], [D, 2], [1, D]]), w=[b_npost])
            self.ld(npre[:], i["npre"][l], w=[b_npre])
            for cb in range(12):
                kind, half = cb // 2, cb % 2
                wt, bw = wblk[cb % 2]
                self.ld(wt[:], i["ada_w"][l, :, cb * 512:(cb + 1) * 512].rearrange("(k p) n -> p k n", p=128), w=[bw])
                src = i["ada_b"][l, cb * 512:(cb + 1) * 512]
                self.ld(adab[:], AP(src.tensor, src.offset, [[0, 128], [1, 512]]), w=[b_adab])
                for which in range(2):
                    pt, bp = self.ps[which]
                    for k in range(8):
                        self.M(lambda k=k: nc.tensor.matmul(pt[:], lhsT=self.screp[:, which * 8 + k, :], rhs=wt[:, k, :],
                                                           start=(k == 0), stop=(k == 7)),
                               [self.b_screp, bw], pw=[bp] if k else (), w=[bp] if k == 0 else ())
                    mt, bm = mb[which]
                    self.V(lambda: nc.vector.tensor_tensor(out=mt[:], in0=pt[:], in1=adab[:], op=ALU.add), [bp, b_adab], [bm])
                    if kind in (2, 5):
                        mf = 0 if kind == 2 else 1
                        self.V(lambda: nc.vector.tensor_tensor(out=self.GG[:, which, mf, half * 512:(half + 1) * 512], in0=mt[:],
                                                               in1=npost[:, mf, half * 512:(half + 1) * 512], op=ALU.mult),
                               [bm, b_npost], pw=[self.b_GG])
                    else:
                        p2, bp2 = self.ps[2 + which]
                        for j in range(4):
                            self.M(lambda j=j: nc.tensor.transpose(out=p2[:, j * 32:(j + 1) * 32], in_=mt[0:32, j * 128:(j + 1) * 128],
                                                                  identity=self.ident[0:32, 0:32]),
                                   [bm, self.b_ident], pw=[bp2] if j else (), w=[bp2] if j == 0 else ())
                        srcv = p2[:, 0:128].rearrange("p (j c) -> p j c", c=32)[:, :, 0]
                        self.V(lambda: nc.vector.tensor_copy(out=self.modc[:, which, kind, half * 4:(half + 1) * 4], in_=srcv),
                               [bp2], pw=[self.b_modc])
            for which in range(2):
                for mf in range(2):
                    ksh, ksc = (0, 1) if mf == 0 else (3, 4)
                    self.V(lambda: nc.vector.scalar_tensor_tensor(out=self.prec[:, which, mf, 0, :], in0=self.modc[:, which, ksc, :],
                                                                  scalar=1.0, in1=npre[:, mf, :], op0=ALU.add, op1=ALU.mult),
                           [self.b_modc, b_npre], pw=[self.b_prec])
                    self.V(lambda: nc.vector.tensor_copy(out=self.prec[:, which, mf, 1, :], in_=self.modc[:, which, ksh, :]),
                           [self.b_modc], pw=[self.b_prec])
            self.dump("GG", self.GG[:], [self.b_GG], [128, 2, 2, D])
            self.dump("prec", self.prec[:], [self.b_prec], [128, 2, 2, 2, 8])
            S.barrier()
            S.recycle(self.phase_bufs)
            self.phase_bufs = []

    def norm_tile(self, xt, bx, rows, which, mf, hT, bh, col0, tmp, pst):
        nc = self.nc
        (xn, bxn), (ss, bss), (rs, brs) = tmp
        self.A(lambda: nc.scalar.activation(out=xn[0:rows, :], in_=xt[0:rows, :], func=AF.Square, accum_out=ss[0:rows, 0:1]),
               [bx], [bxn, bss])
        self.rstd_from_ss(ss[0:rows, 0:1], rs[0:rows, 0:1], bss, brs, D, rows)
        self.A(lambda: nc.scalar.activation(out=xn[0:rows, :], in_=xt[0:rows, :], func=AF.Identity, scale=rs[0:rows, 0:1]),
               [bx, brs], [bxn])
        for hf in range(2):
            pt, bp = pst[hf]
            for kk in range(4):
                k = hf * 4 + kk
                self.M(lambda: nc.tensor.transpose(out=pt[:, kk * 128:kk * 128 + rows], in_=xn[0:rows, k * 128:(k + 1) * 128],
                                                   identity=self.ident[0:rows, 0:rows]),
                       [bxn, self.b_ident], pw=[bp] if kk else (), w=[bp] if kk == 0 else ())
            for kk in range(4):
                k = hf * 4 + kk
                self.A(lambda: nc.scalar.activation(out=hT[:, k, col0:col0 + rows], in_=pt[:, kk * 128:kk * 128 + rows],
                                                    func=AF.Identity, scale=self.prec[:, which, mf, 0, k:k + 1],
                                                    bias=self.prec[:, which, mf, 1, k:k + 1]),
                       [bp, self.b_prec], pw=[bh])

    def phase_a(self, l, xin, xin_b):
        nc, S, i, d = self.nc, self.S, self.inp, self.scr
        db = self.dbufs
        with ExitStack() as es:
            WA, bWA = self.sb(es, "WA", [128, 8, NA], BF16)
            for k in range(8):
                self.ldc(WA[:, k, :], i["w_in"][l, k * 128:(k + 1) * 128, 0:NA], pw=[bWA])
            gw, bgw = self.sb(es, "gw", [32, 2, 256], BF16)
            self.ldc(gw[:], i["gatew"][l], w=[bgw])
            gb, bgb = self.sb(es, "gb", [128, 4])
            self.ld(gb[:], i["gateb"][l], w=[bgb])
            self.V(lambda: nc.vector.tensor_scalar(out=gb[:], in0=gb[:], scalar1=-1.0, scalar2=None, op0=ALU.mult), [bgb], [bgb])
            qkg, bqkg = self.sb(es, "qkg", [128, 384])
            src = i["qkg"][l]
            self.ld(qkg[:], AP(src.tensor, src.offset, [[0, 128], [1, 384]]), w=[bqkg])
            xts = [self.sb(es, "xt%d" % k, [128, D]) for k in range(2)]
            tmp = (self.sb(es, "xn", [128, D]), self.sb(es, "ss", [128, 1]), self.sb(es, "rs", [128, 1]))
            hTs = [self.sb(es, "hT%d" % k, [128, 8, 512], BF16) for k in range(2)]
            qk0 = [self.sb(es, "qk0_%d" % k, [128, 384]) for k in range(4)]
            sqs = [self.sb(es, "sq_%d" % k, [128, 384]) for k in range(4)]
            ss6s = [self.sb(es, "ss6_%d" % k, [128, 6]) for k in range(4)]
            rs6s = [self.sb(es, "rs6_%d" % k, [128, 6]) for k in range(4)]
            qk1s = [self.sb(es, "qk1_%d" % k, [128, 384]) for k in range(4)]
            qk2s = [self.sb(es, "qk2_%d" % k, [128, 384], BF16) for k in range(4)]
            ras = [self.sb(es, "ra_%d" % k, [128, 6, 2, 16]) for k in range(4)]
            rbs = [self.sb(es, "rb_%d" % k, [128, 6, 2, 16]) for k in range(4)]
            ropes = [self.sb(es, "rope_%d" % k, [128, 64]) for k in range(4)]
            vbfs = [self.sb(es, "vbf_%d" % k, [128, 128], BF16) for k in range(4)]
            qkTs = [self.sb(es, "qkT_%d" % k, [128, 3, 128], BF16) for k in range(4)]
            tmA = [self.sb(es, "tmA%d" % k, [128, 512], BF16) for k in range(2)]
            lowT, blowT = self.sb(es, "lowT", [32, 512], BF16)
            fm = [self.sb(es, "fm%d" % k, [128, 512], BF16) for k in range(2)]
            e1s = [self.sb(es, "e1_%d" % q, [128, 512]) for q in range(2)]
            l1s = [self.sb(es, "l1_%d" % q, [128, 512]) for q in range(2)]
            bps = [self.sb(es, "bpl_%d" % q, [128, 512]) for q in range(2)]
            dds = [self.sb(es, "dd_%d" % q, [128, 512]) for q in range(2)]
            Es = [[self.sb(es, "E%d_%d" % (k, q), [128, 512]) for k in range(3)] for q in range(2)]
            qos = [[self.sb(es, "qo%d_%d" % (k, q), [128, 512], BF16) for k in range(3)] for q in range(2)]
            keTs = [self.sb(es, "keT_%d" % q, [128, 4, 128], BF16) for q in range(2)]
            pst = (self.ps[0], self.ps[1])
            tm_i = 0
            fm_i = 0
            for bi, (t0, n, isctx) in enumerate(self.blocks):
                which = 1 if isctx else 0
                nt = n // 128
                hT, bh = hTs[bi % 2]
                for j in range(nt):
                    xt, bx = xts[j % 2]
                    r0 = t0 + j * 128
                    self.ld(xt[:], xin[r0:r0 + 128, :], r=[xin_b], w=[bx])
                    self.norm_tile(xt, bx, 128, which, 0, hT, bh, j * 128, tmp, pst)
                self.st(d["HT"][:, t0:t0 + n].rearrange("(k p) t -> p k t", p=128), hT[:, :, 0:n], r=[bh], pw=[db["HT"]])
                if l == 0 and bi == 1:
                    self.dump("hT", hT[:], [bh], [128, 8, 512], BF16)
                TJ = list(range(nt))
                for j in TJ:
                    r0 = t0 + j * 128
                    pt, bp = self.ps[2]
                    for k in range(8):
                        self.M(lambda k=k: nc.tensor.matmul(pt[:], lhsT=hT[:, k, j * 128:(j + 1) * 128], rhs=WA[:, k, 0:512],
                                                           start=(k == 0), stop=(k == 7)),
                               [bh, bWA], pw=[bp] if k else (), w=[bp] if k == 0 else ())
                    self.A(lambda: nc.scalar.copy(out=qk0[j][0][:], in_=pt[:, 0:384]), [bp], [qk0[j][1]])
                    self.A(lambda: nc.scalar.copy(out=vbfs[j][0][:], in_=pt[:, 384:512]), [bp], [vbfs[j][1]])
                    if isctx:
                        self.st(d["VV"][r0:r0 + 128, :], vbfs[j][0][:], r=[vbfs[j][1]], pw=[db["VV"]])
                    else:
                        kvs = d["KVS"]
                        vdst = AP(kvs.tensor, kvs.offset + 128 * self.NL + (r0 - 256) * 128, [[128, 128], [1, 128]])
                        self.st(vdst, vbfs[j][0][:], r=[vbfs[j][1]], pw=[db["KVS"]])
                        self.ld(ropes[j][0][:], i["rope"][r0 - 256:r0 - 256 + 128, :], w=[ropes[j][1]])
                    for gi, (c0, dst) in enumerate(((1024, "GV"), (1536, "GR"))):
                        pt, bp = self.ps[3 + gi]
                        for k in range(8):
                            self.M(lambda k=k: nc.tensor.matmul(pt[:], lhsT=hT[:, k, j * 128:(j + 1) * 128], rhs=WA[:, k, c0:c0 + 512],
                                                               start=(k == 0), stop=(k == 7)),
                                   [bh, bWA], pw=[bp] if k else (), w=[bp] if k == 0 else ())
                        tt, bt = tmA[tm_i % 2]
                        tm_i += 1
                        if gi == 0:
                            self.A(lambda: nc.scalar.copy(out=tt[:], in_=pt[:]), [bp], [bt])
                        else:
                            self.A(lambda: nc.scalar.activation(out=tt[:], in_=pt[:], func=AF.Silu), [bp], [bt])
                        self.st(d[dst][r0:r0 + 128, :], tt[:], r=[bt], pw=[db[dst]])
                for j in TJ:
                    self.A(lambda: nc.scalar.activation(out=sqs[j][0][:], in_=qk0[j][0][:], func=AF.Square), [qk0[j][1]], [sqs[j][1]])
                for j in TJ:
                    self.V(lambda: nc.vector.tensor_reduce(out=ss6s[j][0][:], in_=sqs[j][0][:].rearrange("p (h e) -> p h e", e=64), axis=AX.X, op=ALU.add),
                           [sqs[j][1]], [ss6s[j][1]])
                for j in TJ:
                    self.A(lambda: nc.scalar.activation(out=rs6s[j][0][:], in_=ss6s[j][0][:], func=AF.Sqrt, bias=self.epsc[:, 0:1], scale=1.0 / 64),
                           [ss6s[j][1], self.b_epsc], [rs6s[j][1]])
                for j in TJ:
                    self.V(lambda: nc.vector.reciprocal(out=rs6s[j][0][:], in_=rs6s[j][0][:]), [rs6s[j][1]], [rs6s[j][1]])
                for j in TJ:
                    self.V(lambda: nc.vector.tensor_tensor(out=qk1s[j][0][:].rearrange("p (h e) -> p h e", e=64),
                                                           in0=qk0[j][0][:].rearrange("p (h e) -> p h e", e=64),
                                                           in1=rs6s[j][0][:].unsqueeze(2).to_broadcast([128, 6, 64]), op=ALU.mult),
                           [qk0[j][1], rs6s[j][1]], [qk1s[j][1]])
                if isctx:
                    for j in TJ:
                        self.V(lambda: nc.vector.tensor_tensor(out=qk2s[j][0][:], in0=qk1s[j][0][:], in1=qkg[:], op=ALU.mult),
                               [qk1s[j][1], bqkg], [qk2s[j][1]])
                else:
                    for j in TJ:
                        self.V(lambda: nc.vector.tensor_tensor(out=qk1s[j][0][:], in0=qk1s[j][0][:], in1=qkg[:], op=ALU.mult),
                               [qk1s[j][1], bqkg], [qk1s[j][1]])

                    def rviews(j):
                        v5 = qk1s[j][0][:].rearrange("p (h a s f) -> p h a s f", h=6, a=2, s=2)
                        o5 = qk2s[j][0][:].rearrange("p (h a s f) -> p h a s f", h=6, a=2, s=2)
                        rp = ropes[j][0]
                        cs = rp[:, 0:32].rearrange("p (a f) -> p a f", a=2).unsqueeze(1).to_broadcast([128, 6, 2, 16])
                        sn = rp[:, 32:64].rearrange("p (a f) -> p a f", a=2).unsqueeze(1).to_broadcast([128, 6, 2, 16])
                        return v5[:, :, :, 0, :], v5[:, :, :, 1, :], o5, cs, sn
                    for half in range(2):
                        for j in TJ:
                            x1, x2, o5, cs, sn = rviews(j)
                            xa, xb = (x1, x2) if half == 0 else (x2, x1)
                            self.V(lambda: nc.vector.tensor_tensor(out=ras[j][0][:], in0=xa, in1=cs, op=ALU.mult), [qk1s[j][1], ropes[j][1]], [ras[j][1]])
                            self.V(lambda: nc.vector.tensor_tensor(out=rbs[j][0][:], in0=xb, in1=sn, op=ALU.mult), [qk1s[j][1], ropes[j][1]], [rbs[j][1]])
                        for j in TJ:
                            x1, x2, o5, cs, sn = rviews(j)
                            self.V(lambda: nc.vector.tensor_tensor(out=o5[:, :, :, half, :], in0=ras[j][0][:], in1=rbs[j][0][:],
                                                                   op=ALU.subtract if half == 0 else ALU.add),
                                   [ras[j][1], rbs[j][1]], pw=[qk2s[j][1]])
                for j in TJ:
                    r0 = t0 + j * 128
                    p7, bp7 = self.ps[7]
                    p7b = p7[:].bitcast(BF16)
                    for c in range(3):
                        self.M(lambda c=c: nc.tensor.transpose(out=p7b[:, c * 128:(c + 1) * 128], in_=qk2s[j][0][:, c * 128:(c + 1) * 128],
                                                              identity=self.identb[:]),
                               [qk2s[j][1], self.b_identb], pw=[bp7] if c else (), w=[bp7] if c == 0 else ())
                    qkT, bqkT = qkTs[j]
                    self.A(lambda: nc.scalar.copy(out=qkT[:].rearrange("p c t -> p (c t)"), in_=p7b[:, 0:384]), [bp7], [bqkT])
                    self.st(d["QT"][:, r0:r0 + 128].rearrange("(c p) t -> p c t", p=128), qkT[:, 0:2, :], r=[bqkT], pw=[db["QT"]])
                    if isctx:
                        self.st(d["KT"][:, r0:r0 + 128], qkT[:, 2, :], r=[bqkT], pw=[db["KT"]])
                    else:
                        self.st(d["KVS"][0:128, r0 - 256:r0 - 256 + 128], qkT[:, 2, :], r=[bqkT], pw=[db["KVS"]])
                pt, bp = self.ps[2]
                for k in range(8):
                    self.M(lambda k=k: nc.tensor.matmul(pt[0:32, 0:n], lhsT=WA[:, k, 2048:2080], rhs=hT[:, k, 0:n], start=(k == 0), stop=(k == 7)),
                           [bh, bWA], pw=[bp] if k else (), w=[bp] if k == 0 else ())
                self.A(lambda: nc.scalar.copy(out=lowT[:, 0:n], in_=pt[0:32, 0:n]), [bp], [blowT])
                for c in range(2):
                    pt, bp = self.ps[3 + c]
                    for k in range(8):
                        self.M(lambda k=k: nc.tensor.matmul(pt[:, 0:n], lhsT=WA[:, k, 2080 + c * 128:2080 + (c + 1) * 128], rhs=hT[:, k, 0:n],
                                                           start=(k == 0), stop=(k == 7)),
                               [bh, bWA], pw=[bp] if k else (), w=[bp] if k == 0 else ())
                    ft, bf = fm[fm_i % 2]
                    fm_i += 1
                    self.A(lambda: nc.scalar.copy(out=ft[:, 0:n], in_=pt[:, 0:n]), [bp], [bf])
                    self.st(d["SU"][c * 128:(c + 1) * 128, t0:t0 + n], ft[:, 0:n], r=[bf], pw=[db["SU"]])
                nchb = n // 64
                ch0 = t0 // 64
                for c in range(2):
                    pq, bpq = self.ps[3]
                    pk, bpk = self.ps[4]
                    for (pp, bpp, c0) in ((pq, bpq, 512), (pk, bpk, 768)):
                        for k in range(8):
                            self.M(lambda k=k: nc.tensor.matmul(pp[:, 0:n], lhsT=WA[:, k, c0 + c * 128:c0 + (c + 1) * 128], rhs=hT[:, k, 0:n],
                                                               start=(k == 0), stop=(k == 7)),
                                   [bh, bWA], pw=[bpp] if k else (), w=[bpp] if k == 0 else ())
                    def chain(dr):
                        e1, be1 = e1s[dr]
                        l1, bl1 = l1s[dr]
                        bp_, bbp = bps[dr]
                        dd, bdd = dds[dr]
                        E = Es[dr]
                        qo = qos[dr]
                        keT, bkeT = keTs[dr]
                        pz, bpz = self.ps[5 + dr]
                        self.M(lambda: nc.tensor.matmul(pz[:, 0:n], lhsT=gw[:, dr, c * 128:(c + 1) * 128], rhs=lowT[:, 0:n], start=True, stop=True),
                               [bgw, blowT], [bpz])
                        self.A(lambda: nc.scalar.activation(out=e1[:, 0:n], in_=pz[:, 0:n], func=AF.Exp, scale=-1.0,
                                                            bias=gb[:, dr * 2 + c:dr * 2 + c + 1]), [bpz, bgb], [be1])
                        yield
                        self.A(lambda: nc.scalar.activation(out=l1[:, 0:n], in_=e1[:, 0:n], func=AF.Ln, bias=self.ones[:, 0:1], scale=1.0),
                               [be1, self.b_ones], [bl1])
                        yield
                        if dr == 0:
                            self.V(lambda: nc.vector.tensor_tensor_scan(out=bp_[:, 0:n], data0=self.scanmask[:, 0, 0:n], data1=l1[:, 0:n],
                                                                        initial=0.0, op0=ALU.mult, op1=ALU.add),
                                   [self.b_scanmask, bl1], [bbp])
                            endcol = 63
                        else:
                            self.V(lambda: nc.vector.tensor_tensor_scan(out=rev(bp_[:, 0:n]), data0=rev(self.scanmask[:, 1, 0:n]),
                                                                        data1=rev(l1[:, 0:n]), initial=0.0, op0=ALU.mult, op1=ALU.add),
                                   [self.b_scanmask, bl1], [bbp])
                            endcol = 0
                        yield
                        b3 = bp_[:, 0:n].rearrange("p (c i) -> p c i", i=64)
                        bend = b3[:, :, endcol:endcol + 1]
                        self.V(lambda: nc.vector.tensor_tensor(out=dd[:, 0:n].rearrange("p (c i) -> p c i", i=64),
                                                               in0=bend.to_broadcast([128, nchb, 64]), in1=b3, op=ALU.subtract),
                               [bbp], [bdd])
                        yield
                        self.A(lambda: nc.scalar.activation(out=E[0][0][:, 0:n], in_=bp_[:, 0:n], func=AF.Exp, scale=-1.0 / 16), [bbp], [E[0][1]])
                        self.A(lambda: nc.scalar.activation(out=E[1][0][:, 0:n], in_=bp_[:, 0:n], func=AF.Exp, scale=1.0 / 16), [bbp], [E[1][1]])
                        self.A(lambda: nc.scalar.activation(out=E[2][0][:, 0:n], in_=dd[:, 0:n], func=AF.Exp, scale=-1.0 / 16), [bdd], [E[2][1]])
                        self.A(lambda: nc.scalar.activation(out=self.dec[:, dr, c, ch0:ch0 + nchb], in_=bend.rearrange("p c o -> p (c o)"),
                                                            func=AF.Exp, scale=-1.0 / 16), [bbp], pw=[self.b_dec])
                        yield
                        self.V(lambda: nc.vector.scalar_tensor_tensor(out=qo[0][0][:, 0:n], in0=pq[:, 0:n], scalar=0.125, in1=E[0][0][:, 0:n],
                                                                      op0=ALU.mult, op1=ALU.mult), [bpq, E[0][1]], [qo[0][1]])
                        self.V(lambda: nc.vector.tensor_tensor(out=qo[1][0][:, 0:n], in0=pk[:, 0:n], in1=E[1][0][:, 0:n], op=ALU.mult),
                               [bpk, E[1][1]], [qo[1][1]])
                        self.V(lambda: nc.vector.tensor_tensor(out=qo[2][0][:, 0:n], in0=pk[:, 0:n], in1=E[2][0][:, 0:n], op=ALU.mult),
                               [bpk, E[2][1]], [qo[2][1]])
                        yield
                        self.st(d["GT"][dr * 2 + 0, c * 128:(c + 1) * 128, t0:t0 + n], qo[0][0][:, 0:n], r=[qo[0][1]], pw=[db["GT"]])
                        self.st(d["GT"][dr * 2 + 1, c * 128:(c + 1) * 128, t0:t0 + n], qo[1][0][:, 0:n], r=[qo[1][1]], pw=[db["GT"]])
                        p7, bp7 = self.ps[7]
                        p7b = p7[:].bitcast(BF16)
                        for j in range(nt):
                            self.M(lambda j=j: nc.tensor.transpose(out=p7b[:, j * 128:(j + 1) * 128], in_=qo[2][0][:, j * 128:(j + 1) * 128],
                                                                  identity=self.identb[:]),
                                   [qo[2][1], self.b_identb], pw=[bp7] if j else (), w=[bp7] if j == 0 else ())
                        self.A(lambda: nc.scalar.copy(out=keT[:, 0:nt, :].rearrange("p j f -> p (j f)"), in_=p7b[:, 0:nt * 128]), [bp7], [bkeT])
                        self.st(d["KE"][dr, t0:t0 + n, c * 128:(c + 1) * 128].rearrange("(j p) f -> p j f", p=128), keT[:, 0:nt, :],
                                r=[bkeT], pw=[db["KE"]])
                    chains = [chain(0), chain(1)]
                    while chains:
                        for g in list(chains):
                            try:
                                next(g)
                            except StopIteration:
                                chains.remove(g)
            S.barrier()
            S.recycle(self.phase_bufs)
            self.phase_bufs = []

    def gen_b(self, l, es):
        nc, S, d, db = self.nc, self.S, self.scr, self.dbufs
        T, NT, NL = 256 + 2 * self.NL, self.NTK, self.NL
        S.collective("AllGather", ALU.bypass, self.groups, d["KVS"], d["KVG"], [db["KVS"]], [db["KVG"]], "kv")
        if True:
            KT, bKT = self.sb(es, "KTs", [128, T], BF16)
            V1, bV1 = self.sb(es, "V1", [128, NT, 2, 65], BF16)
            self.ld(KT[:, 0:256], d["KT"], r=[db["KT"]], w=[bKT])
            for rk in range(2):
                self.ld(KT[:, 256 + rk * NL:256 + (rk + 1) * NL], d["KVG"][rk * 256:rk * 256 + 128, :], r=[db["KVG"]], pw=[bKT])
            self.P(lambda: nc.gpsimd.memset(V1[:], 1.0), w=[bV1])
            for hh in range(2):
                self.ld(V1[:, 0:2, hh, 0:64], d["VV"][:, hh * 64:(hh + 1) * 64].rearrange("(t p) e -> p t e", p=128), r=[db["VV"]], pw=[bV1])
                for rk in range(2):
                    kvg = d["KVG"]
                    vsrc = AP(kvg.tensor, kvg.offset + (rk * 256 + 128) * NL + hh * 64, [[128, 128], [128 * 128, NL // 128], [1, 64]])
                    t_0 = 2 + rk * (NL // 128)
                    self.ld(V1[:, t_0:t_0 + NL // 128, hh, 0:64], vsrc, r=[db["KVG"]], pw=[bV1])
            Qs = [self.sb(es, "Qs%d" % k, [128, 512], BF16) for k in range(2)]
            PT = [self.sb(es, "PT%d" % k, [128, 512], BF16) for k in range(4)]
            rd, brd = self.sb(es, "rd", [128, 512])
            osb, bosb = self.sb(es, "osb", [128, 512])
            obf = [self.sb(es, "obf%d" % k, [128, 512], BF16) for k in range(2)]
            qi = 0
            pi = 0
            oi = 0
            for (q0, nq, isctx) in self.blocks:
                keys = [0, 1] if isctx else list(range(NT))
                for pair in range(2):
                    Q, bQ = Qs[qi % 2]
                    qi += 1
                    for hh in range(2):
                        h = pair + 2 * hh
                        self.ld(Q[hh * 64:(hh + 1) * 64, 0:nq], d["QT"][h * 64:(h + 1) * 64, q0:q0 + nq], r=[db["QT"]], pw=[bQ])
                    pts = {}

                    def stepA(ki, hh):
                        kt = keys[ki]
                        slot = (ki % 2) * 2 + hh
                        st_, bst = self.ps[slot]
                        self.M(lambda: nc.tensor.matmul(st_[:, 0:nq], lhsT=KT[hh * 64:(hh + 1) * 64, kt * 128:(kt + 1) * 128],
                                                        rhs=Q[hh * 64:(hh + 1) * 64, 0:nq], start=True, stop=True),
                               [bKT, bQ], [bst])
                        pt_, bpt = PT[slot]
                        self.A(lambda: nc.scalar.activation(out=pt_[:, 0:nq], in_=st_[:, 0:nq], func=AF.Exp, scale=0.125), [bst], [bpt])
                        pts[(ki, hh)] = (pt_, bpt)

                    def stepB(ki, hh):
                        kt = keys[ki]
                        pt_, bpt = pts.pop((ki, hh))
                        oa, boa = self.ps[4 + hh]
                        first, lastk = (ki == 0), (ki == len(keys) - 1)
                        self.M(lambda: nc.tensor.matmul(oa[0:65, 0:nq], lhsT=V1[:, kt, hh, :], rhs=pt_[:, 0:nq], start=first, stop=lastk),
                               [bV1, bpt], pw=[boa] if not first else (), w=[boa] if first else ())
                    for ki in range(len(keys) + 1):
                        if ki < len(keys):
                            stepA(ki, 0)
                            stepA(ki, 1)
                        if ki >= 1:
                            stepB(ki - 1, 0)
                            stepB(ki - 1, 1)
                        yield
                    for hh in range(2):
                        h = pair + 2 * hh
                        oa, boa = self.ps[4 + hh]
                        self.V(lambda: nc.vector.reciprocal(out=rd[64:65, 0:nq], in_=oa[64:65, 0:nq]), [boa], [brd])
                        bc, bbc = self.ps[3]
                        self.M(lambda: nc.tensor.matmul(bc[0:64, 0:nq], lhsT=self.ones[64:65, 0:64], rhs=rd[64:65, 0:nq], start=True, stop=True),
                               [self.b_ones, brd], [bbc])
                        self.A(lambda: nc.scalar.copy(out=osb[0:64, 0:nq], in_=oa[0:64, 0:nq]), [boa], [bosb])
                        ob, bob = obf[oi % 2]
                        oi += 1
                        self.V(lambda: nc.vector.tensor_tensor(out=ob[0:64, 0:nq], in0=osb[0:64, 0:nq], in1=bc[0:64, 0:nq], op=ALU.mult),
                               [bosb, bbc], [bob])
                        self.st(d["OATT"][h * 64:(h + 1) * 64, q0:q0 + nq], ob[0:64, 0:nq], r=[bob], pw=[db["OATT"]])
                        yield

    def phase_c1(self, l):
        nc, S, d, db = self.nc, self.S, self.scr, self.dbufs
        with ExitStack() as es:
            St, bSt = self.sb(es, "St", [128, 2, 128])
            sfg = [self.sb(es, "sfg%d" % k, [128, 2, 128]) for k in range(2)]
            kes = [self.sb(es, "ke%d" % k, [128, 4, 256], BF16) for k in range(2)]
            gvs = [self.sb(es, "gvs%d" % k, [128, 4, 512], BF16) for k in range(2)]
            hists = [self.sb(es, "hist%d" % k, [128, 8, 2, 128], BF16) for k in range(2)]
            it = 0
            for dr in range(2):
                self.V(lambda: nc.vector.memset(St[:], 0.0), w=[bSt])
                ctxb = [b for b in self.blocks if b[2]]
                latb = [b for b in self.blocks if not b[2]]
                order = ctxb + (latb if dr == 0 else latb[::-1])
                for (t0, n, isctx) in order:
                    if dr == 1 and (t0, n, isctx) == order[len(ctxb)]:
                        for rk in range(2):
                            self.ld(sfg[rk][0][:], d["SFG"][rk * 256:(rk + 1) * 256, :].rearrange("(c p) e -> p c e", p=128), r=[db["SFG"]], w=[sfg[rk][1]])
                        self.V(lambda: nc.vector.tensor_scalar(out=St[:], in0=sfg[0][0][:], scalar1=self.selw[:, 0:1], scalar2=None, op0=ALU.mult),
                               [sfg[0][1], self.b_selw], [bSt])
                        self.V(lambda: nc.vector.scalar_tensor_tensor(out=St[:], in0=sfg[1][0][:], scalar=self.selw[:, 1:2], in1=St[:], op0=ALU.mult, op1=ALU.add),
                               [sfg[1][1], self.b_selw, bSt], [bSt])
                    nt = n // 128
                    ke, bke = kes[it % 2]
                    gv, bgv = gvs[it % 2]
                    hist, bhist = hists[it % 2]
                    it += 1
                    self.ld(ke[:, 0:nt, :], d["KE"][dr, t0:t0 + n, :].rearrange("(j p) f -> p j f", p=128), r=[db["KE"]], w=[bke])
                    self.ld(gv[:, 0:nt, :], d["GV"][t0:t0 + n, :].rearrange("(j p) f -> p j f", p=128), r=[db["GV"]], w=[bgv])
                    nchb = n // 64
                    chs = list(range(nchb)) if dr == 0 else list(range(nchb))[::-1]
                    for ci, cl in enumerate(chs):
                        j, half = cl // 2, cl % 2
                        ng = t0 // 64 + cl
                        self.A(lambda: nc.scalar.copy(out=hist[:, cl, :, :], in_=St[:]), [bSt], pw=[bhist] if ci else (), w=[bhist] if ci == 0 else ())
                        for c in range(2):
                            kv, bkv = self.ps[c + 2 * (ci % 2)]
                            for hl in range(2):
                                h = c * 2 + hl
                                self.M(lambda: nc.tensor.matmul(kv[hl * 64:(hl + 1) * 64, 0:128],
                                                                lhsT=ke[half * 64:(half + 1) * 64, j, c * 128 + hl * 64:c * 128 + (hl + 1) * 64],
                                                                rhs=gv[half * 64:(half + 1) * 64, j, h * 128:(h + 1) * 128], start=True, stop=True),
                                       [bke, bgv], pw=[bkv] if hl else (), w=[bkv] if hl == 0 else ())
                            self.V(lambda: nc.vector.scalar_tensor_tensor(out=St[:, c, :], in0=St[:, c, :], scalar=self.dec[:, dr, c, ng:ng + 1],
                                                                          in1=kv[:, 0:128], op0=ALU.mult, op1=ALU.add),
                                   [bSt, self.b_dec, bkv], [bSt])
                    ch0 = t0 // 64
                    self.st(d["SST"][dr, ch0:ch0 + nchb].rearrange("n c p e -> p n c e"), hist[:, 0:nchb, :, :], r=[bhist], pw=[db["SST"]])
                if dr == 0:
                    self.st(d["SFS"].rearrange("(c p) e -> p c e", p=128), St[:], r=[bSt], w=[db["SFS"]])
                    S.collective("AllGather", ALU.bypass, self.groups, d["SFS"], d["SFG"], [db["SFS"]], [db["SFG"]], "gla")
            S.barrier()
            S.recycle(self.phase_bufs)
            self.phase_bufs = []

    def phase_c2(self, l):
        nc, S, i, d, db = self.nc, self.S, self.inp, self.scr, self.dbufs
        with ExitStack() as es:
            gn, bgn = self.sb(es, "gn", [128, 512])
            src = i["glanorm"][l]
            self.ld(gn[:], AP(src.tensor, src.offset, [[0, 128], [1, 512]]), w=[bgn])
            gts = [self.sb(es, "gt%d" % k, [64, 4, 4, 512], BF16) for k in range(2)]
            gvs = [self.sb(es, "gv2%d" % k, [64, 8, 512], BF16) for k in range(2)]
            grs = [self.sb(es, "gr2%d" % k, [128, 4, 512], BF16) for k in range(2)]
            ssts = [self.sb(es, "sst%d" % k, [64, 2, 4, 8, 128], BF16) for k in range(2)]
            atts = [self.sb(es, "att%d" % k, [64, 2, 64], BF16) for k in range(4)]
            sq, bsq = self.sb(es, "sq2", [128, 512])
            ss4, bss4 = self.sb(es, "ss4", [128, 4])
            rs4, brs4 = self.sb(es, "rs4", [128, 4])
            t1, bt1 = self.sb(es, "t1", [128, 512])
            t3, bt3 = self.sb(es, "t3", [128, 512], BF16)
            ogT, bogT = self.sb(es, "ogT", [128, 4, 128], BF16)
            ai = 0
            for bi, (t0, n, isctx) in enumerate(self.blocks):
                nt = n // 128
                nchb = n // 64
                ch0 = t0 // 64
                gt, bgt = gts[bi % 2]
                gv, bgv = gvs[bi % 2]
                gr, bgr = grs[bi % 2]
                sst, bsst = ssts[bi % 2]
                for kind in range(4):
                    self.ld(gt[:, kind, :, 0:n], d["GT"][kind, :, t0:t0 + n].rearrange("(h p) t -> p h t", p=64), r=[db["GT"]],
                            pw=[bgt] if kind else (), w=[bgt] if kind == 0 else ())
                self.ld(gv[:, 0:nchb, :], d["GV"][t0:t0 + n, :].rearrange("(c p) f -> p c f", p=64), r=[db["GV"]], w=[bgv])
                self.ld(gr[:, 0:nt, :], d["GR"][t0:t0 + n, :].rearrange("(j p) f -> p j f", p=128), r=[db["GR"]], w=[bgr])
                first_ld = True
                for dr in range(2):
                    for h in range(4):
                        c, hl = h // 2, h % 2
                        self.ld(sst[:, dr, h, 0:nchb, :], d["SST"][dr, ch0:ch0 + nchb, c, hl * 64:(hl + 1) * 64, :].rearrange("n p e -> p n e"),
                                r=[db["SST"]], pw=[bsst] if not first_ld else (), w=[bsst] if first_ld else ())
                        first_ld = False
                for j in range(nt):
                    ops_, bops = self.ps[4 + (j % 2)]
                    for h in range(4):
                        asb = []
                        for dr in range(2):
                            ap_, bap = self.ps[dr + 2 * (h % 2)]
                            for ch in range(2):
                                tk = j * 128 + ch * 64
                                self.M(lambda: nc.tensor.matmul(ap_[0:64, ch * 64:(ch + 1) * 64], lhsT=gt[:, dr * 2 + 1, h, tk:tk + 64],
                                                                rhs=gt[:, dr * 2 + 0, h, tk:tk + 64], start=True, stop=True),
                                       [bgt], pw=[bap] if ch else (), w=[bap] if ch == 0 else ())
                            at, bat = atts[ai % 4]
                            ai += 1
                            self.V(lambda: nc.vector.tensor_tensor(out=at[:], in0=ap_[0:64, 0:128].rearrange("p (c i) -> p c i", i=64),
                                                                   in1=self.gmask[0:64, dr, :].unsqueeze(1).to_broadcast([64, 2, 64]), op=ALU.mult),
                                   [bap, self.b_gmask], [bat])
                            asb.append((at, bat))
                        for ch in range(2):
                            tk = j * 128 + ch * 64
                            cl = 2 * j + ch
                            o_ = ops_[ch * 64:(ch + 1) * 64, h * 128:(h + 1) * 128]
                            first = (h == 0 and ch == 0)
                            for dr in range(2):
                                at, bat = asb[dr]
                                self.M(lambda: nc.tensor.matmul(o_, lhsT=at[:, ch, :], rhs=gv[:, cl, h * 128:(h + 1) * 128],
                                                                start=(dr == 0), stop=False),
                                       [bat, bgv], pw=[bops] if not (first and dr == 0) else (), w=[bops] if (first and dr == 0) else ())
                                self.M(lambda: nc.tensor.matmul(o_, lhsT=gt[:, dr * 2 + 0, h, tk:tk + 64], rhs=sst[:, dr, h, cl, :],
                                                                start=False, stop=(dr == 1)),
                                       [bgt, bsst], pw=[bops])
                    self.A(lambda: nc.scalar.activation(out=sq[:], in_=ops_[:], func=AF.Square), [bops], [bsq])
                    self.V(lambda: nc.vector.tensor_reduce(out=ss4[:], in_=sq[:].rearrange("p (h e) -> p h e", e=128), axis=AX.X, op=ALU.add),
                           [bsq], [bss4])
                    self.rstd_from_ss(ss4[:], rs4[:], bss4, brs4, 128)
                    self.V(lambda: nc.vector.tensor_tensor(out=t1[:].rearrange("p (h e) -> p h e", e=128),
                                                           in0=ops_[:].rearrange("p (h e) -> p h e", e=128),
                                                           in1=rs4[:].unsqueeze(2).to_broadcast([128, 4, 128]), op=ALU.mult),
                           [bops, brs4], [bt1])
                    self.P(lambda: nc.gpsimd.tensor_tensor(out=t1[:], in0=t1[:], in1=gn[:], op=ALU.mult), [bt1, bgn], [bt1])
                    self.V(lambda: nc.vector.tensor_tensor(out=t3[:], in0=t1[:], in1=gr[:, j, :], op=ALU.mult), [bt1, bgr], [bt3])
                    p7, bp7 = self.ps[7]
                    p7b = p7[:].bitcast(BF16)
                    for h in range(4):
                        self.M(lambda h=h: nc.tensor.transpose(out=p7b[:, h * 128:(h + 1) * 128], in_=t3[:, h * 128:(h + 1) * 128],
                                                              identity=self.identb[:]),
                               [bt3, self.b_identb], pw=[bp7] if h else (), w=[bp7] if h == 0 else ())
                    self.A(lambda: nc.scalar.copy(out=ogT[:].rearrange("p h t -> p (h t)"), in_=p7b[:, 0:512]), [bp7], [bogT])
                    r0t = t0 + j * 128
                    self.st(d["OGLA"][:, r0t:r0t + 128].rearrange("(h p) t -> p h t", p=128), ogT[:], r=[bogT], pw=[db["OGLA"]])
            S.barrier()
            S.recycle(self.phase_bufs)
            self.phase_bufs = []

    def gen_c3(self, l, es):
        nc, S, i, d, db = self.nc, self.S, self.inp, self.scr, self.dbufs
        V, A, P, M = self.V, self.A, self.P, self.M
        NB5 = 256
        if True:
            R, bR = self.sb(es, "Rtab", [128, 2, 2, 8, NB5])
            mag, bmag = self.sb(es, "mag", [128, 2, 8])
            Bsb, bB = self.sb(es, "Bsb", [32, 2, 2, 8, 128], BF16)
            Csb, bC = self.sb(es, "Csb", [128, 2, 2, 8, 64], BF16)
            dcol, bdcol = self.sb(es, "dcol", [128, 2])
            glub, bglub = self.sb(es, "glub", [128, 2])
            gluw, bgluw = self.sb(es, "gluw", [128, 2, 256], BF16)
            self.ld(dcol[:], i["s5d"][l], w=[bdcol])
            self.ld(glub[:], i["s5glub"][l], w=[bglub])
            self.ldc(gluw[:], i["s5gluw"][l].rearrange("(c p) n -> p c n", p=128), w=[bgluw])
            carry, bcarry = self.sb(es, "carry", [128, 2, 8])
            if True:
                es2 = es
                a3, ba3 = self.sb(es2, "a3", [128, 3, 8])
                w = [self.sb(es2, "w%d" % k, [128, 8]) for k in range(12)]
                wi, bwi = self.sb(es2, "wi", [128, 8], I32)
                cf, bcf = self.sb(es2, "cf", [128, 2, 8, 64])
                fb, bfb = self.sb(es2, "fb", [128, 2, 8])
                tt = [self.sb(es2, "ct%d" % k, [128, 8, 64]) for k in range(2)]
                tr = [self.sb(es2, "tr%d" % k, [128, 8, NB5 // 2]) for k in range(2)]

                def ts(o, a, s1, s2, op0, op1=None):
                    if op1 is None:
                        V(lambda: nc.vector.tensor_scalar(out=o[0][:], in0=a[0][:], scalar1=s1, scalar2=None, op0=op0), [a[1]], [o[1]])
                    else:
                        V(lambda: nc.vector.tensor_scalar(out=o[0][:], in0=a[0][:], scalar1=s1, scalar2=s2, op0=op0, op1=op1), [a[1]], [o[1]])

                def tt_(o, a, b, op):
                    V(lambda: nc.vector.tensor_tensor(out=o[0][:], in0=a[0][:], in1=b[0][:], op=op), [a[1], b[1]], [o[1]])

                for dr in range(2):
                    self.ld(a3[:], i["s5a"][l, dr], w=[ba3])
                    self.ldc(Bsb[:, dr], i["s5b"][l, dr], pw=[bB])
                    self.ld(cf[:], i["s5c"][l, dr], w=[bcf])
                    are, aim, ldt = (a3[:, 0, :], ba3), (a3[:, 1, :], ba3), (a3[:, 2, :], ba3)
                    dt_, adt, ang, kf, r_, m_, sn, cs, t_, den, fre, fim = w
                    A(lambda: nc.scalar.activation(out=dt_[0][:], in_=a3[:, 2, :], func=AF.Exp), [ba3], [dt_[1]])
                    V(lambda: nc.vector.tensor_tensor(out=adt[0][:], in0=a3[:, 0, :], in1=dt_[0][:], op=ALU.mult), [ba3, dt_[1]], [adt[1]])
                    A(lambda: nc.scalar.activation(out=mag[:, dr, :], in_=adt[0][:], func=AF.Exp), [adt[1]], pw=[bmag])
                    V(lambda: nc.vector.tensor_tensor(out=ang[0][:], in0=a3[:, 1, :], in1=dt_[0][:], op=ALU.mult), [ba3, dt_[1]], [ang[1]])
                    ts(kf, ang, 1.0 / (2 * PI), None, ALU.mult)
                    V(lambda: nc.vector.tensor_copy(out=wi[:], in_=kf[0][:]), [kf[1]], [bwi])
                    V(lambda: nc.vector.tensor_copy(out=kf[0][:], in_=wi[:]), [bwi], [kf[1]])
                    V(lambda: nc.vector.scalar_tensor_tensor(out=r_[0][:], in0=kf[0][:], scalar=-2 * PI, in1=ang[0][:], op0=ALU.mult, op1=ALU.add),
                      [kf[1], ang[1]], [r_[1]])
                    ts(m_, r_, PI, -2 * PI, ALU.is_gt, ALU.mult)
                    tt_(r_, r_, m_, ALU.add)
                    ts(m_, r_, -PI, 2 * PI, ALU.is_lt, ALU.mult)
                    tt_(r_, r_, m_, ALU.add)
                    A(lambda: nc.scalar.activation(out=sn[0][:], in_=r_[0][:], func=AF.Sin), [r_[1]], [sn[1]])
                    ts(t_, r_, -1.0, None, ALU.mult)
                    tt_(t_, t_, r_, ALU.max)
                    ts(t_, t_, -1.0, PI / 2, ALU.mult, ALU.add)
                    A(lambda: nc.scalar.activation(out=cs[0][:], in_=t_[0][:], func=AF.Sin), [t_[1]], [cs[1]])
                    V(lambda: nc.vector.tensor_copy(out=R[:, dr, 0, :, 0], in_=cs[0][:]), [cs[1]], pw=[bR])
                    V(lambda: nc.vector.tensor_copy(out=R[:, dr, 1, :, 0], in_=sn[0][:]), [sn[1]], pw=[bR])
                    abre, abim = kf, m_
                    V(lambda: nc.vector.tensor_tensor(out=abre[0][:], in0=mag[:, dr, :], in1=cs[0][:], op=ALU.mult), [bmag, cs[1]], [abre[1]])
                    V(lambda: nc.vector.tensor_tensor(out=abim[0][:], in0=mag[:, dr, :], in1=sn[0][:], op=ALU.mult), [bmag, sn[1]], [abim[1]])
                    ts(abre, abre, -1.0, None, ALU.add)
                    V(lambda: nc.vector.tensor_tensor(out=den[0][:], in0=a3[:, 0, :], in1=a3[:, 0, :], op=ALU.mult), [ba3], [den[1]])
                    V(lambda: nc.vector.tensor_tensor(out=t_[0][:], in0=a3[:, 1, :], in1=a3[:, 1, :], op=ALU.mult), [ba3], [t_[1]])
                    tt_(den, den, t_, ALU.add)
                    V(lambda: nc.vector.reciprocal(out=den[0][:], in_=den[0][:]), [den[1]], [den[1]])
                    V(lambda: nc.vector.tensor_tensor(out=fre[0][:], in0=abre[0][:], in1=a3[:, 0, :], op=ALU.mult), [abre[1], ba3], [fre[1]])
                    V(lambda: nc.vector.tensor_tensor(out=t_[0][:], in0=abim[0][:], in1=a3[:, 1, :], op=ALU.mult), [abim[1], ba3], [t_[1]])
                    tt_(fre, fre, t_, ALU.add)
                    V(lambda: nc.vector.tensor_tensor(out=fb[:, 0, :], in0=fre[0][:], in1=den[0][:], op=ALU.mult), [fre[1], den[1]], pw=[bfb])
                    V(lambda: nc.vector.tensor_tensor(out=fim[0][:], in0=abim[0][:], in1=a3[:, 0, :], op=ALU.mult), [abim[1], ba3], [fim[1]])
                    V(lambda: nc.vector.tensor_tensor(out=t_[0][:], in0=abre[0][:], in1=a3[:, 1, :], op=ALU.mult), [abre[1], ba3], [t_[1]])
                    tt_(fim, fim, t_, ALU.subtract)
                    V(lambda: nc.vector.tensor_tensor(out=fb[:, 1, :], in0=fim[0][:], in1=den[0][:], op=ALU.mult), [fim[1], den[1]], pw=[bfb])
                    frb = fb[:, 0, :].unsqueeze(2).to_broadcast([128, 8, 64])
                    fib = fb[:, 1, :].unsqueeze(2).to_broadcast([128, 8, 64])
                    V(lambda: nc.vector.tensor_tensor(out=tt[0][0][:], in0=cf[:, 0], in1=frb, op=ALU.mult), [bcf, bfb], [tt[0][1]])
                    V(lambda: nc.vector.tensor_tensor(out=tt[1][0][:], in0=cf[:, 1], in1=fib, op=ALU.mult), [bcf, bfb], [tt[1][1]])
                    V(lambda: nc.vector.tensor_tensor(out=Csb[:, dr, 0], in0=tt[0][0][:], in1=tt[1][0][:], op=ALU.subtract), [tt[0][1], tt[1][1]], pw=[bC])
                    V(lambda: nc.vector.tensor_tensor(out=tt[0][0][:], in0=cf[:, 0], in1=fib, op=ALU.mult), [bcf, bfb], [tt[0][1]])
                    V(lambda: nc.vector.tensor_tensor(out=tt[1][0][:], in0=cf[:, 1], in1=frb, op=ALU.mult), [bcf, bfb], [tt[1][1]])
                    V(lambda: nc.vector.tensor_tensor(out=tt[0][0][:], in0=tt[0][0][:], in1=tt[1][0][:], op=ALU.add), [tt[0][1], tt[1][1]], [tt[0][1]])
                    V(lambda: nc.vector.tensor_scalar(out=Csb[:, dr, 1], in0=tt[0][0][:], scalar1=-1.0, scalar2=None, op0=ALU.mult), [tt[0][1]], pw=[bC])
                    nn = 1
                    while nn < NB5:
                        cr = R[:, dr, 0, :, nn - 1:nn].to_broadcast([128, 8, nn])
                        ci = R[:, dr, 1, :, nn - 1:nn].to_broadcast([128, 8, nn])
                        sre, sim = R[:, dr, 0, :, 0:nn], R[:, dr, 1, :, 0:nn]
                        u0, u1 = tr[0][0][:, :, 0:nn], tr[1][0][:, :, 0:nn]
                        V(lambda: nc.vector.tensor_tensor(out=u0, in0=sre, in1=cr, op=ALU.mult), [bR], [tr[0][1]])
                        V(lambda: nc.vector.tensor_tensor(out=u1, in0=sim, in1=ci, op=ALU.mult), [bR], [tr[1][1]])
                        V(lambda: nc.vector.tensor_tensor(out=R[:, dr, 0, :, nn:2 * nn], in0=u0, in1=u1, op=ALU.subtract), [tr[0][1], tr[1][1]], pw=[bR])
                        V(lambda: nc.vector.tensor_tensor(out=u0, in0=sre, in1=ci, op=ALU.mult), [bR], [tr[0][1]])
                        V(lambda: nc.vector.tensor_tensor(out=u1, in0=sim, in1=cr, op=ALU.mult), [bR], [tr[1][1]])
                        V(lambda: nc.vector.tensor_tensor(out=R[:, dr, 1, :, nn:2 * nn], in0=u0, in1=u1, op=ALU.add), [tr[0][1], tr[1][1]], pw=[bR])
                        nn *= 2
                        yield
                    yield
            if l == 0:
                self.dump("Rtab", R[:], [bR], [128, 2, 2, 8, 512])
                self.dump("mag", mag[:], [bmag], [128, 2, 8])

            Rneg, bRneg = self.sb(es, "Rneg", [128, 2, 8])
            for dr in range(2):
                V(lambda: nc.vector.tensor_scalar(out=Rneg[:, dr, :], in0=R[:, dr, 1, :, NB5 - 1], scalar1=-1.0, scalar2=None, op0=ALU.mult),
                  [bR], pw=[bRneg])
            us = [self.sb(es, "us%d" % k, [32, 8, NB5], BF16) for k in range(2)]
            ufs = [self.sb(es, "uf%d" % k, [128, 2, NB5], BF16) for k in range(2)]
            yfls = [self.sb(es, "yfl%d" % k, [128, 2, NB5]) for k in range(2)]
            tqA = [[self.sb(es, "tqA%d_%d" % (q, k), [128, NB5]) for k in range(4)] for q in range(2)]
            tqB = [[self.sb(es, "tqB%d_%d" % (q, k), [128, NB5]) for k in range(4)] for q in range(2)]
            zz = [[self.sb(es, "z%d_%d" % (q, k), [128, NB5]) for k in range(2)] for q in range(2)]
            scs = [[self.sb(es, "sc%d_%d" % (q, k), [128, NB5]) for k in range(2)] for q in range(3)]
            sbfs = [[self.sb(es, "sbf%d_%d" % (q, k), [128, NB5], BF16) for k in range(2)] for q in range(2)]
            ct = [self.sb(es, "ct%d" % k, [128, 1]) for k in range(2)]
            cfg = [self.sb(es, "cfg%d" % k, [128, 2, 8]) for k in range(2)]
            carryb = [S.buf("carry_m%d" % m) for m in range(8)]
            ysbs = [self.sb(es, "ysb%d" % k, [128, 2, NB5]) for k in range(2)]
            yg, byg = self.sb(es, "yg", [128, 2, NB5], BF16)
            sg, bsg = self.sb(es, "sg", [128, NB5])
            os5, bos5 = self.sb(es, "os5", [128, 2, NB5], BF16)
            p6, b6 = self.ps[6]
            p7, b7 = self.ps[7]
            n = NB5
            ctxb = [(0, 256)]
            latb = [(256 + NB5 * j, NB5) for j in range(self.NL // NB5)]
            for dr in range(2):
                for m in range(8):
                    V(lambda: nc.vector.memset(carry[:, :, m:m + 1], 0.0), w=[carryb[m]])
                order = ctxb + (latb if dr == 0 else latb[::-1])
                items = [(bi, blk, m) for bi, blk in enumerate(order) for m in range(8)]
                blkst = {}
                epis = []

                def dirv(ap2):
                    return ap2 if dr == 0 else rev(ap2)

                def s_load(bi):
                    if bi >= len(order) or bi in blkst:
                        return
                    t0, _n = order[bi]
                    u, bu = us[bi % 2]
                    self.ld(u[:, :, 0:n], d["SU"][:, t0:t0 + n].rearrange("(m r) t -> r m t", r=32), r=[db["SU"]], w=[bu])
                    st_ = dict(u=u, bu=bu)
                    if dr == 1:
                        uf, buf_ = ufs[bi % 2]
                        yfl, byfl = yfls[bi % 2]
                        self.ld(uf[:, :, 0:n], d["SU"][:, t0:t0 + n].rearrange("(c p) t -> p c t", p=128), r=[db["SU"]], w=[buf_])
                        self.ld(yfl[:, :, 0:n], d["YF"][:, t0:t0 + n].rearrange("(c p) t -> p c t", p=128), r=[db["YF"]], w=[byfl])
                        st_.update(uf=uf, buf_=buf_, yfl=yfl, byfl=byfl)
                    blkst[bi] = st_

                def stage1(k):
                    bi, (t0, _n), m = items[k]
                    if m == 0:
                        s_load(bi)
                    if m == 4:
                        s_load(bi + 1)
                    u, bu = blkst[bi]["u"], blkst[bi]["bu"]
                    M(lambda: nc.tensor.matmul(p6[:, 0:n], lhsT=Bsb[:, dr, 0, m, :], rhs=u[:, m, 0:n], start=True, stop=True), [bB, bu], [b6])
                    M(lambda: nc.tensor.matmul(p6[:, n:2 * n], lhsT=Bsb[:, dr, 1, m, :], rhs=u[:, m, 0:n], start=True, stop=True), [bB, bu], pw=[b6])

                def stage2(k):
                    bi, (t0, _n), m = items[k]
                    pre, pim = p6[:, 0:n], p6[:, n:2 * n]
                    Rre = dirv(R[:, dr, 0, m, 0:n])
                    Rim = dirv(R[:, dr, 1, m, 0:n])
                    tq = tqA[k % 2]
                    V(lambda: nc.vector.tensor_tensor(out=tq[0][0][:, 0:n], in0=pre, in1=Rre, op=ALU.mult), [b6, bR], [tq[0][1]])
                    V(lambda: nc.vector.tensor_tensor(out=tq[1][0][:, 0:n], in0=pim, in1=Rim, op=ALU.mult), [b6, bR], [tq[1][1]])
                    V(lambda: nc.vector.tensor_tensor(out=tq[2][0][:, 0:n], in0=pim, in1=Rre, op=ALU.mult), [b6, bR], [tq[2][1]])
                    V(lambda: nc.vector.tensor_tensor(out=tq[3][0][:, 0:n], in0=pre, in1=Rim, op=ALU.mult), [b6, bR], [tq[3][1]])
                    z = zz[k % 2]
                    P(lambda: nc.gpsimd.tensor_tensor(out=z[0][0][:, 0:n], in0=tq[0][0][:, 0:n], in1=tq[1][0][:, 0:n], op=ALU.add),
                      [tq[0][1], tq[1][1]], [z[0][1]])
                    P(lambda: nc.gpsimd.tensor_tensor(out=z[1][0][:, 0:n], in0=tq[2][0][:, 0:n], in1=tq[3][0][:, 0:n], op=ALU.subtract),
                      [tq[2][1], tq[3][1]], [z[1][1]])

                def stage3(k):
                    bi, (t0, _n), m = items[k]
                    z = zz[k % 2]
                    sc_ = scs[k % 3]
                    magb = mag[:, dr, m:m + 1].to_broadcast([128, n])
                    for ri in range(2):
                        V(lambda ri=ri: nc.vector.tensor_tensor_scan(out=dirv(sc_[ri][0][:, 0:n]), data0=magb, data1=dirv(z[ri][0][:, 0:n]),
                                                                    initial=carry[:, ri, m:m + 1], op0=ALU.mult, op1=ALU.add),
                          [bmag, z[ri][1], carryb[m]], [sc_[ri][1]])
                    Rre = dirv(R[:, dr, 0, m, 0:n])
                    Rim = dirv(R[:, dr, 1, m, 0:n])
                    tq = tqB[k % 2]
                    P(lambda: nc.gpsimd.tensor_tensor(out=tq[0][0][:, 0:n], in0=sc_[0][0][:, 0:n], in1=Rre, op=ALU.mult), [sc_[0][1], bR], [tq[0][1]])
                    P(lambda: nc.gpsimd.tensor_tensor(out=tq[1][0][:, 0:n], in0=sc_[1][0][:, 0:n], in1=Rim, op=ALU.mult), [sc_[1][1], bR], [tq[1][1]])
                    P(lambda: nc.gpsimd.tensor_tensor(out=tq[2][0][:, 0:n], in0=sc_[0][0][:, 0:n], in1=Rim, op=ALU.mult), [sc_[0][1], bR], [tq[2][1]])
                    P(lambda: nc.gpsimd.tensor_tensor(out=tq[3][0][:, 0:n], in0=sc_[1][0][:, 0:n], in1=Rre, op=ALU.mult), [sc_[1][1], bR], [tq[3][1]])

                def stage4(k):
                    bi, (t0, _n), m = items[k]
                    sc_ = scs[k % 3]
                    lastc = n - 1 if dr == 0 else 0
                    rl_re, rl_im = R[:, dr, 0, m, n - 1:n], R[:, dr, 1, m, n - 1:n]
                    nrl_im = Rneg[:, dr, m:m + 1]
                    s_re, s_im = sc_[0][0][:, lastc:lastc + 1], sc_[1][0][:, lastc:lastc + 1]
                    A(lambda: nc.scalar.activation(out=ct[0][0][:], in_=s_im, func=AF.Identity, scale=nrl_im), [sc_[1][1], bRneg], [ct[0][1]])
                    A(lambda: nc.scalar.activation(out=carry[:, 0, m:m + 1], in_=s_re, func=AF.Identity, scale=rl_re, bias=ct[0][0][:]),
                      [sc_[0][1], bR, ct[0][1]], [carryb[m]])
                    A(lambda: nc.scalar.activation(out=ct[1][0][:], in_=s_re, func=AF.Identity, scale=rl_im), [sc_[0][1], bR], [ct[1][1]])
                    A(lambda: nc.scalar.activation(out=carry[:, 1, m:m + 1], in_=s_im, func=AF.Identity, scale=rl_re, bias=ct[1][0][:]),
                      [sc_[1][1], bR, ct[1][1]], pw=[carryb[m]])
                    tq = tqB[k % 2]
                    (sr, bsr), (sm, bsm) = sbfs[k % 2]
                    V(lambda: nc.vector.tensor_tensor(out=sr[:, 0:n], in0=tq[0][0][:, 0:n], in1=tq[1][0][:, 0:n], op=ALU.subtract),
                      [tq[0][1], tq[1][1]], [bsr])
                    V(lambda: nc.vector.tensor_tensor(out=sm[:, 0:n], in0=tq[2][0][:, 0:n], in1=tq[3][0][:, 0:n], op=ALU.add),
                      [tq[2][1], tq[3][1]], [bsm])

                def stage5(k):
                    bi, (t0, _n), m = items[k]
                    (sr, bsr), (sm, bsm) = sbfs[k % 2]
                    c = m // 4
                    mo = 64 * ((m // 2) % 2)
                    yo = p7[mo:mo + 64, c * n:(c + 1) * n]
                    M(lambda: nc.tensor.matmul(yo, lhsT=Csb[:, dr, 0, m, :], rhs=sr[:, 0:n], start=(m % 2 == 0), stop=False),
                      [bC, bsr], pw=[b7] if m else (), w=[b7] if m == 0 else ())
                    M(lambda: nc.tensor.matmul(yo, lhsT=Csb[:, dr, 1, m, :], rhs=sm[:, 0:n], start=False, stop=(m % 2 == 1)),
                      [bC, bsm], pw=[b7])
                    if m == 7:
                        epis.append(epilogue(bi, t0))

                def epilogue(bi, t0):
                    st_ = blkst.pop(bi)
                    ysb, bysb = ysbs[bi % 2]
                    if dr == 0:
                        V(lambda: nc.vector.tensor_copy(out=ysb[:].rearrange("p c t -> p (c t)"), in_=p7[:, 0:2 * n]), [b7], [bysb])
                        yield
                        self.st(d["YF"][:, t0:t0 + n].rearrange("(c p) t -> p c t", p=128), ysb[:, :, 0:n], r=[bysb], pw=[db["YF"]])
                        return
                    uf, buf_, yfl, byfl = st_["uf"], st_["buf_"], st_["yfl"], st_["byfl"]
                    V(lambda: nc.vector.tensor_tensor(out=ysb[:].rearrange("p c t -> p (c t)"), in0=p7[:, 0:2 * n],
                                                      in1=yfl[:].rearrange("p c t -> p (c t)"), op=ALU.add), [b7, byfl], [bysb])
                    for c in range(2):
                        V(lambda: nc.vector.scalar_tensor_tensor(out=ysb[:, c, 0:n], in0=uf[:, c, 0:n], scalar=dcol[:, c:c + 1], in1=ysb[:, c, 0:n],
                                                                 op0=ALU.mult, op1=ALU.add), [buf_, bdcol, bysb], [bysb])
                    yield
                    for c in range(2):
                        V(lambda: nc.vector.tensor_tensor(out=sg[:, 0:n], in0=ysb[:, c, 0:n], in1=ysb[:, c, 0:n], op=ALU.mult), [bysb], [bsg])
                        V(lambda: nc.vector.tensor_scalar(out=sg[:, 0:n], in0=sg[:, 0:n], scalar1=0.044715, scalar2=1.0, op0=ALU.mult, op1=ALU.add), [bsg], [bsg])
                        V(lambda: nc.vector.tensor_tensor(out=sg[:, 0:n], in0=sg[:, 0:n], in1=ysb[:, c, 0:n], op=ALU.mult), [bsg, bysb], [bsg])
                        A(lambda: nc.scalar.activation(out=sg[:, 0:n], in_=sg[:, 0:n], func=AF.Sigmoid, scale=2.0 * math.sqrt(2.0 / PI)), [bsg], [bsg])
                        V(lambda: nc.vector.tensor_tensor(out=yg[:, c, 0:n], in0=ysb[:, c, 0:n], in1=sg[:, 0:n], op=ALU.mult), [bysb, bsg],
                          pw=[byg] if c else (), w=[byg] if c == 0 else ())
                        yield
                    for c2 in range(2):
                        zp = p6[:, c2 * n:(c2 + 1) * n]
                        for c in range(2):
                            M(lambda c=c: nc.tensor.matmul(zp, lhsT=gluw[:, c, c2 * 128:(c2 + 1) * 128], rhs=yg[:, c, 0:n],
                                                           start=(c == 0), stop=(c == 1)), [bgluw, byg],
                              pw=[b6] if (c or c2) else (), w=[b6] if (c == 0 and c2 == 0) else ())
                    for c2 in range(2):
                        zp = p6[:, c2 * n:(c2 + 1) * n]
                        A(lambda: nc.scalar.activation(out=sg[:, 0:n], in_=zp, func=AF.Sigmoid, bias=glub[:, c2:c2 + 1], scale=1.0),
                          [b6, bglub], [bsg])
                        V(lambda: nc.vector.tensor_tensor(out=os5[:, c2, 0:n], in0=yg[:, c2, 0:n], in1=sg[:, 0:n], op=ALU.mult), [byg, bsg],
                          pw=[bos5] if c2 else (), w=[bos5] if c2 == 0 else ())
                    self.st(d["OS5"][:, t0:t0 + n].rearrange("(c p) t -> p c t", p=128), os5[:, :, 0:n], r=[bos5], pw=[db["OS5"]])

                def step_epis():
                    for g in list(epis):
                        try:
                            next(g)
                        except StopIteration:
                            epis.remove(g)

                def run_range(k0, k1):
                    stages = (stage1, stage2, stage3, stage4, stage5)
                    for k in range(k0, k1 + len(stages) - 1):
                        for si, fn in reversed(list(enumerate(stages))):
                            kk = k - si
                            if k0 <= kk < k1:
                                fn(kk)
                                if fn is stage2:
                                    step_epis()
                                yield
                        if not (k0 <= k - 1 < k1):
                            step_epis()
                    while epis:
                        step_epis()
                        yield
                NI = len(items)
                nctx_items = 8 * len(ctxb)
                if dr == 0:
                    yield from run_range(0, NI)
                    self.st(d["CFS"], carry[:].rearrange("p r m -> p (r m)"), r=carryb, w=[db["CFS"]], sbuf=bcarry)
                    S.collective("AllGather", ALU.bypass, self.groups, d["CFS"], d["CFG"], [db["CFS"]], [db["CFG"]], "s5")
                else:
                    yield from run_range(0, nctx_items)
                    for rk in range(2):
                        self.ld(cfg[rk][0][:], d["CFG"][rk * 128:(rk + 1) * 128, :].rearrange("p (r m) -> p r m", r=2), r=[db["CFG"]], w=[cfg[rk][1]])
                    V(lambda: nc.vector.tensor_scalar(out=carry[:], in0=cfg[0][0][:], scalar1=self.selw[:, 0:1], scalar2=None, op0=ALU.mult),
                      [cfg[0][1], self.b_selw], carryb)
                    V(lambda: nc.vector.scalar_tensor_tensor(out=carry[:], in0=cfg[1][0][:], scalar=self.selw[:, 1:2], in1=carry[:], op0=ALU.mult, op1=ALU.add),
                      [cfg[1][1], self.b_selw] + carryb, carryb)
                    yield from run_range(nctx_items, NI)

    def phase_bc3(self, l):
        with ExitStack() as es:
            gb = self.gen_b(l, es)
            gc = self.gen_c3(l, es)
            gens = [(gb, 2), (gc, 3)]
            while gens:
                for g, reps in list(gens):
                    try:
                        for _ in range(reps):
                            next(g)
                    except StopIteration:
                        gens.remove((g, reps))
            self.S.barrier()
            self.S.recycle(self.phase_bufs)
            self.phase_bufs = []

    def phase_d(self, l, xin, xin_b, last):
        nc, S, i, d, db = self.nc, self.S, self.inp, self.scr, self.dbufs
        V, A, P, M = self.V, self.A, self.P, self.M
        with ExitStack() as es:
            WG, bWG = self.sb(es, "WG", [128, 8, 3072], BF16)
            bWGg = [[S.buf("WG_%d" % f)] * 3 for f in range(8)]
            self.phase_bufs += [g[0] for g in bWGg]

            def load_wg(f):
                for br in range(3):
                    c0 = br * 1024 + f * 128
                    self.ldc(WG[:, :, c0:c0 + 128], i["w_in"][l, :, NA + c0:NA + c0 + 128].rearrange("(k p) n -> p k n", p=128),
                             w=[bWGg[f][br]] if br == 0 else (), pw=[bWGg[f][br]] if br else ())
            load_wg(0)
            Wba, bWba = self.sb(es, "Wba", [128, 2, D], BF16)
            Wbg, bWbg = self.sb(es, "Wbg", [128, 4, D], BF16)
            Wbs, bWbs = self.sb(es, "Wbs", [128, 2, D], BF16)
            Wo, bWo = self.sb(es, "Wo", [128, 8, D], BF16)
            self.ldc(Wba[:], i["w_br_att"][l].rearrange("(k p) n -> p k n", p=128), w=[bWba])
            self.ldc(Wbg[:], i["w_br_gla"][l].rearrange("(k p) n -> p k n", p=128), w=[bWbg])
            self.ldc(Wbs[:], i["w_br_s5"][l].rearrange("(k p) n -> p k n", p=128), w=[bWbs])
            for f in range(1, 8):
                load_wg(f)
            for k in range(8):
                self.ldc(Wo[:, k, :], i["w_out"][l, k * 128:(k + 1) * 128, :], pw=[bWo])
            hTs = [self.sb(es, "dhT%d" % k, [128, 8, 512], BF16) for k in range(2)]
            srcs = [self.sb(es, "dsrc%d" % k, [128, 8, 512], BF16) for k in range(2)]
            sgs = [self.sb(es, "dsg%d" % k, [128, 512]) for k in range(3)]
            macc, bmacc = self.sb(es, "macc", [128, 512])
            tacc, btacc = self.sb(es, "tacc", [128, 512])
            mT, bmT = self.sb(es, "mT", [128, 8, 512], BF16)
            xts = [self.sb(es, "dxt%d" % k, [128, D]) for k in range(2)]
            xos = [self.sb(es, "dxo%d" % k, [128, D]) for k in range(2)]
            junk, bjunk = self.sb(es, "djunk", [128, 512])
            ss2, bss2 = self.sb(es, "dss2", [128, 2])
            ss, bss = self.sb(es, "dss", [128, 1])
            rs, brs = self.sb(es, "drs", [128, 1])
            tt, btt = self.sb(es, "dtt", [128, D])
            branches = ((Wba, bWba, 0, 2), (Wbg, bWbg, 2, 4), (Wbs, bWbs, 6, 2))
            ti = 0
            for bi, (t0, n, isctx) in enumerate(self.blocks):
                if isctx and last:
                    continue
                which = 1 if isctx else 0
                nt = n // 128
                hT, bh = hTs[bi % 2]
                sr, bsr = srcs[bi % 2]
                self.ld(hT[:, :, 0:n], d["HT"][:, t0:t0 + n].rearrange("(k p) t -> p k t", p=128), r=[db["HT"]], w=[bh])
                self.ld(sr[:, 0:2, 0:n], d["OATT"][:, t0:t0 + n].rearrange("(k p) t -> p k t", p=128), r=[db["OATT"]], w=[bsr])
                self.ld(sr[:, 2:6, 0:n], d["OGLA"][:, t0:t0 + n].rearrange("(k p) t -> p k t", p=128), r=[db["OGLA"]], pw=[bsr])
                self.ld(sr[:, 6:8, 0:n], d["OS5"][:, t0:t0 + n].rearrange("(k p) t -> p k t", p=128), r=[db["OS5"]], pw=[bsr])
                for f in range(8):
                    for br, (W, bW, k0, nk) in enumerate(branches):
                        pb, bpb = self.ps[br]
                        pg, bpg = self.ps[3 + br]
                        for k in range(nk):
                            M(lambda k=k: nc.tensor.matmul(pb[:, 0:n], lhsT=W[:, k, f * 128:(f + 1) * 128], rhs=sr[:, k0 + k, 0:n],
                                                           start=(k == 0), stop=(k == nk - 1)), [bW, bsr], pw=[bpb] if k else (), w=[bpb] if k == 0 else ())
                        for k in range(8):
                            M(lambda k=k: nc.tensor.matmul(pg[:, 0:n], lhsT=WG[:, k, br * 1024 + f * 128:br * 1024 + (f + 1) * 128], rhs=hT[:, k, 0:n],
                                                           start=(k == 0), stop=(k == 7)), [bWGg[f][br], bh], pw=[bpg] if k else (), w=[bpg] if k == 0 else ())
                        sg, bsg = sgs[br]
                        A(lambda: nc.scalar.activation(out=sg[:, 0:n], in_=pg[:, 0:n], func=AF.Sigmoid), [bpg], [bsg])
                        if br == 0:
                            V(lambda: nc.vector.tensor_tensor(out=macc[:, 0:n], in0=sg[:, 0:n], in1=pb[:, 0:n], op=ALU.mult), [bsg, bpb], [bmacc])
                        else:
                            V(lambda: nc.vector.tensor_tensor(out=tacc[:, 0:n], in0=sg[:, 0:n], in1=pb[:, 0:n], op=ALU.mult), [bsg, bpb], [btacc])
                            if br == 1:
                                P(lambda: nc.gpsimd.tensor_tensor(out=macc[:, 0:n], in0=macc[:, 0:n], in1=tacc[:, 0:n], op=ALU.add), [bmacc, btacc], [bmacc])
                            else:
                                P(lambda: nc.gpsimd.tensor_tensor(out=mT[:, f, 0:n], in0=macc[:, 0:n], in1=tacc[:, 0:n], op=ALU.add), [bmacc, btacc],
                                  pw=[bmT] if f else (), w=[bmT] if f == 0 else ())
                for j in range(nt):
                    r0 = t0 + j * 128
                    xt, bx = xts[ti % 2]
                    xo, bxo = xos[ti % 2]
                    ti += 1
                    self.ld(xt[:], xin[r0:r0 + 128, :], r=[xin_b], w=[bx])
                    for hf in range(2):
                        py, bpy = self.ps[6 + hf]
                        for k in range(8):
                            M(lambda k=k: nc.tensor.matmul(py[:], lhsT=mT[:, k, j * 128:(j + 1) * 128], rhs=Wo[:, k, hf * 512:(hf + 1) * 512],
                                                           start=(k == 0), stop=(k == 7)), [bmT, bWo], pw=[bpy] if k else (), w=[bpy] if k == 0 else ())
                        if hf == 0:
                            A(lambda: nc.scalar.activation(out=junk[:], in_=py[:], func=AF.Square, accum_out=ss2[:, 0:1]), [bpy], [bjunk, bss2])
                        else:
                            A(lambda: nc.scalar.activation(out=junk[:], in_=py[:], func=AF.Square, accum_out=ss2[:, 1:2]), [bpy], [bjunk], pw=[bss2])
                    V(lambda: nc.vector.tensor_tensor(out=ss[:], in0=ss2[:, 0:1], in1=ss2[:, 1:2], op=ALU.add), [bss2], [bss])
                    self.rstd_from_ss(ss[:], rs[:], bss, brs, D)
                    for hf in range(2):
                        py, bpy = self.ps[6 + hf]
                        cs = slice(hf * 512, (hf + 1) * 512)
                        V(lambda: nc.vector.scalar_tensor_tensor(out=tt[:, cs], in0=py[:], scalar=rs[:, 0:1], in1=self.GG[:, which, 0, cs],
                                                                 op0=ALU.mult, op1=ALU.mult), [bpy, brs, self.b_GG], pw=[btt] if hf else (), w=[btt] if hf == 0 else ())
                    P(lambda: nc.gpsimd.tensor_tensor(out=xo[:], in0=tt[:], in1=xt[:], op=ALU.add), [btt, bx], [bxo])
                    self.st(d["X2"][r0:r0 + 128, :], xo[:], r=[bxo], pw=[db["X2"]])
            S.barrier()
            S.recycle(self.phase_bufs)
            self.phase_bufs = []

    def phase_e(self, l, last):
        nc, S, i, d, db = self.nc, self.S, self.inp, self.scr, self.dbufs
        V, A, P, M = self.V, self.A, self.P, self.M
        with ExitStack() as es:
            Wup, bWup = self.sb(es, "Wup", [128, 8, 2 * DFF], BF16)
            bWupg = [[S.buf("Wup_%d" % g)] * 2 for g in range(11)]
            self.phase_bufs += [g[0] for g in bWupg]
            for g in range(11):
                for av in range(2):
                    c0 = av * DFF + g * 256
                    self.ldc(Wup[:, :, c0:c0 + 256], i["ffn_up"][l, :, c0:c0 + 256].rearrange("(k p) n -> p k n", p=128),
                             w=[bWupg[g][av]] if av == 0 else (), pw=[bWupg[g][av]] if av else ())
            Wdn, bWdn = self.sb(es, "Wdn", [128, 22, D], BF16)
            bWdng = [S.buf("Wdn_%d" % g) for g in range(11)]
            self.phase_bufs += bWdng
            for g in range(11):
                self.ldc(Wdn[:, 2 * g:2 * g + 2, :], i["ffn_down"][l, g * 256:(g + 1) * 256, :].rearrange("(k p) n -> p k n", p=128), w=[bWdng[g]])
            cp, bcp = self.sb(es, "cp", [128, 4, 44])
            self.ld(cp[:], i["convp"][l], w=[bcp])
            xts = [self.sb(es, "ext%d" % k, [128, D]) for k in range(1)]
            tmp = (self.sb(es, "exn", [128, D]), self.sb(es, "ess", [128, 1]), self.sb(es, "ers", [128, 1]))
            hT, bh = self.sb(es, "ehT", [128, 8, 512], BF16)
            gT, bgT = self.sb(es, "gT", [128, 22, 512], BF16)
            ua, bua = self.sb(es, "ua", [128, 512])
            uv, buv = self.sb(es, "uv", [128, 512])
            us_, bus = self.sb(es, "usl", [128, 512])
            junk, bjunk = uv, buv
            ss2, bss2 = self.sb(es, "ess2", [128, 2])
            ss, bss = self.sb(es, "ess1", [128, 1])
            rs, brs = self.sb(es, "ers1", [128, 1])
            tt, btt = self.sb(es, "ett", [128, D])
            xo, bxo = tt, btt
            xm, bxm = xts[0]
            pst = (self.ps[0], self.ps[1])
            segs = [(0, 256, 1)] + [(256, self.T, 0)]
            T_ = self.T
            self.st(d["XHS"], d["X2"][T_ - 1:T_, :], r=[db["X2"]], w=[db["XHS"]], sbuf=S.buf("xh_dma"))
            S.collective("AllGather", ALU.bypass, self.groups, d["XHS"], d["XHG"], [db["XHS"]], [db["XHG"]], "halo")
            xn_t, bxn_t = tmp[0]
            self.ld(tt[0:1, :], d["XHG"][0:1, :], r=[db["XHG"]], w=[btt])
            self.ld(xn_t[0:1, :], d["XHG"][1:2, :], r=[db["XHG"]], w=[bxn_t])
            V(lambda: nc.vector.tensor_scalar(out=tt[0:1, :], in0=tt[0:1, :], scalar1=self.selw[0:1, 0:1], scalar2=None, op0=ALU.mult),
              [btt, self.b_selw], [btt])
            V(lambda: nc.vector.scalar_tensor_tensor(out=tt[0:1, :], in0=xn_t[0:1, :], scalar=self.selw[0:1, 1:2], in1=tt[0:1, :], op0=ALU.mult, op1=ALU.add),
              [bxn_t, self.b_selw, btt], [btt])
            self.st(d["XHX"], tt[0:1, :], r=[btt], w=[db["XHX"]])
            for (s0, s1, which) in segs:
                if which == 1 and last:
                    continue
                oa = s0
                while oa < s1:
                    ob = min(oa + 510, s1)
                    ra = oa - 1 if oa > s0 else oa
                    rb_ = ob + 1 if ob < s1 else ob
                    halo = (which == 0 and ob == s1)
                    nrows = rb_ - ra + (1 if halo else 0)
                    j = 0
                    r = ra
                    while r < rb_:
                        nr = min(128, rb_ - r)
                        xt, bx = xts[0]
                        self.ld(xt[0:nr, :], d["X2"][r:r + nr, :], r=[db["X2"]], w=[bx])
                        nr2 = nr
                        if halo and r + nr == rb_:
                            assert nr < 128
                            self.ld(xt[nr:nr + 1, :], d["XHX"], r=[db["XHX"]], pw=[bx])
                            nr2 = nr + 1
                        self.norm_tile(xt, bx, nr2, which, 1, hT, bh, r - ra, tmp, pst)
                        r += nr
                        j += 1
                    lo, hi = oa - ra, ob - ra
                    nout = hi - lo
                    l0 = 1 if lo == 0 else 0
                    r1 = 1 if hi == nrows else 0
                    for cf in range(22):
                        res = []
                        for av, (uu, buu) in enumerate(((ua, bua), (uv, buv))):
                            pz, bpz = self.ps[2 + av + 2 * (cf % 2)]
                            c0 = av * DFF + cf * 128
                            ci = av * 22 + cf
                            for k in range(8):
                                M(lambda k=k: nc.tensor.matmul(pz[:, 0:nrows], lhsT=Wup[:, k, c0:c0 + 128], rhs=hT[:, k, 0:nrows],
                                                               start=(k == 0), stop=(k == 7)), [bWupg[cf // 2][av], bh], pw=[bpz] if k else (), w=[bpz] if k == 0 else ())
                            V(lambda: nc.vector.tensor_scalar(out=uu[:, 0:nout], in0=pz[:, lo:hi], scalar1=cp[:, 1, ci:ci + 1], scalar2=cp[:, 3, ci:ci + 1],
                                                              op0=ALU.mult, op1=ALU.add), [bpz, bcp], [buu])
                            V(lambda: nc.vector.scalar_tensor_tensor(out=uu[:, l0:nout], in0=pz[:, lo + l0 - 1:hi - 1], scalar=cp[:, 0, ci:ci + 1],
                                                                     in1=uu[:, l0:nout], op0=ALU.mult, op1=ALU.add), [bpz, bcp, buu], [buu])
                            V(lambda: nc.vector.scalar_tensor_tensor(out=uu[:, 0:nout - r1], in0=pz[:, lo + 1:hi + 1 - r1], scalar=cp[:, 2, ci:ci + 1],
                                                                     in1=uu[:, 0:nout - r1], op0=ALU.mult, op1=ALU.add), [bpz, bcp, buu], [buu])
                        A(lambda: nc.scalar.activation(out=us_[:, 0:nout], in_=ua[:, 0:nout], func=AF.Silu), [bua], [bus])
                        P(lambda: nc.gpsimd.tensor_tensor(out=gT[:, cf, 0:nout], in0=us_[:, 0:nout], in1=uv[:, 0:nout], op=ALU.mult), [bus, buv],
                          pw=[bgT] if cf else (), w=[bgT] if cf == 0 else ())
                    jo = 0
                    while jo * 128 < nout:
                        no = min(128, nout - jo * 128)
                        r0 = oa + jo * 128
                        self.ld(xm[0:no, :], d["X2"][r0:r0 + no, :], r=[db["X2"]], w=[bxm])
                        for hf in range(2):
                            py, bpy = self.ps[6 + hf]
                            for k in range(22):
                                M(lambda k=k: nc.tensor.matmul(py[0:no, :], lhsT=gT[:, k, jo * 128:jo * 128 + no], rhs=Wdn[:, k, hf * 512:(hf + 1) * 512],
                                                               start=(k == 0), stop=(k == 21)), [bgT, bWdng[k // 2]], pw=[bpy] if k else (), w=[bpy] if k == 0 else ())
                            if hf == 0:
                                A(lambda: nc.scalar.activation(out=junk[0:no, :], in_=py[0:no, :], func=AF.Square, accum_out=ss2[0:no, 0:1]), [bpy], [bjunk, bss2])
                            else:
                                A(lambda: nc.scalar.activation(out=junk[0:no, :], in_=py[0:no, :], func=AF.Square, accum_out=ss2[0:no, 1:2]), [bpy], [bjunk], pw=[bss2])
                        V(lambda: nc.vector.tensor_tensor(out=ss[0:no, :], in0=ss2[0:no, 0:1], in1=ss2[0:no, 1:2], op=ALU.add), [bss2], [bss])
                        self.rstd_from_ss(ss[0:no, :], rs[0:no, :], bss, brs, D, no)
                        for hf in range(2):
                            py, bpy = self.ps[6 + hf]
                            cs = slice(hf * 512, (hf + 1) * 512)
                            V(lambda: nc.vector.scalar_tensor_tensor(out=tt[0:no, cs], in0=py[0:no, :], scalar=rs[0:no, 0:1], in1=self.GG[0:no, which, 1, cs],
                                                                     op0=ALU.mult, op1=ALU.mult), [bpy, brs, self.b_GG], pw=[btt] if hf else (), w=[btt] if hf == 0 else ())
                        P(lambda: nc.gpsimd.tensor_tensor(out=xo[0:no, :], in0=tt[0:no, :], in1=xm[0:no, :], op=ALU.add), [btt, bxm], [bxo])
                        if last:
                            self.st(self.out[r0 - 256:r0 - 256 + no, :], xo[0:no, :], r=[bxo], pw=[self.out_buf])
                        else:
                            self.st(d["X1"][r0:r0 + no, :], xo[0:no, :], r=[bxo], pw=[db["X1"]])
                        jo += 1
                    oa = ob
            S.barrier()
            S.recycle(self.phase_bufs)
            self.phase_bufs = []


def host_consts(n_lat):
    rows = n_lat // 64
    row = np.repeat(np.arange(rows, dtype=np.float32), 64)
    col = np.tile(np.arange(64, dtype=np.float32), rows)
    n_freq = 16
    inv_freq = (np.float32(10000.0) ** (-np.arange(n_freq, dtype=np.float32) / n_freq)).astype(np.float32)
    ang = np.stack([row[:, None] * inv_freq, col[:, None] * inv_freq], axis=1)
    rope = np.concatenate([np.cos(ang).reshape(n_lat, 32), np.sin(ang).reshape(n_lat, 32)], axis=1).astype(np.float32)
    jj = np.arange(128) % 64
    ii = np.arange(64)
    gmask = np.stack([(jj[:, None] <= ii[None, :]), (jj[:, None] >= ii[None, :])], axis=1).astype(np.float32)
    scanmask = np.ones((128, 2, 512), np.float32)
    scanmask[:, 0, ::64] = 0.0
    scanmask[:, 1, 63::64] = 0.0
    return dict(ident=np.eye(128, dtype=np.float32), rope=rope, gmask=gmask, scanmask=scanmask)


def host_layout(inputs, L):
    f = lambda a: np.ascontiguousarray(np.asarray(a, dtype=np.float32))
    p = {}
    p["ada_w"] = f(inputs["ada_w"])[:L]
    p["ada_b"] = f(inputs["ada_b"])[:L]
    colz = lambda v: v.reshape(L, -1, 128).transpose(0, 2, 1)
    p["npre"] = f(np.stack([colz(f(inputs["norm_mix_pre"])[:L]), colz(f(inputs["norm_ffn_pre"])[:L])], axis=2))
    p["npost"] = f(np.stack([f(inputs["norm_mix_post"])[:L], f(inputs["norm_ffn_post"])[:L]], axis=1))
    p["w_in"] = f(inputs["w_in"])[:L]
    qn, kn = f(inputs["q_norm"])[:L], f(inputs["k_norm"])[:L]
    p["qkg"] = f(np.concatenate([np.tile(qn, (1, 4)), np.tile(kn, (1, 2))], axis=1))
    gw = f(inputs["gla_gate_w"])[:L]
    gwp = np.zeros((L, 32, 2, 256), np.float32)
    gwp[:, 0:16, 0, :] = gw[:, 0]
    gwp[:, 16:32, 1, :] = gw[:, 1]
    p["gatew"] = gwp
    gb = f(inputs["gla_gate_b"])[:L]
    p["gateb"] = f(gb.reshape(L, 2, 2, 128).transpose(0, 3, 1, 2).reshape(L, 128, 4))
    p["glanorm"] = f(np.tile(f(inputs["gla_out_norm"])[:L], (1, 4)))
    sm = lambda a: a.reshape(L, 2, 8, 2, 64).transpose(0, 1, 3, 4, 2).reshape(L, 2, 128, 8)
    are, aim = f(inputs["s5_a_re"])[:L], f(inputs["s5_a_im"])[:L]
    ldt = np.broadcast_to(f(inputs["s5_log_dt"])[:L][..., None], (L, 2, 16, 64))
    p["s5a"] = f(np.stack([sm(are), sm(aim), sm(f(ldt))], axis=3))
    bre, bim = f(inputs["s5_b_re"])[:L], f(inputs["s5_b_im"])[:L]
    s5b = np.zeros((L, 2, 32, 2, 8, 128), np.float32)
    cre, cim = f(inputs["s5_c_re"])[:L], f(inputs["s5_c_im"])[:L]
    s5c = np.zeros((L, 2, 128, 2, 8, 64), np.float32)
    for m in range(8):
        for gl in range(2):
            g = 2 * m + gl
            for ri, (bb, cc_) in enumerate(((bre, cre), (bim, cim))):
                s5b[:, :, gl * 16:(gl + 1) * 16, ri, m, gl * 64:(gl + 1) * 64] = bb[:, :, g].transpose(0, 1, 3, 2)
                s5c[:, :, gl * 64:(gl + 1) * 64, ri, m, (m % 2) * 32 + gl * 16:(m % 2) * 32 + (gl + 1) * 16] = cc_[:, :, g].transpose(0, 1, 3, 2)
    p["s5b"], p["s5c"] = s5b, s5c
    p["s5d"] = f(colz(f(inputs["s5_d"])[:L]))
    p["s5glub"] = f(colz(f(inputs["s5_glu_b"])[:L]))
    p["s5gluw"] = f(inputs["s5_glu_w"])[:L]
    for k in ("w_br_att", "w_br_gla", "w_br_s5", "w_out", "ffn_up", "ffn_down"):
        p[k] = f(inputs[k])[:L]
    cw = f(inputs["ffn_conv_w"])[:L]
    cb = f(inputs["ffn_conv_b"])[:L]
    c4 = np.concatenate([cw, cb[:, None, :]], axis=1)
    p["convp"] = f(c4.reshape(L, 4, 44, 128).transpose(0, 3, 1, 2))
    return p


_CACHE = {}

N_LAT_FULL = 8192


def swap_dirs(p):
    q = dict(p)
    gw = np.zeros_like(p["gatew"])
    gw[:, 16:32, 0, :] = p["gatew"][:, 16:32, 1, :]
    gw[:, 0:16, 1, :] = p["gatew"][:, 0:16, 0, :]
    q["gatew"] = gw
    gb = p["gateb"].reshape(-1, 128, 2, 2)
    q["gateb"] = np.ascontiguousarray(gb[:, :, ::-1, :]).reshape(-1, 128, 4)
    for k in ("s5a", "s5b", "s5c"):
        q[k] = np.ascontiguousarray(p[k][:, ::-1])
    cp = p["convp"]
    q["convp"] = np.ascontiguousarray(np.stack([cp[:, :, 2], cp[:, :, 1], cp[:, :, 0], cp[:, :, 3]], axis=2))
    return q


def run(inputs, n_lat, depth, n_batch, dbg=()):
    x = np.asarray(inputs["x"], dtype=np.float32)
    ctx = np.asarray(inputs["ctx"], dtype=np.float32)
    c = np.asarray(inputs["c"], dtype=np.float32)
    c_ctx = np.asarray(inputs["c_ctx"], dtype=np.float32)
    nl = n_lat // 2
    key = (nl, depth, tuple(dbg))
    if key not in _CACHE:
        b = Builder(nl, depth, dbg)
        b.build()
        _CACHE[key] = b
    b = _CACHE[key]
    p0 = host_layout(inputs, depth)
    cst = host_consts(n_lat)
    rope_full = cst.pop("rope")
    p0.update(cst)
    p1 = swap_dirs(p0)
    in_maps = []
    for core in range(8):
        bi = (core // 2) % n_batch
        r = core % 2
        m = dict(p0 if r == 0 else p1)
        if r == 0:
            m["xcat"] = np.ascontiguousarray(np.concatenate([ctx[bi], x[bi, :nl]], axis=0))
            m["rope"] = np.ascontiguousarray(rope_full[:nl])
            m["selw"] = np.ascontiguousarray(np.tile(np.array([[0.0, 1.0]], np.float32), (128, 1)))
        else:
            m["xcat"] = np.ascontiguousarray(np.concatenate([ctx[bi][::-1], x[bi, nl:n_lat][::-1]], axis=0))
            m["rope"] = np.ascontiguousarray(rope_full[nl:n_lat][::-1])
            m["selw"] = np.ascontiguousarray(np.tile(np.array([[1.0, 0.0]], np.float32), (128, 1)))
        cc = np.concatenate([c[bi].reshape(8, 128).T, c_ctx.reshape(8, 128).T], axis=1)
        m["cc"] = np.ascontiguousarray(cc.astype(np.float32))
        in_maps.append(m)
    res = run_bass_kernel_spmd(b.nc, in_maps, core_ids=list(range(8)))
    outs = []
    for bi in range(n_batch):
        o0 = np.asarray(res.results[2 * bi]["out"], dtype=np.float32)
        o1 = np.asarray(res.results[2 * bi + 1]["out"], dtype=np.float32)[::-1]
        outs.append(np.concatenate([o0, o1], axis=0))
    return np.stack(outs, axis=0), res.results


def kernel(**inputs):
    n_b = np.asarray(inputs["x"]).shape[0]
    out, _ = run(inputs, N_LAT_FULL, 4, n_b)
    return out
```

```python
import math
from contextlib import ExitStack
import numpy as np
import ml_dtypes
import concourse.bass as bass
import concourse.mybir as mybir
from concourse.ap import AP
from concourse.bass_utils import run_bass_kernel_spmd

F32 = mybir.dt.float32
BF16 = mybir.dt.bfloat16
I32 = mybir.dt.int32
AF = mybir.ActivationFunctionType
ALU = mybir.AluOpType
AX = mybir.AxisListType

D = 1024
KC = 8
DIN = 5408
DFF = 2816
NA = 2336
EPS = 1e-6
PI = math.pi
STOP_AFTER = ''


class Buf:
    __slots__ = ("name", "writers", "pws", "readers", "dsem", "dcount", "key")

    def __init__(self, name):
        self.name = name
        self.writers = {}
        self.pws = {}
        self.readers = {}
        self.dsem = None
        self.dcount = 0
        self.key = None


class DSem:
    __slots__ = ("sem", "key", "count", "sw")

    def __init__(self, sem, key, sw):
        self.sem, self.key, self.count, self.sw = sem, key, 0, sw


class Eng:
    def __init__(self, name, eng, sem, is_pe=False):
        self.name, self.eng, self.sem, self.count, self.seen, self.is_pe = name, eng, sem, 0, {}, is_pe


class Sched:
    def __init__(self, nc):
        self.nc = nc
        self.sems = {}
        self.pe = Eng("pe", nc.tensor, nc.alloc_semaphore("s_pe"), True)
        self.dve = Eng("dve", nc.vector, nc.alloc_semaphore("s_dve"))
        self.act = Eng("act", nc.scalar, nc.alloc_semaphore("s_act"))
        self.pool = Eng("pool", nc.gpsimd, nc.alloc_semaphore("s_pool"))
        self.sp = Eng("sp", nc.sync, nc.alloc_semaphore("s_sp"))
        self.engs = [self.pe, self.dve, self.act, self.pool, self.sp]
        for e in self.engs:
            self.sems[e.name] = e.sem
        self.bufs = {}
        self.dkeys = {}
        self.free_dsems = {True: [], False: []}
        self.ccount = {}
        self.ninst = 0
        self.nwait = 0

    def buf(self, name):
        b = self.bufs.get(name)
        if b is None:
            b = Buf(name)
            self.bufs[name] = b
        return b

    def _deps(self, reads, writes, pwrites):
        deps = {}

        def add(dd):
            for k, v in dd.items():
                if deps.get(k, 0) < v:
                    deps[k] = v
        for b in reads:
            add(b.writers)
            add(b.pws)
        for b in writes:
            add(b.writers)
            add(b.pws)
            add(b.readers)
        for b in pwrites:
            add(b.writers)
            add(b.readers)
        return deps

    def _wait(self, e, deps):
        for k, v in deps.items():
            if k == e.name and e.is_pe:
                continue
            if e.seen.get(k, 0) < v and k in self.dkeys:
                v = self.dkeys[k].count
            if e.seen.get(k, 0) < v:
                e.eng.wait_ge(self.sems[k], v)
                e.seen[k] = v
                self.nwait += 1

    def _post(self, ev, reads, writes, pwrites):
        for b in reads:
            if b.readers.get(ev[0], 0) < ev[1]:
                b.readers[ev[0]] = ev[1]
        for b in writes:
            b.writers = {ev[0]: ev[1]}
            b.pws = {}
            b.readers = {}
        for b in pwrites:
            b.pws[ev[0]] = ev[1]

    def op(self, e, fn, reads=(), writes=(), pwrites=()):
        self._wait(e, self._deps(reads, writes, pwrites))
        inst = fn()
        e.count += 1
        inst.then_inc(e.sem, 1)
        self.ninst += 1
        self._post((e.name, e.count), reads, writes, pwrites)
        return inst

    def dma(self, e, out, in_, reads=(), writes=(), pwrites=(), sbuf=None, **kw):
        if sbuf is None:
            sbuf = (list(writes) + list(pwrites) + list(reads))[0]
        sw = e is self.pool
        if sbuf.dsem is None:
            if self.free_dsems[sw]:
                sbuf.dsem = self.free_dsems[sw].pop()
            else:
                key = "D%d" % len(self.sems)
                sbuf.dsem = DSem(self.nc.alloc_semaphore("d%d" % len(self.sems)), key, sw)
                self.sems[key] = sbuf.dsem.sem
                self.dkeys[key] = sbuf.dsem
        ds = sbuf.dsem
        assert ds.sw == sw, ("mixed SW/HW DGE on one buffer semaphore", sbuf.name)
        self._wait(e, self._deps(reads, writes, pwrites))
        inst = e.eng.dma_start(out=out, in_=in_, **kw)
        ds.count += 16
        inst.then_inc(ds.sem, 16)
        self.ninst += 1
        self._post((ds.key, ds.count), reads, writes, pwrites)
        return inst

    def recycle(self, bufs):
        for b in bufs:
            if b.dsem is not None:
                self.free_dsems[b.dsem.sw].append(b.dsem)
                b.dsem = None

    def collective(self, kind, op, groups, src, dst, reads, writes, site):
        key = "C_" + site
        if key not in self.sems:
            self.sems[key] = self.nc.alloc_semaphore("c_" + site)
            self.ccount[key] = 0
        e = self.pool
        self._wait(e, self._deps(reads, writes, ()))
        inst = self.nc.gpsimd.collective_compute(kind, op, replica_groups=groups, ins=[src], outs=[dst])
        self.ccount[key] += 1
        inst.then_inc(self.sems[key], 1)
        self.ninst += 1
        self._post((key, self.ccount[key]), reads, writes, ())

    def all_events(self):
        deps = {e.name: e.count for e in self.engs if e.count > 0}
        for k, ds in self.dkeys.items():
            if ds.count > 0:
                deps[k] = ds.count
        for k, v in self.ccount.items():
            if v > 0:
                deps[k] = v
        return deps

    def barrier(self):
        deps = self.all_events()
        for e in self.engs:
            d = dict(deps)
            d.pop(e.name, None)
            self._wait(e, d)

    def finish(self):
        self._wait(self.sp, self.all_events())


def rev(ap2d):
    a = ap2d.ap
    assert len(a) == 2, a
    n = a[1][1]
    st = a[1][0]
    return AP(ap2d.tensor, ap2d.offset + (n - 1) * st, [list(a[0]), [-st, n]])


class Builder:
    def __init__(self, n_lat, depth, dbg=()):
        assert n_lat % 512 == 0
        self.n_lat, self.depth, self.dbgnames = n_lat, depth, tuple(dbg)
        self.T = 256 + n_lat
        self.NT = self.T // 128
        self.NL = n_lat
        self.NTK = (256 + 2 * n_lat) // 128
        self.groups = [[0, 1], [2, 3], [4, 5], [6, 7]]
        self.NCH = self.T // 64
        self.nc = bass.Bass("TRN2", target_bir_lowering=False)
        self.S = Sched(self.nc)
        self.blocks = [(0, 256, True)] + [(256 + 512 * j, 512, False) for j in range(n_lat // 512)]
        self.uid = 0
        self.phase_bufs = []

    def din(self, name, shape, dt=F32):
        return self.nc.dram_tensor(name, list(shape), dt, kind="ExternalInput").ap()

    def dscr(self, name, shape, dt):
        return self.nc.dram_tensor(name, list(shape), dt, kind="Internal").ap()

    def sb(self, es, name, shape, dt=F32):
        self.uid += 1
        t = es.enter_context(self.nc.sbuf_tensor("%s_%d" % (name, self.uid), list(shape), dt))
        b = self.S.buf(name)
        if es is not getattr(self, "ges", None):
            self.phase_bufs.append(b)
        return t, b

    def V(self, fn, r=(), w=(), pw=()):
        return self.S.op(self.S.dve, fn, r, w, pw)

    def A(self, fn, r=(), w=(), pw=()):
        return self.S.op(self.S.act, fn, r, w, pw)

    def P(self, fn, r=(), w=(), pw=()):
        return self.S.op(self.S.pool, fn, r, w, pw)

    def M(self, fn, r=(), w=(), pw=()):
        return self.S.op(self.S.pe, fn, r, w, pw)

    def ld(self, out, in_, r=(), w=(), pw=(), sbuf=None, **kw):
        return self.S.dma(self.S.sp, out, in_, r, w, pw, sbuf, **kw)

    def ldc(self, out, in_, r=(), w=(), pw=(), sbuf=None):
        return self.S.dma(self.S.pool, out, in_, r, w, pw, sbuf, max_dma_last_dim=4096)

    def st(self, out, in_, r=(), w=(), pw=(), sbuf=None, **kw):
        return self.S.dma(self.S.sp, out, in_, r, w, pw, sbuf, **kw)

    def build(self):
        nc, S = self.nc, self.S
        T, L, n_lat = self.T, self.depth, self.n_lat
        i = self.inp = {}
        i["xcat"] = self.din("xcat", [T, D])
        i["cc"] = self.din("cc", [128, 16])
        i["ada_w"] = self.din("ada_w", [L, D, 6 * D])
        i["ada_b"] = self.din("ada_b", [L, 6 * D])
        i["npre"] = self.din("npre", [L, 128, 2, 8])
        i["npost"] = self.din("npost", [L, 2, D])
        i["w_in"] = self.din("w_in", [L, D, DIN])
        i["qkg"] = self.din("qkg", [L, 384])
        i["gatew"] = self.din("gatew", [L, 32, 2, 256])
        i["gateb"] = self.din("gateb", [L, 128, 4])
        i["glanorm"] = self.din("glanorm", [L, 512])
        i["s5a"] = self.din("s5a", [L, 2, 128, 3, 8])
        i["s5b"] = self.din("s5b", [L, 2, 32, 2, 8, 128])
        i["s5c"] = self.din("s5c", [L, 2, 128, 2, 8, 64])
        i["s5d"] = self.din("s5d", [L, 128, 2])
        i["s5glub"] = self.din("s5glub", [L, 128, 2])
        i["s5gluw"] = self.din("s5gluw", [L, 256, 256])
        i["w_br_att"] = self.din("w_br_att", [L, 256, D])
        i["w_br_gla"] = self.din("w_br_gla", [L, 512, D])
        i["w_br_s5"] = self.din("w_br_s5", [L, 256, D])
        i["w_out"] = self.din("w_out", [L, D, D])
        i["ffn_up"] = self.din("ffn_up", [L, D, 2 * DFF])
        i["ffn_down"] = self.din("ffn_down", [L, DFF, D])
        i["convp"] = self.din("convp", [L, 128, 4, 44])
        i["ident"] = self.din("ident", [128, 128])
        i["rope"] = self.din("rope", [n_lat, 64])
        i["gmask"] = self.din("gmask", [128, 2, 64])
        i["scanmask"] = self.din("scanmask", [128, 2, 512])
        i["selw"] = self.din("selw", [128, 2])
        self.out = nc.dram_tensor("out", [n_lat, D], F32, kind="ExternalOutput").ap()
        self.dbg = {}

        d = self.scr = {}
        d["X1"] = self.dscr("X1", [T, D], F32)
        d["X2"] = self.dscr("X2", [T, D], F32)
        d["HT"] = self.dscr("HT", [D, T], BF16)
        d["QT"] = self.dscr("QT", [256, T], BF16)
        d["KT"] = self.dscr("KT", [128, 256], BF16)
        d["VV"] = self.dscr("VV", [256, 128], BF16)
        d["KVS"] = self.dscr("KVS", [256, n_lat], BF16)
        d["KVG"] = self.dscr("KVG", [512, n_lat], BF16)
        d["SFS"] = self.dscr("SFS", [256, 128], F32)
        d["SFG"] = self.dscr("SFG", [512, 128], F32)
        d["CFS"] = self.dscr("CFS", [128, 16], F32)
        d["CFG"] = self.dscr("CFG", [256, 16], F32)
        d["XHS"] = self.dscr("XHS", [1, D], F32)
        d["XHG"] = self.dscr("XHG", [2, D], F32)
        d["XHX"] = self.dscr("XHX", [1, D], F32)
        d["GT"] = self.dscr("GT", [4, 256, T], BF16)
        d["KE"] = self.dscr("KE", [2, T, 256], BF16)
        d["GV"] = self.dscr("GV", [T, 512], BF16)
        d["GR"] = self.dscr("GR", [T, 512], BF16)
        d["SU"] = self.dscr("SU", [256, T], BF16)
        d["OATT"] = self.dscr("OATT", [256, T], BF16)
        d["OGLA"] = self.dscr("OGLA", [512, T], BF16)
        d["OS5"] = self.dscr("OS5", [256, T], BF16)
        d["SST"] = self.dscr("SST", [2, self.NCH, 2, 128, 128], BF16)
        d["YF"] = self.dscr("YF", [256, T], F32)
        self.dbufs = {k: S.buf("dram_" + k) for k in d}
        self.xin_buf = S.buf("dram_xcat")
        self.out_buf = S.buf("dram_out")

        with ExitStack() as ges:
            self.ges = ges
            self.ps = []
            for k in range(8):
                t = ges.enter_context(nc.psum_tensor("ps%d" % k, [128, 512], F32))
                self.ps.append((t, S.buf("ps%d" % k)))
            self.ident, self.b_ident = self.sb(ges, "ident", [128, 128])
            self.identb, self.b_identb = self.sb(ges, "identb", [128, 128], BF16)
            self.ones, self.b_ones = self.sb(ges, "ones", [128, 128])
            self.gmask, self.b_gmask = self.sb(ges, "gmask", [128, 2, 64])
            self.scanmask, self.b_scanmask = self.sb(ges, "scanmask", [128, 2, 512])
            self.GG, self.b_GG = self.sb(ges, "GG", [128, 2, 2, D])
            self.modc, self.b_modc = self.sb(ges, "modc", [128, 2, 6, 8])
            self.prec, self.b_prec = self.sb(ges, "prec", [128, 2, 2, 2, 8])
            self.dec, self.b_dec = self.sb(ges, "dec", [128, 2, 2, self.NCH])
            self.epsc, self.b_epsc = self.sb(ges, "epsc", [128, 1])

            self.ld(self.ident[:], i["ident"], w=[self.b_ident])
            self.V(lambda: nc.vector.tensor_copy(out=self.identb[:], in_=self.ident[:]), [self.b_ident], [self.b_identb])
            self.V(lambda: nc.vector.memset(self.ones[:], 1.0), w=[self.b_ones])
            self.V(lambda: nc.vector.memset(self.epsc[:], EPS), w=[self.b_epsc])
            self.ld(self.gmask[:], i["gmask"], w=[self.b_gmask])
            self.ld(self.scanmask[:], i["scanmask"], w=[self.b_scanmask])
            self.selw, self.b_selw = self.sb(ges, "selw", [128, 2])
            self.ld(self.selw[:], i["selw"], w=[self.b_selw])
            S.barrier()
            S.recycle(self.phase_bufs)
            self.phase_bufs = []

            for l in range(L):
                last = (l == L - 1)
                xin = i["xcat"] if l == 0 else d["X1"]
                xin_b = self.xin_buf if l == 0 else self.dbufs["X1"]
                phases = [("mod", lambda: self.phase_mod(l)), ("a", lambda: self.phase_a(l, xin, xin_b)), ("bc3", lambda: self.phase_bc3(l)),
                          ("c1", lambda: self.phase_c1(l)), ("c2", lambda: self.phase_c2(l)),
                          ("d", lambda: self.phase_d(l, xin, xin_b, last)), ("e", lambda: self.phase_e(l, last))]
                stopped = False
                for pname, pf in phases:
                    pf()
                    if STOP_AFTER == pname:
                        stopped = True
                        break
                if stopped:
                    break
            S.finish()
        return nc

    def dump(self, name, ap_sb, bufs, shape, dt=F32):
        if name not in self.dbgnames:
            return
        t = self.nc.dram_tensor("dbg_" + name, list(shape), dt, kind="ExternalOutput").ap()
        self.dbg[name] = t
        self.st(t, ap_sb, r=bufs, w=[self.S.buf("dbgd_" + name)], sbuf=bufs[0])

    def rstd_from_ss(self, ss_ap, rs_ap, b_ss, b_rs, n, rows=128):
        nc = self.nc
        self.A(lambda: nc.scalar.activation(out=rs_ap, in_=ss_ap, func=AF.Sqrt, bias=self.epsc[0:rows, 0:1], scale=1.0 / n),
               [b_ss, self.b_epsc], [b_rs])
        self.V(lambda: nc.vector.reciprocal(out=rs_ap, in_=rs_ap), [b_rs], [b_rs])

    def phase_mod(self, l):
        nc, S, i = self.nc, self.S, self.inp
        with ExitStack() as es:
            npost, b_npost = self.sb(es, "npost", [128, 2, D])
            cc, bcc = self.sb(es, "cc", [128, 16])
            sc, bsc = self.sb(es, "sc", [128, 16])
            self.screp, self.b_screp = self.sb(es, "screp", [128, 16, 128])
            self.ld(cc[:], i["cc"], w=[bcc])
            self.A(lambda: nc.scalar.activation(out=sc[:], in_=cc[:], func=AF.Silu), [bcc], [bsc])
            self.V(lambda: nc.vector.tensor_copy(out=self.screp[:], in_=sc[:].unsqueeze(2).to_broadcast([128, 16, 128])),
                   [bsc], [self.b_screp])
            npre, b_npre = self.sb(es, "npre", [128, 2, 8])
            adab, b_adab = self.sb(es, "adab", [128, 512])
            wblk = [self.sb(es, "adaw%d" % k, [128, 8, 512]) for k in range(2)]
            mb = [self.sb(es, "mb%d" % k, [128, 512]) for k in range(2)]
            src = i["npost"][l]
            self.ld(npost[:], AP(src.tensor, src.offset, [[0, 128], [D, 2], [1, D]]), w=[b_npost])
            self.ld(npre[:], i["npre"][l], w=[b_npre])
            for cb in range(12):
                kind, half = cb // 2, cb % 2
                wt, bw = wblk[cb % 2]
                self.ld(wt[:], i["ada_w"][l, :, cb * 512:(cb + 1) * 512].rearrange("(k p) n -> p k n", p=128), w=[bw])
                src = i["ada_b"][l, cb * 512:(cb + 1) * 512]
                self.ld(adab[:], AP(src.tensor, src.offset, [[0, 128], [1, 512]]), w=[b_adab])
                for which in range(2):
                    pt, bp = self.ps[which]
                    for k in range(8):
                        self.M(lambda k=k: nc.tensor.matmul(pt[:], lhsT=self.screp[:, which * 8 + k, :], rhs=wt[:, k, :],
                                                           start=(k == 0), stop=(k == 7)),
                               [self.b_screp, bw], pw=[bp] if k else (), w=[bp] if k == 0 else ())
                    mt, bm = mb[which]
                    self.V(lambda: nc.vector.tensor_tensor(out=mt[:], in0=pt[:], in1=adab[:], op=ALU.add), [bp, b_adab], [bm])
                    if kind in (2, 5):
                        mf = 0 if kind == 2 else 1
                        self.V(lambda: nc.vector.tensor_tensor(out=self.GG[:, which, mf, half * 512:(half + 1) * 512], in0=mt[:],
                                                               in1=npost[:, mf, half * 512:(half + 1) * 512], op=ALU.mult),
                               [bm, b_npost], pw=[self.b_GG])
                    else:
                        p2, bp2 = self.ps[2 + which]
                        for j in range(4):
                            self.M(lambda j=j: nc.tensor.transpose(out=p2[:, j * 32:(j + 1) * 32], in_=mt[0:32, j * 128:(j + 1) * 128],
                                                                  identity=self.ident[0:32, 0:32]),
                                   [bm, self.b_ident], pw=[bp2] if j else (), w=[bp2] if j == 0 else ())
                        srcv = p2[:, 0:128].rearrange("p (j c) -> p j c", c=32)[:, :, 0]
                        self.V(lambda: nc.vector.tensor_copy(out=self.modc[:, which, kind, half * 4:(half + 1) * 4], in_=srcv),
                               [bp2], pw=[self.b_modc])
            for which in range(2):
                for mf in range(2):
                    ksh, ksc = (0, 1) if mf == 0 else (3, 4)
                    self.V(lambda: nc.vector.scalar_tensor_tensor(out=self.prec[:, which, mf, 0, :], in0=self.modc[:, which, ksc, :],
                                                                  scalar=1.0, in1=npre[:, mf, :], op0=ALU.add, op1=ALU.mult),
                           [self.b_modc, b_npre], pw=[self.b_prec])
                    self.V(lambda: nc.vector.tensor_copy(out=self.prec[:, which, mf, 1, :], in_=self.modc[:, which, ksh, :]),
                           [self.b_modc], pw=[self.b_prec])
            self.dump("GG", self.GG[:], [self.b_GG], [128, 2, 2, D])
            self.dump("prec", self.prec[:], [self.b_prec], [128, 2, 2, 2, 8])
            S.barrier()
            S.recycle(self.phase_bufs)
            self.phase_bufs = []

    def norm_tile(self, xt, bx, rows, which, mf, hT, bh, col0, tmp, pst):
        nc = self.nc
        (xn, bxn), (ss, bss), (rs, brs) = tmp
        self.A(lambda: nc.scalar.activation(out=xn[0:rows, :], in_=xt[0:rows, :], func=AF.Square, accum_out=ss[0:rows, 0:1]),
               [bx], [bxn, bss])
        self.rstd_from_ss(ss[0:rows, 0:1], rs[0:rows, 0:1], bss, brs, D, rows)
        self.A(lambda: nc.scalar.activation(out=xn[0:rows, :], in_=xt[0:rows, :], func=AF.Identity, scale=rs[0:rows, 0:1]),
               [bx, brs], [bxn])
        for hf in range(2):
            pt, bp = pst[hf]
            for kk in range(4):
                k = hf * 4 + kk
                self.M(lambda: nc.tensor.transpose(out=pt[:, kk * 128:kk * 128 + rows], in_=xn[0:rows, k * 128:(k + 1) * 128],
                                                   identity=self.ident[0:rows, 0:rows]),
                       [bxn, self.b_ident], pw=[bp] if kk else (), w=[bp] if kk == 0 else ())
            for kk in range(4):
                k = hf * 4 + kk
                self.A(lambda: nc.scalar.activation(out=hT[:, k, col0:col0 + rows], in_=pt[:, kk * 128:kk * 128 + rows],
                                                    func=AF.Identity, scale=self.prec[:, which, mf, 0, k:k + 1],
                                                    bias=self.prec[:, which, mf, 1, k:k + 1]),
                       [bp, self.b_prec], pw=[bh])

    def phase_a(self, l, xin, xin_b):
        nc, S, i, d = self.nc, self.S, self.inp, self.scr
        db = self.dbufs
        with ExitStack() as es:
            WA, bWA = self.sb(es, "WA", [128, 8, NA], BF16)
            for k in range(8):
                self.ldc(WA[:, k, :], i["w_in"][l, k * 128:(k + 1) * 128, 0:NA], pw=[bWA])
            gw, bgw = self.sb(es, "gw", [32, 2, 256], BF16)
            self.ldc(gw[:], i["gatew"][l], w=[bgw])
            gb, bgb = self.sb(es, "gb", [128, 4])
            self.ld(gb[:], i["gateb"][l], w=[bgb])
            self.V(lambda: nc.vector.tensor_scalar(out=gb[:], in0=gb[:], scalar1=-1.0, scalar2=None, op0=ALU.mult), [bgb], [bgb])
            qkg, bqkg = self.sb(es, "qkg", [128, 384])
            src = i["qkg"][l]
            self.ld(qkg[:], AP(src.tensor, src.offset, [[0, 128], [1, 384]]), w=[bqkg])
            xts = [self.sb(es, "xt%d" % k, [128, D]) for k in range(2)]
            tmp = (self.sb(es, "xn", [128, D]), self.sb(es, "ss", [128, 1]), self.sb(es, "rs", [128, 1]))
            hTs = [self.sb(es, "hT%d" % k, [128, 8, 512], BF16) for k in range(2)]
            qk0 = [self.sb(es, "qk0_%d" % k, [128, 384]) for k in range(4)]
            sqs = [self.sb(es, "sq_%d" % k, [128, 384]) for k in range(4)]
            ss6s = [self.sb(es, "ss6_%d" % k, [128, 6]) for k in range(4)]
            rs6s = [self.sb(es, "rs6_%d" % k, [128, 6]) for k in range(4)]
            qk1s = [self.sb(es, "qk1_%d" % k, [128, 384]) for k in range(4)]
            qk2s = [self.sb(es, "qk2_%d" % k, [128, 384], BF16) for k in range(4)]
            ras = [self.sb(es, "ra_%d" % k, [128, 6, 2, 16]) for k in range(4)]
            rbs = [self.sb(es, "rb_%d" % k, [128, 6, 2, 16]) for k in range(4)]
            ropes = [self.sb(es, "rope_%d" % k, [128, 64]) for k in range(4)]
            vbfs = [self.sb(es, "vbf_%d" % k, [128, 128], BF16) for k in range(4)]
            qkTs = [self.sb(es, "qkT_%d" % k, [128, 3, 128], BF16) for k in range(4)]
            tmA = [self.sb(es, "tmA%d" % k, [128, 512], BF16) for k in range(2)]
            lowT, blowT = self.sb(es, "lowT", [32, 512], BF16)
            fm = [self.sb(es, "fm%d" % k, [128, 512], BF16) for k in range(2)]
            e1s = [self.sb(es, "e1_%d" % q, [128, 512]) for q in range(2)]
            l1s = [self.sb(es, "l1_%d" % q, [128, 512]) for q in range(2)]
            bps = [self.sb(es, "bpl_%d" % q, [128, 512]) for q in range(2)]
            dds = [self.sb(es, "dd_%d" % q, [128, 512]) for q in range(2)]
            Es = [[self.sb(es, "E%d_%d" % (k, q), [128, 512]) for k in range(3)] for q in range(2)]
            qos = [[self.sb(es, "qo%d_%d" % (k, q), [128, 512], BF16) for k in range(3)] for q in range(2)]
            keTs = [self.sb(es, "keT_%d" % q, [128, 4, 128], BF16) for q in range(2)]
            pst = (self.ps[0], self.ps[1])
            tm_i = 0
            fm_i = 0
            for bi, (t0, n, isctx) in enumerate(self.blocks):
                which = 1 if isctx else 0
                nt = n // 128
                hT, bh = hTs[bi % 2]
                for j in range(nt):
                    xt, bx = xts[j % 2]
                    r0 = t0 + j * 128
                    self.ld(xt[:], xin[r0:r0 + 128, :], r=[xin_b], w=[bx])
                    self.norm_tile(xt, bx, 128, which, 0, hT, bh, j * 128, tmp, pst)
                self.st(d["HT"][:, t0:t0 + n].rearrange("(k p) t -> p k t", p=128), hT[:, :, 0:n], r=[bh], pw=[db["HT"]])
                if l == 0 and bi == 1:
                    self.dump("hT", hT[:], [bh], [128, 8, 512], BF16)
                TJ = list(range(nt))
                for j in TJ:
                    r0 = t0 + j * 128
                    pt, bp = self.ps[2]
                    for k in range(8):
                        self.M(lambda k=k: nc.tensor.matmul(pt[:], lhsT=hT[:, k, j * 128:(j + 1) * 128], rhs=WA[:, k, 0:512],
                                                           start=(k == 0), stop=(k == 7)),
                               [bh, bWA], pw=[bp] if k else (), w=[bp] if k == 0 else ())
                    self.A(lambda: nc.scalar.copy(out=qk0[j][0][:], in_=pt[:, 0:384]), [bp], [qk0[j][1]])
                    self.A(lambda: nc.scalar.copy(out=vbfs[j][0][:], in_=pt[:, 384:512]), [bp], [vbfs[j][1]])
                    if isctx:
                        self.st(d["VV"][r0:r0 + 128, :], vbfs[j][0][:], r=[vbfs[j][1]], pw=[db["VV"]])
                    else:
                        kvs = d["KVS"]
                        vdst = AP(kvs.tensor, kvs.offset + 128 * self.NL + (r0 - 256) * 128, [[128, 128], [1, 128]])
                        self.st(vdst, vbfs[j][0][:], r=[vbfs[j][1]], pw=[db["KVS"]])
                        self.ld(ropes[j][0][:], i["rope"][r0 - 256:r0 - 256 + 128, :], w=[ropes[j][1]])
                    for gi, (c0, dst) in enumerate(((1024, "GV"), (1536, "GR"))):
                        pt, bp = self.ps[3 + gi]
                        for k in range(8):
                            self.M(lambda k=k: nc.tensor.matmul(pt[:], lhsT=hT[:, k, j * 128:(j + 1) * 128], rhs=WA[:, k, c0:c0 + 512],
                                                               start=(k == 0), stop=(k == 7)),
                                   [bh, bWA], pw=[bp] if k else (), w=[bp] if k == 0 else ())
                        tt, bt = tmA[tm_i % 2]
                        tm_i += 1
                        if gi == 0:
                            self.A(lambda: nc.scalar.copy(out=tt[:], in_=pt[:]), [bp], [bt])
                        else:
                            self.A(lambda: nc.scalar.activation(out=tt[:], in_=pt[:], func=AF.Silu), [bp], [bt])
                        self.st(d[dst][r0:r0 + 128, :], tt[:], r=[bt], pw=[db[dst]])
                for j in TJ:
                    self.A(lambda: nc.scalar.activation(out=sqs[j][0][:], in_=qk0[j][0][:], func=AF.Square), [qk0[j][1]], [sqs[j][1]])
                for j in TJ:
                    self.V(lambda: nc.vector.tensor_reduce(out=ss6s[j][0][:], in_=sqs[j][0][:].rearrange("p (h e) -> p h e", e=64), axis=AX.X, op=ALU.add),
                           [sqs[j][1]], [ss6s[j][1]])
                for j in TJ:
                    self.A(lambda: nc.scalar.activation(out=rs6s[j][0][:], in_=ss6s[j][0][:], func=AF.Sqrt, bias=self.epsc[:, 0:1], scale=1.0 / 64),
                           [ss6s[j][1], self.b_epsc], [rs6s[j][1]])
                for j in TJ:
                    self.V(lambda: nc.vector.reciprocal(out=rs6s[j][0][:], in_=rs6s[j][0][:]), [rs6s[j][1]], [rs6s[j][1]])
                for j in TJ:
                    self.V(lambda: nc.vector.tensor_tensor(out=qk1s[j][0][:].rearrange("p (h e) -> p h e", e=64),
                                                           in0=qk0[j][0][:].rearrange("p (h e) -> p h e", e=64),
                                                           in1=rs6s[j][0][:].unsqueeze(2).to_broadcast([128, 6, 64]), op=ALU.mult),
                           [qk0[j][1], rs6s[j][1]], [qk1s[j][1]])
                if isctx:
                    for j in TJ:
                        self.V(lambda: nc.vector.tensor_tensor(out=qk2s[j][0][:], in0=qk1s[j][0][:], in1=qkg[:], op=ALU.mult),
                               [qk1s[j][1], bqkg], [qk2s[j][1]])
                else:
                    for j in TJ:
                        self.V(lambda: nc.vector.tensor_tensor(out=qk1s[j][0][:], in0=qk1s[j][0][:], in1=qkg[:], op=ALU.mult),
                               [qk1s[j][1], bqkg], [qk1s[j][1]])

                    def rviews(j):
                        v5 = qk1s[j][0][:].rearrange("p (h a s f) -> p h a s f", h=6, a=2, s=2)
                        o5 = qk2s[j][0][:].rearrange("p (h a s f) -> p h a s f", h=6, a=2, s=2)
                        rp = ropes[j][0]
                        cs = rp[:, 0:32].rearrange("p (a f) -> p a f", a=2).unsqueeze(1).to_broadcast([128, 6, 2, 16])
                        sn = rp[:, 32:64].rearrange("p (a f) -> p a f", a=2).unsqueeze(1).to_broadcast([128, 6, 2, 16])
                        return v5[:, :, :, 0, :], v5[:, :, :, 1, :], o5, cs, sn
                    for half in range(2):
                        for j in TJ:
                            x1, x2, o5, cs, sn = rviews(j)
                            xa, xb = (x1, x2) if half == 0 else (x2, x1)
                            self.V(lambda: nc.vector.tensor_tensor(out=ras[j][0][:], in0=xa, in1=cs, op=ALU.mult), [qk1s[j][1], ropes[j][1]], [ras[j][1]])
                            self.V(lambda: nc.vector.tensor_tensor(out=rbs[j][0][:], in0=xb, in1=sn, op=ALU.mult), [qk1s[j][1], ropes[j][1]], [rbs[j][1]])
                        for j in TJ:
                            x1, x2, o5, cs, sn = rviews(j)
                            self.V(lambda: nc.vector.tensor_tensor(out=o5[:, :, :, half, :], in0=ras[j][0][:], in1=rbs[j][0][:],
                                                                   op=ALU.subtract if half == 0 else ALU.add),
                                   [ras[j][1], rbs[j][1]], pw=[qk2s[j][1]])
                for j in TJ:
                    r0 = t0 + j * 128
                    p7, bp7 = self.ps[7]
                    p7b = p7[:].bitcast(BF16)
                    for c in range(3):
                        self.M(lambda c=c: nc.tensor.transpose(out=p7b[:, c * 128:(c + 1) * 128], in_=qk2s[j][0][:, c * 128:(c + 1) * 128],
                                                              identity=self.identb[:]),
                               [qk2s[j][1], self.b_identb], pw=[bp7] if c else (), w=[bp7] if c == 0 else ())
                    qkT, bqkT = qkTs[j]
                    self.A(lambda: nc.scalar.copy(out=qkT[:].rearrange("p c t -> p (c t)"), in_=p7b[:, 0:384]), [bp7], [bqkT])
                    self.st(d["QT"][:, r0:r0 + 128].rearrange("(c p) t -> p c t", p=128), qkT[:, 0:2, :], r=[bqkT], pw=[db["QT"]])
                    if isctx:
                        self.st(d["KT"][:, r0:r0 + 128], qkT[:, 2, :], r=[bqkT], pw=[db["KT"]])
                    else:
                        self.st(d["KVS"][0:128, r0 - 256:r0 - 256 + 128], qkT[:, 2, :], r=[bqkT], pw=[db["KVS"]])
                pt, bp = self.ps[2]
                for k in range(8):
                    self.M(lambda k=k: nc.tensor.matmul(pt[0:32, 0:n], lhsT=WA[:, k, 2048:2080], rhs=hT[:, k, 0:n], start=(k == 0), stop=(k == 7)),
                           [bh, bWA], pw=[bp] if k else (), w=[bp] if k == 0 else ())
                self.A(lambda: nc.scalar.copy(out=lowT[:, 0:n], in_=pt[0:32, 0:n]), [bp], [blowT])
                for c in range(2):
                    pt, bp = self.ps[3 + c]
                    for k in range(8):
                        self.M(lambda k=k: nc.tensor.matmul(pt[:, 0:n], lhsT=WA[:, k, 2080 + c * 128:2080 + (c + 1) * 128], rhs=hT[:, k, 0:n],
                                                           start=(k == 0), stop=(k == 7)),
                               [bh, bWA], pw=[bp] if k else (), w=[bp] if k == 0 else ())
                    ft, bf = fm[fm_i % 2]
                    fm_i += 1
                    self.A(lambda: nc.scalar.copy(out=ft[:, 0:n], in_=pt[:, 0:n]), [bp], [bf])
                    self.st(d["SU"][c * 128:(c + 1) * 128, t0:t0 + n], ft[:, 0:n], r=[bf], pw=[db["SU"]])
                nchb = n // 64
                ch0 = t0 // 64
                for c in range(2):
                    pq, bpq = self.ps[3]
                    pk, bpk = self.ps[4]
                    for (pp, bpp, c0) in ((pq, bpq, 512), (pk, bpk, 768)):
                        for k in range(8):
                            self.M(lambda k=k: nc.tensor.matmul(pp[:, 0:n], lhsT=WA[:, k, c0 + c * 128:c0 + (c + 1) * 128], rhs=hT[:, k, 0:n],
                                                               start=(k == 0), stop=(k == 7)),
                                   [bh, bWA], pw=[bpp] if k else (), w=[bpp] if k == 0 else ())
                    def chain(dr):
                        e1, be1 = e1s[dr]
                        l1, bl1 = l1s[dr]
                        bp_, bbp = bps[dr]
                        dd, bdd = dds[dr]
                        E = Es[dr]
                        qo = qos[dr]
                        keT, bkeT = keTs[dr]
                        pz, bpz = self.ps[5 + dr]
                        self.M(lambda: nc.tensor.matmul(pz[:, 0:n], lhsT=gw[:, dr, c * 128:(c + 1) * 128], rhs=lowT[:, 0:n], start=True, stop=True),
                               [bgw, blowT], [bpz])
                        self.A(lambda: nc.scalar.activation(out=e1[:, 0:n], in_=pz[:, 0:n], func=AF.Exp, scale=-1.0,
                                                            bias=gb[:, dr * 2 + c:dr * 2 + c + 1]), [bpz, bgb], [be1])
                        yield
                        self.A(lambda: nc.scalar.activation(out=l1[:, 0:n], in_=e1[:, 0:n], func=AF.Ln, bias=self.ones[:, 0:1], scale=1.0),
                               [be1, self.b_ones], [bl1])
                        yield
                        if dr == 0:
                            self.V(lambda: nc.vector.tensor_tensor_scan(out=bp_[:, 0:n], data0=self.scanmask[:, 0, 0:n], data1=l1[:, 0:n],
                                                                        initial=0.0, op0=ALU.mult, op1=ALU.add),
                                   [self.b_scanmask, bl1], [bbp])
                            endcol = 63
                        else:
                            self.V(lambda: nc.vector.tensor_tensor_scan(out=rev(bp_[:, 0:n]), data0=rev(self.scanmask[:, 1, 0:n]),
                                                                        data1=rev(l1[:, 0:n]), initial=0.0, op0=ALU.mult, op1=ALU.add),
                                   [self.b_scanmask, bl1], [bbp])
                            endcol = 0
                        yield
                        b3 = bp_[:, 0:n].rearrange("p (c i) -> p c i", i=64)
                        bend = b3[:, :, endcol:endcol + 1]
                        self.V(lambda: nc.vector.tensor_tensor(out=dd[:, 0:n].rearrange("p (c i) -> p c i", i=64),
                                                               in0=bend.to_broadcast([128, nchb, 64]), in1=b3, op=ALU.subtract),
                               [bbp], [bdd])
                        yield
                        self.A(lambda: nc.scalar.activation(out=E[0][0][:, 0:n], in_=bp_[:, 0:n], func=AF.Exp, scale=-1.0 / 16), [bbp], [E[0][1]])
                        self.A(lambda: nc.scalar.activation(out=E[1][0][:, 0:n], in_=bp_[:, 0:n], func=AF.Exp, scale=1.0 / 16), [bbp], [E[1][1]])
                        self.A(lambda: nc.scalar.activation(out=E[2][0][:, 0:n], in_=dd[:, 0:n], func=AF.Exp, scale=-1.0 / 16), [bdd], [E[2][1]])
                        self.A(lambda: nc.scalar.activation(out=self.dec[:, dr, c, ch0:ch0 + nchb], in_=bend.rearrange("p c o -> p (c o)"),
                                                            func=AF.Exp, scale=-1.0 / 16), [bbp], pw=[self.b_dec])
                        yield
                        self.V(lambda: nc.vector.scalar_tensor_tensor(out=qo[0][0][:, 0:n], in0=pq[:, 0:n], scalar=0.125, in1=E[0][0][:, 0:n],
                                                                      op0=ALU.mult, op1=ALU.mult), [bpq, E[0][1]], [qo[0][1]])
                        self.V(lambda: nc.vector.tensor_tensor(out=qo[1][0][:, 0:n], in0=pk[:, 0:n], in1=E[1][0][:, 0:n], op=ALU.mult),
                               [bpk, E[1][1]], [qo[1][1]])
                        self.V(lambda: nc.vector.tensor_tensor(out=qo[2][0][:, 0:n], in0=pk[:, 0:n], in1=E[2][0][:, 0:n], op=ALU.mult),
                               [bpk, E[2][1]], [qo[2][1]])
                        yield
                        self.st(d["GT"][dr * 2 + 0, c * 128:(c + 1) * 128, t0:t0 + n], qo[0][0][:, 0:n], r=[qo[0][1]], pw=[db["GT"]])
                        self.st(d["GT"][dr * 2 + 1, c * 128:(c + 1) * 128, t0:t0 + n], qo[1][0][:, 0:n], r=[qo[1][1]], pw=[db["GT"]])
                        p7, bp7 = self.ps[7]
                        p7b = p7[:].bitcast(BF16)
                        for j in range(nt):
                            self.M(lambda j=j: nc.tensor.transpose(out=p7b[:, j * 128:(j + 1) * 128], in_=qo[2][0][:, j * 128:(j + 1) * 128],
                                                                  identity=self.identb[:]),
                                   [qo[2][1], self.b_identb], pw=[bp7] if j else (), w=[bp7] if j == 0 else ())
                        self.A(lambda: nc.scalar.copy(out=keT[:, 0:nt, :].rearrange("p j f -> p (j f)"), in_=p7b[:, 0:nt * 128]), [bp7], [bkeT])
                        self.st(d["KE"][dr, t0:t0 + n, c * 128:(c + 1) * 128].rearrange("(j p) f -> p j f", p=128), keT[:, 0:nt, :],
                                r=[bkeT], pw=[db["KE"]])
                    chains = [chain(0), chain(1)]
                    while chains:
                        for g in list(chains):
                            try:
                                next(g)
                            except StopIteration:
                                chains.remove(g)
            S.barrier()
            S.recycle(self.phase_bufs)
            self.phase_bufs = []

    def gen_b(self, l, es):
        nc, S, d, db = self.nc, self.S, self.scr, self.dbufs
        T, NT, NL = 256 + 2 * self.NL, self.NTK, self.NL
        S.collective("AllGather", ALU.bypass, self.groups, d["KVS"], d["KVG"], [db["KVS"]], [db["KVG"]], "kv")
        if True:
            KT, bKT = self.sb(es, "KTs", [128, T], BF16)
            V1, bV1 = self.sb(es, "V1", [128, NT, 2, 65], BF16)
            self.ld(KT[:, 0:256], d["KT"], r=[db["KT"]], w=[bKT])
            for rk in range(2):
                self.ld(KT[:, 256 + rk * NL:256 + (rk + 1) * NL], d["KVG"][rk * 256:rk * 256 + 128, :], r=[db["KVG"]], pw=[bKT])
            self.P(lambda: nc.gpsimd.memset(V1[:], 1.0), w=[bV1])
            for hh in range(2):
                self.ld(V1[:, 0:2, hh, 0:64], d["VV"][:, hh * 64:(hh + 1) * 64].rearrange("(t p) e -> p t e", p=128), r=[db["VV"]], pw=[bV1])
                for rk in range(2):
                    kvg = d["KVG"]
                    vsrc = AP(kvg.tensor, kvg.offset + (rk * 256 + 128) * NL + hh * 64, [[128, 128], [128 * 128, NL // 128], [1, 64]])
                    t_0 = 2 + rk * (NL // 128)
                    self.ld(V1[:, t_0:t_0 + NL // 128, hh, 0:64], vsrc, r=[db["KVG"]], pw=[bV1])
            Qs = [self.sb(es, "Qs%d" % k, [128, 512], BF16) for k in range(2)]
            PT = [self.sb(es, "PT%d" % k, [128, 512], BF16) for k in range(4)]
            rd, brd = self.sb(es, "rd", [128, 512])
            osb, bosb = self.sb(es, "osb", [128, 512])
            obf = [self.sb(es, "obf%d" % k, [128, 512], BF16) for k in range(2)]
            qi = 0
            pi = 0
            oi = 0
            for (q0, nq, isctx) in self.blocks:
                keys = [0, 1] if isctx else list(range(NT))
                for pair in range(2):
                    Q, bQ = Qs[qi % 2]
                    qi += 1
                    for hh in range(2):
                        h = pair + 2 * hh
                        self.ld(Q[hh * 64:(hh + 1) * 64, 0:nq], d["QT"][h * 64:(h + 1) * 64, q0:q0 + nq], r=[db["QT"]], pw=[bQ])
                    pts = {}

                    def stepA(ki, hh):
                        kt = keys[ki]
                        slot = (ki % 2) * 2 + hh
                        st_, bst = self.ps[slot]
                        self.M(lambda: nc.tensor.matmul(st_[:, 0:nq], lhsT=KT[hh * 64:(hh + 1) * 64, kt * 128:(kt + 1) * 128],
                                                        rhs=Q[hh * 64:(hh + 1) * 64, 0:nq], start=True, stop=True),
                               [bKT, bQ], [bst])
                        pt_, bpt = PT[slot]
                        self.A(lambda: nc.scalar.activation(out=pt_[:, 0:nq], in_=st_[:, 0:nq], func=AF.Exp, scale=0.125), [bst], [bpt])
                        pts[(ki, hh)] = (pt_, bpt)

                    def stepB(ki, hh):
                        kt = keys[ki]
                        pt_, bpt = pts.pop((ki, hh))
                        oa, boa = self.ps[4 + hh]
                        first, lastk = (ki == 0), (ki == len(keys) - 1)
                        self.M(lambda: nc.tensor.matmul(oa[0:65, 0:nq], lhsT=V1[:, kt, hh, :], rhs=pt_[:, 0:nq], start=first, stop=lastk),
                               [bV1, bpt], pw=[boa] if not first else (), w=[boa] if first else ())
                    for ki in range(len(keys) + 1):
                        if ki < len(keys):
                            stepA(ki, 0)
                            stepA(ki, 1)
                        if ki >= 1:
                            stepB(ki - 1, 0)
                            stepB(ki - 1, 1)
                        yield
                    for hh in range(2):
                        h = pair + 2 * hh
                        oa, boa = self.ps[4 + hh]
                        self.V(lambda: nc.vector.reciprocal(out=rd[64:65, 0:nq], in_=oa[64:65, 0:nq]), [boa], [brd])
                        bc, bbc = self.ps[3]
                        self.M(lambda: nc.tensor.matmul(bc[0:64, 0:nq], lhsT=self.ones[64:65, 0:64], rhs=rd[64:65, 0:nq], start=True, stop=True),
                               [self.b_ones, brd], [bbc])
                        self.A(lambda: nc.scalar.copy(out=osb[0:64, 0:nq], in_=oa[0:64, 0:nq]), [boa], [bosb])
                        ob, bob = obf[oi % 2]
                        oi += 1
                        self.V(lambda: nc.vector.tensor_tensor(out=ob[0:64, 0:nq], in0=osb[0:64, 0:nq], in1=bc[0:64, 0:nq], op=ALU.mult),
                               [bosb, bbc], [bob])
                        self.st(d["OATT"][h * 64:(h + 1) * 64, q0:q0 + nq], ob[0:64, 0:nq], r=[bob], pw=[db["OATT"]])
                        yield

    def phase_c1(self, l):
        nc, S, d, db = self.nc, self.S, self.scr, self.dbufs
        with ExitStack() as es:
            St, bSt = self.sb(es, "St", [128, 2, 128])
            sfg = [self.sb(es, "sfg%d" % k, [128, 2, 128]) for k in range(2)]
            kes = [self.sb(es, "ke%d" % k, [128, 4, 256], BF16) for k in range(2)]
            gvs = [self.sb(es, "gvs%d" % k, [128, 4, 512], BF16) for k in range(2)]
            hists = [self.sb(es, "hist%d" % k, [128, 8, 2, 128], BF16) for k in range(2)]
            it = 0
            for dr in range(2):
                self.V(lambda: nc.vector.memset(St[:], 0.0), w=[bSt])
                ctxb = [b for b in self.blocks if b[2]]
                latb = [b for b in self.blocks if not b[2]]
                order = ctxb + (latb if dr == 0 else latb[::-1])
                for (t0, n, isctx) in order:
                    if dr == 1 and (t0, n, isctx) == order[len(ctxb)]:
                        for rk in range(2):
                            self.ld(sfg[rk][0][:], d["SFG"][rk * 256:(rk + 1) * 256, :].rearrange("(c p) e -> p c e", p=128), r=[db["SFG"]], w=[sfg[rk][1]])
                        self.V(lambda: nc.vector.tensor_scalar(out=St[:], in0=sfg[0][0][:], scalar1=self.selw[:, 0:1], scalar2=None, op0=ALU.mult),
                               [sfg[0][1], self.b_selw], [bSt])
                        self.V(lambda: nc.vector.scalar_tensor_tensor(out=St[:], in0=sfg[1][0][:], scalar=self.selw[:, 1:2], in1=St[:], op0=ALU.mult, op1=ALU.add),
                               [sfg[1][1], self.b_selw, bSt], [bSt])
                    nt = n // 128
                    ke, bke = kes[it % 2]
                    gv, bgv = gvs[it % 2]
                    hist, bhist = hists[it % 2]
                    it += 1
                    self.ld(ke[:, 0:nt, :], d["KE"][dr, t0:t0 + n, :].rearrange("(j p) f -> p j f", p=128), r=[db["KE"]], w=[bke])
                    self.ld(gv[:, 0:nt, :], d["GV"][t0:t0 + n, :].rearrange("(j p) f -> p j f", p=128), r=[db["GV"]], w=[bgv])
                    nchb = n // 64
                    chs = list(range(nchb)) if dr == 0 else list(range(nchb))[::-1]
                    for ci, cl in enumerate(chs):
                        j, half = cl // 2, cl % 2
                        ng = t0 // 64 + cl
                        self.A(lambda: nc.scalar.copy(out=hist[:, cl, :, :], in_=St[:]), [bSt], pw=[bhist] if ci else (), w=[bhist] if ci == 0 else ())
                        for c in range(2):
                            kv, bkv = self.ps[c + 2 * (ci % 2)]
                            for hl in range(2):
                                h = c * 2 + hl
                                self.M(lambda: nc.tensor.matmul(kv[hl * 64:(hl + 1) * 64, 0:128],
                                                                lhsT=ke[half * 64:(half + 1) * 64, j, c * 128 + hl * 64:c * 128 + (hl + 1) * 64],
                                                                rhs=gv[half * 64:(half + 1) * 64, j, h * 128:(h + 1) * 128], start=True, stop=True),
                                       [bke, bgv], pw=[bkv] if hl else (), w=[bkv] if hl == 0 else ())
                            self.V(lambda: nc.vector.scalar_tensor_tensor(out=St[:, c, :], in0=St[:, c, :], scalar=self.dec[:, dr, c, ng:ng + 1],
                                                                          in1=kv[:, 0:128], op0=ALU.mult, op1=ALU.add),
                                   [bSt, self.b_dec, bkv], [bSt])
                    ch0 = t0 // 64
                    self.st(d["SST"][dr, ch0:ch0 + nchb].rearrange("n c p e -> p n c e"), hist[:, 0:nchb, :, :], r=[bhist], pw=[db["SST"]])
                if dr == 0:
                    self.st(d["SFS"].rearrange("(c p) e -> p c e", p=128), St[:], r=[bSt], w=[db["SFS"]])
                    S.collective("AllGather", ALU.bypass, self.groups, d["SFS"], d["SFG"], [db["SFS"]], [db["SFG"]], "gla")
            S.barrier()
            S.recycle(self.phase_bufs)
            self.phase_bufs = []

    def phase_c2(self, l):
        nc, S, i, d, db = self.nc, self.S, self.inp, self.scr, self.dbufs
        with ExitStack() as es:
            gn, bgn = self.sb(es, "gn", [128, 512])
            src = i["glanorm"][l]
            self.ld(gn[:], AP(src.tensor, src.offset, [[0, 128], [1, 512]]), w=[bgn])
            gts = [self.sb(es, "gt%d" % k, [64, 4, 4, 512], BF16) for k in range(2)]
            gvs = [self.sb(es, "gv2%d" % k, [64, 8, 512], BF16) for k in range(2)]
            grs = [self.sb(es, "gr2%d" % k, [128, 4, 512], BF16) for k in range(2)]
            ssts = [self.sb(es, "sst%d" % k, [64, 2, 4, 8, 128], BF16) for k in range(2)]
            atts = [self.sb(es, "att%d" % k, [64, 2, 64], BF16) for k in range(4)]
            sq2s = [self.sb(es, "sq2_%d" % k, [128, 512]) for k in range(2)]
            ss4s = [self.sb(es, "ss4_%d" % k, [128, 4]) for k in range(2)]
            rs4s = [self.sb(es, "rs4_%d" % k, [128, 4]) for k in range(2)]
            t1s = [self.sb(es, "t1_%d" % k, [128, 512]) for k in range(2)]
            t3s = [self.sb(es, "t3_%d" % k, [128, 512], BF16) for k in range(2)]
            ogTs = [self.sb(es, "ogT_%d" % k, [128, 4, 128], BF16) for k in range(2)]
            ai = 0
            for bi, (t0, n, isctx) in enumerate(self.blocks):
                nt = n // 128
                nchb = n // 64
                ch0 = t0 // 64
                gt, bgt = gts[bi % 2]
                gv, bgv = gvs[bi % 2]
                gr, bgr = grs[bi % 2]
                sst, bsst = ssts[bi % 2]
                for kind in range(4):
                    self.ld(gt[:, kind, :, 0:n], d["GT"][kind, :, t0:t0 + n].rearrange("(h p) t -> p h t", p=64), r=[db["GT"]],
                            pw=[bgt] if kind else (), w=[bgt] if kind == 0 else ())
                self.ld(gv[:, 0:nchb, :], d["GV"][t0:t0 + n, :].rearrange("(c p) f -> p c f", p=64), r=[db["GV"]], w=[bgv])
                self.ld(gr[:, 0:nt, :], d["GR"][t0:t0 + n, :].rearrange("(j p) f -> p j f", p=128), r=[db["GR"]], w=[bgr])
                first_ld = True
                for dr in range(2):
                    for h in range(4):
                        c, hl = h // 2, h % 2
                        self.ld(sst[:, dr, h, 0:nchb, :], d["SST"][dr, ch0:ch0 + nchb, c, hl * 64:(hl + 1) * 64, :].rearrange("n p e -> p n e"),
                                r=[db["SST"]], pw=[bsst] if not first_ld else (), w=[bsst] if first_ld else ())
                        first_ld = False
                for j in range(nt):
                    ops_, bops = self.ps[4 + (j % 2)]
                    asbs = {}

                    def stA(h):
                        nonlocal ai
                        asb = []
                        for dr in range(2):
                            ap_, bap = self.ps[dr + 2 * (h % 2)]
                            for ch in range(2):
                                tk = j * 128 + ch * 64
                                self.M(lambda: nc.tensor.matmul(ap_[0:64, ch * 64:(ch + 1) * 64], lhsT=gt[:, dr * 2 + 1, h, tk:tk + 64],
                                                                rhs=gt[:, dr * 2 + 0, h, tk:tk + 64], start=True, stop=True),
                                       [bgt], pw=[bap] if ch else (), w=[bap] if ch == 0 else ())
                            at, bat = atts[ai % 4]
                            ai += 1
                            self.V(lambda: nc.vector.tensor_tensor(out=at[:], in0=ap_[0:64, 0:128].rearrange("p (c i) -> p c i", i=64),
                                                                   in1=self.gmask[0:64, dr, :].unsqueeze(1).to_broadcast([64, 2, 64]), op=ALU.mult),
                                   [bap, self.b_gmask], [bat])
                            asb.append((at, bat))
                        asbs[h] = asb

                    def stB(h):
                        asb = asbs.pop(h)
                        for ch in range(2):
                            tk = j * 128 + ch * 64
                            cl = 2 * j + ch
                            o_ = ops_[ch * 64:(ch + 1) * 64, h * 128:(h + 1) * 128]
                            first = (h == 0 and ch == 0)
                            for dr in range(2):
                                at, bat = asb[dr]
                                self.M(lambda: nc.tensor.matmul(o_, lhsT=at[:, ch, :], rhs=gv[:, cl, h * 128:(h + 1) * 128],
                                                                start=(dr == 0), stop=False),
                                       [bat, bgv], pw=[bops] if not (first and dr == 0) else (), w=[bops] if (first and dr == 0) else ())
                                self.M(lambda: nc.tensor.matmul(o_, lhsT=gt[:, dr * 2 + 0, h, tk:tk + 64], rhs=sst[:, dr, h, cl, :],
                                                                start=False, stop=(dr == 1)),
                                       [bgt, bsst], pw=[bops])
                    stA(0)
                    for h in range(4):
                        if h + 1 < 4:
                            stA(h + 1)
                        stB(h)
                    sq, bsq = sq2s[j % 2]
                    ss4, bss4 = ss4s[j % 2]
                    rs4, brs4 = rs4s[j % 2]
                    t1, bt1 = t1s[j % 2]
                    t3, bt3 = t3s[j % 2]
                    ogT, bogT = ogTs[j % 2]
                    self.A(lambda: nc.scalar.activation(out=sq[:], in_=ops_[:], func=AF.Square), [bops], [bsq])
                    self.V(lambda: nc.vector.tensor_reduce(out=ss4[:], in_=sq[:].rearrange("p (h e) -> p h e", e=128), axis=AX.X, op=ALU.add),
                           [bsq], [bss4])
                    self.rstd_from_ss(ss4[:], rs4[:], bss4, brs4, 128)
                    self.V(lambda: nc.vector.tensor_tensor(out=t1[:].rearrange("p (h e) -> p h e", e=128),
                                                           in0=ops_[:].rearrange("p (h e) -> p h e", e=128),
                                                           in1=rs4[:].unsqueeze(2).to_broadcast([128, 4, 128]), op=ALU.mult),
                           [bops, brs4], [bt1])
                    self.P(lambda: nc.gpsimd.tensor_tensor(out=t1[:], in0=t1[:], in1=gn[:], op=ALU.mult), [bt1, bgn], [bt1])
                    self.V(lambda: nc.vector.tensor_tensor(out=t3[:], in0=t1[:], in1=gr[:, j, :], op=ALU.mult), [bt1, bgr], [bt3])
                    p7, bp7 = self.ps[7]
                    p7b = p7[:].bitcast(BF16)
                    for h in range(4):
                        self.M(lambda h=h: nc.tensor.transpose(out=p7b[:, h * 128:(h + 1) * 128], in_=t3[:, h * 128:(h + 1) * 128],
                                                              identity=self.identb[:]),
                               [bt3, self.b_identb], pw=[bp7] if h else (), w=[bp7] if h == 0 else ())
                    self.A(lambda: nc.scalar.copy(out=ogT[:].rearrange("p h t -> p (h t)"), in_=p7b[:, 0:512]), [bp7], [bogT])
                    r0t = t0 + j * 128
                    self.st(d["OGLA"][:, r0t:r0t + 128].rearrange("(h p) t -> p h t", p=128), ogT[:], r=[bogT], pw=[db["OGLA"]])
            S.barrier()
            S.recycle(self.phase_bufs)
            self.phase_bufs = []

    def gen_c3(self, l, es):
        nc, S, i, d, db = self.nc, self.S, self.inp, self.scr, self.dbufs
        V, A, P, M = self.V, self.A, self.P, self.M
        NB5 = 256
        if True:
            R, bR = self.sb(es, "Rtab", [128, 2, 2, 8, NB5])
            mag, bmag = self.sb(es, "mag", [128, 2, 8])
            Bsb, bB = self.sb(es, "Bsb", [32, 2, 2, 8, 128], BF16)
            Csb, bC = self.sb(es, "Csb", [128, 2, 2, 8, 64], BF16)
            dcol, bdcol = self.sb(es, "dcol", [128, 2])
            glub, bglub = self.sb(es, "glub", [128, 2])
            gluw, bgluw = self.sb(es, "gluw", [128, 2, 256], BF16)
            self.ld(dcol[:], i["s5d"][l], w=[bdcol])
            self.ld(glub[:], i["s5glub"][l], w=[bglub])
            self.ldc(gluw[:], i["s5gluw"][l].rearrange("(c p) n -> p c n", p=128), w=[bgluw])
            carry, bcarry = self.sb(es, "carry", [128, 2, 8])
            if True:
                es2 = es
                a3, ba3 = self.sb(es2, "a3", [128, 3, 8])
                w = [self.sb(es2, "w%d" % k, [128, 8]) for k in range(12)]
                wi, bwi = self.sb(es2, "wi", [128, 8], I32)
                cf, bcf = self.sb(es2, "cf", [128, 2, 8, 64])
                fb, bfb = self.sb(es2, "fb", [128, 2, 8])
                tt = [self.sb(es2, "ct%d" % k, [128, 8, 64]) for k in range(2)]
                tr = [self.sb(es2, "tr%d" % k, [128, 8, NB5 // 2]) for k in range(2)]

                def ts(o, a, s1, s2, op0, op1=None):
                    if op1 is None:
                        V(lambda: nc.vector.tensor_scalar(out=o[0][:], in0=a[0][:], scalar1=s1, scalar2=None, op0=op0), [a[1]], [o[1]])
                    else:
                        V(lambda: nc.vector.tensor_scalar(out=o[0][:], in0=a[0][:], scalar1=s1, scalar2=s2, op0=op0, op1=op1), [a[1]], [o[1]])

                def tt_(o, a, b, op):
                    V(lambda: nc.vector.tensor_tensor(out=o[0][:], in0=a[0][:], in1=b[0][:], op=op), [a[1], b[1]], [o[1]])

                for dr in range(2):
                    self.ld(a3[:], i["s5a"][l, dr], w=[ba3])
                    self.ldc(Bsb[:, dr], i["s5b"][l, dr], pw=[bB])
                    self.ld(cf[:], i["s5c"][l, dr], w=[bcf])
                    are, aim, ldt = (a3[:, 0, :], ba3), (a3[:, 1, :], ba3), (a3[:, 2, :], ba3)
                    dt_, adt, ang, kf, r_, m_, sn, cs, t_, den, fre, fim = w
                    A(lambda: nc.scalar.activation(out=dt_[0][:], in_=a3[:, 2, :], func=AF.Exp), [ba3], [dt_[1]])
                    V(lambda: nc.vector.tensor_tensor(out=adt[0][:], in0=a3[:, 0, :], in1=dt_[0][:], op=ALU.mult), [ba3, dt_[1]], [adt[1]])
                    A(lambda: nc.scalar.activation(out=mag[:, dr, :], in_=adt[0][:], func=AF.Exp), [adt[1]], pw=[bmag])
                    V(lambda: nc.vector.tensor_tensor(out=ang[0][:], in0=a3[:, 1, :], in1=dt_[0][:], op=ALU.mult), [ba3, dt_[1]], [ang[1]])
                    ts(kf, ang, 1.0 / (2 * PI), None, ALU.mult)
                    V(lambda: nc.vector.tensor_copy(out=wi[:], in_=kf[0][:]), [kf[1]], [bwi])
                    V(lambda: nc.vector.tensor_copy(out=kf[0][:], in_=wi[:]), [bwi], [kf[1]])
                    V(lambda: nc.vector.scalar_tensor_tensor(out=r_[0][:], in0=kf[0][:], scalar=-2 * PI, in1=ang[0][:], op0=ALU.mult, op1=ALU.add),
                      [kf[1], ang[1]], [r_[1]])
                    ts(m_, r_, PI, -2 * PI, ALU.is_gt, ALU.mult)
                    tt_(r_, r_, m_, ALU.add)
                    ts(m_, r_, -PI, 2 * PI, ALU.is_lt, ALU.mult)
                    tt_(r_, r_, m_, ALU.add)
                    A(lambda: nc.scalar.activation(out=sn[0][:], in_=r_[0][:], func=AF.Sin), [r_[1]], [sn[1]])
                    ts(t_, r_, -1.0, None, ALU.mult)
                    tt_(t_, t_, r_, ALU.max)
                    ts(t_, t_, -1.0, PI / 2, ALU.mult, ALU.add)
                    A(lambda: nc.scalar.activation(out=cs[0][:], in_=t_[0][:], func=AF.Sin), [t_[1]], [cs[1]])
                    V(lambda: nc.vector.tensor_copy(out=R[:, dr, 0, :, 0], in_=cs[0][:]), [cs[1]], pw=[bR])
                    V(lambda: nc.vector.tensor_copy(out=R[:, dr, 1, :, 0], in_=sn[0][:]), [sn[1]], pw=[bR])
                    abre, abim = kf, m_
                    V(lambda: nc.vector.tensor_tensor(out=abre[0][:], in0=mag[:, dr, :], in1=cs[0][:], op=ALU.mult), [bmag, cs[1]], [abre[1]])
                    V(lambda: nc.vector.tensor_tensor(out=abim[0][:], in0=mag[:, dr, :], in1=sn[0][:], op=ALU.mult), [bmag, sn[1]], [abim[1]])
                    ts(abre, abre, -1.0, None, ALU.add)
                    V(lambda: nc.vector.tensor_tensor(out=den[0][:], in0=a3[:, 0, :], in1=a3[:, 0, :], op=ALU.mult), [ba3], [den[1]])
                    V(lambda: nc.vector.tensor_tensor(out=t_[0][:], in0=a3[:, 1, :], in1=a3[:, 1, :], op=ALU.mult), [ba3], [t_[1]])
                    tt_(den, den, t_, ALU.add)
                    V(lambda: nc.vector.reciprocal(out=den[0][:], in_=den[0][:]), [den[1]], [den[1]])
                    V(lambda: nc.vector.tensor_tensor(out=fre[0][:], in0=abre[0][:], in1=a3[:, 0, :], op=ALU.mult), [abre[1], ba3], [fre[1]])
                    V(lambda: nc.vector.tensor_tensor(out=t_[0][:], in0=abim[0][:], in1=a3[:, 1, :], op=ALU.mult), [abim[1], ba3], [t_[1]])
                    tt_(fre, fre, t_, ALU.add)
                    V(lambda: nc.vector.tensor_tensor(out=fb[:, 0, :], in0=fre[0][:], in1=den[0][:], op=ALU.mult), [fre[1], den[1]], pw=[bfb])
                    V(lambda: nc.vector.tensor_tensor(out=fim[0][:], in0=abim[0][:], in1=a3[:, 0, :], op=ALU.mult), [abim[1], ba3], [fim[1]])
                    V(lambda: nc.vector.tensor_tensor(out=t_[0][:], in0=abre[0][:], in1=a3[:, 1, :], op=ALU.mult), [abre[1], ba3], [t_[1]])
                    tt_(fim, fim, t_, ALU.subtract)
                    V(lambda: nc.vector.tensor_tensor(out=fb[:, 1, :], in0=fim[0][:], in1=den[0][:], op=ALU.mult), [fim[1], den[1]], pw=[bfb])
                    frb = fb[:, 0, :].unsqueeze(2).to_broadcast([128, 8, 64])
                    fib = fb[:, 1, :].unsqueeze(2).to_broadcast([128, 8, 64])
                    V(lambda: nc.vector.tensor_tensor(out=tt[0][0][:], in0=cf[:, 0], in1=frb, op=ALU.mult), [bcf, bfb], [tt[0][1]])
                    V(lambda: nc.vector.tensor_tensor(out=tt[1][0][:], in0=cf[:, 1], in1=fib, op=ALU.mult), [bcf, bfb], [tt[1][1]])
                    V(lambda: nc.vector.tensor_tensor(out=Csb[:, dr, 0], in0=tt[0][0][:], in1=tt[1][0][:], op=ALU.subtract), [tt[0][1], tt[1][1]], pw=[bC])
                    V(lambda: nc.vector.tensor_tensor(out=tt[0][0][:], in0=cf[:, 0], in1=fib, op=ALU.mult), [bcf, bfb], [tt[0][1]])
                    V(lambda: nc.vector.tensor_tensor(out=tt[1][0][:], in0=cf[:, 1], in1=frb, op=ALU.mult), [bcf, bfb], [tt[1][1]])
                    V(lambda: nc.vector.tensor_tensor(out=tt[0][0][:], in0=tt[0][0][:], in1=tt[1][0][:], op=ALU.add), [tt[0][1], tt[1][1]], [tt[0][1]])
                    V(lambda: nc.vector.tensor_scalar(out=Csb[:, dr, 1], in0=tt[0][0][:], scalar1=-1.0, scalar2=None, op0=ALU.mult), [tt[0][1]], pw=[bC])
                    nn = 1
                    while nn < NB5:
                        cr = R[:, dr, 0, :, nn - 1:nn].to_broadcast([128, 8, nn])
                        ci = R[:, dr, 1, :, nn - 1:nn].to_broadcast([128, 8, nn])
                        sre, sim = R[:, dr, 0, :, 0:nn], R[:, dr, 1, :, 0:nn]
                        u0, u1 = tr[0][0][:, :, 0:nn], tr[1][0][:, :, 0:nn]
                        V(lambda: nc.vector.tensor_tensor(out=u0, in0=sre, in1=cr, op=ALU.mult), [bR], [tr[0][1]])
                        V(lambda: nc.vector.tensor_tensor(out=u1, in0=sim, in1=ci, op=ALU.mult), [bR], [tr[1][1]])
                        V(lambda: nc.vector.tensor_tensor(out=R[:, dr, 0, :, nn:2 * nn], in0=u0, in1=u1, op=ALU.subtract), [tr[0][1], tr[1][1]], pw=[bR])
                        V(lambda: nc.vector.tensor_tensor(out=u0, in0=sre, in1=ci, op=ALU.mult), [bR], [tr[0][1]])
                        V(lambda: nc.vector.tensor_tensor(out=u1, in0=sim, in1=cr, op=ALU.mult), [bR], [tr[1][1]])
                        V(lambda: nc.vector.tensor_tensor(out=R[:, dr, 1, :, nn:2 * nn], in0=u0, in1=u1, op=ALU.add), [tr[0][1], tr[1][1]], pw=[bR])
                        nn *= 2
                        yield
                    yield
            if l == 0:
                self.dump("Rtab", R[:], [bR], [128, 2, 2, 8, 512])
                self.dump("mag", mag[:], [bmag], [128, 2, 8])

            Rneg, bRneg = self.sb(es, "Rneg", [128, 2, 8])
            for dr in range(2):
                V(lambda: nc.vector.tensor_scalar(out=Rneg[:, dr, :], in0=R[:, dr, 1, :, NB5 - 1], scalar1=-1.0, scalar2=None, op0=ALU.mult),
                  [bR], pw=[bRneg])
            us = [self.sb(es, "us%d" % k, [32, 8, NB5], BF16) for k in range(2)]
            ufs = [self.sb(es, "uf%d" % k, [128, 2, NB5], BF16) for k in range(2)]
            yfls = [self.sb(es, "yfl%d" % k, [128, 2, NB5]) for k in range(2)]
            tqA = [[self.sb(es, "tqA%d_%d" % (q, k), [128, NB5]) for k in range(4)] for q in range(2)]
            tqB = [[self.sb(es, "tqB%d_%d" % (q, k), [128, NB5]) for k in range(4)] for q in range(2)]
            zz = [[self.sb(es, "z%d_%d" % (q, k), [128, NB5]) for k in range(2)] for q in range(2)]
            scs = [[self.sb(es, "sc%d_%d" % (q, k), [128, NB5]) for k in range(2)] for q in range(3)]
            sbfs = [[self.sb(es, "sbf%d_%d" % (q, k), [128, NB5], BF16) for k in range(2)] for q in range(2)]
            ct = [self.sb(es, "ct%d" % k, [128, 1]) for k in range(2)]
            cfg = [self.sb(es, "cfg%d" % k, [128, 2, 8]) for k in range(2)]
            carryb = [S.buf("carry_m%d" % m) for m in range(8)]
            ysbs = [self.sb(es, "ysb%d" % k, [128, 2, NB5]) for k in range(2)]
            yg, byg = self.sb(es, "yg", [128, 2, NB5], BF16)
            sg, bsg = self.sb(es, "sg", [128, NB5])
            os5, bos5 = self.sb(es, "os5", [128, 2, NB5], BF16)
            p6, b6 = self.ps[6]
            p7, b7 = self.ps[7]
            n = NB5
            ctxb = [(0, 256)]
            latb = [(256 + NB5 * j, NB5) for j in range(self.NL // NB5)]
            for dr in range(2):
                for m in range(8):
                    V(lambda: nc.vector.memset(carry[:, :, m:m + 1], 0.0), w=[carryb[m]])
                order = ctxb + (latb if dr == 0 else latb[::-1])
                items = [(bi, blk, m) for bi, blk in enumerate(order) for m in range(8)]
                blkst = {}
                epis = []

                def dirv(ap2):
                    return ap2 if dr == 0 else rev(ap2)

                def s_load(bi):
                    if bi >= len(order) or bi in blkst:
                        return
                    t0, _n = order[bi]
                    u, bu = us[bi % 2]
                    self.ld(u[:, :, 0:n], d["SU"][:, t0:t0 + n].rearrange("(m r) t -> r m t", r=32), r=[db["SU"]], w=[bu])
                    st_ = dict(u=u, bu=bu)
                    if dr == 1:
                        uf, buf_ = ufs[bi % 2]
                        yfl, byfl = yfls[bi % 2]
                        self.ld(uf[:, :, 0:n], d["SU"][:, t0:t0 + n].rearrange("(c p) t -> p c t", p=128), r=[db["SU"]], w=[buf_])
                        self.ld(yfl[:, :, 0:n], d["YF"][:, t0:t0 + n].rearrange("(c p) t -> p c t", p=128), r=[db["YF"]], w=[byfl])
                        st_.update(uf=uf, buf_=buf_, yfl=yfl, byfl=byfl)
                    blkst[bi] = st_

                def stage1(k):
                    bi, (t0, _n), m = items[k]
                    if m == 0:
                        s_load(bi)
                    if m == 4:
                        s_load(bi + 1)
                    u, bu = blkst[bi]["u"], blkst[bi]["bu"]
                    M(lambda: nc.tensor.matmul(p6[:, 0:n], lhsT=Bsb[:, dr, 0, m, :], rhs=u[:, m, 0:n], start=True, stop=True), [bB, bu], [b6])
                    M(lambda: nc.tensor.matmul(p6[:, n:2 * n], lhsT=Bsb[:, dr, 1, m, :], rhs=u[:, m, 0:n], start=True, stop=True), [bB, bu], pw=[b6])

                def stage2(k):
                    bi, (t0, _n), m = items[k]
                    pre, pim = p6[:, 0:n], p6[:, n:2 * n]
                    Rre = dirv(R[:, dr, 0, m, 0:n])
                    Rim = dirv(R[:, dr, 1, m, 0:n])
                    tq = tqA[k % 2]
                    V(lambda: nc.vector.tensor_tensor(out=tq[0][0][:, 0:n], in0=pre, in1=Rre, op=ALU.mult), [b6, bR], [tq[0][1]])
                    V(lambda: nc.vector.tensor_tensor(out=tq[1][0][:, 0:n], in0=pim, in1=Rim, op=ALU.mult), [b6, bR], [tq[1][1]])
                    V(lambda: nc.vector.tensor_tensor(out=tq[2][0][:, 0:n], in0=pim, in1=Rre, op=ALU.mult), [b6, bR], [tq[2][1]])
                    V(lambda: nc.vector.tensor_tensor(out=tq[3][0][:, 0:n], in0=pre, in1=Rim, op=ALU.mult), [b6, bR], [tq[3][1]])
                    z = zz[k % 2]
                    P(lambda: nc.gpsimd.tensor_tensor(out=z[0][0][:, 0:n], in0=tq[0][0][:, 0:n], in1=tq[1][0][:, 0:n], op=ALU.add),
                      [tq[0][1], tq[1][1]], [z[0][1]])
                    P(lambda: nc.gpsimd.tensor_tensor(out=z[1][0][:, 0:n], in0=tq[2][0][:, 0:n], in1=tq[3][0][:, 0:n], op=ALU.subtract),
                      [tq[2][1], tq[3][1]], [z[1][1]])

                def stage3(k):
                    bi, (t0, _n), m = items[k]
                    z = zz[k % 2]
                    sc_ = scs[k % 3]
                    magb = mag[:, dr, m:m + 1].to_broadcast([128, n])
                    for ri in range(2):
                        V(lambda ri=ri: nc.vector.tensor_tensor_scan(out=dirv(sc_[ri][0][:, 0:n]), data0=magb, data1=dirv(z[ri][0][:, 0:n]),
                                                                    initial=carry[:, ri, m:m + 1], op0=ALU.mult, op1=ALU.add),
                          [bmag, z[ri][1], carryb[m]], [sc_[ri][1]])
                    Rre = dirv(R[:, dr, 0, m, 0:n])
                    Rim = dirv(R[:, dr, 1, m, 0:n])
                    tq = tqB[k % 2]
                    P(lambda: nc.gpsimd.tensor_tensor(out=tq[0][0][:, 0:n], in0=sc_[0][0][:, 0:n], in1=Rre, op=ALU.mult), [sc_[0][1], bR], [tq[0][1]])
                    P(lambda: nc.gpsimd.tensor_tensor(out=tq[1][0][:, 0:n], in0=sc_[1][0][:, 0:n], in1=Rim, op=ALU.mult), [sc_[1][1], bR], [tq[1][1]])
                    P(lambda: nc.gpsimd.tensor_tensor(out=tq[2][0][:, 0:n], in0=sc_[0][0][:, 0:n], in1=Rim, op=ALU.mult), [sc_[0][1], bR], [tq[2][1]])
                    P(lambda: nc.gpsimd.tensor_tensor(out=tq[3][0][:, 0:n], in0=sc_[1][0][:, 0:n], in1=Rre, op=ALU.mult), [sc_[1][1], bR], [tq[3][1]])

                def stage4(k):
                    bi, (t0, _n), m = items[k]
                    sc_ = scs[k % 3]
                    lastc = n - 1 if dr == 0 else 0
                    rl_re, rl_im = R[:, dr, 0, m, n - 1:n], R[:, dr, 1, m, n - 1:n]
                    nrl_im = Rneg[:, dr, m:m + 1]
                    s_re, s_im = sc_[0][0][:, lastc:lastc + 1], sc_[1][0][:, lastc:lastc + 1]
                    A(lambda: nc.scalar.activation(out=ct[0][0][:], in_=s_im, func=AF.Identity, scale=nrl_im), [sc_[1][1], bRneg], [ct[0][1]])
                    A(lambda: nc.scalar.activation(out=carry[:, 0, m:m + 1], in_=s_re, func=AF.Identity, scale=rl_re, bias=ct[0][0][:]),
                      [sc_[0][1], bR, ct[0][1]], [carryb[m]])
                    A(lambda: nc.scalar.activation(out=ct[1][0][:], in_=s_re, func=AF.Identity, scale=rl_im), [sc_[0][1], bR], [ct[1][1]])
                    A(lambda: nc.scalar.activation(out=carry[:, 1, m:m + 1], in_=s_im, func=AF.Identity, scale=rl_re, bias=ct[1][0][:]),
                      [sc_[1][1], bR, ct[1][1]], pw=[carryb[m]])
                    tq = tqB[k % 2]
                    (sr, bsr), (sm, bsm) = sbfs[k % 2]
                    V(lambda: nc.vector.tensor_tensor(out=sr[:, 0:n], in0=tq[0][0][:, 0:n], in1=tq[1][0][:, 0:n], op=ALU.subtract),
                      [tq[0][1], tq[1][1]], [bsr])
                    V(lambda: nc.vector.tensor_tensor(out=sm[:, 0:n], in0=tq[2][0][:, 0:n], in1=tq[3][0][:, 0:n], op=ALU.add),
                      [tq[2][1], tq[3][1]], [bsm])

                def stage5(k):
                    bi, (t0, _n), m = items[k]
                    (sr, bsr), (sm, bsm) = sbfs[k % 2]
                    c = m // 4
                    mo = 64 * ((m // 2) % 2)
                    yo = p7[mo:mo + 64, c * n:(c + 1) * n]
                    M(lambda: nc.tensor.matmul(yo, lhsT=Csb[:, dr, 0, m, :], rhs=sr[:, 0:n], start=(m % 2 == 0), stop=False),
                      [bC, bsr], pw=[b7] if m else (), w=[b7] if m == 0 else ())
                    M(lambda: nc.tensor.matmul(yo, lhsT=Csb[:, dr, 1, m, :], rhs=sm[:, 0:n], start=False, stop=(m % 2 == 1)),
                      [bC, bsm], pw=[b7])
                    if m == 7:
                        epis.append(epilogue(bi, t0))

                def epilogue(bi, t0):
                    st_ = blkst.pop(bi)
                    ysb, bysb = ysbs[bi % 2]
                    if dr == 0:
                        V(lambda: nc.vector.tensor_copy(out=ysb[:].rearrange("p c t -> p (c t)"), in_=p7[:, 0:2 * n]), [b7], [bysb])
                        yield
                        self.st(d["YF"][:, t0:t0 + n].rearrange("(c p) t -> p c t", p=128), ysb[:, :, 0:n], r=[bysb], pw=[db["YF"]])
                        return
                    uf, buf_, yfl, byfl = st_["uf"], st_["buf_"], st_["yfl"], st_["byfl"]
                    V(lambda: nc.vector.tensor_tensor(out=ysb[:].rearrange("p c t -> p (c t)"), in0=p7[:, 0:2 * n],
                                                      in1=yfl[:].rearrange("p c t -> p (c t)"), op=ALU.add), [b7, byfl], [bysb])
                    for c in range(2):
                        V(lambda: nc.vector.scalar_tensor_tensor(out=ysb[:, c, 0:n], in0=uf[:, c, 0:n], scalar=dcol[:, c:c + 1], in1=ysb[:, c, 0:n],
                                                                 op0=ALU.mult, op1=ALU.add), [buf_, bdcol, bysb], [bysb])
                    yield
                    for c in range(2):
                        V(lambda: nc.vector.tensor_tensor(out=sg[:, 0:n], in0=ysb[:, c, 0:n], in1=ysb[:, c, 0:n], op=ALU.mult), [bysb], [bsg])
                        V(lambda: nc.vector.tensor_scalar(out=sg[:, 0:n], in0=sg[:, 0:n], scalar1=0.044715, scalar2=1.0, op0=ALU.mult, op1=ALU.add), [bsg], [bsg])
                        V(lambda: nc.vector.tensor_tensor(out=sg[:, 0:n], in0=sg[:, 0:n], in1=ysb[:, c, 0:n], op=ALU.mult), [bsg, bysb], [bsg])
                        A(lambda: nc.scalar.activation(out=sg[:, 0:n], in_=sg[:, 0:n], func=AF.Sigmoid, scale=2.0 * math.sqrt(2.0 / PI)), [bsg], [bsg])
                        V(lambda: nc.vector.tensor_tensor(out=yg[:, c, 0:n], in0=ysb[:, c, 0:n], in1=sg[:, 0:n], op=ALU.mult), [bysb, bsg],
                          pw=[byg] if c else (), w=[byg] if c == 0 else ())
                        yield
                    for c2 in range(2):
                        zp = p6[:, c2 * n:(c2 + 1) * n]
                        for c in range(2):
                            M(lambda c=c: nc.tensor.matmul(zp, lhsT=gluw[:, c, c2 * 128:(c2 + 1) * 128], rhs=yg[:, c, 0:n],
                                                           start=(c == 0), stop=(c == 1)), [bgluw, byg],
                              pw=[b6] if (c or c2) else (), w=[b6] if (c == 0 and c2 == 0) else ())
                    for c2 in range(2):
                        zp = p6[:, c2 * n:(c2 + 1) * n]
                        A(lambda: nc.scalar.activation(out=sg[:, 0:n], in_=zp, func=AF.Sigmoid, bias=glub[:, c2:c2 + 1], scale=1.0),
                          [b6, bglub], [bsg])
                        V(lambda: nc.vector.tensor_tensor(out=os5[:, c2, 0:n], in0=yg[:, c2, 0:n], in1=sg[:, 0:n], op=ALU.mult), [byg, bsg],
                          pw=[bos5] if c2 else (), w=[bos5] if c2 == 0 else ())
                    self.st(d["OS5"][:, t0:t0 + n].rearrange("(c p) t -> p c t", p=128), os5[:, :, 0:n], r=[bos5], pw=[db["OS5"]])

                def step_epis():
                    for g in list(epis):
                        try:
                            next(g)
                        except StopIteration:
                            epis.remove(g)

                def run_range(k0, k1):
                    stages = (stage1, stage2, stage3, stage4, stage5)
                    for k in range(k0, k1 + len(stages) - 1):
                        for si, fn in reversed(list(enumerate(stages))):
                            kk = k - si
                            if k0 <= kk < k1:
                                fn(kk)
                                if fn is stage2:
                                    step_epis()
                                yield
                        if not (k0 <= k - 1 < k1):
                            step_epis()
                    while epis:
                        step_epis()
                        yield
                NI = len(items)
                nctx_items = 8 * len(ctxb)
                if dr == 0:
                    yield from run_range(0, NI)
                    self.st(d["CFS"], carry[:].rearrange("p r m -> p (r m)"), r=carryb, w=[db["CFS"]], sbuf=bcarry)
                    S.collective("AllGather", ALU.bypass, self.groups, d["CFS"], d["CFG"], [db["CFS"]], [db["CFG"]], "s5")
                else:
                    yield from run_range(0, nctx_items)
                    for rk in range(2):
                        self.ld(cfg[rk][0][:], d["CFG"][rk * 128:(rk + 1) * 128, :].rearrange("p (r m) -> p r m", r=2), r=[db["CFG"]], w=[cfg[rk][1]])
                    V(lambda: nc.vector.tensor_scalar(out=carry[:], in0=cfg[0][0][:], scalar1=self.selw[:, 0:1], scalar2=None, op0=ALU.mult),
                      [cfg[0][1], self.b_selw], carryb)
                    V(lambda: nc.vector.scalar_tensor_tensor(out=carry[:], in0=cfg[1][0][:], scalar=self.selw[:, 1:2], in1=carry[:], op0=ALU.mult, op1=ALU.add),
                      [cfg[1][1], self.b_selw] + carryb, carryb)
                    yield from run_range(nctx_items, NI)

    def phase_bc3(self, l):
        with ExitStack() as es:
            gb = self.gen_b(l, es)
            gc = self.gen_c3(l, es)
            gens = [(gb, 2), (gc, 3)]
            while gens:
                for g, reps in list(gens):
                    try:
                        for _ in range(reps):
                            next(g)
                    except StopIteration:
                        gens.remove((g, reps))
            self.S.barrier()
            self.S.recycle(self.phase_bufs)
            self.phase_bufs = []

    def phase_d(self, l, xin, xin_b, last):
        nc, S, i, d, db = self.nc, self.S, self.inp, self.scr, self.dbufs
        V, A, P, M = self.V, self.A, self.P, self.M
        with ExitStack() as es:
            WG, bWG = self.sb(es, "WG", [128, 8, 3072], BF16)
            bWGg = [[S.buf("WG_%d" % f)] * 3 for f in range(8)]
            self.phase_bufs += [g[0] for g in bWGg]

            def load_wg(f):
                for br in range(3):
                    c0 = br * 1024 + f * 128
                    self.ldc(WG[:, :, c0:c0 + 128], i["w_in"][l, :, NA + c0:NA + c0 + 128].rearrange("(k p) n -> p k n", p=128),
                             w=[bWGg[f][br]] if br == 0 else (), pw=[bWGg[f][br]] if br else ())
            load_wg(0)
            Wba, bWba = self.sb(es, "Wba", [128, 2, D], BF16)
            Wbg, bWbg = self.sb(es, "Wbg", [128, 4, D], BF16)
            Wbs, bWbs = self.sb(es, "Wbs", [128, 2, D], BF16)
            Wo, bWo = self.sb(es, "Wo", [128, 8, D], BF16)
            self.ldc(Wba[:], i["w_br_att"][l].rearrange("(k p) n -> p k n", p=128), w=[bWba])
            self.ldc(Wbg[:], i["w_br_gla"][l].rearrange("(k p) n -> p k n", p=128), w=[bWbg])
            self.ldc(Wbs[:], i["w_br_s5"][l].rearrange("(k p) n -> p k n", p=128), w=[bWbs])
            for f in range(1, 8):
                load_wg(f)
            for k in range(8):
                self.ldc(Wo[:, k, :], i["w_out"][l, k * 128:(k + 1) * 128, :], pw=[bWo])
            hTs = [self.sb(es, "dhT%d" % k, [128, 8, 512], BF16) for k in range(2)]
            srcs = [self.sb(es, "dsrc%d" % k, [128, 8, 512], BF16) for k in range(2)]
            sgs = [self.sb(es, "dsg%d" % k, [128, 512]) for k in range(3)]
            macc, bmacc = self.sb(es, "macc", [128, 512])
            tacc, btacc = self.sb(es, "tacc", [128, 512])
            mT, bmT = self.sb(es, "mT", [128, 8, 512], BF16)
            xts = [self.sb(es, "dxt%d" % k, [128, D]) for k in range(2)]
            xos = [self.sb(es, "dxo%d" % k, [128, D]) for k in range(2)]
            junk, bjunk = self.sb(es, "djunk", [128, 512])
            ss2, bss2 = self.sb(es, "dss2", [128, 2])
            ss, bss = self.sb(es, "dss", [128, 1])
            rs, brs = self.sb(es, "drs", [128, 1])
            tt, btt = self.sb(es, "dtt", [128, D])
            branches = ((Wba, bWba, 0, 2), (Wbg, bWbg, 2, 4), (Wbs, bWbs, 6, 2))
            ti = 0
            for bi, (t0, n, isctx) in enumerate(self.blocks):
                if isctx and last:
                    continue
                which = 1 if isctx else 0
                nt = n // 128
                hT, bh = hTs[bi % 2]
                sr, bsr = srcs[bi % 2]
                self.ld(hT[:, :, 0:n], d["HT"][:, t0:t0 + n].rearrange("(k p) t -> p k t", p=128), r=[db["HT"]], w=[bh])
                self.ld(sr[:, 0:2, 0:n], d["OATT"][:, t0:t0 + n].rearrange("(k p) t -> p k t", p=128), r=[db["OATT"]], w=[bsr])
                self.ld(sr[:, 2:6, 0:n], d["OGLA"][:, t0:t0 + n].rearrange("(k p) t -> p k t", p=128), r=[db["OGLA"]], pw=[bsr])
                self.ld(sr[:, 6:8, 0:n], d["OS5"][:, t0:t0 + n].rearrange("(k p) t -> p k t", p=128), r=[db["OS5"]], pw=[bsr])
                for f in range(8):
                    for br, (W, bW, k0, nk) in enumerate(branches):
                        pb, bpb = self.ps[br]
                        pg, bpg = self.ps[3 + br]
                        for k in range(nk):
                            M(lambda k=k: nc.tensor.matmul(pb[:, 0:n], lhsT=W[:, k, f * 128:(f + 1) * 128], rhs=sr[:, k0 + k, 0:n],
                                                           start=(k == 0), stop=(k == nk - 1)), [bW, bsr], pw=[bpb] if k else (), w=[bpb] if k == 0 else ())
                        for k in range(8):
                            M(lambda k=k: nc.tensor.matmul(pg[:, 0:n], lhsT=WG[:, k, br * 1024 + f * 128:br * 1024 + (f + 1) * 128], rhs=hT[:, k, 0:n],
                                                           start=(k == 0), stop=(k == 7)), [bWGg[f][br], bh], pw=[bpg] if k else (), w=[bpg] if k == 0 else ())
                        sg, bsg = sgs[br]
                        A(lambda: nc.scalar.activation(out=sg[:, 0:n], in_=pg[:, 0:n], func=AF.Sigmoid), [bpg], [bsg])
                        if br == 0:
                            V(lambda: nc.vector.tensor_tensor(out=macc[:, 0:n], in0=sg[:, 0:n], in1=pb[:, 0:n], op=ALU.mult), [bsg, bpb], [bmacc])
                        else:
                            V(lambda: nc.vector.tensor_tensor(out=tacc[:, 0:n], in0=sg[:, 0:n], in1=pb[:, 0:n], op=ALU.mult), [bsg, bpb], [btacc])
                            if br == 1:
                                P(lambda: nc.gpsimd.tensor_tensor(out=macc[:, 0:n], in0=macc[:, 0:n], in1=tacc[:, 0:n], op=ALU.add), [bmacc, btacc], [bmacc])
                            else:
                                P(lambda: nc.gpsimd.tensor_tensor(out=mT[:, f, 0:n], in0=macc[:, 0:n], in1=tacc[:, 0:n], op=ALU.add), [bmacc, btacc],
                                  pw=[bmT] if f else (), w=[bmT] if f == 0 else ())
                for j in range(nt):
                    r0 = t0 + j * 128
                    xt, bx = xts[ti % 2]
                    xo, bxo = xos[ti % 2]
                    ti += 1
                    self.ld(xt[:], xin[r0:r0 + 128, :], r=[xin_b], w=[bx])
                    for hf in range(2):
                        py, bpy = self.ps[6 + hf]
                        for k in range(8):
                            M(lambda k=k: nc.tensor.matmul(py[:], lhsT=mT[:, k, j * 128:(j + 1) * 128], rhs=Wo[:, k, hf * 512:(hf + 1) * 512],
                                                           start=(k == 0), stop=(k == 7)), [bmT, bWo], pw=[bpy] if k else (), w=[bpy] if k == 0 else ())
                        if hf == 0:
                            A(lambda: nc.scalar.activation(out=junk[:], in_=py[:], func=AF.Square, accum_out=ss2[:, 0:1]), [bpy], [bjunk, bss2])
                        else:
                            A(lambda: nc.scalar.activation(out=junk[:], in_=py[:], func=AF.Square, accum_out=ss2[:, 1:2]), [bpy], [bjunk], pw=[bss2])
                    V(lambda: nc.vector.tensor_tensor(out=ss[:], in0=ss2[:, 0:1], in1=ss2[:, 1:2], op=ALU.add), [bss2], [bss])
                    self.rstd_from_ss(ss[:], rs[:], bss, brs, D)
                    for hf in range(2):
                        py, bpy = self.ps[6 + hf]
                        cs = slice(hf * 512, (hf + 1) * 512)
                        V(lambda: nc.vector.scalar_tensor_tensor(out=tt[:, cs], in0=py[:], scalar=rs[:, 0:1], in1=self.GG[:, which, 0, cs],
                                                                 op0=ALU.mult, op1=ALU.mult), [bpy, brs, self.b_GG], pw=[btt] if hf else (), w=[btt] if hf == 0 else ())
                    P(lambda: nc.gpsimd.tensor_tensor(out=xo[:], in0=tt[:], in1=xt[:], op=ALU.add), [btt, bx], [bxo])
                    self.st(d["X2"][r0:r0 + 128, :], xo[:], r=[bxo], pw=[db["X2"]])
            S.barrier()
            S.recycle(self.phase_bufs)
            self.phase_bufs = []

    def phase_e(self, l, last):
        nc, S, i, d, db = self.nc, self.S, self.inp, self.scr, self.dbufs
        V, A, P, M = self.V, self.A, self.P, self.M
        with ExitStack() as es:
            Wup, bWup = self.sb(es, "Wup", [128, 8, 2 * DFF], BF16)
            bWupg = [[S.buf("Wup_%d" % g)] * 2 for g in range(11)]
            self.phase_bufs += [g[0] for g in bWupg]
            for g in range(11):
                for av in range(2):
                    c0 = av * DFF + g * 256
                    self.ldc(Wup[:, :, c0:c0 + 256], i["ffn_up"][l, :, c0:c0 + 256].rearrange("(k p) n -> p k n", p=128),
                             w=[bWupg[g][av]] if av == 0 else (), pw=[bWupg[g][av]] if av else ())
            Wdn, bWdn = self.sb(es, "Wdn", [128, 22, D], BF16)
            bWdng = [S.buf("Wdn_%d" % g) for g in range(11)]
            self.phase_bufs += bWdng
            for g in range(11):
                self.ldc(Wdn[:, 2 * g:2 * g + 2, :], i["ffn_down"][l, g * 256:(g + 1) * 256, :].rearrange("(k p) n -> p k n", p=128), w=[bWdng[g]])
            cp, bcp = self.sb(es, "cp", [128, 4, 44])
            self.ld(cp[:], i["convp"][l], w=[bcp])
            xts = [self.sb(es, "ext%d" % k, [128, D]) for k in range(1)]
            tmp = (self.sb(es, "exn", [128, D]), self.sb(es, "ess", [128, 1]), self.sb(es, "ers", [128, 1]))
            hT, bh = self.sb(es, "ehT", [128, 8, 512], BF16)
            gT, bgT = self.sb(es, "gT", [128, 22, 512], BF16)
            ua, bua = self.sb(es, "ua", [128, 512])
            uv, buv = self.sb(es, "uv", [128, 512])
            us_, bus = self.sb(es, "usl", [128, 512])
            junk, bjunk = uv, buv
            ss2, bss2 = self.sb(es, "ess2", [128, 2])
            ss, bss = self.sb(es, "ess1", [128, 1])
            rs, brs = self.sb(es, "ers1", [128, 1])
            tt, btt = self.sb(es, "ett", [128, D])
            xo, bxo = tt, btt
            xm, bxm = xts[0]
            pst = (self.ps[0], self.ps[1])
            segs = [(0, 256, 1)] + [(256, self.T, 0)]
            T_ = self.T
            self.st(d["XHS"], d["X2"][T_ - 1:T_, :], r=[db["X2"]], w=[db["XHS"]], sbuf=S.buf("xh_dma"))
            S.collective("AllGather", ALU.bypass, self.groups, d["XHS"], d["XHG"], [db["XHS"]], [db["XHG"]], "halo")
            xn_t, bxn_t = tmp[0]
            self.ld(tt[0:1, :], d["XHG"][0:1, :], r=[db["XHG"]], w=[btt])
            self.ld(xn_t[0:1, :], d["XHG"][1:2, :], r=[db["XHG"]], w=[bxn_t])
            V(lambda: nc.vector.tensor_scalar(out=tt[0:1, :], in0=tt[0:1, :], scalar1=self.selw[0:1, 0:1], scalar2=None, op0=ALU.mult),
              [btt, self.b_selw], [btt])
            V(lambda: nc.vector.scalar_tensor_tensor(out=tt[0:1, :], in0=xn_t[0:1, :], scalar=self.selw[0:1, 1:2], in1=tt[0:1, :], op0=ALU.mult, op1=ALU.add),
              [bxn_t, self.b_selw, btt], [btt])
            self.st(d["XHX"], tt[0:1, :], r=[btt], w=[db["XHX"]])
            for (s0, s1, which) in segs:
                if which == 1 and last:
                    continue
                oa = s0
                while oa < s1:
                    ob = min(oa + 510, s1)
                    ra = oa - 1 if oa > s0 else oa
                    rb_ = ob + 1 if ob < s1 else ob
                    halo = (which == 0 and ob == s1)
                    nrows = rb_ - ra + (1 if halo else 0)
                    j = 0
                    r = ra
                    while r < rb_:
                        nr = min(128, rb_ - r)
                        xt, bx = xts[0]
                        self.ld(xt[0:nr, :], d["X2"][r:r + nr, :], r=[db["X2"]], w=[bx])
                        nr2 = nr
                        if halo and r + nr == rb_:
                            assert nr < 128
                            self.ld(xt[nr:nr + 1, :], d["XHX"], r=[db["XHX"]], pw=[bx])
                            nr2 = nr + 1
                        self.norm_tile(xt, bx, nr2, which, 1, hT, bh, r - ra, tmp, pst)
                        r += nr
                        j += 1
                    lo, hi = oa - ra, ob - ra
                    nout = hi - lo
                    l0 = 1 if lo == 0 else 0
                    r1 = 1 if hi == nrows else 0
                    for cf in range(22):
                        res = []
                        for av, (uu, buu) in enumerate(((ua, bua), (uv, buv))):
                            pz, bpz = self.ps[2 + av + 2 * (cf % 2)]
                            c0 = av * DFF + cf * 128
                            ci = av * 22 + cf
                            for k in range(8):
                                M(lambda k=k: nc.tensor.matmul(pz[:, 0:nrows], lhsT=Wup[:, k, c0:c0 + 128], rhs=hT[:, k, 0:nrows],
                                                               start=(k == 0), stop=(k == 7)), [bWupg[cf // 2][av], bh], pw=[bpz] if k else (), w=[bpz] if k == 0 else ())
                            V(lambda: nc.vector.tensor_scalar(out=uu[:, 0:nout], in0=pz[:, lo:hi], scalar1=cp[:, 1, ci:ci + 1], scalar2=cp[:, 3, ci:ci + 1],
                                                              op0=ALU.mult, op1=ALU.add), [bpz, bcp], [buu])
                            V(lambda: nc.vector.scalar_tensor_tensor(out=uu[:, l0:nout], in0=pz[:, lo + l0 - 1:hi - 1], scalar=cp[:, 0, ci:ci + 1],
                                                                     in1=uu[:, l0:nout], op0=ALU.mult, op1=ALU.add), [bpz, bcp, buu], [buu])
                            V(lambda: nc.vector.scalar_tensor_tensor(out=uu[:, 0:nout - r1], in0=pz[:, lo + 1:hi + 1 - r1], scalar=cp[:, 2, ci:ci + 1],
                                                                     in1=uu[:, 0:nout - r1], op0=ALU.mult, op1=ALU.add), [bpz, bcp, buu], [buu])
                        A(lambda: nc.scalar.activation(out=us_[:, 0:nout], in_=ua[:, 0:nout], func=AF.Silu), [bua], [bus])
                        P(lambda: nc.gpsimd.tensor_tensor(out=gT[:, cf, 0:nout], in0=us_[:, 0:nout], in1=uv[:, 0:nout], op=ALU.mult), [bus, buv],
                          pw=[bgT] if cf else (), w=[bgT] if cf == 0 else ())
                    jo = 0
                    while jo * 128 < nout:
                        no = min(128, nout - jo * 128)
                        r0 = oa + jo * 128
                        self.ld(xm[0:no, :], d["X2"][r0:r0 + no, :], r=[db["X2"]], w=[bxm])
                        for hf in range(2):
                            py, bpy = self.ps[6 + hf]
                            for k in range(22):
                                M(lambda k=k: nc.tensor.matmul(py[0:no, :], lhsT=gT[:, k, jo * 128:jo * 128 + no], rhs=Wdn[:, k, hf * 512:(hf + 1) * 512],
                                                               start=(k == 0), stop=(k == 21)), [bgT, bWdng[k // 2]], pw=[bpy] if k else (), w=[bpy] if k == 0 else ())
                            if hf == 0:
                                A(lambda: nc.scalar.activation(out=junk[0:no, :], in_=py[0:no, :], func=AF.Square, accum_out=ss2[0:no, 0:1]), [bpy], [bjunk, bss2])
                            else:
                                A(lambda: nc.scalar.activation(out=junk[0:no, :], in_=py[0:no, :], func=AF.Square, accum_out=ss2[0:no, 1:2]), [bpy], [bjunk], pw=[bss2])
                        V(lambda: nc.vector.tensor_tensor(out=ss[0:no, :], in0=ss2[0:no, 0:1], in1=ss2[0:no, 1:2], op=ALU.add), [bss2], [bss])
                        self.rstd_from_ss(ss[0:no, :], rs[0:no, :], bss, brs, D, no)
                        for hf in range(2):
                            py, bpy = self.ps[6 + hf]
                            cs = slice(hf * 512, (hf + 1) * 512)
                            V(lambda: nc.vector.scalar_tensor_tensor(out=tt[0:no, cs], in0=py[0:no, :], scalar=rs[0:no, 0:1], in1=self.GG[0:no, which, 1, cs],
                                                                     op0=ALU.mult, op1=ALU.mult), [bpy, brs, self.b_GG], pw=[btt] if hf else (), w=[btt] if hf == 0 else ())
                        P(lambda: nc.gpsimd.tensor_tensor(out=xo[0:no, :], in0=tt[0:no, :], in1=xm[0:no, :], op=ALU.add), [btt, bxm], [bxo])
                        if last:
                            self.st(self.out[r0 - 256:r0 - 256 + no, :], xo[0:no, :], r=[bxo], pw=[self.out_buf])
                        else:
                            self.st(d["X1"][r0:r0 + no, :], xo[0:no, :], r=[bxo], pw=[db["X1"]])
                        jo += 1
                    oa = ob
            S.barrier()
            S.recycle(self.phase_bufs)
            self.phase_bufs = []


def host_consts(n_lat):
    rows = n_lat // 64
    row = np.repeat(np.arange(rows, dtype=np.float32), 64)
    col = np.tile(np.arange(64, dtype=np.float32), rows)
    n_freq = 16
    inv_freq = (np.float32(10000.0) ** (-np.arange(n_freq, dtype=np.float32) / n_freq)).astype(np.float32)
    ang = np.stack([row[:, None] * inv_freq, col[:, None] * inv_freq], axis=1)
    rope = np.concatenate([np.cos(ang).reshape(n_lat, 32), np.sin(ang).reshape(n_lat, 32)], axis=1).astype(np.float32)
    jj = np.arange(128) % 64
    ii = np.arange(64)
    gmask = np.stack([(jj[:, None] <= ii[None, :]), (jj[:, None] >= ii[None, :])], axis=1).astype(np.float32)
    scanmask = np.ones((128, 2, 512), np.float32)
    scanmask[:, 0, ::64] = 0.0
    scanmask[:, 1, 63::64] = 0.0
    return dict(ident=np.eye(128, dtype=np.float32), rope=rope, gmask=gmask, scanmask=scanmask)


def host_layout(inputs, L):
    f = lambda a: np.ascontiguousarray(np.asarray(a, dtype=np.float32))
    p = {}
    p["ada_w"] = f(inputs["ada_w"])[:L]
    p["ada_b"] = f(inputs["ada_b"])[:L]
    colz = lambda v: v.reshape(L, -1, 128).transpose(0, 2, 1)
    p["npre"] = f(np.stack([colz(f(inputs["norm_mix_pre"])[:L]), colz(f(inputs["norm_ffn_pre"])[:L])], axis=2))
    p["npost"] = f(np.stack([f(inputs["norm_mix_post"])[:L], f(inputs["norm_ffn_post"])[:L]], axis=1))
    p["w_in"] = f(inputs["w_in"])[:L]
    qn, kn = f(inputs["q_norm"])[:L], f(inputs["k_norm"])[:L]
    p["qkg"] = f(np.concatenate([np.tile(qn, (1, 4)), np.tile(kn, (1, 2))], axis=1))
    gw = f(inputs["gla_gate_w"])[:L]
    gwp = np.zeros((L, 32, 2, 256), np.float32)
    gwp[:, 0:16, 0, :] = gw[:, 0]
    gwp[:, 16:32, 1, :] = gw[:, 1]
    p["gatew"] = gwp
    gb = f(inputs["gla_gate_b"])[:L]
    p["gateb"] = f(gb.reshape(L, 2, 2, 128).transpose(0, 3, 1, 2).reshape(L, 128, 4))
    p["glanorm"] = f(np.tile(f(inputs["gla_out_norm"])[:L], (1, 4)))
    sm = lambda a: a.reshape(L, 2, 8, 2, 64).transpose(0, 1, 3, 4, 2).reshape(L, 2, 128, 8)
    are, aim = f(inputs["s5_a_re"])[:L], f(inputs["s5_a_im"])[:L]
    ldt = np.broadcast_to(f(inputs["s5_log_dt"])[:L][..., None], (L, 2, 16, 64))
    p["s5a"] = f(np.stack([sm(are), sm(aim), sm(f(ldt))], axis=3))
    bre, bim = f(inputs["s5_b_re"])[:L], f(inputs["s5_b_im"])[:L]
    s5b = np.zeros((L, 2, 32, 2, 8, 128), np.float32)
    cre, cim = f(inputs["s5_c_re"])[:L], f(inputs["s5_c_im"])[:L]
    s5c = np.zeros((L, 2, 128, 2, 8, 64), np.float32)
    for m in range(8):
        for gl in range(2):
            g = 2 * m + gl
            for ri, (bb, cc_) in enumerate(((bre, cre), (bim, cim))):
                s5b[:, :, gl * 16:(gl + 1) * 16, ri, m, gl * 64:(gl + 1) * 64] = bb[:, :, g].transpose(0, 1, 3, 2)
                s5c[:, :, gl * 64:(gl + 1) * 64, ri, m, (m % 2) * 32 + gl * 16:(m % 2) * 32 + (gl + 1) * 16] = cc_[:, :, g].transpose(0, 1, 3, 2)
    p["s5b"], p["s5c"] = s5b, s5c
    p["s5d"] = f(colz(f(inputs["s5_d"])[:L]))
    p["s5glub"] = f(colz(f(inputs["s5_glu_b"])[:L]))
    p["s5gluw"] = f(inputs["s5_glu_w"])[:L]
    for k in ("w_br_att", "w_br_gla", "w_br_s5", "w_out", "ffn_up", "ffn_down"):
        p[k] = f(inputs[k])[:L]
    cw = f(inputs["ffn_conv_w"])[:L]
    cb = f(inputs["ffn_conv_b"])[:L]
    c4 = np.concatenate([cw, cb[:, None, :]], axis=1)
    p["convp"] = f(c4.reshape(L, 4, 44, 128).transpose(0, 3, 1, 2))
    return p


_CACHE = {}

N_LAT_FULL = 8192


def swap_dirs(p):
    q = dict(p)
    gw = np.zeros_like(p["gatew"])
    gw[:, 16:32, 0, :] = p["gatew"][:, 16:32, 1, :]
    gw[:, 0:16, 1, :] = p["gatew"][:, 0:16, 0, :]
    q["gatew"] = gw
    gb = p["gateb"].reshape(-1, 128, 2, 2)
    q["gateb"] = np.ascontiguousarray(gb[:, :, ::-1, :]).reshape(-1, 128, 4)
    for k in ("s5a", "s5b", "s5c"):
        q[k] = np.ascontiguousarray(p[k][:, ::-1])
    cp = p["convp"]
    q["convp"] = np.ascontiguousarray(np.stack([cp[:, :, 2], cp[:, :, 1], cp[:, :, 0], cp[:, :, 3]], axis=2))
    return q


def run(inputs, n_lat, depth, n_batch, dbg=()):
    x = np.asarray(inputs["x"], dtype=np.float32)
    ctx = np.asarray(inputs["ctx"], dtype=np.float32)
    c = np.asarray(inputs["c"], dtype=np.float32)
    c_ctx = np.asarray(inputs["c_ctx"], dtype=np.float32)
    nl = n_lat // 2
    key = (nl, depth, tuple(dbg))
    if key not in _CACHE:
        b = Builder(nl, depth, dbg)
        b.build()
        _CACHE[key] = b
    b = _CACHE[key]
    p0 = host_layout(inputs, depth)
    cst = host_consts(n_lat)
    rope_full = cst.pop("rope")
    p0.update(cst)
    p1 = swap_dirs(p0)
    in_maps = []
    for core in range(8):
        bi = (core // 2) % n_batch
        r = core % 2
        m = dict(p0 if r == 0 else p1)
        if r == 0:
            m["xcat"] = np.ascontiguousarray(np.concatenate([ctx[bi], x[bi, :nl]], axis=0))
            m["rope"] = np.ascontiguousarray(rope_full[:nl])
            m["selw"] = np.ascontiguousarray(np.tile(np.array([[0.0, 1.0]], np.float32), (128, 1)))
        else:
            m["xcat"] = np.ascontiguousarray(np.concatenate([ctx[bi][::-1], x[bi, nl:n_lat][::-1]], axis=0))
            m["rope"] = np.ascontiguousarray(rope_full[nl:n_lat][::-1])
            m["selw"] = np.ascontiguousarray(np.tile(np.array([[1.0, 0.0]], np.float32), (128, 1)))
        cc = np.concatenate([c[bi].reshape(8, 128).T, c_ctx.reshape(8, 128).T], axis=1)
        m["cc"] = np.ascontiguousarray(cc.astype(np.float32))
        in_maps.append(m)
    res = run_bass_kernel_spmd(b.nc, in_maps, core_ids=list(range(8)))
    outs = []
    for bi in range(n_batch):
        o0 = np.asarray(res.results[2 * bi]["out"], dtype=np.float32)
        o1 = np.asarray(res.results[2 * bi + 1]["out"], dtype=np.float32)[::-1]
        outs.append(np.concatenate([o0, o1], axis=0))
    return np.stack(outs, axis=0), res.results


def kernel(**inputs):
    n_b = np.asarray(inputs["x"]).shape[0]
    out, _ = run(inputs, N_LAT_FULL, 4, n_b)
    return out
```
